# Optimizing a Trainium2 kernel written in Bass

```python
import jax
import jax.numpy as jnp
from jax import lax
import numpy as np

D_MODEL = 2048
BATCH = 2
SEQ = 8192
DEPTH = 4

N_MIXERS = 2
SC_KERNEL = 3
DN_HEAD_DIM = 128
DN_KEY_HEADS = D_MODEL // DN_HEAD_DIM
DN_VALUE_HEADS = 2 * DN_KEY_HEADS
DN_KEY_DIM = DN_KEY_HEADS * DN_HEAD_DIM
DN_VALUE_DIM = DN_VALUE_HEADS * DN_HEAD_DIM
DN_QKV_DIM = 2 * DN_KEY_DIM + DN_VALUE_DIM
DN_PROJ_DIM = DN_QKV_DIM + DN_VALUE_DIM + 2 * DN_VALUE_HEADS
DN_CONV_KERNEL = 4
DN_CHUNK = 64
FFN_HIDDEN = -(-8 * D_MODEL // (3 * 256)) * 256
DEEPNORM_ALPHA = (2 * DEPTH) ** 0.25
DEEPNORM_BETA = (8 * DEPTH) ** -0.25
LN_EPS = 1e-5
RMS_EPS = 1e-6
N_SC_LAYERS = (DEPTH + 1) // 2
N_DN_LAYERS = DEPTH // 2

kernel_name = 'hybrid_shortconv_gdn_deepnorm'


def layer_norm(x, gain, bias):
    xf = x.astype(jnp.float32)
    mu = jnp.mean(xf, -1, keepdims=True)
    var = jnp.mean(jnp.square(xf - mu), -1, keepdims=True)
    return ((xf - mu) * lax.rsqrt(var + LN_EPS) * gain + bias).astype(x.dtype)


def l2_normalize(t):
    return t * lax.rsqrt(jnp.sum(t * t, -1, keepdims=True) + RMS_EPS)


def causal_depthwise_conv(u, w):
    k = w.shape[0]
    s = u.shape[1]
    u_pad = jnp.pad(u, ((0, 0), (k - 1, 0), (0, 0)))
    out = u_pad[:, 0:s] * w[0]
    for j in range(1, k):
        out = out + u_pad[:, j:j + s] * w[j]
    return out


def short_conv_mixer(x, w_in, conv_w, w_out):
    proj = x @ w_in
    gate_b = proj[..., :D_MODEL]
    gate_c = proj[..., D_MODEL:2 * D_MODEL]
    h = proj[..., 2 * D_MODEL:]
    y = causal_depthwise_conv(gate_c * h, conv_w)
    return (gate_b * y) @ w_out


def chunk_gated_delta_rule(q, k, v, beta, g):
    b, s, h, dk = q.shape
    dv = v.shape[-1]
    c = DN_CHUNK
    n = s // c

    def blocks(t):
        return t.reshape(b, n, c, h, -1).transpose(0, 3, 1, 2, 4)

    q, k, v = blocks(q), blocks(k), blocks(v)
    beta = blocks(beta[..., None])[..., 0]
    g = lax.cumsum(blocks(g[..., None])[..., 0], axis=3)
    causal = jnp.tril(jnp.ones((c, c), dtype=bool))
    strict = jnp.tril(jnp.ones((c, c), dtype=bool), -1)
    decay = jnp.exp(jnp.where(causal, g[..., :, None] - g[..., None, :], -jnp.inf))
    k_beta = k * beta[..., None]
    v_beta = v * beta[..., None]
    a_kk = jnp.where(strict, jnp.einsum('bhncd,bhnmd->bhncm', k_beta, k) * decay, 0.0) + jnp.eye(c, dtype=q.dtype)
    rhs = jnp.concatenate([v_beta, k_beta * jnp.exp(g)[..., None]], axis=-1)
    sol = lax.linalg.triangular_solve(a_kk, rhs, left_side=True, lower=True)
    u = sol[..., :dv]
    w = sol[..., dv:]
    a_qk = jnp.einsum('bhncd,bhnmd->bhncm', q, k) * decay
    q_dec = q * jnp.exp(g)[..., None]
    k_tail = k * jnp.exp(g[..., -1:] - g)[..., None]
    chunk_decay = jnp.exp(g[..., -1])

    def step(state, inp):
        qd, kt, uc, wc, aqk, cd = inp
        v_new = uc - jnp.einsum('bhck,bhkv->bhcv', wc, state)
        o = jnp.einsum('bhck,bhkv->bhcv', qd, state) + jnp.einsum('bhcm,bhmv->bhcv', aqk, v_new)
        state = state * cd[..., None, None] + jnp.einsum('bhck,bhcv->bhkv', kt, v_new)
        return state, o

    xs = (jnp.moveaxis(q_dec, 2, 0), jnp.moveaxis(k_tail, 2, 0), jnp.moveaxis(u, 2, 0),
          jnp.moveaxis(w, 2, 0), jnp.moveaxis(a_qk, 2, 0), jnp.moveaxis(chunk_decay, 2, 0))
    _, o = lax.scan(step, jnp.zeros((b, h, dk, dv), q.dtype), xs)
    return o.transpose(1, 0, 3, 2, 4).reshape(b, s, h, dv)


def gated_deltanet(x, w_in, conv_w, a_log, dt_bias, norm_w, w_out):
    b, s, _ = x.shape
    proj = x @ w_in
    o1 = DN_QKV_DIM
    o2 = o1 + DN_VALUE_DIM
    o3 = o2 + DN_VALUE_HEADS
    qkv = proj[..., :o1]
    z = proj[..., o1:o2]
    beta_raw = proj[..., o2:o3]
    a_raw = proj[..., o3:]
    qkv = jax.nn.silu(causal_depthwise_conv(qkv, conv_w)).astype(jnp.float32)
    q = qkv[..., :DN_KEY_DIM].reshape(b, s, DN_KEY_HEADS, DN_HEAD_DIM)
    k = qkv[..., DN_KEY_DIM:2 * DN_KEY_DIM].reshape(b, s, DN_KEY_HEADS, DN_HEAD_DIM)
    v = qkv[..., 2 * DN_KEY_DIM:].reshape(b, s, DN_VALUE_HEADS, DN_HEAD_DIM)
    rep = DN_VALUE_HEADS // DN_KEY_HEADS
    q = jnp.repeat(l2_normalize(q) * (DN_HEAD_DIM ** -0.5), rep, axis=2)
    k = jnp.repeat(l2_normalize(k), rep, axis=2)
    beta = jax.nn.sigmoid(beta_raw.astype(jnp.float32))
    g = -jnp.exp(a_log.astype(jnp.float32)) * jax.nn.softplus(
        a_raw.astype(jnp.float32) + dt_bias.astype(jnp.float32))
    o = chunk_gated_delta_rule(q, k, v, beta, g)
    o = o * lax.rsqrt(jnp.mean(jnp.square(o), -1, keepdims=True) + RMS_EPS) * norm_w.astype(jnp.float32)
    o = o * jax.nn.silu(z.astype(jnp.float32).reshape(b, s, DN_VALUE_HEADS, DN_HEAD_DIM))
    return o.reshape(b, s, DN_VALUE_DIM).astype(x.dtype) @ w_out


def swiglu(x, w_gate_up, w_down):
    gu = x @ w_gate_up
    return (jax.nn.silu(gu[..., :FFN_HIDDEN]) * gu[..., FFN_HIDDEN:]) @ w_down


def setup_inputs(seed: int = 0) -> dict:
    key = jax.random.key(seed)
    ks = jax.random.split(key, 16)

    def normal(k, shape, scale):
        return jax.random.normal(k, shape, jnp.float32) * scale

    dt = jnp.exp(jax.random.uniform(ks[7], (N_DN_LAYERS, DN_VALUE_HEADS), jnp.float32,
                                    np.log(1e-3), np.log(1e-1)))
    return {
        'x': normal(ks[0], (BATCH, SEQ, D_MODEL), 1.0),
        'sc_w_in': normal(ks[1], (N_SC_LAYERS, D_MODEL, 3 * D_MODEL), D_MODEL ** -0.5),
        'sc_conv_w': normal(ks[2], (N_SC_LAYERS, SC_KERNEL, D_MODEL), SC_KERNEL ** -0.5),
        'sc_w_out': normal(ks[3], (N_SC_LAYERS, D_MODEL, D_MODEL), DEEPNORM_BETA * D_MODEL ** -0.5),
        'dn_w_in': normal(ks[4], (N_DN_LAYERS, D_MODEL, DN_PROJ_DIM), D_MODEL ** -0.5),
        'dn_conv_w': normal(ks[5], (N_DN_LAYERS, DN_CONV_KERNEL, DN_QKV_DIM), DN_CONV_KERNEL ** -0.5),
        'dn_a_log': jnp.log(jax.random.uniform(ks[6], (N_DN_LAYERS, DN_VALUE_HEADS), jnp.float32, 1.0, 16.0)),
        'dn_dt_bias': dt + jnp.log(-jnp.expm1(-dt)),
        'dn_norm_w': 1.0 + normal(ks[8], (N_DN_LAYERS, DN_HEAD_DIM), 0.01),
        'dn_w_out': normal(ks[9], (N_DN_LAYERS, DN_VALUE_DIM, D_MODEL), DEEPNORM_BETA * DN_VALUE_DIM ** -0.5),
        'ffn_w_gate_up': normal(ks[10], (DEPTH, D_MODEL, 2 * FFN_HIDDEN), D_MODEL ** -0.5),
        'ffn_w_down': normal(ks[11], (DEPTH, FFN_HIDDEN, D_MODEL), DEEPNORM_BETA * FFN_HIDDEN ** -0.5),
        'ln_gain': 1.0 + normal(ks[12], (DEPTH, 2, D_MODEL), 0.02),
        'ln_bias': normal(ks[13], (DEPTH, 2, D_MODEL), 0.02),
    }


def reference(x, sc_w_in, sc_conv_w, sc_w_out, dn_w_in, dn_conv_w, dn_a_log, dn_dt_bias,
              dn_norm_w, dn_w_out, ffn_w_gate_up, ffn_w_down, ln_gain, ln_bias):
    for i in range(DEPTH):
        j = i // N_MIXERS
        if i % N_MIXERS == 0:
            h = short_conv_mixer(x, sc_w_in[j], sc_conv_w[j], sc_w_out[j])
        else:
            h = gated_deltanet(x, dn_w_in[j], dn_conv_w[j], dn_a_log[j], dn_dt_bias[j],
                               dn_norm_w[j], dn_w_out[j])
        x = layer_norm(DEEPNORM_ALPHA * x + h, ln_gain[i, 0], ln_bias[i, 0])
        x = layer_norm(DEEPNORM_ALPHA * x + swiglu(x, ffn_w_gate_up[i], ffn_w_down[i]),
                       ln_gain[i, 1], ln_bias[i, 1])
    return x
```

```python
import numpy as np
from contextlib import ExitStack
import concourse.bass as bass
import concourse.mybir as mybir
from concourse.bass_utils import run_bass_kernel_spmd

F32 = mybir.dt.float32
BF16 = mybir.dt.bfloat16
AF = mybir.ActivationFunctionType
ALU = mybir.AluOpType

D_MODEL = 2048
BATCH = 2
SEQ = 8192
DEPTH = 4
N_CORES = 8
FFN_HIDDEN = 5632
ALPHA = (2 * DEPTH) ** 0.25
LN_EPS = 1e-5
RMS_EPS = 1e-6
HD = 128


class Buf:
    def __init__(self, name, t=None):
        self.name = name
        self.t = t
        self.w = None
        self.r = {}
        self.dsem = None
        self.dcnt = 0
        self.parent = None
        self.children = []
        self.is_psum = False

    def view(self, name, ap):
        c = Buf(name, ap)
        c.parent = self
        self.children.append(c)
        return c


class K:
    def __init__(self, nc, es):
        self.nc = nc
        self.es = es
        self.eng = {"pe": nc.tensor, "act": nc.scalar, "dve": nc.vector,
                    "pool": nc.gpsimd, "sp": nc.sync}
        self.sem = {e: es.enter_context(nc.semaphore("s_" + e)) for e in self.eng}
        self.cnt = {e: 0 for e in self.eng}
        self.waited = {e: {} for e in self.eng}
        self.semobj = {}
        self.n_dsem = 0
        self.uid = 0
        self.out_stamps = []
        self.allbufs = []
        self.es_cur = None
        self.names = {}

    def sbuf(self, name, shape, dt):
        es = self.es_cur if getattr(self, "es_cur", None) is not None else self.es
        n = self.names.get(name, 0)
        self.names[name] = n + 1
        if n:
            name = "%s_v%d" % (name, n)
        t = es.enter_context(self.nc.sbuf_tensor(name, list(shape), dt))
        b = Buf(name, t)
        self.allbufs.append(b)
        return b

    def barrier(self):
        tgt = {}
        for e in self.eng:
            if self.cnt[e] > 0:
                tgt[id(self.sem[e])] = (self.sem[e], self.cnt[e])
        for b in self.allbufs:
            if b.dsem is not None and b.dcnt > 0:
                tgt[id(b.dsem)] = (b.dsem, 16 * b.dcnt)
        for e in self.eng:
            w = self.waited[e]
            for key, (sm, v) in tgt.items():
                if w.get(key, 0) < v:
                    self.eng[e].wait_ge(sm, v)
                    w[key] = v

    def psum(self, name, shape, dt):
        t = self.es.enter_context(self.nc.psum_tensor(name, list(shape), dt))
        b = Buf(name, t)
        b.is_psum = True
        return b

    def dsem_for(self, b):
        if b.dsem is None:
            b.dsem = self.es.enter_context(self.nc.semaphore("d%d" % self.n_dsem))
            self.n_dsem += 1
        return b.dsem

    def _wait(self, e, reads, writes):
        need = {}

        def add(st):
            if st is None:
                return
            s, v = st
            key = id(s)
            self.semobj[key] = s
            if need.get(key, 0) < v:
                need[key] = v

        def addr(b):
            for key, v in b.r.items():
                if need.get(key, 0) < v:
                    need[key] = v

        for b in reads:
            add(b.w)
            if b.is_psum:
                addr(b)
            if b.parent is not None:
                add(b.parent.w)
            for ch in b.children:
                add(ch.w)
        for b in writes:
            add(b.w)
            addr(b)
            if b.parent is not None:
                add(b.parent.w)
                addr(b.parent)
            for ch in b.children:
                add(ch.w)
                addr(ch)
        w = self.waited[e]
        own = id(self.sem[e])
        for key, v in need.items():
            if key == own and (e == "pe" or v > self.cnt[e]):
                continue
            if w.get(key, 0) < v:
                self.eng[e].wait_ge(self.semobj[key], v)
                w[key] = v

    def _stamp(self, st, reads, writes):
        key = id(st[0])
        self.semobj[key] = st[0]
        for b in reads:
            if b.r.get(key, 0) < st[1]:
                b.r[key] = st[1]
        for b in writes:
            b.w = st
            b.r = {}

    def op(self, e, fn, reads=(), writes=(), inc=True):
        self.nops = getattr(self, "nops", 0) + 1
        if self.nops > getattr(self, "limit", 1 << 60):
            return None
        self._wait(e, reads, writes)
        ins = fn(self.eng[e])
        if inc:
            ins.then_inc(self.sem[e], 1)
            self.cnt[e] += 1
            st = (self.sem[e], self.cnt[e])
        else:
            st = (self.sem[e], self.cnt[e] + 1)
        self._stamp(st, reads, writes)
        return ins

    def dma(self, e, out, in_, reads=(), writes=(), owner=None, is_output=False):
        self.nops = getattr(self, "nops", 0) + 1
        if self.nops > getattr(self, "limit", 1 << 60):
            return None
        self._wait(e, reads, writes)
        sem = self.dsem_for(owner)
        ins = self.eng[e].dma_start(out=out, in_=in_)
        ins.then_inc(sem, 16)
        owner.dcnt += 1
        st = (sem, 16 * owner.dcnt)
        self._stamp(st, reads, writes)
        if is_output:
            self.out_stamps.append(st)
        return ins

    def finish(self):
        need = {}
        for s, v in self.out_stamps:
            key = id(s)
            self.semobj[key] = s
            need[key] = max(need.get(key, 0), v)
        for key, v in need.items():
            self.eng["sp"].wait_ge(self.semobj[key], v)


class WStream:
    def __init__(self, k, name, ch_elems, n_stage, n_bf, cast_engines=("pool",)):
        self.k = k
        self.ch = ch_elems
        self.stage = [k.sbuf("%s_st%d" % (name, i), [128, ch_elems], F32) for i in range(n_stage)]
        self.bf = [k.sbuf("%s_bf%d" % (name, i), [128, ch_elems], BF16) for i in range(n_bf)]
        self.n = 0
        self.cast_engines = cast_engines
        self.queue = []
        self.ready = []

    def push(self, dram_ap):
        self.queue.append(dram_ap)

    def _fetch(self, item):
        k = self.k
        dram_ap, n = item
        st = self.stage[self.n % len(self.stage)]
        bf = self.bf[self.n % len(self.bf)]
        ce = self.cast_engines[self.n % len(self.cast_engines)]
        self.n += 1
        k.dma("sp", out=st.t[:, 0:n], in_=dram_ap, writes=[st], owner=st)
        if ce == "act":
            k.op("act", lambda e: e.activation(out=bf.t[:, 0:n], in_=st.t[:, 0:n], func=AF.Copy),
                 reads=[st], writes=[bf])
        else:
            k.op(ce, lambda e: e.tensor_copy(out=bf.t[:, 0:n], in_=st.t[:, 0:n]), reads=[st], writes=[bf])
        return bf

    def prefetch(self, depth):
        while self.queue and len(self.ready) < depth:
            self.ready.append(self._fetch(self.queue.pop(0)))

    def take(self, n):
        self.prefetch(n)
        out = [self.ready.pop(0) for _ in range(n)]
        self.prefetch(len(self.bf) - n)
        return out


class Cfg:
    def __init__(self, D=D_MODEL, NT=2048, F=FFN_HIDDEN, fgroups=(16, 16, 12), NKH=None):
        self.D = D
        self.NT = NT
        self.F = F
        self.KC = D // 128
        self.NTT = NT // 128
        self.NTG = NT // 512
        self.fgroups = tuple(fgroups)
        assert sum(fgroups) * 128 == F
        self.NKH = NKH if NKH is not None else D // HD
        self.NVH = 2 * self.NKH
        self.CH = max(self.KC * 128, 2048)


class Env:
    def __init__(self, k, cfg, consts, alloc_act=True, ws_slots=6):
        self.k = k
        self.cfg = cfg
        c = cfg
        nc = k.nc
        self.xT = [k.sbuf("xT%d" % g, [128, c.KC, 512], BF16) for g in range(c.NTG)]
        self.xhT = k.sbuf("xhT", [128, c.KC, 4], BF16)
        self.AK = max(c.KC, max(c.fgroups), min(16, c.NVH))
        self.ws = WStream(k, "w", c.CH, 2, ws_slots)
        self.lnviews = {}
        self.rings = {}
        self.pA = [k.psum("pA%d" % i, [128, 512], F32) for i in range(6)]
        self.pT = [k.psum("pT%d" % i, [128, 1024], BF16) for i in range(2)]
        self.pi = 0
        self.pti = 0
        self.ident_bf = k.sbuf("ident_bf", [128, 128], BF16)
        self.ident_f = k.sbuf("ident_f", [128, 128], F32)
        k.dma("sp", out=self.ident_f.t[:], in_=consts["ident"], writes=[self.ident_f], owner=self.ident_f)
        k.op("dve", lambda e: e.tensor_copy(out=self.ident_bf.t[:], in_=self.ident_f.t[:]),
             reads=[self.ident_f], writes=[self.ident_bf])
        if alloc_act:
            self.alloc_act()

    def alloc_act(self):
        k, c = self.k, self.cfg
        self.actT = [k.sbuf("actT%d" % g, [128, self.AK, 512], BF16) for g in range(c.NTG)]
        fl = [a.t[:].rearrange("p a b -> p (a b)") for a in self.actT]
        nb = self.AK * 512
        self.lnviews = {}

        def carve(g, off_bf, n_bf, dt, name):
            ap = fl[g][:, off_bf:off_bf + n_bf]
            if dt == F32:
                ap = ap.bitcast(F32)
            return self.actT[g].view(name, ap)

        if 2 * c.D * 2 <= nb and c.NTG >= 3:
            self.lnviews["lnr"] = [carve(0, 0, 2 * c.D, F32, "lnr0"), carve(0, 2 * c.D, 2 * c.D, F32, "lnr1")]
            self.lnviews["lnh"] = [carve(1, 0, 2 * c.D, F32, "lnh0")]
            self.lnviews["xb"] = [carve(1, 2 * c.D, c.D, BF16, "xb0"), carve(1, 3 * c.D, c.D, BF16, "xb1")]
            self.lnviews["gain"] = [carve(2, 0, 2 * c.D, F32, "gain")]
            self.lnviews["bias"] = [carve(2, 2 * c.D, 2 * c.D, F32, "bias")]

    def phase(self):
        env = self

        class _P:
            def __enter__(p):
                p.before = set(env.rings)
                p.es2 = ExitStack()
                p.es2.__enter__()
                p.prev = env.k.es_cur
                env.k.es_cur = p.es2
                return p

            def __exit__(p, *a):
                if a[0] is None:
                    env.k.barrier()
                env.k.es_cur = p.prev
                p.es2.__exit__(*a)
                for n in list(env.rings):
                    if n not in p.before:
                        del env.rings[n]
                return False

        return _P()

    def pnext(self):
        b = self.pA[self.pi % len(self.pA)]
        self.pi += 1
        return b

    def ptnext(self):
        b = self.pT[self.pti % len(self.pT)]
        self.pti += 1
        return b

    def ring(self, name, shape, dt, n):
        if name in self.lnviews and name not in self.rings:
            self.rings[name] = [self.lnviews[name], 0]
        if name not in self.rings:
            self.rings[name] = [[self.k.sbuf("%s%d" % (name, i), shape, dt) for i in range(n)], 0]
        r = self.rings[name]
        b = r[0][r[1] % len(r[0])]
        r[1] += 1
        return b


def mm(env, ps_ap, lhsT, rhs, start, stop, reads, writes, inc=None):
    env.k.op("pe", lambda e: e.matmul(ps_ap, lhsT=lhsT, rhs=rhs, start=start, stop=stop),
             reads=reads, writes=writes, inc=(stop if inc is None else inc))


def emit_xT_tile(env, xf, tt):
    k, c = env.k, env.cfg
    xb = env.ring("xb", [128, c.D], BF16, 2)
    k.op("act", lambda e: e.activation(out=xb.t[:], in_=xf.t[:], func=AF.Copy), reads=[xf], writes=[xb])
    g, o = tt // 4, (tt % 4) * 128
    for k0 in range(0, c.KC, 8):
        n = min(8, c.KC - k0)
        pt = env.ptnext()
        for i in range(n):
            kc = k0 + i
            k.op("pe", lambda e: e.transpose(pt.t[:, i * 128:(i + 1) * 128], xb.t[:, kc * 128:(kc + 1) * 128],
                                             env.ident_bf.t[:]),
                 reads=[xb, env.ident_bf], writes=[pt], inc=(i == n - 1))
        k.op("dve", lambda e: e.tensor_copy(out=env.xT[g].t[:, k0:k0 + n, o:o + 128],
                                            in_=pt.t[:, 0:n * 128].rearrange("p (a b) -> p a b", b=128)),
             reads=[pt], writes=[env.xT[g]])


def emit_halo(env, xh_dram):
    k, c = env.k, env.cfg
    hf = k.sbuf("halo_f", [4, c.D], F32)
    hb = k.sbuf("halo_b", [4, c.D], BF16)
    k.dma("sp", out=hf.t[:], in_=xh_dram, writes=[hf], owner=hf)
    k.op("act", lambda e: e.activation(out=hb.t[:], in_=hf.t[:], func=AF.Copy), reads=[hf], writes=[hb])
    pt = env.ptnext()
    for kc in range(c.KC):
        k.op("pe", lambda e: e.transpose(pt.t[:, kc * 4:(kc + 1) * 4], hb.t[:, kc * 128:(kc + 1) * 128],
                                         env.ident_bf.t[0:4, 0:4]),
             reads=[hb, env.ident_bf], writes=[pt], inc=(kc == c.KC - 1))
    k.op("dve", lambda e: e.tensor_copy(out=env.xhT.t[:],
                                        in_=pt.t[:, 0:c.KC * 4].rearrange("p (a b) -> p a b", b=4)),
         reads=[pt], writes=[env.xhT])


def load_x_to_xT(env, x_dram):
    k, c = env.k, env.cfg
    for tt in range(c.NTT):
        xf = env.ring("lnr", [128, c.D], F32, 2)
        k.dma("sp", out=xf.t[:], in_=x_dram[tt * 128:(tt + 1) * 128, :], writes=[xf], owner=xf)
        emit_xT_tile(env, xf, tt)


def type_a(env, units, n_per_unit, epilogue, halo_fn=None):
    k, c = env.k, env.cfg
    ws = env.ws
    for u in units:
        for ap in u:
            ws.push((ap, c.KC * 128))
    for ui, u in enumerate(units):
        wb = ws.take(n_per_unit)
        if halo_fn is not None:
            halo_fn(ui, wb)
        for tg in range(c.NTG):
            ps = [env.pnext() for _ in range(n_per_unit)]
            for ci in range(n_per_unit):
                for kc in range(c.KC):
                    mm(env, ps[ci].t[:, :], wb[ci].t[:, kc * 128:(kc + 1) * 128], env.xT[tg].t[:, kc, :],
                       kc == 0, kc == c.KC - 1, [wb[ci], env.xT[tg]], [ps[ci]])
            epilogue(ui, tg, ps)


def type_b(env, KCb, slab_aps, hbufs, h_dram):
    k, c = env.k, env.cfg
    ws = env.ws
    npc = KCb // 4
    for s in range(c.D // 512):
        for ap in slab_aps[s]:
            ws.push((ap, 2048))
    for s in range(c.D // 512):
        pcs = ws.take(npc)
        for tt in range(c.NTT):
            g, o = tt // 4, (tt % 4) * 128
            ps = env.pnext()
            for kc in range(KCb):
                pc = pcs[kc // 4]
                mm(env, ps.t[:, :], env.actT[g].t[:, kc, o:o + 128], pc.t[:, (kc % 4) * 512:(kc % 4 + 1) * 512],
                   kc == 0, kc == KCb - 1, [env.actT[g], pc], [ps])
            ob = env.ring("ob", [128, 512], F32, 3)
            k.op("act", lambda e: e.activation(out=ob.t[:], in_=ps.t[:, :], func=AF.Copy), reads=[ps], writes=[ob])
            k.dma("act", out=h_dram[tt * 128:(tt + 1) * 128, s * 512:(s + 1) * 512], in_=ob.t[:],
                  reads=[ob], writes=[hbufs[tt]], owner=ob)


def ln_phase(env, x_dram, xbufs, parts, gain_dram, bias_dram, out_dram, obufs, make_xT, final):
    k, c = env.k, env.cfg
    gb = env.ring("gain", [128, c.D], F32, 1)
    bb = env.ring("bias", [128, c.D], F32, 1)
    k.dma("sp", out=gb.t[:], in_=gain_dram.partition_broadcast(128), writes=[gb], owner=gb)
    k.dma("sp", out=bb.t[:], in_=bias_dram.partition_broadcast(128), writes=[bb], owner=bb)
    nch = c.D // 512
    for tt in range(c.NTT):
        r = env.ring("lnr", [128, c.D], F32, 2)
        rows = slice(tt * 128, (tt + 1) * 128)
        k.dma("sp", out=r.t[:], in_=x_dram[rows, :], reads=[xbufs[tt]] if xbufs else [], writes=[r], owner=r)
        for pi, (hd, hb) in enumerate(parts):
            hin = env.ring("lnh", [128, c.D], F32, 2)
            k.dma("sp", out=hin.t[:], in_=hd[rows, :], reads=[hb[tt]], writes=[hin], owner=hin)
            if pi == 0:
                k.op("act", lambda e: e.activation(out=r.t[:], in_=r.t[:], func=AF.Copy, scale=float(ALPHA)),
                     reads=[r], writes=[r])
            k.op("pool", lambda e: e.tensor_tensor(out=r.t[:], in0=r.t[:], in1=hin.t[:], op=ALU.add),
                 reads=[r, hin], writes=[r])
        st = env.ring("lnst", [128, nch * 6], F32, 2)
        for i in range(nch):
            k.op("dve", lambda e: e.bn_stats(out=st.t[:, i * 6:(i + 1) * 6], in_=r.t[:, i * 512:(i + 1) * 512]),
                 reads=[r], writes=[st])
        mv = env.ring("lnmv", [128, 4], F32, 2)
        k.op("dve", lambda e: e.bn_aggr(out=mv.t[:, 0:2], in_=st.t[:]), reads=[st], writes=[mv])
        k.op("act", lambda e: e.activation(out=mv.t[:, 2:3], in_=mv.t[:, 1:2], func=AF.Sqrt, bias=float(LN_EPS)),
             reads=[mv], writes=[mv])
        k.op("dve", lambda e: e.reciprocal(out=mv.t[:, 2:3], in_=mv.t[:, 2:3]), reads=[mv], writes=[mv])
        k.op("dve", lambda e: e.tensor_scalar(out=r.t[:], in0=r.t[:], scalar1=mv.t[:, 0:1], scalar2=mv.t[:, 2:3],
                                              op0=ALU.subtract, op1=ALU.mult), reads=[r, mv], writes=[r])
        k.op("pool", lambda e: e.tensor_tensor(out=r.t[:], in0=r.t[:], in1=gb.t[:], op=ALU.mult),
             reads=[r, gb], writes=[r])
        k.op("pool", lambda e: e.tensor_tensor(out=r.t[:], in0=r.t[:], in1=bb.t[:], op=ALU.add),
             reads=[r, bb], writes=[r])
        k.dma("act", out=out_dram[rows, :], in_=r.t[:], reads=[r], writes=[obufs[tt]], owner=r, is_output=final)
        if make_xT:
            emit_xT_tile(env, r, tt)


def tile_a(W, col_starts):
    K_ = W.shape[0]
    KC = K_ // 128
    Wr = W.reshape(KC, 128, W.shape[1])
    out = np.empty((len(col_starts), 128, KC * 128), np.float32)
    for i, c0 in enumerate(col_starts):
        out[i] = Wr[:, :, c0:c0 + 128].transpose(1, 0, 2).reshape(128, KC * 128)
    return out


def tile_b(W, r0, KCb):
    D = W.shape[1]
    Wr = W[r0:r0 + KCb * 128].reshape(KCb // 4, 4, 128, D // 512, 512)
    return np.ascontiguousarray(Wr.transpose(3, 0, 2, 1, 4)).reshape(D // 512, KCb // 4, 128, 2048)


def dram_in(nc, name, shape):
    return nc.dram_tensor(name, list(shape), F32, kind="ExternalInput").ap()


def ffn_and_ln(env, io, x_dram, xbufs, h_dram, hbufs, out_dram, obufs, final, make_xT):
    k, c = env.k, env.cfg
    hc0 = 0
    parts = []
    for gi, G in enumerate(c.fgroups):
        units = [[io["w_gu_t"][hc0 + j, 0], io["w_gu_t"][hc0 + j, 1]] for j in range(G)]

        def epi(ui, tg, ps):
            sg = env.ring("sg", [128, 512], F32, 2)
            k.op("act", lambda e: e.activation(out=sg.t[:], in_=ps[0].t[:, :], func=AF.Silu), reads=[ps[0]], writes=[sg])
            k.op("dve", lambda e: e.tensor_tensor(out=env.actT[tg].t[:, ui, :], in0=ps[1].t[:, :], in1=sg.t[:], op=ALU.mult),
                 reads=[ps[1], sg], writes=[env.actT[tg]])

        type_a(env, units, 2, epi)
        slabs = [[io["w_dn_t"][gi][s, pc] for pc in range(G // 4)] for s in range(c.D // 512)]
        type_b(env, G, slabs, hbufs[gi], h_dram[gi])
        parts.append((h_dram[gi], hbufs[gi]))
        hc0 += G
    ln_phase(env, x_dram, xbufs, parts, io["ln"][2, :], io["ln"][3, :], out_dram, obufs, make_xT, final)


def build_sc_program(cfg, stages=9):
    c = cfg
    nc = bass.Bass("TRN2", target_bir_lowering=False)
    io = {}
    io["x"] = dram_in(nc, "x", [c.NT, c.D])
    io["xh"] = dram_in(nc, "xh", [4, c.D])
    io["ident"] = dram_in(nc, "ident", [128, 128])
    io["w_in_t"] = dram_in(nc, "w_in_t", [c.KC, 3, 128, c.KC * 128])
    io["cw"] = dram_in(nc, "cw", [128, c.KC * 3])
    io["w_out_t"] = dram_in(nc, "w_out_t", [c.D // 512, c.KC // 4, 128, 2048])
    io["w_gu_t"] = dram_in(nc, "w_gu_t", [c.F // 128, 2, 128, c.KC * 128])
    io["w_dn_t"] = [dram_in(nc, "w_dn_t%d" % gi, [c.D // 512, G // 4, 128, 2048]) for gi, G in enumerate(c.fgroups)]
    io["ln"] = dram_in(nc, "ln", [4, c.D])
    xo = nc.dram_tensor("xo", [c.NT, c.D], F32, kind="ExternalOutput").ap()
    P = max(1, len(c.fgroups))
    h_dram = [nc.dram_tensor("h%d" % p, [c.NT, c.D], F32, kind="Internal").ap() for p in range(P)]
    x1 = nc.dram_tensor("x1", [c.NT, c.D], F32, kind="Internal").ap()
    with ExitStack() as es:
        k = K(nc, es)
        env = Env(k, c, io)
        hbufs = [[Buf("h%d_%d" % (p, tt)) for tt in range(c.NTT)] for p in range(P)]
        x1bufs = [Buf("x1_%d" % tt) for tt in range(c.NTT)]
        xobufs = [Buf("xo_%d" % tt) for tt in range(c.NTT)]
        cw = k.sbuf("cw_sb", [128, c.KC * 3], F32)
        k.dma("sp", out=cw.t[:], in_=io["cw"], writes=[cw], owner=cw)
        emit_halo(env, io["xh"])
        load_x_to_xT(env, io["x"])

        units = [[io["w_in_t"][j, 0], io["w_in_t"][j, 1], io["w_in_t"][j, 2]] for j in range(c.KC)]
        state = {}

        def halo_fn(j, wb):
            ps = env.pnext()
            for ci in (1, 2):
                for kc in range(c.KC):
                    mm(env, ps.t[:, (ci - 1) * 4:(ci - 1) * 4 + 4], wb[ci].t[:, kc * 128:(kc + 1) * 128],
                       env.xhT.t[:, kc, :], kc == 0, kc == c.KC - 1, [wb[ci], env.xhT], [ps],
                       inc=(kc == c.KC - 1))
            hh = env.ring("hh", [128, 4], F32, 2)
            k.op("act", lambda e: e.activation(out=hh.t[:], in_=ps.t[:, 4:8], func=AF.Copy), reads=[ps], writes=[hh])
            u = env.ring("u", [128, 2 + 512], F32, 3)
            k.op("dve", lambda e: e.tensor_tensor(out=u.t[:, 0:2], in0=ps.t[:, 2:4], in1=hh.t[:, 2:4], op=ALU.mult),
                 reads=[ps, hh], writes=[u])
            state["u"] = u

        def epi(j, tg, ps):
            hs = env.ring("hs", [128, 512], F32, 2)
            k.op("act", lambda e: e.activation(out=hs.t[:], in_=ps[2].t[:, :], func=AF.Copy), reads=[ps[2]], writes=[hs])
            if tg == 0:
                u = state["u"]
            else:
                u = env.ring("u", [128, 2 + 512], F32, 3)
                up = state["u"]
                k.op("dve", lambda e: e.tensor_copy(out=u.t[:, 0:2], in_=up.t[:, 512:514]), reads=[up], writes=[u])
                state["u"] = u
            k.op("dve", lambda e: e.tensor_tensor(out=u.t[:, 2:514], in0=ps[1].t[:, :], in1=hs.t[:], op=ALU.mult),
                 reads=[ps[1], hs], writes=[u])
            y = env.ring("y", [128, 512], F32, 2)
            k.op("act", lambda e: e.activation(out=y.t[:], in_=u.t[:, 0:512], func=AF.Copy, scale=cw.t[:, j * 3:j * 3 + 1]),
                 reads=[u, cw], writes=[y])
            k.op("dve", lambda e: e.scalar_tensor_tensor(out=y.t[:], in0=u.t[:, 1:513], scalar=cw.t[:, j * 3 + 1:j * 3 + 2],
                                                         in1=y.t[:], op0=ALU.mult, op1=ALU.add), reads=[u, cw, y], writes=[y])
            k.op("dve", lambda e: e.scalar_tensor_tensor(out=y.t[:], in0=u.t[:, 2:514], scalar=cw.t[:, j * 3 + 2:j * 3 + 3],
                                                         in1=y.t[:], op0=ALU.mult, op1=ALU.add), reads=[u, cw, y], writes=[y])
            k.op("dve", lambda e: e.tensor_tensor(out=env.actT[tg].t[:, j, :], in0=ps[0].t[:, :], in1=y.t[:], op=ALU.mult),
                 reads=[ps[0], y], writes=[env.actT[tg]])

        type_a(env, units, 3, epi, halo_fn)
        slabs = [[io["w_out_t"][s, pc] for pc in range(c.KC // 4)] for s in range(c.D // 512)]
        if stages >= 2:
            type_b(env, c.KC, slabs, hbufs[0], h_dram[0])
        if stages >= 3:
            ln_phase(env, io["x"], None, [(h_dram[0], hbufs[0])], io["ln"][0, :], io["ln"][1, :], x1, x1bufs, True, False)
        if stages >= 4:
            ffn_and_ln(env, io, x1, x1bufs, h_dram, hbufs, xo, xobufs, True, False)
        k.finish()
    return nc


def sc_inputs(cfg, w_in, conv_w, w_out, w_gu, w_dn, ln4):
    c = cfg
    D, F = c.D, c.F
    d = {}
    d["ident"] = np.eye(128, dtype=np.float32)
    cols = []
    for j in range(c.KC):
        cols += [j * 128, D + j * 128, 2 * D + j * 128]
    d["w_in_t"] = tile_a(w_in, cols).reshape(c.KC, 3, 128, c.KC * 128)
    d["cw"] = np.ascontiguousarray(conv_w.reshape(3, c.KC, 128).transpose(2, 1, 0)).reshape(128, c.KC * 3)
    d["w_out_t"] = tile_b(w_out, 0, c.KC)
    d.update(ffn_inputs(cfg, w_gu, w_dn))
    d["ln"] = np.ascontiguousarray(ln4, dtype=np.float32)
    return d


def ffn_inputs(cfg, w_gu, w_dn):
    c = cfg
    d = {}
    cols = []
    for j in range(c.F // 128):
        cols += [j * 128, c.F + j * 128]
    d["w_gu_t"] = tile_a(w_gu, cols).reshape(c.F // 128, 2, 128, c.KC * 128)
    r0 = 0
    for gi, G in enumerate(c.fgroups):
        d["w_dn_t%d" % gi] = tile_b(w_dn, r0, G)
        r0 += G * 128
    return d


def gdn_consts():
    i = np.arange(128)
    same = (i[:, None] // 64) == (i[None, :] // 64)
    c = np.zeros((8, 128, 128), np.float32)
    c[0] = ((i[:, None] <= i[None, :]) & same)
    c[1] = (i[:, None] > i[None, :])
    c[2] = ((i[:, None] > i[None, :]) & same)
    c[3] = ((i[None, :] >= i[:, None]) & same)
    c[4] = same
    c[5] = 1.0
    c[6] = (i[:, None] < 64) * np.ones((1, 128))
    c[7] = (i[:, None] >= 64) * np.ones((1, 128))
    return c


def roundrobin(gens):
    gens = list(gens)
    while gens:
        nxt = []
        for g in gens:
            try:
                next(g)
                nxt.append(g)
            except StopIteration:
                pass
        gens = nxt


LIMIT = [1 << 60]


def build_gdn_program(cfg, mode, dbg=99):
    c = cfg
    NVH, NKH, KC = c.NVH, c.NKH, c.KC
    NV = 256 if mode == "a" else 128
    nch = 4 if mode == "a" else 6
    vgroups = [min(16, NVH - g0) for g0 in range(0, NVH, 16)]
    nc = bass.Bass("TRN2", target_bir_lowering=False)
    io = {}
    io["x"] = dram_in(nc, "x", [c.NT, c.D])
    io["xh"] = dram_in(nc, "xh", [4, c.D])
    io["ident"] = dram_in(nc, "ident", [128, 128])
    io["consts"] = dram_in(nc, "consts", [8, 128, 128])
    io["w_t"] = dram_in(nc, "w_t", [NKH, 6, 128, KC * 128])
    io["w_ba"] = dram_in(nc, "w_ba", [128, KC * 2 * NVH])
    io["cwg"] = dram_in(nc, "cwg", [128, NKH * 16])
    io["alog"] = dram_in(nc, "alog", [1, NVH])
    io["dtb"] = dram_in(nc, "dtb", [1, NVH])
    if mode == "a":
        st_out = nc.dram_tensor("st", [NVH, 128, 256], F32, kind="ExternalOutput").ap()
    else:
        io["normw"] = dram_in(nc, "normw", [1, 128])
        io["sin_pt"] = dram_in(nc, "sin_pt", [3, NVH, 128, 128])
        io["sin_s"] = dram_in(nc, "sin_s", [3, NVH, 128, 128])
        io["w_o_t"] = [dram_in(nc, "w_o_t%d" % gi, [c.D // 512, G // 4, 128, 2048]) for gi, G in enumerate(vgroups)]
        io["w_gu_t"] = dram_in(nc, "w_gu_t", [c.F // 128, 2, 128, KC * 128])
        io["w_dn_t"] = [dram_in(nc, "w_dn_t%d" % gi, [c.D // 512, G // 4, 128, 2048]) for gi, G in enumerate(c.fgroups)]
        io["ln"] = dram_in(nc, "ln", [4, c.D])
        xo = nc.dram_tensor("xo", [c.NT, c.D], F32, kind="ExternalOutput").ap()
        P = max(len(vgroups), len(c.fgroups))
        h_dram = [nc.dram_tensor("h%d" % p, [c.NT, c.D], F32, kind="Internal").ap() for p in range(P)]
        x1 = nc.dram_tensor("x1", [c.NT, c.D], F32, kind="Internal").ap()
        onT_d = nc.dram_tensor("onT_d", [NVH, 128, c.NT], BF16, kind="Internal").ap()
    with ExitStack() as es:
        k = K(nc, es)
        k.limit = LIMIT[0]
        env = Env(k, c, io, alloc_act=False, ws_slots=(6 if mode == "a" else 7))
        with env.phase():
            emit_halo(env, io["xh"])
            load_x_to_xT(env, io["x"])
        onbufs = [[Buf("on_%d_%d" % (h, tg)) for tg in range(c.NTG)] for h in range(NVH)]
        if dbg <= 1:
            k.finish()
            return nc

        with env.phase():
            def f32t(name, shape=(128, 128)):
                return k.sbuf(name, list(shape), F32)

            CN = []
            for i in range(8):
                t = f32t("cst%d" % i)
                k.dma("sp", out=t.t[:], in_=io["consts"][i], writes=[t], owner=t)
                CN.append(t)
            TRIBD, UU, STRICT, CAUSALT, BLK, ONES, CH0, CH1 = CN
            cwg = f32t("cwg_sb", (128, NKH * 16))
            k.dma("sp", out=cwg.t[:], in_=io["cwg"], writes=[cwg], owner=cwg)
            wba_f = f32t("wba_f", (128, KC * 2 * NVH))
            wba = k.sbuf("wba_b", [128, KC, 2 * NVH], BF16)
            k.dma("sp", out=wba_f.t[:], in_=io["w_ba"], writes=[wba_f], owner=wba_f)
            k.op("dve", lambda e: e.tensor_copy(out=wba.t[:].rearrange("p a b -> p (a b)"), in_=wba_f.t[:]),
                 reads=[wba_f], writes=[wba])
            alog = f32t("alog_sb", (128, NVH))
            dtb = f32t("dtb_sb", (128, NVH))
            k.dma("sp", out=alog.t[:], in_=io["alog"][0, :].partition_broadcast(128), writes=[alog], owner=alog)
            k.dma("sp", out=dtb.t[:], in_=io["dtb"][0, :].partition_broadcast(128), writes=[dtb], owner=dtb)
            negA = f32t("negA", (128, NVH))
            k.op("act", lambda e: e.activation(out=negA.t[:], in_=alog.t[:], func=AF.Exp), reads=[alog], writes=[negA])
            k.op("dve", lambda e: e.tensor_scalar(out=negA.t[:], in0=negA.t[:], scalar1=-1.0, scalar2=None, op0=ALU.mult),
                 reads=[negA], writes=[negA])
            if mode == "b":
                nwbc = f32t("nwbc")
                k.dma("sp", out=nwbc.t[:], in_=io["normw"][0, :].partition_broadcast(128), writes=[nwbc], owner=nwbc)
            shp = (128, c.NTT, NVH)
            BETA, AX, GRAW, GC, EG = [f32t(n, shp) for n in ("BETA", "AX", "GRAW", "GC", "EG")]
            EKT, BEG = AX, GC
            CD = f32t("CD", (128, c.NTT, 2 * NVH))

            for tt in range(c.NTT):
                g, o = tt // 4, (tt % 4) * 128
                ps = env.pnext()
                for kc in range(KC):
                    mm(env, ps.t[:, 0:2 * NVH], env.xT[g].t[:, kc, o:o + 128], wba.t[:, kc, :], kc == 0, kc == KC - 1,
                       [env.xT[g], wba], [ps])
                k.op("act", lambda e: e.activation(out=BETA.t[:, tt, :], in_=ps.t[:, 0:NVH], func=AF.Sigmoid),
                     reads=[ps], writes=[BETA])
                k.op("dve", lambda e: e.tensor_tensor(out=AX.t[:, tt, :], in0=ps.t[:, NVH:2 * NVH], in1=dtb.t[:], op=ALU.add),
                     reads=[ps, dtb], writes=[AX])
            k.op("act", lambda e: e.activation(out=AX.t[:], in_=AX.t[:], func=AF.Exp), reads=[AX], writes=[AX])
            k.op("act", lambda e: e.activation(out=AX.t[:], in_=AX.t[:], func=AF.Ln, bias=1.0), reads=[AX], writes=[AX])
            for tt in range(c.NTT):
                k.op("dve", lambda e: e.tensor_tensor(out=GRAW.t[:, tt, :], in0=AX.t[:, tt, :], in1=negA.t[:], op=ALU.mult),
                     reads=[AX, negA], writes=[GRAW])
            for tt in range(c.NTT):
                gm = env.ring("gm", [128, 2 * NVH], F32, 2)
                k.op("dve", lambda e: e.tensor_scalar(out=gm.t[:, 0:NVH], in0=GRAW.t[:, tt, :], scalar1=CH0.t[:, 0:1],
                                                      scalar2=None, op0=ALU.mult), reads=[GRAW, CH0], writes=[gm])
                k.op("dve", lambda e: e.tensor_scalar(out=gm.t[:, NVH:2 * NVH], in0=GRAW.t[:, tt, :], scalar1=CH1.t[:, 0:1],
                                                      scalar2=None, op0=ALU.mult), reads=[GRAW, CH1], writes=[gm])
                ps = env.pnext()
                mm(env, ps.t[:, 0:NVH], TRIBD.t[:], GRAW.t[:, tt, :], True, True, [TRIBD, GRAW], [ps])
                mm(env, ps.t[:, 64:64 + NVH], BLK.t[:], GRAW.t[:, tt, :], True, True, [BLK, GRAW], [ps])
                mm(env, ps.t[:, 128:128 + 2 * NVH], ONES.t[:], gm.t[:], True, True, [ONES, gm], [ps])
                k.op("dve", lambda e: e.tensor_copy(out=GC.t[:, tt, :], in_=ps.t[:, 0:NVH]), reads=[ps], writes=[GC])
                k.op("act", lambda e: e.activation(out=EG.t[:, tt, :], in_=ps.t[:, 0:NVH], func=AF.Exp), reads=[ps], writes=[EG])
                k.op("dve", lambda e: e.tensor_tensor(out=EKT.t[:, tt, :], in0=ps.t[:, 64:64 + NVH], in1=GC.t[:, tt, :],
                                                      op=ALU.subtract), reads=[ps, GC], writes=[EKT])
                k.op("act", lambda e: e.activation(out=EKT.t[:, tt, :], in_=EKT.t[:, tt, :], func=AF.Exp),
                     reads=[EKT], writes=[EKT])
                k.op("act", lambda e: e.activation(out=CD.t[:, tt, :], in_=ps.t[:, 128:128 + 2 * NVH], func=AF.Exp),
                     reads=[ps], writes=[CD])
                k.op("dve", lambda e: e.tensor_tensor(out=BEG.t[:, tt, :], in0=BETA.t[:, tt, :], in1=EG.t[:, tt, :], op=ALU.mult),
                     reads=[BETA, EG], writes=[BEG])

            if dbg <= 2:
                k.finish()
                return nc
            Sf = [f32t("Sf%d" % i, (128, NV)) for i in range(2)]
            Sb = [k.sbuf("Sb%d" % i, [128, NV], BF16) for i in range(2)]
            ubuf = [[f32t("u%d_%d" % (i, j), (128, NV)) for j in range(2)] for i in range(2)]
            if NV == 256:
                for i in range(2):
                    for j in range(2):
                        k.op("pool", lambda e: e.memset(ubuf[i][j].t[:], 0.0), writes=[ubuf[i][j]])
            ucnt = [0, 0]
            carry = [f32t("carry%d" % ci, (128, 4)) for ci in range(4)]
            ws = env.ws
            for kh in range(NKH):
                for ci in range(nch):
                    ws.push((io["w_t"][kh, ci], KC * 128))

            for kh in range(NKH):
                wb = ws.take(nch)
                for hh in range(2):
                    h = 2 * kh + hh
                    if mode == "a":
                        k.op("pool", lambda e: e.memset(Sf[hh].t[:, 0:128], 0.0), writes=[Sf[hh]])
                        k.op("pool", lambda e: e.tensor_copy(out=Sf[hh].t[:, 128:256], in_=env.ident_f.t[:]),
                             reads=[env.ident_f], writes=[Sf[hh]])
                    else:
                        k.dma("sp", out=Sf[hh].t[:], in_=io["sin_s"][0, h], writes=[Sf[hh]], owner=Sf[hh])
                        for j in (1, 2):
                            pt_ = env.ring("foldp", [128, 128], F32, 2)
                            sl_ = env.ring("folds", [128, 128], F32, 2)
                            k.dma("sp", out=pt_.t[:], in_=io["sin_pt"][j, h], writes=[pt_], owner=pt_)
                            k.dma("sp", out=sl_.t[:], in_=io["sin_s"][j, h], writes=[sl_], owner=sl_)
                            ps = env.pnext()
                            mm(env, ps.t[:, 0:128], pt_.t[:], Sf[hh].t[:], True, True, [pt_, Sf[hh]], [ps])
                            k.op("dve", lambda e: e.tensor_tensor(out=Sf[hh].t[:], in0=ps.t[:, 0:128], in1=sl_.t[:], op=ALU.add),
                                 reads=[ps, sl_], writes=[Sf[hh]])
                    k.op("act", lambda e: e.activation(out=Sb[hh].t[:], in_=Sf[hh].t[:], func=AF.Copy),
                         reads=[Sf[hh]], writes=[Sb[hh]])
                psh = env.pnext()
                for ci in range(4):
                    for kc in range(KC):
                        mm(env, psh.t[:, ci * 4:ci * 4 + 4], wb[ci].t[:, kc * 128:(kc + 1) * 128], env.xhT.t[:, kc, :],
                           kc == 0, kc == KC - 1, [wb[ci], env.xhT], [psh])
                for ci in range(4):
                    k.op("dve", lambda e: e.tensor_copy(out=carry[ci].t[:, 0:3], in_=psh.t[:, ci * 4 + 1:ci * 4 + 4]),
                         reads=[psh], writes=[carry[ci]])
                TG = {}

                def proj(tg):
                    knT = env.ring("knT", [128, 512], F32, 2)
                    qnT = env.ring("qnT", [128, 512], F32, 2)
                    qb = env.ring("qb", [128, 512], BF16, 2)
                    vT = [env.ring("vT%d" % i, [128, 512], F32, 2) for i in range(2)]
                    sz = [env.ring("sz%d" % i, [128, 512], BF16, 2) for i in range(2)] if mode == "b" else None
                    TG[tg] = dict(knT=knT, qnT=qnT, qb=qb, vT=vT, sz=sz)
                    for ci in range(nch):
                        ps = env.pnext()
                        for kc in range(KC):
                            mm(env, ps.t[:, :], wb[ci].t[:, kc * 128:(kc + 1) * 128], env.xT[tg].t[:, kc, :],
                               kc == 0, kc == KC - 1, [wb[ci], env.xT[tg]], [ps])
                        if ci >= 4:
                            k.op("act", lambda e: e.activation(out=sz[ci - 4].t[:], in_=ps.t[:, :], func=AF.Silu),
                                 reads=[ps], writes=[sz[ci - 4]])
                            continue
                        pre = env.ring("pre", [128, 3 + 512], F32, 2)
                        k.op("dve", lambda e: e.tensor_copy(out=pre.t[:, 0:3], in_=carry[ci].t[:, 0:3]), reads=[carry[ci]], writes=[pre])
                        k.op("act", lambda e: e.activation(out=pre.t[:, 3:515], in_=ps.t[:, :], func=AF.Copy), reads=[ps], writes=[pre])
                        k.op("dve", lambda e: e.tensor_copy(out=carry[ci].t[:, 0:3], in_=pre.t[:, 512:515]), reads=[pre], writes=[carry[ci]])
                        cb = (kh * 4 + ci) * 4
                        ca = env.ring("cacc", [128, 512], F32, 2)
                        k.op("act", lambda e: e.activation(out=ca.t[:], in_=pre.t[:, 0:512], func=AF.Copy, scale=cwg.t[:, cb:cb + 1]),
                             reads=[pre, cwg], writes=[ca])
                        for tap in (1, 2, 3):
                            k.op("dve", lambda e: e.scalar_tensor_tensor(out=ca.t[:], in0=pre.t[:, tap:tap + 512],
                                                                         scalar=cwg.t[:, cb + tap:cb + tap + 1], in1=ca.t[:],
                                                                         op0=ALU.mult, op1=ALU.add), reads=[pre, cwg, ca], writes=[ca])
                        if ci >= 2:
                            k.op("act", lambda e: e.activation(out=vT[ci - 2].t[:], in_=ca.t[:], func=AF.Silu),
                                 reads=[ca], writes=[vT[ci - 2]])
                            continue
                        sl = env.ring("sl", [128, 512], F32, 2)
                        k.op("act", lambda e: e.activation(out=sl.t[:], in_=ca.t[:], func=AF.Silu), reads=[ca], writes=[sl])
                        k.op("pool", lambda e: e.tensor_tensor(out=ca.t[:], in0=sl.t[:], in1=sl.t[:], op=ALU.mult), reads=[sl], writes=[ca])
                        ps2 = env.pnext()
                        mm(env, ps2.t[:, :], ONES.t[:], ca.t[:], True, True, [ONES, ca], [ps2])
                        k.op("act", lambda e: e.activation(out=ca.t[:], in_=ps2.t[:, :], func=AF.Sqrt, bias=float(RMS_EPS)),
                             reads=[ps2], writes=[ca])
                        k.op("dve", lambda e: e.reciprocal(out=ca.t[:], in_=ca.t[:]), reads=[ca], writes=[ca])
                        if ci == 1:
                            k.op("dve", lambda e: e.tensor_tensor(out=knT.t[:], in0=sl.t[:], in1=ca.t[:], op=ALU.mult),
                                 reads=[sl, ca], writes=[knT])
                        else:
                            k.op("dve", lambda e: e.scalar_tensor_tensor(out=qnT.t[:], in0=sl.t[:], scalar=float(HD ** -0.5),
                                                                         in1=ca.t[:], op0=ALU.mult, op1=ALU.mult),
                                 reads=[sl, ca], writes=[qnT])
                            k.op("pool", lambda e: e.tensor_copy(out=qb.t[:], in_=qnT.t[:]), reads=[qnT], writes=[qb])

                tile = {}

                def prep(tt):
                    cols = slice((tt % 4) * 128, (tt % 4 + 1) * 128)
                    knT, qnT, vT = TG[tt // 4]["knT"], TG[tt // 4]["qnT"], TG[tt // 4]["vT"]
                    T = {}
                    psk = env.pnext()
                    k.op("pe", lambda e: e.transpose(psk.t[:, 0:128], knT.t[:, cols], env.ident_f.t[:]),
                         reads=[knT, env.ident_f], writes=[psk])
                    psv = env.pnext()
                    for hh in range(2):
                        k.op("pe", lambda e: e.transpose(psv.t[:, hh * 128:(hh + 1) * 128], vT[hh].t[:, cols], env.ident_f.t[:]),
                             reads=[vT[hh], env.ident_f], writes=[psv])
                    T["kbg"], T["ktl"], T["vb"] = [], [], []
                    for hh in range(2):
                        h = 2 * kh + hh
                        kbg = env.ring("kbg%d" % hh, [128, 128], F32, 2)
                        ktl = env.ring("ktl%d" % hh, [128, 128], BF16, 2)
                        vb = env.ring("vb%d" % hh, [128, 128], F32, 2)
                        k.op("act", lambda e: e.activation(out=kbg.t[:], in_=psk.t[:, 0:128], func=AF.Copy, scale=BEG.t[:, tt, h:h + 1]),
                             reads=[psk, BEG], writes=[kbg])
                        k.op("dve", lambda e: e.tensor_scalar(out=ktl.t[:], in0=psk.t[:, 0:128], scalar1=EKT.t[:, tt, h:h + 1],
                                                              scalar2=None, op0=ALU.mult), reads=[psk, EKT], writes=[ktl])
                        k.op("dve", lambda e: e.tensor_scalar(out=vb.t[:], in0=psv.t[:, hh * 128:(hh + 1) * 128],
                                                              scalar1=BETA.t[:, tt, h:h + 1], scalar2=None, op0=ALU.mult),
                             reads=[psv, BETA], writes=[vb])
                        T["kbg"].append(kbg); T["ktl"].append(ktl); T["vb"].append(vb)
                    yield
                    pskk = env.pnext()
                    mm(env, pskk.t[:, 0:128], knT.t[:, cols], knT.t[:, cols], True, True, [knT], [pskk])
                    mm(env, pskk.t[:, 128:256], knT.t[:, cols], qnT.t[:, cols], True, True, [knT, qnT], [pskk])
                    KKs = env.ring("KKs", [128, 128], F32, 2)
                    QKm = env.ring("QKm", [128, 128], F32, 2)
                    k.op("dve", lambda e: e.tensor_tensor(out=KKs.t[:], in0=pskk.t[:, 0:128], in1=STRICT.t[:], op=ALU.mult),
                         reads=[pskk, STRICT], writes=[KKs])
                    k.op("dve", lambda e: e.tensor_tensor(out=QKm.t[:], in0=pskk.t[:, 128:256], in1=CAUSALT.t[:], op=ALU.mult),
                         reads=[pskk, CAUSALT], writes=[QKm])
                    T["KKs"], T["QKm"] = KKs, QKm
                    T["u"], T["wT"], T["aqk"] = [None, None], [None, None], [None, None]
                    tile[tt] = T
                    yield

                def solve(tt, hh):
                    T = tile[tt]
                    h = 2 * kh + hh
                    G = env.ring("G%d" % hh, [128, 128], F32, 2)
                    k.op("act", lambda e: e.activation(out=G.t[:], in_=TRIBD.t[:], func=AF.Copy, scale=GRAW.t[:, tt, h:h + 1]),
                         reads=[TRIBD, GRAW], writes=[G])
                    ps = env.pnext()
                    mm(env, ps.t[:, 0:128], G.t[:], UU.t[:], True, True, [G, UU], [ps])
                    mm(env, ps.t[:, 128:256], UU.t[:], G.t[:], True, True, [G, UU], [ps])
                    Dec = env.ring("Dec%d" % hh, [128, 256], F32, 1)
                    k.op("act", lambda e: e.activation(out=Dec.t[:], in_=ps.t[:, 0:256], func=AF.Exp), reads=[ps], writes=[Dec])
                    yield
                    A = env.ring("A%d" % hh, [128, 128], F32, 3)
                    k.op("dve", lambda e: e.scalar_tensor_tensor(out=A.t[:], in0=T["KKs"].t[:], scalar=BETA.t[:, tt, h:h + 1],
                                                                 in1=Dec.t[:, 0:128], op0=ALU.mult, op1=ALU.mult),
                         reads=[T["KKs"], BETA, Dec], writes=[A])
                    aqk = env.ring("aqk%d" % hh, [128, 128], BF16, 2)
                    k.op("pool", lambda e: e.tensor_tensor(out=aqk.t[:], in0=T["QKm"].t[:], in1=Dec.t[:, 128:256], op=ALU.mult),
                         reads=[T["QKm"], Dec], writes=[aqk])
                    T["aqk"][hh] = aqk
                    ps = env.pnext()
                    k.op("pe", lambda e: e.transpose(ps.t[:, 0:128], A.t[:], env.ident_f.t[:]), reads=[A, env.ident_f], writes=[ps])
                    X = env.ring("X%d" % hh, [128, 128], F32, 3)
                    B = env.ring("B%d" % hh, [128, 128], F32, 3)
                    k.op("dve", lambda e: e.tensor_tensor(out=X.t[:], in0=env.ident_f.t[:], in1=ps.t[:, 0:128], op=ALU.subtract),
                         reads=[env.ident_f, ps], writes=[X])
                    k.op("act", lambda e: e.activation(out=B.t[:], in_=ps.t[:, 0:128], func=AF.Copy), reads=[ps], writes=[B])
                    yield
                    for m in (1, 2, 4, 8, 16):
                        ps = env.pnext()
                        mm(env, ps.t[:, 0:128], B.t[:], A.t[:], True, True, [A, B], [ps])
                        if m < 16:
                            mm(env, ps.t[:, 128:256], A.t[:], B.t[:], True, True, [A, B], [ps])
                        A2 = env.ring("A%d" % hh, [128, 128], F32, 3)
                        k.op("act", lambda e: e.activation(out=A2.t[:], in_=ps.t[:, 0:128], func=AF.Copy), reads=[ps], writes=[A2])
                        if m < 16:
                            B2 = env.ring("B%d" % hh, [128, 128], F32, 3)
                            k.op("dve", lambda e: e.tensor_copy(out=B2.t[:], in_=ps.t[:, 128:256]), reads=[ps], writes=[B2])
                        ps2 = env.pnext()
                        mm(env, ps2.t[:, 0:128], A2.t[:], X.t[:], True, True, [A2, X], [ps2])
                        X2 = env.ring("X%d" % hh, [128, 128], F32, 3)
                        k.op("dve", lambda e: e.tensor_tensor(out=X2.t[:], in0=X.t[:], in1=ps2.t[:, 0:128], op=ALU.add),
                             reads=[X, ps2], writes=[X2])
                        A, X = A2, X2
                        if m < 16:
                            B = B2
                        yield
                    ps = env.pnext()
                    mm(env, ps.t[:, 0:128], X.t[:], T["vb"][hh].t[:], True, True, [X, T["vb"][hh]], [ps])
                    mm(env, ps.t[:, 128:256], T["kbg"][hh].t[:], X.t[:], True, True, [X, T["kbg"][hh]], [ps])
                    u = ubuf[hh][ucnt[hh] % 2]
                    ucnt[hh] += 1
                    k.op("act", lambda e: e.activation(out=u.t[:, 0:128], in_=ps.t[:, 0:128], func=AF.Copy), reads=[ps], writes=[u])
                    wT = env.ring("wT%d" % hh, [128, 128], BF16, 2)
                    k.op("dve", lambda e: e.tensor_copy(out=wT.t[:], in_=ps.t[:, 128:256]), reads=[ps], writes=[wT])
                    T["u"][hh], T["wT"][hh] = u, wT
                    yield

                def recur(tt, hh):
                    T = tile[tt]
                    h = 2 * kh + hh
                    cols = slice((tt % 4) * 128, (tt % 4 + 1) * 128)
                    tg, o = tt // 4, (tt % 4) * 128
                    qb, sz = TG[tg]["qb"], TG[tg]["sz"]
                    u, wT, aqk, ktl = T["u"][hh], T["wT"][hh], T["aqk"][hh], T["ktl"][hh]
                    ot = env.ring("ot%d" % hh, [128, 128], F32, 2) if mode == "b" else None
                    for cc in range(2):
                        r = slice(64 * cc, 64 * cc + 64)
                        ps = env.pnext()
                        mm(env, ps.t[r, 0:NV], wT.t[:, r], Sb[hh].t[:, :], True, True, [wT, Sb[hh]], [ps])
                        vn = env.ring("vn%d" % hh, [128, NV], BF16, 2)
                        k.op("dve", lambda e: e.tensor_tensor(out=vn.t[r, :], in0=u.t[r, :], in1=ps.t[r, 0:NV], op=ALU.subtract),
                             reads=[u, ps], writes=[vn])
                        yield
                        if mode == "b":
                            ps1 = env.pnext()
                            mm(env, ps1.t[r, 0:NV], qb.t[:, o + 64 * cc:o + 64 * cc + 64], Sb[hh].t[:, :], True, True,
                               [qb, Sb[hh]], [ps1])
                            ps2 = env.pnext()
                            mm(env, ps2.t[r, 0:NV], aqk.t[r, r], vn.t[r, :], True, True, [aqk, vn], [ps2])
                        ps3 = env.pnext()
                        mm(env, ps3.t[:, 0:NV], ktl.t[r, :], vn.t[r, :], True, True, [ktl, vn], [ps3])
                        if mode == "b":
                            o2 = env.ring("o2%d" % hh, [128, NV], F32, 2)
                            k.op("act", lambda e: e.activation(out=o2.t[r, :], in_=ps2.t[r, 0:NV], func=AF.Copy), reads=[ps2], writes=[o2])
                            k.op("dve", lambda e: e.scalar_tensor_tensor(out=ot.t[r, :], in0=ps1.t[r, 0:NV], scalar=EG.t[r, tt, h:h + 1],
                                                                         in1=o2.t[r, :], op0=ALU.mult, op1=ALU.add),
                                 reads=[ps1, EG, o2], writes=[ot])
                        k.op("dve", lambda e: e.scalar_tensor_tensor(out=Sf[hh].t[:], in0=Sf[hh].t[:],
                                                                     scalar=CD.t[:, tt, cc * NVH + h:cc * NVH + h + 1],
                                                                     in1=ps3.t[:, 0:NV], op0=ALU.mult, op1=ALU.add),
                             reads=[Sf[hh], CD, ps3], writes=[Sf[hh]])
                        k.op("act", lambda e: e.activation(out=Sb[hh].t[:], in_=Sf[hh].t[:], func=AF.Copy), reads=[Sf[hh]], writes=[Sb[hh]])
                        yield
                    if mode == "b":
                        sq = env.ring("osq", [128, 128], F32, 2)
                        ss = env.ring("oss", [128, 2], F32, 4)
                        k.op("pool", lambda e: e.tensor_tensor(out=sq.t[:], in0=ot.t[:], in1=ot.t[:], op=ALU.mult), reads=[ot], writes=[sq])
                        k.op("dve", lambda e: e.reduce_sum(out=ss.t[:, 0:1], in_=sq.t[:], axis=mybir.AxisListType.X), reads=[sq], writes=[ss])
                        k.op("act", lambda e: e.activation(out=ss.t[:, 1:2], in_=ss.t[:, 0:1], func=AF.Sqrt, scale=1.0 / HD,
                                                           bias=float(RMS_EPS)), reads=[ss], writes=[ss])
                        k.op("dve", lambda e: e.reciprocal(out=ss.t[:, 1:2], in_=ss.t[:, 1:2]), reads=[ss], writes=[ss])
                        onb = env.ring("onb", [128, 128], BF16, 2)
                        k.op("dve", lambda e: e.scalar_tensor_tensor(out=onb.t[:], in0=ot.t[:], scalar=ss.t[:, 1:2], in1=nwbc.t[:],
                                                                     op0=ALU.mult, op1=ALU.mult), reads=[ot, ss, nwbc], writes=[onb])
                        pt = env.ptnext()
                        k.op("pe", lambda e: e.transpose(pt.t[:, 0:128], onb.t[:], env.ident_bf.t[:]), reads=[onb, env.ident_bf], writes=[pt])
                        if tt % 4 == 0:
                            tile["og%d" % hh] = env.ring("og%d" % hh, [128, 512], BF16, 2)
                        og = tile["og%d" % hh]
                        k.op("dve", lambda e: e.tensor_tensor(out=og.t[:, o:o + 128], in0=pt.t[:, 0:128], in1=sz[hh].t[:, cols], op=ALU.mult),
                             reads=[pt, sz[hh]], writes=[og])
                        if tt % 4 == 3:
                            k.dma("act", out=onT_d[h, :, tg * 512:(tg + 1) * 512], in_=og.t[:], reads=[og], writes=[onbufs[h][tg]], owner=og)
                        yield

                def tile_front(tt):
                    for _ in prep(tt):
                        yield
                    gens = [solve(tt, 0), solve(tt, 1)]
                    while gens:
                        nxt = []
                        for g_ in gens:
                            try:
                                next(g_)
                                nxt.append(g_)
                            except StopIteration:
                                pass
                        gens = nxt
                        yield

                def tile_back(tt):
                    gens = [recur(tt, 0), recur(tt, 1)]
                    while gens:
                        nxt = []
                        for g_ in gens:
                            try:
                                next(g_)
                                nxt.append(g_)
                            except StopIteration:
                                pass
                        gens = nxt
                        yield
                    del tile[tt]

                proj(0)
                roundrobin([tile_front(0)])
                if dbg <= 4:
                    k.finish()
                    return nc
                for tt in range(c.NTT):
                    if tt + 1 < c.NTT and (tt + 1) % 4 == 0:
                        proj((tt + 1) // 4)
                    gens = [tile_back(tt)]
                    if tt + 1 < c.NTT:
                        gens.append(tile_front(tt + 1))
                    roundrobin(gens)
                if mode == "a":
                    for hh in range(2):
                        k.dma("act", out=st_out[2 * kh + hh], in_=Sf[hh].t[:], reads=[Sf[hh]], writes=[], owner=Sf[hh], is_output=True)

        if mode == "b":
            env.alloc_act()
            hbufs = [[Buf("h%d_%d" % (p, tt)) for tt in range(c.NTT)] for p in range(P)]
            x1bufs = [Buf("x1_%d" % tt) for tt in range(c.NTT)]
            xobufs = [Buf("xo_%d" % tt) for tt in range(c.NTT)]
            parts = []
            h0 = 0
            for gi, G in enumerate(vgroups):
                for tg in range(c.NTG):
                    k.dma("sp", out=env.actT[tg].t[:, 0:G, :],
                          in_=onT_d[h0:h0 + G, :, tg * 512:(tg + 1) * 512].rearrange("h p t -> p h t"),
                          reads=[onbufs[h][tg] for h in range(h0, h0 + G)], writes=[env.actT[tg]], owner=env.actT[tg])
                slabs = [[io["w_o_t"][gi][s, pc] for pc in range(G // 4)] for s in range(c.D // 512)]
                type_b(env, G, slabs, hbufs[gi], h_dram[gi])
                parts.append((h_dram[gi], hbufs[gi]))
                h0 += G
            ln_phase(env, io["x"], None, parts, io["ln"][0, :], io["ln"][1, :], x1, x1bufs, True, False)
            ffn_and_ln(env, io, x1, x1bufs, h_dram, hbufs, xo, xobufs, True, False)
        k.finish()
    return nc


def gdn_inputs(cfg, mode, w_in, conv_w, a_log, dt_bias, norm_w=None, w_out=None, w_gu=None, w_dn=None, ln4=None):
    c = cfg
    NKH, NVH, KC = c.NKH, c.NVH, c.KC
    KD, VD = NKH * 128, NVH * 128
    o1 = 2 * KD + VD
    o2 = o1 + VD
    d = {"ident": np.eye(128, dtype=np.float32), "consts": gdn_consts()}
    cols = []
    for kh in range(NKH):
        cols += [kh * 128, KD + kh * 128, 2 * KD + 2 * kh * 128, 2 * KD + (2 * kh + 1) * 128,
                 o1 + 2 * kh * 128, o1 + (2 * kh + 1) * 128]
    d["w_t"] = tile_a(w_in, cols).reshape(NKH, 6, 128, KC * 128)
    wba = w_in[:, o2:o2 + 2 * NVH].reshape(KC, 128, 2 * NVH).transpose(1, 0, 2)
    d["w_ba"] = np.ascontiguousarray(wba).reshape(128, KC * 2 * NVH)
    cwg = np.empty((128, NKH, 4, 4), np.float32)
    for kh in range(NKH):
        ch = [kh * 128, KD + kh * 128, 2 * KD + 2 * kh * 128, 2 * KD + (2 * kh + 1) * 128]
        for ci in range(4):
            cwg[:, kh, ci, :] = conv_w[:, ch[ci]:ch[ci] + 128].T
    d["cwg"] = cwg.reshape(128, NKH * 16)
    d["alog"] = np.ascontiguousarray(a_log, dtype=np.float32).reshape(1, NVH)
    d["dtb"] = np.ascontiguousarray(dt_bias, dtype=np.float32).reshape(1, NVH)
    if mode == "b":
        d["normw"] = np.ascontiguousarray(norm_w, dtype=np.float32).reshape(1, 128)
        g0 = 0
        gi = 0
        while g0 < NVH:
            G = min(16, NVH - g0)
            d["w_o_t%d" % gi] = tile_b(w_out, g0 * 128, G)
            g0 += G
            gi += 1
        d.update(ffn_inputs(cfg, w_gu, w_dn))
        d["ln"] = np.ascontiguousarray(ln4, dtype=np.float32)
    return d


_PROGS = {}


def _prog(name, cfg):
    if name not in _PROGS:
        if name == "sc":
            _PROGS[name] = build_sc_program(cfg)
        elif name == "ga":
            _PROGS[name] = build_gdn_program(cfg, "a")
        else:
            _PROGS[name] = build_gdn_program(cfg, "b")
    return _PROGS[name]


def _halos(xs, cfg):
    out = []
    for cidx in range(N_CORES):
        if cidx % 4 == 0:
            out.append(np.zeros((4, cfg.D), np.float32))
        else:
            out.append(np.ascontiguousarray(xs[cidx - 1][-4:]))
    return out


def kernel(x, sc_w_in, sc_conv_w, sc_w_out, dn_w_in, dn_conv_w, dn_a_log, dn_dt_bias, dn_norm_w, dn_w_out,
           ffn_w_gate_up, ffn_w_down, ln_gain, ln_bias):
    cfg = Cfg()
    f = lambda a: np.ascontiguousarray(np.asarray(a), dtype=np.float32)
    x = f(x)
    NT = cfg.NT
    xs = [np.ascontiguousarray(x[ci // 4, (ci % 4) * NT:(ci % 4 + 1) * NT]) for ci in range(N_CORES)]
    cores = list(range(N_CORES))
    for layer in range(DEPTH):
        j = layer // 2
        ln4 = np.stack([f(ln_gain)[layer, 0], f(ln_bias)[layer, 0], f(ln_gain)[layer, 1], f(ln_bias)[layer, 1]])
        halos = _halos(xs, cfg)
        if layer % 2 == 0:
            base = sc_inputs(cfg, f(sc_w_in)[j], f(sc_conv_w)[j], f(sc_w_out)[j], f(ffn_w_gate_up)[layer],
                             f(ffn_w_down)[layer], ln4)
            maps = [dict(base, x=xs[ci], xh=halos[ci]) for ci in cores]
            res = run_bass_kernel_spmd(_prog("sc", cfg), maps, core_ids=cores)
            xs = [np.asarray(res.results[ci]["xo"]) for ci in cores]
        else:
            base_a = gdn_inputs(cfg, "a", f(dn_w_in)[j], f(dn_conv_w)[j], f(dn_a_log)[j], f(dn_dt_bias)[j])
            maps = [dict(base_a, x=xs[ci], xh=halos[ci]) for ci in cores]
            res = run_bass_kernel_spmd(_prog("ga", cfg), maps, core_ids=cores)
            sts = [np.asarray(res.results[ci]["st"]) for ci in cores]
            base_b = gdn_inputs(cfg, "b", f(dn_w_in)[j], f(dn_conv_w)[j], f(dn_a_log)[j], f(dn_dt_bias)[j],
                                f(dn_norm_w)[j], f(dn_w_out)[j], f(ffn_w_gate_up)[layer], f(ffn_w_down)[layer], ln4)
            eye = np.ascontiguousarray(np.broadcast_to(np.eye(128, dtype=np.float32), (cfg.NVH, 128, 128)))
            zer = np.zeros((cfg.NVH, 128, 128), np.float32)
            maps = []
            for ci in cores:
                q = ci % 4
                pts, ss = [eye] * (3 - q), [zer] * (3 - q)
                for p in range(ci - q, ci):
                    pts.append(np.ascontiguousarray(sts[p][:, :, 128:].transpose(0, 2, 1)))
                    ss.append(np.ascontiguousarray(sts[p][:, :, :128]))
                maps.append(dict(base_b, x=xs[ci], xh=halos[ci], sin_pt=np.stack(pts), sin_s=np.stack(ss)))
            res = run_bass_kernel_spmd(_prog("gb", cfg), maps, core_ids=cores)
            xs = [np.asarray(res.results[ci]["xo"]) for ci in cores]
    out = np.empty((BATCH, SEQ, D_MODEL), np.float32)
    for ci in cores:
        out[ci // 4, (ci % 4) * NT:(ci % 4 + 1) * NT] = xs[ci]
    return out
```

```python
import numpy as np
from contextlib import ExitStack
import concourse.bass as bass
import concourse.mybir as mybir
from concourse.bass_utils import run_bass_kernel_spmd

F32 = mybir.dt.float32
BF16 = mybir.dt.bfloat16
AF = mybir.ActivationFunctionType
ALU = mybir.AluOpType

D_MODEL = 2048
BATCH = 2
SEQ = 8192
DEPTH = 4
N_CORES = 8
FFN_HIDDEN = 5632
ALPHA = (2 * DEPTH) ** 0.25
LN_EPS = 1e-5
RMS_EPS = 1e-6
HD = 128


class Buf:
    def __init__(self, name, t=None):
        self.name = name
        self.t = t
        self.w = None
        self.r = {}
        self.dsem = None
        self.dcnt = 0
        self.parent = None
        self.children = []
        self.is_psum = False

    def view(self, name, ap):
        c = Buf(name, ap)
        c.parent = self
        self.children.append(c)
        return c


class K:
    def __init__(self, nc, es):
        self.nc = nc
        self.es = es
        self.eng = {"pe": nc.tensor, "act": nc.scalar, "dve": nc.vector,
                    "pool": nc.gpsimd, "sp": nc.sync}
        self.sem = {e: es.enter_context(nc.semaphore("s_" + e)) for e in self.eng}
        self.cnt = {e: 0 for e in self.eng}
        self.waited = {e: {} for e in self.eng}
        self.semobj = {}
        self.n_dsem = 0
        self.uid = 0
        self.out_stamps = []
        self.allbufs = []
        self.es_cur = None
        self.names = {}

    def sbuf(self, name, shape, dt):
        es = self.es_cur if getattr(self, "es_cur", None) is not None else self.es
        n = self.names.get(name, 0)
        self.names[name] = n + 1
        if n:
            name = "%s_v%d" % (name, n)
        t = es.enter_context(self.nc.sbuf_tensor(name, list(shape), dt))
        b = Buf(name, t)
        self.allbufs.append(b)
        return b

    def barrier(self):
        tgt = {}
        for e in self.eng:
            if self.cnt[e] > 0:
                tgt[id(self.sem[e])] = (self.sem[e], self.cnt[e])
        for b in self.allbufs:
            if b.dsem is not None and b.dcnt > 0:
                tgt[id(b.dsem)] = (b.dsem, 16 * b.dcnt)
        for e in self.eng:
            w = self.waited[e]
            for key, (sm, v) in tgt.items():
                if w.get(key, 0) < v:
                    self.eng[e].wait_ge(sm, v)
                    w[key] = v

    def psum(self, name, shape, dt):
        t = self.es.enter_context(self.nc.psum_tensor(name, list(shape), dt))
        b = Buf(name, t)
        b.is_psum = True
        return b

    def dsem_for(self, b):
        if b.dsem is None:
            b.dsem = self.es.enter_context(self.nc.semaphore("d%d" % self.n_dsem))
            self.n_dsem += 1
        return b.dsem

    def _wait(self, e, reads, writes):
        need = {}

        def add(st):
            if st is None:
                return
            s, v = st
            key = id(s)
            self.semobj[key] = s
            if need.get(key, 0) < v:
                need[key] = v

        def addr(b):
            for key, v in b.r.items():
                if need.get(key, 0) < v:
                    need[key] = v

        for b in reads:
            add(b.w)
            if b.is_psum:
                addr(b)
            if b.parent is not None:
                add(b.parent.w)
            for ch in b.children:
                add(ch.w)
        for b in writes:
            add(b.w)
            addr(b)
            if b.parent is not None:
                add(b.parent.w)
                addr(b.parent)
            for ch in b.children:
                add(ch.w)
                addr(ch)
        w = self.waited[e]
        own = id(self.sem[e])
        for key, v in need.items():
            if key == own and (e == "pe" or v > self.cnt[e]):
                continue
            if w.get(key, 0) < v:
                self.eng[e].wait_ge(self.semobj[key], v)
                w[key] = v

    def _stamp(self, st, reads, writes):
        key = id(st[0])
        self.semobj[key] = st[0]
        for b in reads:
            if b.r.get(key, 0) < st[1]:
                b.r[key] = st[1]
        for b in writes:
            b.w = st
            b.r = {}

    def op(self, e, fn, reads=(), writes=(), inc=True):
        self.nops = getattr(self, "nops", 0) + 1
        if self.nops > getattr(self, "limit", 1 << 60):
            return None
        self._wait(e, reads, writes)
        ins = fn(self.eng[e])
        if inc:
            ins.then_inc(self.sem[e], 1)
            self.cnt[e] += 1
            st = (self.sem[e], self.cnt[e])
        else:
            st = (self.sem[e], self.cnt[e] + 1)
        self._stamp(st, reads, writes)
        return ins

    def dma(self, e, out, in_, reads=(), writes=(), owner=None, is_output=False):
        self.nops = getattr(self, "nops", 0) + 1
        if self.nops > getattr(self, "limit", 1 << 60):
            return None
        self._wait(e, reads, writes)
        sem = self.dsem_for(owner)
        ins = self.eng[e].dma_start(out=out, in_=in_)
        ins.then_inc(sem, 16)
        owner.dcnt += 1
        st = (sem, 16 * owner.dcnt)
        self._stamp(st, reads, writes)
        if is_output:
            self.out_stamps.append(st)
        return ins

    def finish(self):
        need = {}
        for s, v in self.out_stamps:
            key = id(s)
            self.semobj[key] = s
            need[key] = max(need.get(key, 0), v)
        for key, v in need.items():
            self.eng["sp"].wait_ge(self.semobj[key], v)


class WStream:
    def __init__(self, k, name, ch_elems, n_stage, n_bf, cast_engines=("pool",)):
        self.k = k
        self.ch = ch_elems
        self.stage = [k.sbuf("%s_st%d" % (name, i), [128, ch_elems], F32) for i in range(n_stage)]
        self.bf = [k.sbuf("%s_bf%d" % (name, i), [128, ch_elems], BF16) for i in range(n_bf)]
        self.n = 0
        self.cast_engines = cast_engines
        self.queue = []
        self.ready = []

    def push(self, dram_ap):
        self.queue.append(dram_ap)

    def _fetch(self, item):
        k = self.k
        dram_ap, n = item
        st = self.stage[self.n % len(self.stage)]
        bf = self.bf[self.n % len(self.bf)]
        ce = self.cast_engines[self.n % len(self.cast_engines)]
        self.n += 1
        k.dma("sp", out=st.t[:, 0:n], in_=dram_ap, writes=[st], owner=st)
        if ce == "act":
            k.op("act", lambda e: e.activation(out=bf.t[:, 0:n], in_=st.t[:, 0:n], func=AF.Copy),
                 reads=[st], writes=[bf])
        else:
            k.op(ce, lambda e: e.tensor_copy(out=bf.t[:, 0:n], in_=st.t[:, 0:n]), reads=[st], writes=[bf])
        return bf

    def prefetch(self, depth):
        while self.queue and len(self.ready) < depth:
            self.ready.append(self._fetch(self.queue.pop(0)))

    def take(self, n):
        self.prefetch(n)
        out = [self.ready.pop(0) for _ in range(n)]
        self.prefetch(len(self.bf) - n)
        return out


class Cfg:
    def __init__(self, D=D_MODEL, NT=2048, F=FFN_HIDDEN, fgroups=(16, 16, 12), NKH=None):
        self.D = D
        self.NT = NT
        self.F = F
        self.KC = D // 128
        self.NTT = NT // 128
        self.NTG = NT // 512
        self.fgroups = tuple(fgroups)
        assert sum(fgroups) * 128 == F
        self.NKH = NKH if NKH is not None else D // HD
        self.NVH = 2 * self.NKH
        self.CH = max(self.KC * 128, 2048)


class Env:
    def __init__(self, k, cfg, consts, alloc_act=True, ws_slots=6):
        self.k = k
        self.cfg = cfg
        c = cfg
        nc = k.nc
        self.xT = [k.sbuf("xT%d" % g, [128, c.KC, 512], BF16) for g in range(c.NTG)]
        self.xhT = k.sbuf("xhT", [128, c.KC, 4], BF16)
        self.AK = max(c.KC, max(c.fgroups), min(16, c.NVH))
        self.ws = WStream(k, "w", c.CH, 2, ws_slots)
        self.lnviews = {}
        self.rings = {}
        self.pA = [k.psum("pA%d" % i, [128, 512], F32) for i in range(6)]
        self.pT = [k.psum("pT%d" % i, [128, 1024], BF16) for i in range(2)]
        self.pi = 0
        self.pti = 0
        self.ident_bf = k.sbuf("ident_bf", [128, 128], BF16)
        self.ident_f = k.sbuf("ident_f", [128, 128], F32)
        k.dma("sp", out=self.ident_f.t[:], in_=consts["ident"], writes=[self.ident_f], owner=self.ident_f)
        k.op("dve", lambda e: e.tensor_copy(out=self.ident_bf.t[:], in_=self.ident_f.t[:]),
             reads=[self.ident_f], writes=[self.ident_bf])
        if alloc_act:
            self.alloc_act()

    def alloc_act(self):
        k, c = self.k, self.cfg
        self.actT = [k.sbuf("actT%d" % g, [128, self.AK, 512], BF16) for g in range(c.NTG)]
        fl = [a.t[:].rearrange("p a b -> p (a b)") for a in self.actT]
        nb = self.AK * 512
        self.lnviews = {}

        def carve(g, off_bf, n_bf, dt, name):
            ap = fl[g][:, off_bf:off_bf + n_bf]
            if dt == F32:
                ap = ap.bitcast(F32)
            return self.actT[g].view(name, ap)

        if 2 * c.D * 2 <= nb and c.NTG >= 3:
            self.lnviews["lnr"] = [carve(0, 0, 2 * c.D, F32, "lnr0"), carve(0, 2 * c.D, 2 * c.D, F32, "lnr1")]
            self.lnviews["lnh"] = [carve(1, 0, 2 * c.D, F32, "lnh0")]
            self.lnviews["xb"] = [carve(1, 2 * c.D, c.D, BF16, "xb0"), carve(1, 3 * c.D, c.D, BF16, "xb1")]
            self.lnviews["gain"] = [carve(2, 0, 2 * c.D, F32, "gain")]
            self.lnviews["bias"] = [carve(2, 2 * c.D, 2 * c.D, F32, "bias")]

    def phase(self):
        env = self

        class _P:
            def __enter__(p):
                p.before = set(env.rings)
                p.es2 = ExitStack()
                p.es2.__enter__()
                p.prev = env.k.es_cur
                env.k.es_cur = p.es2
                return p

            def __exit__(p, *a):
                if a[0] is None:
                    env.k.barrier()
                env.k.es_cur = p.prev
                p.es2.__exit__(*a)
                for n in list(env.rings):
                    if n not in p.before:
                        del env.rings[n]
                return False

        return _P()

    def pnext(self):
        b = self.pA[self.pi % len(self.pA)]
        self.pi += 1
        return b

    def ptnext(self):
        b = self.pT[self.pti % len(self.pT)]
        self.pti += 1
        return b

    def ring(self, name, shape, dt, n):
        if name in self.lnviews and name not in self.rings:
            self.rings[name] = [self.lnviews[name], 0]
        if name not in self.rings:
            self.rings[name] = [[self.k.sbuf("%s%d" % (name, i), shape, dt) for i in range(n)], 0]
        r = self.rings[name]
        b = r[0][r[1] % len(r[0])]
        r[1] += 1
        return b


def mm(env, ps_ap, lhsT, rhs, start, stop, reads, writes, inc=None):
    env.k.op("pe", lambda e: e.matmul(ps_ap, lhsT=lhsT, rhs=rhs, start=start, stop=stop),
             reads=reads, writes=writes, inc=(stop if inc is None else inc))


def emit_xT_tile(env, xf, tt):
    k, c = env.k, env.cfg
    xb = env.ring("xb", [128, c.D], BF16, 2)
    k.op("act", lambda e: e.activation(out=xb.t[:], in_=xf.t[:], func=AF.Copy), reads=[xf], writes=[xb])
    g, o = tt // 4, (tt % 4) * 128
    for k0 in range(0, c.KC, 8):
        n = min(8, c.KC - k0)
        pt = env.ptnext()
        for i in range(n):
            kc = k0 + i
            k.op("pe", lambda e: e.transpose(pt.t[:, i * 128:(i + 1) * 128], xb.t[:, kc * 128:(kc + 1) * 128],
                                             env.ident_bf.t[:]),
                 reads=[xb, env.ident_bf], writes=[pt], inc=(i == n - 1))
        k.op("dve", lambda e: e.tensor_copy(out=env.xT[g].t[:, k0:k0 + n, o:o + 128],
                                            in_=pt.t[:, 0:n * 128].rearrange("p (a b) -> p a b", b=128)),
             reads=[pt], writes=[env.xT[g]])


def emit_halo(env, xh_dram):
    k, c = env.k, env.cfg
    hf = k.sbuf("halo_f", [4, c.D], F32)
    hb = k.sbuf("halo_b", [4, c.D], BF16)
    k.dma("sp", out=hf.t[:], in_=xh_dram, writes=[hf], owner=hf)
    k.op("act", lambda e: e.activation(out=hb.t[:], in_=hf.t[:], func=AF.Copy), reads=[hf], writes=[hb])
    pt = env.ptnext()
    for kc in range(c.KC):
        k.op("pe", lambda e: e.transpose(pt.t[:, kc * 4:(kc + 1) * 4], hb.t[:, kc * 128:(kc + 1) * 128],
                                         env.ident_bf.t[0:4, 0:4]),
             reads=[hb, env.ident_bf], writes=[pt], inc=(kc == c.KC - 1))
    k.op("dve", lambda e: e.tensor_copy(out=env.xhT.t[:],
                                        in_=pt.t[:, 0:c.KC * 4].rearrange("p (a b) -> p a b", b=4)),
         reads=[pt], writes=[env.xhT])


def load_x_to_xT(env, x_dram):
    k, c = env.k, env.cfg
    for tt in range(c.NTT):
        xf = env.ring("lnr", [128, c.D], F32, 2)
        k.dma("sp", out=xf.t[:], in_=x_dram[tt * 128:(tt + 1) * 128, :], writes=[xf], owner=xf)
        emit_xT_tile(env, xf, tt)


def type_a(env, units, n_per_unit, epilogue, halo_fn=None):
    k, c = env.k, env.cfg
    ws = env.ws
    for u in units:
        for ap in u:
            ws.push((ap, c.KC * 128))
    for ui, u in enumerate(units):
        wb = ws.take(n_per_unit)
        if halo_fn is not None:
            halo_fn(ui, wb)
        for tg in range(c.NTG):
            ps = [env.pnext() for _ in range(n_per_unit)]
            for ci in range(n_per_unit):
                for kc in range(c.KC):
                    mm(env, ps[ci].t[:, :], wb[ci].t[:, kc * 128:(kc + 1) * 128], env.xT[tg].t[:, kc, :],
                       kc == 0, kc == c.KC - 1, [wb[ci], env.xT[tg]], [ps[ci]])
            epilogue(ui, tg, ps)


def type_b(env, KCb, slab_aps, hbufs, h_dram):
    k, c = env.k, env.cfg
    ws = env.ws
    npc = KCb // 4
    for s in range(c.D // 512):
        for ap in slab_aps[s]:
            ws.push((ap, 2048))
    for s in range(c.D // 512):
        pcs = ws.take(npc)
        for tt in range(c.NTT):
            g, o = tt // 4, (tt % 4) * 128
            ps = env.pnext()
            for kc in range(KCb):
                pc = pcs[kc // 4]
                mm(env, ps.t[:, :], env.actT[g].t[:, kc, o:o + 128], pc.t[:, (kc % 4) * 512:(kc % 4 + 1) * 512],
                   kc == 0, kc == KCb - 1, [env.actT[g], pc], [ps])
            ob = env.ring("ob", [128, 512], F32, 3)
            k.op("act", lambda e: e.activation(out=ob.t[:], in_=ps.t[:, :], func=AF.Copy), reads=[ps], writes=[ob])
            k.dma("act", out=h_dram[tt * 128:(tt + 1) * 128, s * 512:(s + 1) * 512], in_=ob.t[:],
                  reads=[ob], writes=[hbufs[tt]], owner=ob)


def ln_phase(env, x_dram, xbufs, parts, gain_dram, bias_dram, out_dram, obufs, make_xT, final):
    k, c = env.k, env.cfg
    gb = env.ring("gain", [128, c.D], F32, 1)
    bb = env.ring("bias", [128, c.D], F32, 1)
    k.dma("sp", out=gb.t[:], in_=gain_dram.partition_broadcast(128), writes=[gb], owner=gb)
    k.dma("sp", out=bb.t[:], in_=bias_dram.partition_broadcast(128), writes=[bb], owner=bb)
    nch = c.D // 512
    for tt in range(c.NTT):
        r = env.ring("lnr", [128, c.D], F32, 2)
        rows = slice(tt * 128, (tt + 1) * 128)
        k.dma("sp", out=r.t[:], in_=x_dram[rows, :], reads=[xbufs[tt]] if xbufs else [], writes=[r], owner=r)
        for pi, (hd, hb) in enumerate(parts):
            hin = env.ring("lnh", [128, c.D], F32, 2)
            k.dma("sp", out=hin.t[:], in_=hd[rows, :], reads=[hb[tt]], writes=[hin], owner=hin)
            if pi == 0:
                k.op("act", lambda e: e.activation(out=r.t[:], in_=r.t[:], func=AF.Copy, scale=float(ALPHA)),
                     reads=[r], writes=[r])
            k.op("pool", lambda e: e.tensor_tensor(out=r.t[:], in0=r.t[:], in1=hin.t[:], op=ALU.add),
                 reads=[r, hin], writes=[r])
        st = env.ring("lnst", [128, nch * 6], F32, 2)
        for i in range(nch):
            k.op("dve", lambda e: e.bn_stats(out=st.t[:, i * 6:(i + 1) * 6], in_=r.t[:, i * 512:(i + 1) * 512]),
                 reads=[r], writes=[st])
        mv = env.ring("lnmv", [128, 4], F32, 2)
        k.op("dve", lambda e: e.bn_aggr(out=mv.t[:, 0:2], in_=st.t[:]), reads=[st], writes=[mv])
        k.op("act", lambda e: e.activation(out=mv.t[:, 2:3], in_=mv.t[:, 1:2], func=AF.Sqrt, bias=float(LN_EPS)),
             reads=[mv], writes=[mv])
        k.op("dve", lambda e: e.reciprocal(out=mv.t[:, 2:3], in_=mv.t[:, 2:3]), reads=[mv], writes=[mv])
        k.op("dve", lambda e: e.tensor_scalar(out=r.t[:], in0=r.t[:], scalar1=mv.t[:, 0:1], scalar2=mv.t[:, 2:3],
                                              op0=ALU.subtract, op1=ALU.mult), reads=[r, mv], writes=[r])
        k.op("pool", lambda e: e.tensor_tensor(out=r.t[:], in0=r.t[:], in1=gb.t[:], op=ALU.mult),
             reads=[r, gb], writes=[r])
        k.op("pool", lambda e: e.tensor_tensor(out=r.t[:], in0=r.t[:], in1=bb.t[:], op=ALU.add),
             reads=[r, bb], writes=[r])
        k.dma("act", out=out_dram[rows, :], in_=r.t[:], reads=[r], writes=[obufs[tt]], owner=r, is_output=final)
        if make_xT:
            emit_xT_tile(env, r, tt)


def tile_a(W, col_starts):
    K_ = W.shape[0]
    KC = K_ // 128
    Wr = W.reshape(KC, 128, W.shape[1])
    out = np.empty((len(col_starts), 128, KC * 128), np.float32)
    for i, c0 in enumerate(col_starts):
        out[i] = Wr[:, :, c0:c0 + 128].transpose(1, 0, 2).reshape(128, KC * 128)
    return out


def tile_b(W, r0, KCb):
    D = W.shape[1]
    Wr = W[r0:r0 + KCb * 128].reshape(KCb // 4, 4, 128, D // 512, 512)
    return np.ascontiguousarray(Wr.transpose(3, 0, 2, 1, 4)).reshape(D // 512, KCb // 4, 128, 2048)


def dram_in(nc, name, shape):
    return nc.dram_tensor(name, list(shape), F32, kind="ExternalInput").ap()


def ffn_and_ln(env, io, x_dram, xbufs, h_dram, hbufs, out_dram, obufs, final, make_xT):
    k, c = env.k, env.cfg
    hc0 = 0
    parts = []
    for gi, G in enumerate(c.fgroups):
        units = [[io["w_gu_t"][hc0 + j, 0], io["w_gu_t"][hc0 + j, 1]] for j in range(G)]

        def epi(ui, tg, ps):
            sg = env.ring("sg", [128, 512], F32, 2)
            k.op("act", lambda e: e.activation(out=sg.t[:], in_=ps[0].t[:, :], func=AF.Silu), reads=[ps[0]], writes=[sg])
            k.op("dve", lambda e: e.tensor_tensor(out=env.actT[tg].t[:, ui, :], in0=ps[1].t[:, :], in1=sg.t[:], op=ALU.mult),
                 reads=[ps[1], sg], writes=[env.actT[tg]])

        type_a(env, units, 2, epi)
        slabs = [[io["w_dn_t"][gi][s, pc] for pc in range(G // 4)] for s in range(c.D // 512)]
        type_b(env, G, slabs, hbufs[gi], h_dram[gi])
        parts.append((h_dram[gi], hbufs[gi]))
        hc0 += G
    ln_phase(env, x_dram, xbufs, parts, io["ln"][2, :], io["ln"][3, :], out_dram, obufs, make_xT, final)


def build_sc_program(cfg, stages=9):
    c = cfg
    nc = bass.Bass("TRN2", target_bir_lowering=False)
    io = {}
    io["x"] = dram_in(nc, "x", [c.NT, c.D])
    io["xh"] = dram_in(nc, "xh", [4, c.D])
    io["ident"] = dram_in(nc, "ident", [128, 128])
    io["w_in_t"] = dram_in(nc, "w_in_t", [c.KC, 3, 128, c.KC * 128])
    io["cw"] = dram_in(nc, "cw", [128, c.KC * 3])
    io["w_out_t"] = dram_in(nc, "w_out_t", [c.D // 512, c.KC // 4, 128, 2048])
    io["w_gu_t"] = dram_in(nc, "w_gu_t", [c.F // 128, 2, 128, c.KC * 128])
    io["w_dn_t"] = [dram_in(nc, "w_dn_t%d" % gi, [c.D // 512, G // 4, 128, 2048]) for gi, G in enumerate(c.fgroups)]
    io["ln"] = dram_in(nc, "ln", [4, c.D])
    xo = nc.dram_tensor("xo", [c.NT, c.D], F32, kind="ExternalOutput").ap()
    P = max(1, len(c.fgroups))
    h_dram = [nc.dram_tensor("h%d" % p, [c.NT, c.D], F32, kind="Internal").ap() for p in range(P)]
    x1 = nc.dram_tensor("x1", [c.NT, c.D], F32, kind="Internal").ap()
    with ExitStack() as es:
        k = K(nc, es)
        env = Env(k, c, io)
        hbufs = [[Buf("h%d_%d" % (p, tt)) for tt in range(c.NTT)] for p in range(P)]
        x1bufs = [Buf("x1_%d" % tt) for tt in range(c.NTT)]
        xobufs = [Buf("xo_%d" % tt) for tt in range(c.NTT)]
        cw = k.sbuf("cw_sb", [128, c.KC * 3], F32)
        k.dma("sp", out=cw.t[:], in_=io["cw"], writes=[cw], owner=cw)
        emit_halo(env, io["xh"])
        load_x_to_xT(env, io["x"])

        units = [[io["w_in_t"][j, 0], io["w_in_t"][j, 1], io["w_in_t"][j, 2]] for j in range(c.KC)]
        state = {}

        def halo_fn(j, wb):
            ps = env.pnext()
            for ci in (1, 2):
                for kc in range(c.KC):
                    mm(env, ps.t[:, (ci - 1) * 4:(ci - 1) * 4 + 4], wb[ci].t[:, kc * 128:(kc + 1) * 128],
                       env.xhT.t[:, kc, :], kc == 0, kc == c.KC - 1, [wb[ci], env.xhT], [ps],
                       inc=(kc == c.KC - 1))
            hh = env.ring("hh", [128, 4], F32, 2)
            k.op("act", lambda e: e.activation(out=hh.t[:], in_=ps.t[:, 4:8], func=AF.Copy), reads=[ps], writes=[hh])
            u = env.ring("u", [128, 2 + 512], F32, 3)
            k.op("dve", lambda e: e.tensor_tensor(out=u.t[:, 0:2], in0=ps.t[:, 2:4], in1=hh.t[:, 2:4], op=ALU.mult),
                 reads=[ps, hh], writes=[u])
            state["u"] = u

        def epi(j, tg, ps):
            hs = env.ring("hs", [128, 512], F32, 2)
            k.op("act", lambda e: e.activation(out=hs.t[:], in_=ps[2].t[:, :], func=AF.Copy), reads=[ps[2]], writes=[hs])
            if tg == 0:
                u = state["u"]
            else:
                u = env.ring("u", [128, 2 + 512], F32, 3)
                up = state["u"]
                k.op("dve", lambda e: e.tensor_copy(out=u.t[:, 0:2], in_=up.t[:, 512:514]), reads=[up], writes=[u])
                state["u"] = u
            k.op("dve", lambda e: e.tensor_tensor(out=u.t[:, 2:514], in0=ps[1].t[:, :], in1=hs.t[:], op=ALU.mult),
                 reads=[ps[1], hs], writes=[u])
            y = env.ring("y", [128, 512], F32, 2)
            k.op("act", lambda e: e.activation(out=y.t[:], in_=u.t[:, 0:512], func=AF.Copy, scale=cw.t[:, j * 3:j * 3 + 1]),
                 reads=[u, cw], writes=[y])
            k.op("dve", lambda e: e.scalar_tensor_tensor(out=y.t[:], in0=u.t[:, 1:513], scalar=cw.t[:, j * 3 + 1:j * 3 + 2],
                                                         in1=y.t[:], op0=ALU.mult, op1=ALU.add), reads=[u, cw, y], writes=[y])
            k.op("dve", lambda e: e.scalar_tensor_tensor(out=y.t[:], in0=u.t[:, 2:514], scalar=cw.t[:, j * 3 + 2:j * 3 + 3],
                                                         in1=y.t[:], op0=ALU.mult, op1=ALU.add), reads=[u, cw, y], writes=[y])
            k.op("dve", lambda e: e.tensor_tensor(out=env.actT[tg].t[:, j, :], in0=ps[0].t[:, :], in1=y.t[:], op=ALU.mult),
                 reads=[ps[0], y], writes=[env.actT[tg]])

        type_a(env, units, 3, epi, halo_fn)
        slabs = [[io["w_out_t"][s, pc] for pc in range(c.KC // 4)] for s in range(c.D // 512)]
        if stages >= 2:
            type_b(env, c.KC, slabs, hbufs[0], h_dram[0])
        if stages >= 3:
            ln_phase(env, io["x"], None, [(h_dram[0], hbufs[0])], io["ln"][0, :], io["ln"][1, :], x1, x1bufs, True, False)
        if stages >= 4:
            ffn_and_ln(env, io, x1, x1bufs, h_dram, hbufs, xo, xobufs, True, False)
        k.finish()
    return nc


def sc_inputs(cfg, w_in, conv_w, w_out, w_gu, w_dn, ln4):
    c = cfg
    D, F = c.D, c.F
    d = {}
    d["ident"] = np.eye(128, dtype=np.float32)
    cols = []
    for j in range(c.KC):
        cols += [j * 128, D + j * 128, 2 * D + j * 128]
    d["w_in_t"] = tile_a(w_in, cols).reshape(c.KC, 3, 128, c.KC * 128)
    d["cw"] = np.ascontiguousarray(conv_w.reshape(3, c.KC, 128).transpose(2, 1, 0)).reshape(128, c.KC * 3)
    d["w_out_t"] = tile_b(w_out, 0, c.KC)
    d.update(ffn_inputs(cfg, w_gu, w_dn))
    d["ln"] = np.ascontiguousarray(ln4, dtype=np.float32)
    return d


def ffn_inputs(cfg, w_gu, w_dn):
    c = cfg
    d = {}
    cols = []
    for j in range(c.F // 128):
        cols += [j * 128, c.F + j * 128]
    d["w_gu_t"] = tile_a(w_gu, cols).reshape(c.F // 128, 2, 128, c.KC * 128)
    r0 = 0
    for gi, G in enumerate(c.fgroups):
        d["w_dn_t%d" % gi] = tile_b(w_dn, r0, G)
        r0 += G * 128
    return d


def gdn_consts():
    i = np.arange(128)
    same = (i[:, None] // 64) == (i[None, :] // 64)
    c = np.zeros((8, 128, 128), np.float32)
    c[0] = ((i[:, None] <= i[None, :]) & same)
    c[1] = (i[:, None] > i[None, :])
    c[2] = ((i[:, None] > i[None, :]) & same)
    c[3] = ((i[None, :] >= i[:, None]) & same)
    c[4] = same
    c[5] = 1.0
    c[6] = (i[:, None] < 64) * np.ones((1, 128))
    c[7] = (i[:, None] >= 64) * np.ones((1, 128))
    return c


def roundrobin(gens):
    gens = list(gens)
    while gens:
        nxt = []
        for g in gens:
            try:
                next(g)
                nxt.append(g)
            except StopIteration:
                pass
        gens = nxt


LIMIT = [1 << 60]


def build_gdn_program(cfg, mode, dbg=99):
    c = cfg
    NVH, NKH, KC = c.NVH, c.NKH, c.KC
    NV = 256 if mode == "a" else 128
    nch = 4 if mode == "a" else 6
    vgroups = [min(16, NVH - g0) for g0 in range(0, NVH, 16)]
    nc = bass.Bass("TRN2", target_bir_lowering=False)
    io = {}
    io["x"] = dram_in(nc, "x", [c.NT, c.D])
    io["xh"] = dram_in(nc, "xh", [4, c.D])
    io["ident"] = dram_in(nc, "ident", [128, 128])
    io["consts"] = dram_in(nc, "consts", [8, 128, 128])
    io["w_t"] = dram_in(nc, "w_t", [NKH, 6, 128, KC * 128])
    io["w_ba"] = dram_in(nc, "w_ba", [128, KC * 2 * NVH])
    io["cwg"] = dram_in(nc, "cwg", [128, NKH * 16])
    io["alog"] = dram_in(nc, "alog", [1, NVH])
    io["dtb"] = dram_in(nc, "dtb", [1, NVH])
    if mode == "a":
        st_out = nc.dram_tensor("st", [NVH, 128, 256], F32, kind="ExternalOutput").ap()
        oq_out = nc.dram_tensor("oq", [NVH, c.NT, 256], F32, kind="ExternalOutput").ap()
    else:
        io["normw"] = dram_in(nc, "normw", [1, 128])
        io["sin_pt"] = dram_in(nc, "sin_pt", [3, NVH, 128, 128])
        io["sin_s"] = dram_in(nc, "sin_s", [3, NVH, 128, 128])
        io["w_o_t"] = [dram_in(nc, "w_o_t%d" % gi, [c.D // 512, G // 4, 128, 2048]) for gi, G in enumerate(vgroups)]
        io["w_gu_t"] = dram_in(nc, "w_gu_t", [c.F // 128, 2, 128, KC * 128])
        io["w_dn_t"] = [dram_in(nc, "w_dn_t%d" % gi, [c.D // 512, G // 4, 128, 2048]) for gi, G in enumerate(c.fgroups)]
        io["ln"] = dram_in(nc, "ln", [4, c.D])
        xo = nc.dram_tensor("xo", [c.NT, c.D], F32, kind="ExternalOutput").ap()
        P = max(len(vgroups), len(c.fgroups))
        h_dram = [nc.dram_tensor("h%d" % p, [c.NT, c.D], F32, kind="Internal").ap() for p in range(P)]
        x1 = nc.dram_tensor("x1", [c.NT, c.D], F32, kind="Internal").ap()
        onT_d = nc.dram_tensor("onT_d", [NVH, 128, c.NT], BF16, kind="Internal").ap()
    with ExitStack() as es:
        k = K(nc, es)
        k.limit = LIMIT[0]
        env = Env(k, c, io, alloc_act=False, ws_slots=(6 if mode == "a" else 7))
        with env.phase():
            emit_halo(env, io["xh"])
            load_x_to_xT(env, io["x"])
        onbufs = [[Buf("on_%d_%d" % (h, tg)) for tg in range(c.NTG)] for h in range(NVH)]
        if dbg <= 1:
            k.finish()
            return nc

        with env.phase():
            def f32t(name, shape=(128, 128)):
                return k.sbuf(name, list(shape), F32)

            CN = []
            for i in range(8):
                t = f32t("cst%d" % i)
                k.dma("sp", out=t.t[:], in_=io["consts"][i], writes=[t], owner=t)
                CN.append(t)
            TRIBD, UU, STRICT, CAUSALT, BLK, ONES, CH0, CH1 = CN
            cwg = f32t("cwg_sb", (128, NKH * 16))
            k.dma("sp", out=cwg.t[:], in_=io["cwg"], writes=[cwg], owner=cwg)
            wba_f = f32t("wba_f", (128, KC * 2 * NVH))
            wba = k.sbuf("wba_b", [128, KC, 2 * NVH], BF16)
            k.dma("sp", out=wba_f.t[:], in_=io["w_ba"], writes=[wba_f], owner=wba_f)
            k.op("dve", lambda e: e.tensor_copy(out=wba.t[:].rearrange("p a b -> p (a b)"), in_=wba_f.t[:]),
                 reads=[wba_f], writes=[wba])
            alog = f32t("alog_sb", (128, NVH))
            dtb = f32t("dtb_sb", (128, NVH))
            k.dma("sp", out=alog.t[:], in_=io["alog"][0, :].partition_broadcast(128), writes=[alog], owner=alog)
            k.dma("sp", out=dtb.t[:], in_=io["dtb"][0, :].partition_broadcast(128), writes=[dtb], owner=dtb)
            negA = f32t("negA", (128, NVH))
            k.op("act", lambda e: e.activation(out=negA.t[:], in_=alog.t[:], func=AF.Exp), reads=[alog], writes=[negA])
            k.op("dve", lambda e: e.tensor_scalar(out=negA.t[:], in0=negA.t[:], scalar1=-1.0, scalar2=None, op0=ALU.mult),
                 reads=[negA], writes=[negA])
            if mode == "b":
                nwbc = f32t("nwbc")
                k.dma("sp", out=nwbc.t[:], in_=io["normw"][0, :].partition_broadcast(128), writes=[nwbc], owner=nwbc)
            shp = (128, c.NTT, NVH)
            BETA, AX, GRAW, GC, EG = [f32t(n, shp) for n in ("BETA", "AX", "GRAW", "GC", "EG")]
            EKT, BEG = AX, GC
            CD = f32t("CD", (128, c.NTT, 2 * NVH))

            for tt in range(c.NTT):
                g, o = tt // 4, (tt % 4) * 128
                ps = env.pnext()
                for kc in range(KC):
                    mm(env, ps.t[:, 0:2 * NVH], env.xT[g].t[:, kc, o:o + 128], wba.t[:, kc, :], kc == 0, kc == KC - 1,
                       [env.xT[g], wba], [ps])
                k.op("act", lambda e: e.activation(out=BETA.t[:, tt, :], in_=ps.t[:, 0:NVH], func=AF.Sigmoid),
                     reads=[ps], writes=[BETA])
                k.op("dve", lambda e: e.tensor_tensor(out=AX.t[:, tt, :], in0=ps.t[:, NVH:2 * NVH], in1=dtb.t[:], op=ALU.add),
                     reads=[ps, dtb], writes=[AX])
            k.op("act", lambda e: e.activation(out=AX.t[:], in_=AX.t[:], func=AF.Exp), reads=[AX], writes=[AX])
            k.op("act", lambda e: e.activation(out=AX.t[:], in_=AX.t[:], func=AF.Ln, bias=1.0), reads=[AX], writes=[AX])
            for tt in range(c.NTT):
                k.op("dve", lambda e: e.tensor_tensor(out=GRAW.t[:, tt, :], in0=AX.t[:, tt, :], in1=negA.t[:], op=ALU.mult),
                     reads=[AX, negA], writes=[GRAW])
            for tt in range(c.NTT):
                gm = env.ring("gm", [128, 2 * NVH], F32, 2)
                k.op("dve", lambda e: e.tensor_scalar(out=gm.t[:, 0:NVH], in0=GRAW.t[:, tt, :], scalar1=CH0.t[:, 0:1],
                                                      scalar2=None, op0=ALU.mult), reads=[GRAW, CH0], writes=[gm])
                k.op("dve", lambda e: e.tensor_scalar(out=gm.t[:, NVH:2 * NVH], in0=GRAW.t[:, tt, :], scalar1=CH1.t[:, 0:1],
                                                      scalar2=None, op0=ALU.mult), reads=[GRAW, CH1], writes=[gm])
                ps = env.pnext()
                mm(env, ps.t[:, 0:NVH], TRIBD.t[:], GRAW.t[:, tt, :], True, True, [TRIBD, GRAW], [ps])
                mm(env, ps.t[:, 64:64 + NVH], BLK.t[:], GRAW.t[:, tt, :], True, True, [BLK, GRAW], [ps])
                mm(env, ps.t[:, 128:128 + 2 * NVH], ONES.t[:], gm.t[:], True, True, [ONES, gm], [ps])
                k.op("dve", lambda e: e.tensor_copy(out=GC.t[:, tt, :], in_=ps.t[:, 0:NVH]), reads=[ps], writes=[GC])
                k.op("act", lambda e: e.activation(out=EG.t[:, tt, :], in_=ps.t[:, 0:NVH], func=AF.Exp), reads=[ps], writes=[EG])
                k.op("dve", lambda e: e.tensor_tensor(out=EKT.t[:, tt, :], in0=ps.t[:, 64:64 + NVH], in1=GC.t[:, tt, :],
                                                      op=ALU.subtract), reads=[ps, GC], writes=[EKT])
                k.op("act", lambda e: e.activation(out=EKT.t[:, tt, :], in_=EKT.t[:, tt, :], func=AF.Exp),
                     reads=[EKT], writes=[EKT])
                k.op("act", lambda e: e.activation(out=CD.t[:, tt, :], in_=ps.t[:, 128:128 + 2 * NVH], func=AF.Exp),
                     reads=[ps], writes=[CD])
                k.op("dve", lambda e: e.tensor_tensor(out=BEG.t[:, tt, :], in0=BETA.t[:, tt, :], in1=EG.t[:, tt, :], op=ALU.mult),
                     reads=[BETA, EG], writes=[BEG])

            if dbg <= 2:
                k.finish()
                return nc
            Sf = [f32t("Sf%d" % i, (128, NV)) for i in range(2)]
            Sb = [k.sbuf("Sb%d" % i, [128, NV], BF16) for i in range(2)]
            ubuf = [[f32t("u%d_%d" % (i, j), (128, NV)) for j in range(2)] for i in range(2)]
            if NV == 256:
                for i in range(2):
                    for j in range(2):
                        k.op("pool", lambda e: e.memset(ubuf[i][j].t[:], 0.0), writes=[ubuf[i][j]])
            ucnt = [0, 0]
            carry = [f32t("carry%d" % ci, (128, 4)) for ci in range(4)]
            ws = env.ws
            for kh in range(NKH):
                for ci in range(nch):
                    ws.push((io["w_t"][kh, ci], KC * 128))

            for kh in range(NKH):
                wb = ws.take(nch)
                for hh in range(2):
                    h = 2 * kh + hh
                    if mode == "a":
                        k.op("pool", lambda e: e.memset(Sf[hh].t[:, 0:128], 0.0), writes=[Sf[hh]])
                        k.op("pool", lambda e: e.tensor_copy(out=Sf[hh].t[:, 128:256], in_=env.ident_f.t[:]),
                             reads=[env.ident_f], writes=[Sf[hh]])
                    else:
                        k.dma("sp", out=Sf[hh].t[:], in_=io["sin_s"][0, h], writes=[Sf[hh]], owner=Sf[hh])
                        for j in (1, 2):
                            pt_ = env.ring("foldp", [128, 128], F32, 2)
                            sl_ = env.ring("folds", [128, 128], F32, 2)
                            k.dma("sp", out=pt_.t[:], in_=io["sin_pt"][j, h], writes=[pt_], owner=pt_)
                            k.dma("sp", out=sl_.t[:], in_=io["sin_s"][j, h], writes=[sl_], owner=sl_)
                            ps = env.pnext()
                            mm(env, ps.t[:, 0:128], pt_.t[:], Sf[hh].t[:], True, True, [pt_, Sf[hh]], [ps])
                            k.op("dve", lambda e: e.tensor_tensor(out=Sf[hh].t[:], in0=ps.t[:, 0:128], in1=sl_.t[:], op=ALU.add),
                                 reads=[ps, sl_], writes=[Sf[hh]])
                    k.op("act", lambda e: e.activation(out=Sb[hh].t[:], in_=Sf[hh].t[:], func=AF.Copy),
                         reads=[Sf[hh]], writes=[Sb[hh]])
                psh = env.pnext()
                for ci in range(4):
                    for kc in range(KC):
                        mm(env, psh.t[:, ci * 4:ci * 4 + 4], wb[ci].t[:, kc * 128:(kc + 1) * 128], env.xhT.t[:, kc, :],
                           kc == 0, kc == KC - 1, [wb[ci], env.xhT], [psh])
                for ci in range(4):
                    k.op("dve", lambda e: e.tensor_copy(out=carry[ci].t[:, 0:3], in_=psh.t[:, ci * 4 + 1:ci * 4 + 4]),
                         reads=[psh], writes=[carry[ci]])
                TG = {}

                def proj(tg):
                    knT = env.ring("knT", [128, 512], F32, 2)
                    qnT = env.ring("qnT", [128, 512], F32, 2)
                    qb = env.ring("qb", [128, 512], BF16, 2)
                    vT = [env.ring("vT%d" % i, [128, 512], F32, 2) for i in range(2)]
                    sz = [env.ring("sz%d" % i, [128, 512], BF16, 2) for i in range(2)] if mode == "b" else None
                    TG[tg] = dict(knT=knT, qnT=qnT, qb=qb, vT=vT, sz=sz)
                    for ci in range(nch):
                        ps = env.pnext()
                        for kc in range(KC):
                            mm(env, ps.t[:, :], wb[ci].t[:, kc * 128:(kc + 1) * 128], env.xT[tg].t[:, kc, :],
                               kc == 0, kc == KC - 1, [wb[ci], env.xT[tg]], [ps])
                        if ci >= 4:
                            k.op("act", lambda e: e.activation(out=sz[ci - 4].t[:], in_=ps.t[:, :], func=AF.Silu),
                                 reads=[ps], writes=[sz[ci - 4]])
                            continue
                        pre = env.ring("pre", [128, 3 + 512], F32, 2)
                        k.op("dve", lambda e: e.tensor_copy(out=pre.t[:, 0:3], in_=carry[ci].t[:, 0:3]), reads=[carry[ci]], writes=[pre])
                        k.op("act", lambda e: e.activation(out=pre.t[:, 3:515], in_=ps.t[:, :], func=AF.Copy), reads=[ps], writes=[pre])
                        k.op("dve", lambda e: e.tensor_copy(out=carry[ci].t[:, 0:3], in_=pre.t[:, 512:515]), reads=[pre], writes=[carry[ci]])
                        cb = (kh * 4 + ci) * 4
                        ca = env.ring("cacc", [128, 512], F32, 2)
                        k.op("act", lambda e: e.activation(out=ca.t[:], in_=pre.t[:, 0:512], func=AF.Copy, scale=cwg.t[:, cb:cb + 1]),
                             reads=[pre, cwg], writes=[ca])
                        for tap in (1, 2, 3):
                            k.op("dve", lambda e: e.scalar_tensor_tensor(out=ca.t[:], in0=pre.t[:, tap:tap + 512],
                                                                         scalar=cwg.t[:, cb + tap:cb + tap + 1], in1=ca.t[:],
                                                                         op0=ALU.mult, op1=ALU.add), reads=[pre, cwg, ca], writes=[ca])
                        if ci >= 2:
                            k.op("act", lambda e: e.activation(out=vT[ci - 2].t[:], in_=ca.t[:], func=AF.Silu),
                                 reads=[ca], writes=[vT[ci - 2]])
                            continue
                        sl = env.ring("sl", [128, 512], F32, 2)
                        k.op("act", lambda e: e.activation(out=sl.t[:], in_=ca.t[:], func=AF.Silu), reads=[ca], writes=[sl])
                        k.op("pool", lambda e: e.tensor_tensor(out=ca.t[:], in0=sl.t[:], in1=sl.t[:], op=ALU.mult), reads=[sl], writes=[ca])
                        ps2 = env.pnext()
                        mm(env, ps2.t[:, :], ONES.t[:], ca.t[:], True, True, [ONES, ca], [ps2])
                        k.op("act", lambda e: e.activation(out=ca.t[:], in_=ps2.t[:, :], func=AF.Sqrt, bias=float(RMS_EPS)),
                             reads=[ps2], writes=[ca])
                        k.op("dve", lambda e: e.reciprocal(out=ca.t[:], in_=ca.t[:]), reads=[ca], writes=[ca])
                        if ci == 1:
                            k.op("dve", lambda e: e.tensor_tensor(out=knT.t[:], in0=sl.t[:], in1=ca.t[:], op=ALU.mult),
                                 reads=[sl, ca], writes=[knT])
                        else:
                            k.op("dve", lambda e: e.scalar_tensor_tensor(out=qnT.t[:], in0=sl.t[:], scalar=float(HD ** -0.5),
                                                                         in1=ca.t[:], op0=ALU.mult, op1=ALU.mult),
                                 reads=[sl, ca], writes=[qnT])
                            k.op("pool", lambda e: e.tensor_copy(out=qb.t[:], in_=qnT.t[:]), reads=[qnT], writes=[qb])

                tile = {}

                def prep(tt):
                    cols = slice((tt % 4) * 128, (tt % 4 + 1) * 128)
                    knT, qnT, vT = TG[tt // 4]["knT"], TG[tt // 4]["qnT"], TG[tt // 4]["vT"]
                    T = {}
                    psk = env.pnext()
                    k.op("pe", lambda e: e.transpose(psk.t[:, 0:128], knT.t[:, cols], env.ident_f.t[:]),
                         reads=[knT, env.ident_f], writes=[psk])
                    psv = env.pnext()
                    for hh in range(2):
                        k.op("pe", lambda e: e.transpose(psv.t[:, hh * 128:(hh + 1) * 128], vT[hh].t[:, cols], env.ident_f.t[:]),
                             reads=[vT[hh], env.ident_f], writes=[psv])
                    T["kbg"], T["ktl"], T["vb"] = [], [], []
                    for hh in range(2):
                        h = 2 * kh + hh
                        kbg = env.ring("kbg%d" % hh, [128, 128], F32, 2)
                        ktl = env.ring("ktl%d" % hh, [128, 128], BF16, 2)
                        vb = env.ring("vb%d" % hh, [128, 128], F32, 2)
                        k.op("act", lambda e: e.activation(out=kbg.t[:], in_=psk.t[:, 0:128], func=AF.Copy, scale=BEG.t[:, tt, h:h + 1]),
                             reads=[psk, BEG], writes=[kbg])
                        k.op("dve", lambda e: e.tensor_scalar(out=ktl.t[:], in0=psk.t[:, 0:128], scalar1=EKT.t[:, tt, h:h + 1],
                                                              scalar2=None, op0=ALU.mult), reads=[psk, EKT], writes=[ktl])
                        k.op("dve", lambda e: e.tensor_scalar(out=vb.t[:], in0=psv.t[:, hh * 128:(hh + 1) * 128],
                                                              scalar1=BETA.t[:, tt, h:h + 1], scalar2=None, op0=ALU.mult),
                             reads=[psv, BETA], writes=[vb])
                        T["kbg"].append(kbg); T["ktl"].append(ktl); T["vb"].append(vb)
                    yield
                    pskk = env.pnext()
                    mm(env, pskk.t[:, 0:128], knT.t[:, cols], knT.t[:, cols], True, True, [knT], [pskk])
                    mm(env, pskk.t[:, 128:256], knT.t[:, cols], qnT.t[:, cols], True, True, [knT, qnT], [pskk])
                    KKs = env.ring("KKs", [128, 128], F32, 2)
                    QKm = env.ring("QKm", [128, 128], F32, 2)
                    k.op("dve", lambda e: e.tensor_tensor(out=KKs.t[:], in0=pskk.t[:, 0:128], in1=STRICT.t[:], op=ALU.mult),
                         reads=[pskk, STRICT], writes=[KKs])
                    k.op("dve", lambda e: e.tensor_tensor(out=QKm.t[:], in0=pskk.t[:, 128:256], in1=CAUSALT.t[:], op=ALU.mult),
                         reads=[pskk, CAUSALT], writes=[QKm])
                    T["KKs"], T["QKm"] = KKs, QKm
                    T["u"], T["wT"], T["aqk"] = [None, None], [None, None], [None, None]
                    tile[tt] = T
                    yield

                def solve(tt, hh):
                    T = tile[tt]
                    h = 2 * kh + hh
                    G = env.ring("G%d" % hh, [128, 128], F32, 2)
                    k.op("act", lambda e: e.activation(out=G.t[:], in_=TRIBD.t[:], func=AF.Copy, scale=GRAW.t[:, tt, h:h + 1]),
                         reads=[TRIBD, GRAW], writes=[G])
                    ps = env.pnext()
                    mm(env, ps.t[:, 0:128], G.t[:], UU.t[:], True, True, [G, UU], [ps])
                    mm(env, ps.t[:, 128:256], UU.t[:], G.t[:], True, True, [G, UU], [ps])
                    Dec = env.ring("Dec%d" % hh, [128, 256], F32, 1)
                    k.op("act", lambda e: e.activation(out=Dec.t[:], in_=ps.t[:, 0:256], func=AF.Exp), reads=[ps], writes=[Dec])
                    yield
                    A = env.ring("A%d" % hh, [128, 128], F32, 3)
                    k.op("dve", lambda e: e.scalar_tensor_tensor(out=A.t[:], in0=T["KKs"].t[:], scalar=BETA.t[:, tt, h:h + 1],
                                                                 in1=Dec.t[:, 0:128], op0=ALU.mult, op1=ALU.mult),
                         reads=[T["KKs"], BETA, Dec], writes=[A])
                    aqk = env.ring("aqk%d" % hh, [128, 128], BF16, 2)
                    k.op("pool", lambda e: e.tensor_tensor(out=aqk.t[:], in0=T["QKm"].t[:], in1=Dec.t[:, 128:256], op=ALU.mult),
                         reads=[T["QKm"], Dec], writes=[aqk])
                    T["aqk"][hh] = aqk
                    ps = env.pnext()
                    k.op("pe", lambda e: e.transpose(ps.t[:, 0:128], A.t[:], env.ident_f.t[:]), reads=[A, env.ident_f], writes=[ps])
                    X = env.ring("X%d" % hh, [128, 128], F32, 3)
                    B = env.ring("B%d" % hh, [128, 128], F32, 3)
                    k.op("dve", lambda e: e.tensor_tensor(out=X.t[:], in0=env.ident_f.t[:], in1=ps.t[:, 0:128], op=ALU.subtract),
                         reads=[env.ident_f, ps], writes=[X])
                    k.op("act", lambda e: e.activation(out=B.t[:], in_=ps.t[:, 0:128], func=AF.Copy), reads=[ps], writes=[B])
                    yield
                    for m in (1, 2, 4, 8, 16):
                        ps = env.pnext()
                        mm(env, ps.t[:, 0:128], B.t[:], A.t[:], True, True, [A, B], [ps])
                        if m < 16:
                            mm(env, ps.t[:, 128:256], A.t[:], B.t[:], True, True, [A, B], [ps])
                        A2 = env.ring("A%d" % hh, [128, 128], F32, 3)
                        k.op("act", lambda e: e.activation(out=A2.t[:], in_=ps.t[:, 0:128], func=AF.Copy), reads=[ps], writes=[A2])
                        if m < 16:
                            B2 = env.ring("B%d" % hh, [128, 128], F32, 3)
                            k.op("dve", lambda e: e.tensor_copy(out=B2.t[:], in_=ps.t[:, 128:256]), reads=[ps], writes=[B2])
                        ps2 = env.pnext()
                        mm(env, ps2.t[:, 0:128], A2.t[:], X.t[:], True, True, [A2, X], [ps2])
                        X2 = env.ring("X%d" % hh, [128, 128], F32, 3)
                        k.op("dve", lambda e: e.tensor_tensor(out=X2.t[:], in0=X.t[:], in1=ps2.t[:, 0:128], op=ALU.add),
                             reads=[X, ps2], writes=[X2])
                        A, X = A2, X2
                        if m < 16:
                            B = B2
                        yield
                    ps = env.pnext()
                    mm(env, ps.t[:, 0:128], X.t[:], T["vb"][hh].t[:], True, True, [X, T["vb"][hh]], [ps])
                    mm(env, ps.t[:, 128:256], T["kbg"][hh].t[:], X.t[:], True, True, [X, T["kbg"][hh]], [ps])
                    u = ubuf[hh][ucnt[hh] % 2]
                    ucnt[hh] += 1
                    k.op("act", lambda e: e.activation(out=u.t[:, 0:128], in_=ps.t[:, 0:128], func=AF.Copy), reads=[ps], writes=[u])
                    wT = env.ring("wT%d" % hh, [128, 128], BF16, 2)
                    k.op("dve", lambda e: e.tensor_copy(out=wT.t[:], in_=ps.t[:, 128:256]), reads=[ps], writes=[wT])
                    T["u"][hh], T["wT"][hh] = u, wT
                    yield

                def recur(tt, hh):
                    T = tile[tt]
                    h = 2 * kh + hh
                    cols = slice((tt % 4) * 128, (tt % 4 + 1) * 128)
                    tg, o = tt // 4, (tt % 4) * 128
                    qb, sz = TG[tg]["qb"], TG[tg]["sz"]
                    u, wT, aqk, ktl = T["u"][hh], T["wT"][hh], T["aqk"][hh], T["ktl"][hh]
                    ot = env.ring("ot%d" % hh, [128, NV], F32, 2)
                    for cc in range(2):
                        r = slice(64 * cc, 64 * cc + 64)
                        ps = env.pnext()
                        mm(env, ps.t[r, 0:NV], wT.t[:, r], Sb[hh].t[:, :], True, True, [wT, Sb[hh]], [ps])
                        vn = env.ring("vn%d" % hh, [128, NV], BF16, 2)
                        k.op("dve", lambda e: e.tensor_tensor(out=vn.t[r, :], in0=u.t[r, :], in1=ps.t[r, 0:NV], op=ALU.subtract),
                             reads=[u, ps], writes=[vn])
                        yield
                        if True:
                            ps1 = env.pnext()
                            mm(env, ps1.t[r, 0:NV], qb.t[:, o + 64 * cc:o + 64 * cc + 64], Sb[hh].t[:, :], True, True,
                               [qb, Sb[hh]], [ps1])
                            ps2 = env.pnext()
                            mm(env, ps2.t[r, 0:NV], aqk.t[r, r], vn.t[r, :], True, True, [aqk, vn], [ps2])
                        ps3 = env.pnext()
                        mm(env, ps3.t[:, 0:NV], ktl.t[r, :], vn.t[r, :], True, True, [ktl, vn], [ps3])
                        if True:
                            o2 = env.ring("o2%d" % hh, [128, NV], F32, 2)
                            k.op("act", lambda e: e.activation(out=o2.t[r, :], in_=ps2.t[r, 0:NV], func=AF.Copy), reads=[ps2], writes=[o2])
                            k.op("dve", lambda e: e.scalar_tensor_tensor(out=ot.t[r, :], in0=ps1.t[r, 0:NV], scalar=EG.t[r, tt, h:h + 1],
                                                                         in1=o2.t[r, :], op0=ALU.mult, op1=ALU.add),
                                 reads=[ps1, EG, o2], writes=[ot])
                        k.op("dve", lambda e: e.scalar_tensor_tensor(out=Sf[hh].t[:], in0=Sf[hh].t[:],
                                                                     scalar=CD.t[:, tt, cc * NVH + h:cc * NVH + h + 1],
                                                                     in1=ps3.t[:, 0:NV], op0=ALU.mult, op1=ALU.add),
                             reads=[Sf[hh], CD, ps3], writes=[Sf[hh]])
                        k.op("act", lambda e: e.activation(out=Sb[hh].t[:], in_=Sf[hh].t[:], func=AF.Copy), reads=[Sf[hh]], writes=[Sb[hh]])
                        yield
                    if mode == "a":
                        k.dma("act", out=oq_out[h, tt * 128:(tt + 1) * 128, :], in_=ot.t[:], reads=[ot], writes=[], owner=ot,
                              is_output=True)
                    if mode == "b":
                        sq = env.ring("osq", [128, 128], F32, 2)
                        ss = env.ring("oss", [128, 2], F32, 4)
                        k.op("pool", lambda e: e.tensor_tensor(out=sq.t[:], in0=ot.t[:], in1=ot.t[:], op=ALU.mult), reads=[ot], writes=[sq])
                        k.op("dve", lambda e: e.reduce_sum(out=ss.t[:, 0:1], in_=sq.t[:], axis=mybir.AxisListType.X), reads=[sq], writes=[ss])
                        k.op("act", lambda e: e.activation(out=ss.t[:, 1:2], in_=ss.t[:, 0:1], func=AF.Sqrt, scale=1.0 / HD,
                                                           bias=float(RMS_EPS)), reads=[ss], writes=[ss])
                        k.op("dve", lambda e: e.reciprocal(out=ss.t[:, 1:2], in_=ss.t[:, 1:2]), reads=[ss], writes=[ss])
                        onb = env.ring("onb", [128, 128], BF16, 2)
                        k.op("dve", lambda e: e.scalar_tensor_tensor(out=onb.t[:], in0=ot.t[:], scalar=ss.t[:, 1:2], in1=nwbc.t[:],
                                                                     op0=ALU.mult, op1=ALU.mult), reads=[ot, ss, nwbc], writes=[onb])
                        pt = env.ptnext()
                        k.op("pe", lambda e: e.transpose(pt.t[:, 0:128], onb.t[:], env.ident_bf.t[:]), reads=[onb, env.ident_bf], writes=[pt])
                        if tt % 4 == 0:
                            tile["og%d" % hh] = env.ring("og%d" % hh, [128, 512], BF16, 2)
                        og = tile["og%d" % hh]
                        k.op("dve", lambda e: e.tensor_tensor(out=og.t[:, o:o + 128], in0=pt.t[:, 0:128], in1=sz[hh].t[:, cols], op=ALU.mult),
                             reads=[pt, sz[hh]], writes=[og])
                        if tt % 4 == 3:
                            k.dma("act", out=onT_d[h, :, tg * 512:(tg + 1) * 512], in_=og.t[:], reads=[og], writes=[onbufs[h][tg]], owner=og)
                        yield

                def tile_front(tt):
                    for _ in prep(tt):
                        yield
                    gens = [solve(tt, 0), solve(tt, 1)]
                    while gens:
                        nxt = []
                        for g_ in gens:
                            try:
                                next(g_)
                                nxt.append(g_)
                            except StopIteration:
                                pass
                        gens = nxt
                        yield

                def tile_back(tt):
                    gens = [recur(tt, 0), recur(tt, 1)]
                    while gens:
                        nxt = []
                        for g_ in gens:
                            try:
                                next(g_)
                                nxt.append(g_)
                            except StopIteration:
                                pass
                        gens = nxt
                        yield
                    del tile[tt]

                proj(0)
                roundrobin([tile_front(0)])
                if dbg <= 4:
                    k.finish()
                    return nc
                for tt in range(c.NTT):
                    if tt + 1 < c.NTT and (tt + 1) % 4 == 0:
                        proj((tt + 1) // 4)
                    gens = [tile_back(tt)]
                    if tt + 1 < c.NTT:
                        gens.append(tile_front(tt + 1))
                    roundrobin(gens)
                if mode == "a":
                    for hh in range(2):
                        k.dma("act", out=st_out[2 * kh + hh], in_=Sf[hh].t[:], reads=[Sf[hh]], writes=[], owner=Sf[hh], is_output=True)

        if mode == "b":
            gdn_tail(env, io, vgroups, onT_d, onbufs, h_dram, x1, xo, P)
        k.finish()
    return nc


def gdn_tail(env, io, vgroups, onT_d, onbufs, h_dram, x1, xo, P):
    k, c = env.k, env.cfg
    env.alloc_act()
    hbufs = [[Buf("h%d_%d" % (p, tt)) for tt in range(c.NTT)] for p in range(P)]
    x1bufs = [Buf("x1_%d" % tt) for tt in range(c.NTT)]
    xobufs = [Buf("xo_%d" % tt) for tt in range(c.NTT)]
    parts = []
    h0 = 0
    for gi, G in enumerate(vgroups):
        for tg in range(c.NTG):
            k.dma("sp", out=env.actT[tg].t[:, 0:G, :],
                  in_=onT_d[h0:h0 + G, :, tg * 512:(tg + 1) * 512].rearrange("h p t -> p h t"),
                  reads=[onbufs[h][tg] for h in range(h0, h0 + G)], writes=[env.actT[tg]], owner=env.actT[tg])
        slabs = [[io["w_o_t"][gi][s, pc] for pc in range(G // 4)] for s in range(c.D // 512)]
        type_b(env, G, slabs, hbufs[gi], h_dram[gi])
        parts.append((h_dram[gi], hbufs[gi]))
        h0 += G
    ln_phase(env, io["x"], None, parts, io["ln"][0, :], io["ln"][1, :], x1, x1bufs, True, False)
    ffn_and_ln(env, io, x1, x1bufs, h_dram, hbufs, xo, xobufs, True, False)


def build_gdn_c_program(cfg):
    c = cfg
    NVH, KC = c.NVH, c.KC
    vgroups = [min(16, NVH - g0) for g0 in range(0, NVH, 16)]
    nc = bass.Bass("TRN2", target_bir_lowering=False)
    io = {}
    io["x"] = dram_in(nc, "x", [c.NT, c.D])
    io["ident"] = dram_in(nc, "ident", [128, 128])
    io["oq"] = dram_in(nc, "oq", [NVH, c.NT, 256])
    io["w_z"] = dram_in(nc, "w_z", [NVH, 128, KC * 128])
    io["normw"] = dram_in(nc, "normw", [1, 128])
    io["sin_pt"] = dram_in(nc, "sin_pt", [3, NVH, 128, 128])
    io["sin_s"] = dram_in(nc, "sin_s", [3, NVH, 128, 128])
    io["w_o_t"] = [dram_in(nc, "w_o_t%d" % gi, [c.D // 512, G // 4, 128, 2048]) for gi, G in enumerate(vgroups)]
    io["w_gu_t"] = dram_in(nc, "w_gu_t", [c.F // 128, 2, 128, KC * 128])
    io["w_dn_t"] = [dram_in(nc, "w_dn_t%d" % gi, [c.D // 512, G // 4, 128, 2048]) for gi, G in enumerate(c.fgroups)]
    io["ln"] = dram_in(nc, "ln", [4, c.D])
    xo = nc.dram_tensor("xo", [c.NT, c.D], F32, kind="ExternalOutput").ap()
    P = max(len(vgroups), len(c.fgroups))
    h_dram = [nc.dram_tensor("h%d" % p, [c.NT, c.D], F32, kind="Internal").ap() for p in range(P)]
    x1 = nc.dram_tensor("x1", [c.NT, c.D], F32, kind="Internal").ap()
    onT_d = nc.dram_tensor("onT_d", [NVH, 128, c.NT], BF16, kind="Internal").ap()
    with ExitStack() as es:
        k = K(nc, es)
        env = Env(k, c, io, alloc_act=False, ws_slots=4)
        with env.phase():
            load_x_to_xT(env, io["x"])
        onbufs = [[Buf("on_%d_%d" % (h, tg)) for tg in range(c.NTG)] for h in range(NVH)]
        with env.phase():
            nwbc = k.sbuf("nwbc", [128, 128], F32)
            k.dma("sp", out=nwbc.t[:], in_=io["normw"][0, :].partition_broadcast(128), writes=[nwbc], owner=nwbc)
            ws = env.ws
            for h in range(NVH):
                ws.push((io["w_z"][h], KC * 128))
            for h in range(NVH):
                Sf = env.ring("Sf", [128, 128], F32, 2)
                Sb = env.ring("Sb", [128, 128], BF16, 2)
                k.dma("sp", out=Sf.t[:], in_=io["sin_s"][0, h], writes=[Sf], owner=Sf)
                for j in (1, 2):
                    pt_ = env.ring("foldp", [128, 128], F32, 2)
                    sl_ = env.ring("folds", [128, 128], F32, 2)
                    k.dma("sp", out=pt_.t[:], in_=io["sin_pt"][j, h], writes=[pt_], owner=pt_)
                    k.dma("sp", out=sl_.t[:], in_=io["sin_s"][j, h], writes=[sl_], owner=sl_)
                    ps = env.pnext()
                    mm(env, ps.t[:, 0:128], pt_.t[:], Sf.t[:], True, True, [pt_, Sf], [ps])
                    k.op("dve", lambda e: e.tensor_tensor(out=Sf.t[:], in0=ps.t[:, 0:128], in1=sl_.t[:], op=ALU.add),
                         reads=[ps, sl_], writes=[Sf])
                k.op("act", lambda e: e.activation(out=Sb.t[:], in_=Sf.t[:], func=AF.Copy), reads=[Sf], writes=[Sb])
                wb = ws.take(1)[0]
                for tg in range(c.NTG):
                    ps = env.pnext()
                    for kc in range(KC):
                        mm(env, ps.t[:, :], wb.t[:, kc * 128:(kc + 1) * 128], env.xT[tg].t[:, kc, :],
                           kc == 0, kc == KC - 1, [wb, env.xT[tg]], [ps])
                    sz = env.ring("sz", [128, 512], BF16, 2)
                    k.op("act", lambda e: e.activation(out=sz.t[:], in_=ps.t[:, :], func=AF.Silu), reads=[ps], writes=[sz])
                    og = env.ring("og", [128, 512], BF16, 2)
                    for t4 in range(4):
                        tt = tg * 4 + t4
                        o = t4 * 128
                        oqt = env.ring("oqt", [128, 256], F32, 3)
                        k.dma("sp", out=oqt.t[:], in_=io["oq"][h, tt * 128:(tt + 1) * 128, :], writes=[oqt], owner=oqt)
                        pq = env.pnext()
                        k.op("pe", lambda e: e.transpose(pq.t[:, 0:128], oqt.t[:, 128:256], env.ident_f.t[:]),
                             reads=[oqt, env.ident_f], writes=[pq])
                        qT = env.ring("qT", [128, 128], BF16, 2)
                        k.op("act", lambda e: e.activation(out=qT.t[:], in_=pq.t[:, 0:128], func=AF.Copy), reads=[pq], writes=[qT])
                        pc_ = env.pnext()
                        mm(env, pc_.t[:, 0:128], qT.t[:], Sb.t[:], True, True, [qT, Sb], [pc_])
                        ot = env.ring("ot", [128, 128], F32, 2)
                        k.op("dve", lambda e: e.tensor_tensor(out=ot.t[:], in0=pc_.t[:, 0:128], in1=oqt.t[:, 0:128], op=ALU.add),
                             reads=[pc_, oqt], writes=[ot])
                        sq = env.ring("osq", [128, 128], F32, 2)
                        ss = env.ring("oss", [128, 2], F32, 4)
                        k.op("pool", lambda e: e.tensor_tensor(out=sq.t[:], in0=ot.t[:], in1=ot.t[:], op=ALU.mult), reads=[ot], writes=[sq])
                        k.op("dve", lambda e: e.reduce_sum(out=ss.t[:, 0:1], in_=sq.t[:], axis=mybir.AxisListType.X), reads=[sq], writes=[ss])
                        k.op("act", lambda e: e.activation(out=ss.t[:, 1:2], in_=ss.t[:, 0:1], func=AF.Sqrt, scale=1.0 / HD,
                                                           bias=float(RMS_EPS)), reads=[ss], writes=[ss])
                        k.op("dve", lambda e: e.reciprocal(out=ss.t[:, 1:2], in_=ss.t[:, 1:2]), reads=[ss], writes=[ss])
                        onb = env.ring("onb", [128, 128], BF16, 2)
                        k.op("dve", lambda e: e.scalar_tensor_tensor(out=onb.t[:], in0=ot.t[:], scalar=ss.t[:, 1:2], in1=nwbc.t[:],
                                                                     op0=ALU.mult, op1=ALU.mult), reads=[ot, ss, nwbc], writes=[onb])
                        pt = env.ptnext()
                        k.op("pe", lambda e: e.transpose(pt.t[:, 0:128], onb.t[:], env.ident_bf.t[:]), reads=[onb, env.ident_bf], writes=[pt])
                        k.op("dve", lambda e: e.tensor_tensor(out=og.t[:, o:o + 128], in0=pt.t[:, 0:128], in1=sz.t[:, o:o + 128], op=ALU.mult),
                             reads=[pt, sz], writes=[og])
                    k.dma("act", out=onT_d[h, :, tg * 512:(tg + 1) * 512], in_=og.t[:], reads=[og], writes=[onbufs[h][tg]], owner=og)
        gdn_tail(env, io, vgroups, onT_d, onbufs, h_dram, x1, xo, P)
        k.finish()
    return nc


def gdn_c_inputs(cfg, w_in, norm_w, w_out, w_gu, w_dn, ln4):
    c = cfg
    NKH, NVH, KC = c.NKH, c.NVH, c.KC
    KD, VD = NKH * 128, NVH * 128
    o1 = 2 * KD + VD
    d = {"ident": np.eye(128, dtype=np.float32)}
    d["w_z"] = tile_a(w_in, [o1 + h * 128 for h in range(NVH)])
    d["normw"] = np.ascontiguousarray(norm_w, dtype=np.float32).reshape(1, 128)
    g0 = 0
    gi = 0
    while g0 < NVH:
        G = min(16, NVH - g0)
        d["w_o_t%d" % gi] = tile_b(w_out, g0 * 128, G)
        g0 += G
        gi += 1
    d.update(ffn_inputs(cfg, w_gu, w_dn))
    d["ln"] = np.ascontiguousarray(ln4, dtype=np.float32)
    return d


def gdn_inputs(cfg, mode, w_in, conv_w, a_log, dt_bias, norm_w=None, w_out=None, w_gu=None, w_dn=None, ln4=None):
    c = cfg
    NKH, NVH, KC = c.NKH, c.NVH, c.KC
    KD, VD = NKH * 128, NVH * 128
    o1 = 2 * KD + VD
    o2 = o1 + VD
    d = {"ident": np.eye(128, dtype=np.float32), "consts": gdn_consts()}
    cols = []
    for kh in range(NKH):
        cols += [kh * 128, KD + kh * 128, 2 * KD + 2 * kh * 128, 2 * KD + (2 * kh + 1) * 128,
                 o1 + 2 * kh * 128, o1 + (2 * kh + 1) * 128]
    d["w_t"] = tile_a(w_in, cols).reshape(NKH, 6, 128, KC * 128)
    wba = w_in[:, o2:o2 + 2 * NVH].reshape(KC, 128, 2 * NVH).transpose(1, 0, 2)
    d["w_ba"] = np.ascontiguousarray(wba).reshape(128, KC * 2 * NVH)
    cwg = np.empty((128, NKH, 4, 4), np.float32)
    for kh in range(NKH):
        ch = [kh * 128, KD + kh * 128, 2 * KD + 2 * kh * 128, 2 * KD + (2 * kh + 1) * 128]
        for ci in range(4):
            cwg[:, kh, ci, :] = conv_w[:, ch[ci]:ch[ci] + 128].T
    d["cwg"] = cwg.reshape(128, NKH * 16)
    d["alog"] = np.ascontiguousarray(a_log, dtype=np.float32).reshape(1, NVH)
    d["dtb"] = np.ascontiguousarray(dt_bias, dtype=np.float32).reshape(1, NVH)
    if mode == "b":
        d["normw"] = np.ascontiguousarray(norm_w, dtype=np.float32).reshape(1, 128)
        g0 = 0
        gi = 0
        while g0 < NVH:
            G = min(16, NVH - g0)
            d["w_o_t%d" % gi] = tile_b(w_out, g0 * 128, G)
            g0 += G
            gi += 1
        d.update(ffn_inputs(cfg, w_gu, w_dn))
        d["ln"] = np.ascontiguousarray(ln4, dtype=np.float32)
    return d


_PROGS = {}


def _prog(name, cfg):
    if name not in _PROGS:
        if name == "sc":
            _PROGS[name] = build_sc_program(cfg)
        elif name == "ga":
            _PROGS[name] = build_gdn_program(cfg, "a")
        else:
            _PROGS[name] = build_gdn_c_program(cfg)
    return _PROGS[name]


def _halos(xs, cfg):
    out = []
    for cidx in range(N_CORES):
        if cidx % 4 == 0:
            out.append(np.zeros((4, cfg.D), np.float32))
        else:
            out.append(np.ascontiguousarray(xs[cidx - 1][-4:]))
    return out


def kernel(x, sc_w_in, sc_conv_w, sc_w_out, dn_w_in, dn_conv_w, dn_a_log, dn_dt_bias, dn_norm_w, dn_w_out,
           ffn_w_gate_up, ffn_w_down, ln_gain, ln_bias):
    cfg = Cfg()
    f = lambda a: np.ascontiguousarray(np.asarray(a), dtype=np.float32)
    x = f(x)
    NT = cfg.NT
    xs = [np.ascontiguousarray(x[ci // 4, (ci % 4) * NT:(ci % 4 + 1) * NT]) for ci in range(N_CORES)]
    cores = list(range(N_CORES))
    for layer in range(DEPTH):
        j = layer // 2
        ln4 = np.stack([f(ln_gain)[layer, 0], f(ln_bias)[layer, 0], f(ln_gain)[layer, 1], f(ln_bias)[layer, 1]])
        halos = _halos(xs, cfg)
        if layer % 2 == 0:
            base = sc_inputs(cfg, f(sc_w_in)[j], f(sc_conv_w)[j], f(sc_w_out)[j], f(ffn_w_gate_up)[layer],
                             f(ffn_w_down)[layer], ln4)
            maps = [dict(base, x=xs[ci], xh=halos[ci]) for ci in cores]
            res = run_bass_kernel_spmd(_prog("sc", cfg), maps, core_ids=cores)
            xs = [np.asarray(res.results[ci]["xo"]) for ci in cores]
        else:
            base_a = gdn_inputs(cfg, "a", f(dn_w_in)[j], f(dn_conv_w)[j], f(dn_a_log)[j], f(dn_dt_bias)[j])
            maps = [dict(base_a, x=xs[ci], xh=halos[ci]) for ci in cores]
            res = run_bass_kernel_spmd(_prog("ga", cfg), maps, core_ids=cores)
            sts = [np.asarray(res.results[ci]["st"]) for ci in cores]
            oqs = [np.asarray(res.results[ci]["oq"]) for ci in cores]
            base_b = gdn_c_inputs(cfg, f(dn_w_in)[j], f(dn_norm_w)[j], f(dn_w_out)[j], f(ffn_w_gate_up)[layer],
                                  f(ffn_w_down)[layer], ln4)
            eye = np.ascontiguousarray(np.broadcast_to(np.eye(128, dtype=np.float32), (cfg.NVH, 128, 128)))
            zer = np.zeros((cfg.NVH, 128, 128), np.float32)
            maps = []
            for ci in cores:
                q = ci % 4
                pts, ss = [eye] * (3 - q), [zer] * (3 - q)
                for p in range(ci - q, ci):
                    pts.append(np.ascontiguousarray(sts[p][:, :, 128:].transpose(0, 2, 1)))
                    ss.append(np.ascontiguousarray(sts[p][:, :, :128]))
                maps.append(dict(base_b, x=xs[ci], oq=oqs[ci], sin_pt=np.stack(pts), sin_s=np.stack(ss)))
            res = run_bass_kernel_spmd(_prog("gc", cfg), maps, core_ids=cores)
            xs = [np.asarray(res.results[ci]["xo"]) for ci in cores]
    out = np.empty((BATCH, SEQ, D_MODEL), np.float32)
    for ci in cores:
        out[ci // 4, (ci % 4) * NT:(ci % 4 + 1) * NT] = xs[ci]
    return out
```

```python
import numpy as np
from contextlib import ExitStack
import concourse.bass as bass
import concourse.mybir as mybir
from concourse.bass_utils import run_bass_kernel_spmd

F32 = mybir.dt.float32
BF16 = mybir.dt.bfloat16
AF = mybir.ActivationFunctionType
ALU = mybir.AluOpType

D_MODEL = 2048
BATCH = 2
SEQ = 8192
DEPTH = 4
N_CORES = 8
FFN_HIDDEN = 5632
ALPHA = (2 * DEPTH) ** 0.25
LN_EPS = 1e-5
RMS_EPS = 1e-6
HD = 128


class Buf:
    def __init__(self, name, t=None):
        self.name = name
        self.t = t
        self.w = None
        self.r = {}
        self.dsem = None
        self.dcnt = 0
        self.parent = None
        self.children = []
        self.is_psum = False

    def view(self, name, ap):
        c = Buf(name, ap)
        c.parent = self
        self.children.append(c)
        return c


class K:
    def __init__(self, nc, es):
        self.nc = nc
        self.es = es
        self.eng = {"pe": nc.tensor, "act": nc.scalar, "dve": nc.vector,
                    "pool": nc.gpsimd, "sp": nc.sync}
        self.sem = {e: es.enter_context(nc.semaphore("s_" + e)) for e in self.eng}
        self.cnt = {e: 0 for e in self.eng}
        self.waited = {e: {} for e in self.eng}
        self.semobj = {}
        self.n_dsem = 0
        self.uid = 0
        self.out_stamps = []
        self.allbufs = []
        self.es_cur = None
        self.names = {}

    def sbuf(self, name, shape, dt):
        es = self.es_cur if getattr(self, "es_cur", None) is not None else self.es
        n = self.names.get(name, 0)
        self.names[name] = n + 1
        if n:
            name = "%s_v%d" % (name, n)
        t = es.enter_context(self.nc.sbuf_tensor(name, list(shape), dt))
        b = Buf(name, t)
        self.allbufs.append(b)
        return b

    def barrier(self):
        tgt = {}
        for e in self.eng:
            if self.cnt[e] > 0:
                tgt[id(self.sem[e])] = (self.sem[e], self.cnt[e])
        for b in self.allbufs:
            if b.dsem is not None and b.dcnt > 0:
                tgt[id(b.dsem)] = (b.dsem, 16 * b.dcnt)
        for e in self.eng:
            w = self.waited[e]
            for key, (sm, v) in tgt.items():
                if w.get(key, 0) < v:
                    self.eng[e].wait_ge(sm, v)
                    w[key] = v

    def psum(self, name, shape, dt):
        t = self.es.enter_context(self.nc.psum_tensor(name, list(shape), dt))
        b = Buf(name, t)
        b.is_psum = True
        return b

    def dsem_for(self, b):
        if b.dsem is None:
            b.dsem = self.es.enter_context(self.nc.semaphore("d%d" % self.n_dsem))
            self.n_dsem += 1
        return b.dsem

    def _wait(self, e, reads, writes):
        need = {}

        def add(st):
            if st is None:
                return
            s, v = st
            key = id(s)
            self.semobj[key] = s
            if need.get(key, 0) < v:
                need[key] = v

        def addr(b):
            for key, v in b.r.items():
                if need.get(key, 0) < v:
                    need[key] = v

        for b in reads:
            add(b.w)
            if b.is_psum:
                addr(b)
            if b.parent is not None:
                add(b.parent.w)
            for ch in b.children:
                add(ch.w)
        for b in writes:
            add(b.w)
            addr(b)
            if b.parent is not None:
                add(b.parent.w)
                addr(b.parent)
            for ch in b.children:
                add(ch.w)
                addr(ch)
        w = self.waited[e]
        own = id(self.sem[e])
        for key, v in need.items():
            if key == own and (e == "pe" or v > self.cnt[e]):
                continue
            if w.get(key, 0) < v:
                self.eng[e].wait_ge(self.semobj[key], v)
                w[key] = v

    def _stamp(self, st, reads, writes):
        key = id(st[0])
        self.semobj[key] = st[0]
        for b in reads:
            if b.r.get(key, 0) < st[1]:
                b.r[key] = st[1]
        for b in writes:
            b.w = st
            b.r = {}

    def op(self, e, fn, reads=(), writes=(), inc=True):
        self.nops = getattr(self, "nops", 0) + 1
        if self.nops > getattr(self, "limit", 1 << 60):
            return None
        self._wait(e, reads, writes)
        ins = fn(self.eng[e])
        if inc:
            ins.then_inc(self.sem[e], 1)
            self.cnt[e] += 1
            st = (self.sem[e], self.cnt[e])
        else:
            st = (self.sem[e], self.cnt[e] + 1)
        self._stamp(st, reads, writes)
        return ins

    def dma(self, e, out, in_, reads=(), writes=(), owner=None, is_output=False):
        self.nops = getattr(self, "nops", 0) + 1
        if self.nops > getattr(self, "limit", 1 << 60):
            return None
        self._wait(e, reads, writes)
        sem = self.dsem_for(owner)
        ins = self.eng[e].dma_start(out=out, in_=in_)
        ins.then_inc(sem, 16)
        owner.dcnt += 1
        st = (sem, 16 * owner.dcnt)
        self._stamp(st, reads, writes)
        if is_output:
            self.out_stamps.append(st)
        return ins

    def finish(self):
        need = {}
        for s, v in self.out_stamps:
            key = id(s)
            self.semobj[key] = s
            need[key] = max(need.get(key, 0), v)
        for key, v in need.items():
            self.eng["sp"].wait_ge(self.semobj[key], v)


class WStream:
    def __init__(self, k, name, ch_elems, n_stage, n_bf, cast_engines=("pool",)):
        self.k = k
        self.ch = ch_elems
        self.stage = [k.sbuf("%s_st%d" % (name, i), [128, ch_elems], F32) for i in range(n_stage)]
        self.bf = [k.sbuf("%s_bf%d" % (name, i), [128, ch_elems], BF16) for i in range(n_bf)]
        self.n = 0
        self.cast_engines = cast_engines
        self.queue = []
        self.ready = []

    def push(self, dram_ap):
        self.queue.append(dram_ap)

    def _fetch(self, item):
        k = self.k
        dram_ap, n = item
        st = self.stage[self.n % len(self.stage)]
        bf = self.bf[self.n % len(self.bf)]
        ce = self.cast_engines[self.n % len(self.cast_engines)]
        self.n += 1
        k.dma("sp", out=st.t[:, 0:n], in_=dram_ap, writes=[st], owner=st)
        if ce == "act":
            k.op("act", lambda e: e.activation(out=bf.t[:, 0:n], in_=st.t[:, 0:n], func=AF.Copy),
                 reads=[st], writes=[bf])
        else:
            k.op(ce, lambda e: e.tensor_copy(out=bf.t[:, 0:n], in_=st.t[:, 0:n]), reads=[st], writes=[bf])
        return bf

    def prefetch(self, depth):
        while self.queue and len(self.ready) < depth:
            self.ready.append(self._fetch(self.queue.pop(0)))

    def take(self, n):
        self.prefetch(n)
        out = [self.ready.pop(0) for _ in range(n)]
        self.prefetch(len(self.bf) - n)
        return out


class Cfg:
    def __init__(self, D=D_MODEL, NT=2048, F=FFN_HIDDEN, fgroups=(16, 16, 12), NKH=None):
        self.D = D
        self.NT = NT
        self.F = F
        self.KC = D // 128
        self.NTT = NT // 128
        self.NTG = NT // 512
        self.fgroups = tuple(fgroups)
        assert sum(fgroups) * 128 == F
        self.NKH = NKH if NKH is not None else D // HD
        self.NVH = 2 * self.NKH
        self.CH = max(self.KC * 128, 2048)


class Env:
    def __init__(self, k, cfg, consts, alloc_act=True, ws_slots=6):
        self.k = k
        self.cfg = cfg
        c = cfg
        nc = k.nc
        self.xT = [k.sbuf("xT%d" % g, [128, c.KC, 512], BF16) for g in range(c.NTG)]
        self.xhT = k.sbuf("xhT", [128, c.KC, 4], BF16)
        self.AK = max(c.KC, max(c.fgroups), min(16, c.NVH))
        self.ws = WStream(k, "w", c.CH, 2, ws_slots)
        self.lnviews = {}
        self.rings = {}
        self.pA = [k.psum("pA%d" % i, [128, 512], F32) for i in range(6)]
        self.pT = [k.psum("pT%d" % i, [128, 1024], BF16) for i in range(2)]
        self.pi = 0
        self.pti = 0
        self.ident_bf = k.sbuf("ident_bf", [128, 128], BF16)
        self.ident_f = k.sbuf("ident_f", [128, 128], F32)
        k.dma("sp", out=self.ident_f.t[:], in_=consts["ident"], writes=[self.ident_f], owner=self.ident_f)
        k.op("dve", lambda e: e.tensor_copy(out=self.ident_bf.t[:], in_=self.ident_f.t[:]),
             reads=[self.ident_f], writes=[self.ident_bf])
        if alloc_act:
            self.alloc_act()

    def alloc_act(self):
        k, c = self.k, self.cfg
        self.actT = [k.sbuf("actT%d" % g, [128, self.AK, 512], BF16) for g in range(c.NTG)]
        fl = [a.t[:].rearrange("p a b -> p (a b)") for a in self.actT]
        nb = self.AK * 512
        self.lnviews = {}

        def carve(g, off_bf, n_bf, dt, name):
            ap = fl[g][:, off_bf:off_bf + n_bf]
            if dt == F32:
                ap = ap.bitcast(F32)
            return self.actT[g].view(name, ap)

        if 2 * c.D * 2 <= nb and c.NTG >= 3:
            self.lnviews["lnr"] = [carve(0, 0, 2 * c.D, F32, "lnr0"), carve(0, 2 * c.D, 2 * c.D, F32, "lnr1")]
            self.lnviews["lnh"] = [carve(1, 0, 2 * c.D, F32, "lnh0")]
            self.lnviews["xb"] = [carve(1, 2 * c.D, c.D, BF16, "xb0"), carve(1, 3 * c.D, c.D, BF16, "xb1")]
            self.lnviews["gain"] = [carve(2, 0, 2 * c.D, F32, "gain")]
            self.lnviews["bias"] = [carve(2, 2 * c.D, 2 * c.D, F32, "bias")]

    def phase(self):
        env = self

        class _P:
            def __enter__(p):
                p.before = set(env.rings)
                p.es2 = ExitStack()
                p.es2.__enter__()
                p.prev = env.k.es_cur
                env.k.es_cur = p.es2
                return p

            def __exit__(p, *a):
                if a[0] is None:
                    env.k.barrier()
                env.k.es_cur = p.prev
                p.es2.__exit__(*a)
                for n in list(env.rings):
                    if n not in p.before:
                        del env.rings[n]
                return False

        return _P()

    def pnext(self):
        b = self.pA[self.pi % len(self.pA)]
        self.pi += 1
        return b

    def ptnext(self):
        b = self.pT[self.pti % len(self.pT)]
        self.pti += 1
        return b

    def ring(self, name, shape, dt, n):
        if name in self.lnviews and name not in self.rings:
            self.rings[name] = [self.lnviews[name], 0]
        if name not in self.rings:
            self.rings[name] = [[self.k.sbuf("%s%d" % (name, i), shape, dt) for i in range(n)], 0]
        r = self.rings[name]
        b = r[0][r[1] % len(r[0])]
        r[1] += 1
        return b


def mm(env, ps_ap, lhsT, rhs, start, stop, reads, writes, inc=None):
    env.k.op("pe", lambda e: e.matmul(ps_ap, lhsT=lhsT, rhs=rhs, start=start, stop=stop),
             reads=reads, writes=writes, inc=(stop if inc is None else inc))


def emit_xT_tile(env, xf, tt):
    k, c = env.k, env.cfg
    xb = env.ring("xb", [128, c.D], BF16, 2)
    k.op("act", lambda e: e.activation(out=xb.t[:], in_=xf.t[:], func=AF.Copy), reads=[xf], writes=[xb])
    g, o = tt // 4, (tt % 4) * 128
    for k0 in range(0, c.KC, 8):
        n = min(8, c.KC - k0)
        pt = env.ptnext()
        for i in range(n):
            kc = k0 + i
            k.op("pe", lambda e: e.transpose(pt.t[:, i * 128:(i + 1) * 128], xb.t[:, kc * 128:(kc + 1) * 128],
                                             env.ident_bf.t[:]),
                 reads=[xb, env.ident_bf], writes=[pt], inc=(i == n - 1))
        k.op("dve", lambda e: e.tensor_copy(out=env.xT[g].t[:, k0:k0 + n, o:o + 128],
                                            in_=pt.t[:, 0:n * 128].rearrange("p (a b) -> p a b", b=128)),
             reads=[pt], writes=[env.xT[g]])


def emit_halo(env, xh_dram):
    k, c = env.k, env.cfg
    hf = k.sbuf("halo_f", [4, c.D], F32)
    hb = k.sbuf("halo_b", [4, c.D], BF16)
    k.dma("sp", out=hf.t[:], in_=xh_dram, writes=[hf], owner=hf)
    k.op("act", lambda e: e.activation(out=hb.t[:], in_=hf.t[:], func=AF.Copy), reads=[hf], writes=[hb])
    pt = env.ptnext()
    for kc in range(c.KC):
        k.op("pe", lambda e: e.transpose(pt.t[:, kc * 4:(kc + 1) * 4], hb.t[:, kc * 128:(kc + 1) * 128],
                                         env.ident_bf.t[0:4, 0:4]),
             reads=[hb, env.ident_bf], writes=[pt], inc=(kc == c.KC - 1))
    k.op("dve", lambda e: e.tensor_copy(out=env.xhT.t[:],
                                        in_=pt.t[:, 0:c.KC * 4].rearrange("p (a b) -> p a b", b=4)),
         reads=[pt], writes=[env.xhT])


def load_x_to_xT(env, x_dram):
    k, c = env.k, env.cfg
    for tt in range(c.NTT):
        xf = env.ring("lnr", [128, c.D], F32, 2)
        k.dma("sp", out=xf.t[:], in_=x_dram[tt * 128:(tt + 1) * 128, :], writes=[xf], owner=xf)
        emit_xT_tile(env, xf, tt)


def type_a(env, units, n_per_unit, epilogue, halo_fn=None):
    k, c = env.k, env.cfg
    ws = env.ws
    for u in units:
        for ap in u:
            ws.push((ap, c.KC * 128))
    for ui, u in enumerate(units):
        wb = ws.take(n_per_unit)
        if halo_fn is not None:
            halo_fn(ui, wb)
        for tg in range(c.NTG):
            ps = [env.pnext() for _ in range(n_per_unit)]
            for ci in range(n_per_unit):
                for kc in range(c.KC):
                    mm(env, ps[ci].t[:, :], wb[ci].t[:, kc * 128:(kc + 1) * 128], env.xT[tg].t[:, kc, :],
                       kc == 0, kc == c.KC - 1, [wb[ci], env.xT[tg]], [ps[ci]])
            epilogue(ui, tg, ps)


def type_b(env, KCb, slab_aps, hbufs, h_dram):
    k, c = env.k, env.cfg
    ws = env.ws
    npc = KCb // 4
    for s in range(c.D // 512):
        for ap in slab_aps[s]:
            ws.push((ap, 2048))
    for s in range(c.D // 512):
        pcs = ws.take(npc)
        for tt in range(c.NTT):
            g, o = tt // 4, (tt % 4) * 128
            ps = env.pnext()
            for kc in range(KCb):
                pc = pcs[kc // 4]
                mm(env, ps.t[:, :], env.actT[g].t[:, kc, o:o + 128], pc.t[:, (kc % 4) * 512:(kc % 4 + 1) * 512],
                   kc == 0, kc == KCb - 1, [env.actT[g], pc], [ps])
            ob = env.ring("ob", [128, 512], F32, 3)
            k.op("act", lambda e: e.activation(out=ob.t[:], in_=ps.t[:, :], func=AF.Copy), reads=[ps], writes=[ob])
            k.dma("act", out=h_dram[tt * 128:(tt + 1) * 128, s * 512:(s + 1) * 512], in_=ob.t[:],
                  reads=[ob], writes=[hbufs[tt]], owner=ob)


def ln_phase(env, x_dram, xbufs, parts, gain_dram, bias_dram, out_dram, obufs, make_xT, final):
    k, c = env.k, env.cfg
    gb = env.ring("gain", [128, c.D], F32, 1)
    bb = env.ring("bias", [128, c.D], F32, 1)
    k.dma("sp", out=gb.t[:], in_=gain_dram.partition_broadcast(128), writes=[gb], owner=gb)
    k.dma("sp", out=bb.t[:], in_=bias_dram.partition_broadcast(128), writes=[bb], owner=bb)
    nch = c.D // 512
    for tt in range(c.NTT):
        r = env.ring("lnr", [128, c.D], F32, 2)
        rows = slice(tt * 128, (tt + 1) * 128)
        k.dma("sp", out=r.t[:], in_=x_dram[rows, :], reads=[xbufs[tt]] if xbufs else [], writes=[r], owner=r)
        for pi, (hd, hb) in enumerate(parts):
            hin = env.ring("lnh", [128, c.D], F32, 2)
            k.dma("sp", out=hin.t[:], in_=hd[rows, :], reads=[hb[tt]], writes=[hin], owner=hin)
            if pi == 0:
                k.op("act", lambda e: e.activation(out=r.t[:], in_=r.t[:], func=AF.Copy, scale=float(ALPHA)),
                     reads=[r], writes=[r])
            k.op("pool", lambda e: e.tensor_tensor(out=r.t[:], in0=r.t[:], in1=hin.t[:], op=ALU.add),
                 reads=[r, hin], writes=[r])
        st = env.ring("lnst", [128, nch * 6], F32, 2)
        for i in range(nch):
            k.op("dve", lambda e: e.bn_stats(out=st.t[:, i * 6:(i + 1) * 6], in_=r.t[:, i * 512:(i + 1) * 512]),
                 reads=[r], writes=[st])
        mv = env.ring("lnmv", [128, 4], F32, 2)
        k.op("dve", lambda e: e.bn_aggr(out=mv.t[:, 0:2], in_=st.t[:]), reads=[st], writes=[mv])
        k.op("act", lambda e: e.activation(out=mv.t[:, 2:3], in_=mv.t[:, 1:2], func=AF.Sqrt, bias=float(LN_EPS)),
             reads=[mv], writes=[mv])
        k.op("dve", lambda e: e.reciprocal(out=mv.t[:, 2:3], in_=mv.t[:, 2:3]), reads=[mv], writes=[mv])
        k.op("dve", lambda e: e.tensor_scalar(out=r.t[:], in0=r.t[:], scalar1=mv.t[:, 0:1], scalar2=mv.t[:, 2:3],
                                              op0=ALU.subtract, op1=ALU.mult), reads=[r, mv], writes=[r])
        k.op("pool", lambda e: e.tensor_tensor(out=r.t[:], in0=r.t[:], in1=gb.t[:], op=ALU.mult),
             reads=[r, gb], writes=[r])
        k.op("pool", lambda e: e.tensor_tensor(out=r.t[:], in0=r.t[:], in1=bb.t[:], op=ALU.add),
             reads=[r, bb], writes=[r])
        k.dma("act", out=out_dram[rows, :], in_=r.t[:], reads=[r], writes=[obufs[tt]], owner=r, is_output=final)
        if make_xT:
            emit_xT_tile(env, r, tt)


def tile_a(W, col_starts):
    K_ = W.shape[0]
    KC = K_ // 128
    Wr = W.reshape(KC, 128, W.shape[1])
    out = np.empty((len(col_starts), 128, KC * 128), np.float32)
    for i, c0 in enumerate(col_starts):
        out[i] = Wr[:, :, c0:c0 + 128].transpose(1, 0, 2).reshape(128, KC * 128)
    return out


def tile_b(W, r0, KCb):
    D = W.shape[1]
    Wr = W[r0:r0 + KCb * 128].reshape(KCb // 4, 4, 128, D // 512, 512)
    return np.ascontiguousarray(Wr.transpose(3, 0, 2, 1, 4)).reshape(D // 512, KCb // 4, 128, 2048)


def dram_in(nc, name, shape):
    return nc.dram_tensor(name, list(shape), F32, kind="ExternalInput").ap()


def ffn_and_ln(env, io, x_dram, xbufs, h_dram, hbufs, out_dram, obufs, final, make_xT):
    k, c = env.k, env.cfg
    hc0 = 0
    parts = []
    for gi, G in enumerate(c.fgroups):
        units = [[io["w_gu_t"][hc0 + j, 0], io["w_gu_t"][hc0 + j, 1]] for j in range(G)]

        def epi(ui, tg, ps):
            sg = env.ring("sg", [128, 512], F32, 2)
            k.op("act", lambda e: e.activation(out=sg.t[:], in_=ps[0].t[:, :], func=AF.Silu), reads=[ps[0]], writes=[sg])
            k.op("dve", lambda e: e.tensor_tensor(out=env.actT[tg].t[:, ui, :], in0=ps[1].t[:, :], in1=sg.t[:], op=ALU.mult),
                 reads=[ps[1], sg], writes=[env.actT[tg]])

        type_a(env, units, 2, epi)
        slabs = [[io["w_dn_t"][gi][s, pc] for pc in range(G // 4)] for s in range(c.D // 512)]
        type_b(env, G, slabs, hbufs[gi], h_dram[gi])
        parts.append((h_dram[gi], hbufs[gi]))
        hc0 += G
    ln_phase(env, x_dram, xbufs, parts, io["ln"][2, :], io["ln"][3, :], out_dram, obufs, make_xT, final)


def build_sc_program(cfg, stages=9):
    c = cfg
    nc = bass.Bass("TRN2", target_bir_lowering=False)
    io = {}
    io["x"] = dram_in(nc, "x", [c.NT, c.D])
    io["xh"] = dram_in(nc, "xh", [4, c.D])
    io["ident"] = dram_in(nc, "ident", [128, 128])
    io["w_in_t"] = dram_in(nc, "w_in_t", [c.KC, 3, 128, c.KC * 128])
    io["cw"] = dram_in(nc, "cw", [128, c.KC * 3])
    io["w_out_t"] = dram_in(nc, "w_out_t", [c.D // 512, c.KC // 4, 128, 2048])
    io["w_gu_t"] = dram_in(nc, "w_gu_t", [c.F // 128, 2, 128, c.KC * 128])
    io["w_dn_t"] = [dram_in(nc, "w_dn_t%d" % gi, [c.D // 512, G // 4, 128, 2048]) for gi, G in enumerate(c.fgroups)]
    io["ln"] = dram_in(nc, "ln", [4, c.D])
    xo = nc.dram_tensor("xo", [c.NT, c.D], F32, kind="ExternalOutput").ap()
    P = max(1, len(c.fgroups))
    h_dram = [nc.dram_tensor("h%d" % p, [c.NT, c.D], F32, kind="Internal").ap() for p in range(P)]
    x1 = nc.dram_tensor("x1", [c.NT, c.D], F32, kind="Internal").ap()
    with ExitStack() as es:
        k = K(nc, es)
        env = Env(k, c, io)
        hbufs = [[Buf("h%d_%d" % (p, tt)) for tt in range(c.NTT)] for p in range(P)]
        x1bufs = [Buf("x1_%d" % tt) for tt in range(c.NTT)]
        xobufs = [Buf("xo_%d" % tt) for tt in range(c.NTT)]
        cw = k.sbuf("cw_sb", [128, c.KC * 3], F32)
        k.dma("sp", out=cw.t[:], in_=io["cw"], writes=[cw], owner=cw)
        emit_halo(env, io["xh"])
        load_x_to_xT(env, io["x"])

        units = [[io["w_in_t"][j, 0], io["w_in_t"][j, 1], io["w_in_t"][j, 2]] for j in range(c.KC)]
        state = {}

        def halo_fn(j, wb):
            ps = env.pnext()
            for ci in (1, 2):
                for kc in range(c.KC):
                    mm(env, ps.t[:, (ci - 1) * 4:(ci - 1) * 4 + 4], wb[ci].t[:, kc * 128:(kc + 1) * 128],
                       env.xhT.t[:, kc, :], kc == 0, kc == c.KC - 1, [wb[ci], env.xhT], [ps],
                       inc=(kc == c.KC - 1))
            hh = env.ring("hh", [128, 4], F32, 2)
            k.op("act", lambda e: e.activation(out=hh.t[:], in_=ps.t[:, 4:8], func=AF.Copy), reads=[ps], writes=[hh])
            u = env.ring("u", [128, 2 + 512], F32, 3)
            k.op("dve", lambda e: e.tensor_tensor(out=u.t[:, 0:2], in0=ps.t[:, 2:4], in1=hh.t[:, 2:4], op=ALU.mult),
                 reads=[ps, hh], writes=[u])
            state["u"] = u

        def epi(j, tg, ps):
            hs = env.ring("hs", [128, 512], F32, 2)
            k.op("act", lambda e: e.activation(out=hs.t[:], in_=ps[2].t[:, :], func=AF.Copy), reads=[ps[2]], writes=[hs])
            if tg == 0:
                u = state["u"]
            else:
                u = env.ring("u", [128, 2 + 512], F32, 3)
                up = state["u"]
                k.op("dve", lambda e: e.tensor_copy(out=u.t[:, 0:2], in_=up.t[:, 512:514]), reads=[up], writes=[u])
                state["u"] = u
            k.op("dve", lambda e: e.tensor_tensor(out=u.t[:, 2:514], in0=ps[1].t[:, :], in1=hs.t[:], op=ALU.mult),
                 reads=[ps[1], hs], writes=[u])
            y = env.ring("y", [128, 512], F32, 2)
            k.op("act", lambda e: e.activation(out=y.t[:], in_=u.t[:, 0:512], func=AF.Copy, scale=cw.t[:, j * 3:j * 3 + 1]),
                 reads=[u, cw], writes=[y])
            k.op("dve", lambda e: e.scalar_tensor_tensor(out=y.t[:], in0=u.t[:, 1:513], scalar=cw.t[:, j * 3 + 1:j * 3 + 2],
                                                         in1=y.t[:], op0=ALU.mult, op1=ALU.add), reads=[u, cw, y], writes=[y])
            k.op("dve", lambda e: e.scalar_tensor_tensor(out=y.t[:], in0=u.t[:, 2:514], scalar=cw.t[:, j * 3 + 2:j * 3 + 3],
                                                         in1=y.t[:], op0=ALU.mult, op1=ALU.add), reads=[u, cw, y], writes=[y])
            k.op("dve", lambda e: e.tensor_tensor(out=env.actT[tg].t[:, j, :], in0=ps[0].t[:, :], in1=y.t[:], op=ALU.mult),
                 reads=[ps[0], y], writes=[env.actT[tg]])

        type_a(env, units, 3, epi, halo_fn)
        slabs = [[io["w_out_t"][s, pc] for pc in range(c.KC // 4)] for s in range(c.D // 512)]
        if stages >= 2:
            type_b(env, c.KC, slabs, hbufs[0], h_dram[0])
        if stages >= 3:
            ln_phase(env, io["x"], None, [(h_dram[0], hbufs[0])], io["ln"][0, :], io["ln"][1, :], x1, x1bufs, True, False)
        if stages >= 4:
            ffn_and_ln(env, io, x1, x1bufs, h_dram, hbufs, xo, xobufs, True, False)
        k.finish()
    return nc


def sc_inputs(cfg, w_in, conv_w, w_out, w_gu, w_dn, ln4):
    c = cfg
    D, F = c.D, c.F
    d = {}
    d["ident"] = np.eye(128, dtype=np.float32)
    cols = []
    for j in range(c.KC):
        cols += [j * 128, D + j * 128, 2 * D + j * 128]
    d["w_in_t"] = tile_a(w_in, cols).reshape(c.KC, 3, 128, c.KC * 128)
    d["cw"] = np.ascontiguousarray(conv_w.reshape(3, c.KC, 128).transpose(2, 1, 0)).reshape(128, c.KC * 3)
    d["w_out_t"] = tile_b(w_out, 0, c.KC)
    d.update(ffn_inputs(cfg, w_gu, w_dn))
    d["ln"] = np.ascontiguousarray(ln4, dtype=np.float32)
    return d


def ffn_inputs(cfg, w_gu, w_dn):
    c = cfg
    d = {}
    cols = []
    for j in range(c.F // 128):
        cols += [j * 128, c.F + j * 128]
    d["w_gu_t"] = tile_a(w_gu, cols).reshape(c.F // 128, 2, 128, c.KC * 128)
    r0 = 0
    for gi, G in enumerate(c.fgroups):
        d["w_dn_t%d" % gi] = tile_b(w_dn, r0, G)
        r0 += G * 128
    return d


def gdn_consts():
    i = np.arange(128)
    same = (i[:, None] // 64) == (i[None, :] // 64)
    c = np.zeros((8, 128, 128), np.float32)
    c[0] = ((i[:, None] <= i[None, :]) & same)
    c[1] = (i[:, None] > i[None, :])
    c[2] = ((i[:, None] > i[None, :]) & same)
    c[3] = ((i[None, :] >= i[:, None]) & same)
    c[4] = same
    c[5] = 1.0
    c[6] = (i[:, None] < 64) * np.ones((1, 128))
    c[7] = (i[:, None] >= 64) * np.ones((1, 128))
    return c


def roundrobin(gens):
    gens = list(gens)
    while gens:
        nxt = []
        for g in gens:
            try:
                next(g)
                nxt.append(g)
            except StopIteration:
                pass
        gens = nxt


LIMIT = [1 << 60]
USE_PT_VIEWS = False


def build_gdn_program(cfg, mode, dbg=99):
    c = cfg
    NVH, NKH, KC = c.NVH, c.NKH, c.KC
    NV = 256 if mode == "a" else 128
    nch = 4 if mode == "a" else 6
    vgroups = [min(16, NVH - g0) for g0 in range(0, NVH, 16)]
    nc = bass.Bass("TRN2", target_bir_lowering=False)
    io = {}
    io["x"] = dram_in(nc, "x", [c.NT, c.D])
    io["xh"] = dram_in(nc, "xh", [4, c.D])
    io["ident"] = dram_in(nc, "ident", [128, 128])
    io["consts"] = dram_in(nc, "consts", [8, 128, 128])
    io["w_t"] = dram_in(nc, "w_t", [NKH, 6, 128, KC * 128])
    io["w_ba"] = dram_in(nc, "w_ba", [128, KC * 2 * NVH])
    io["cwg"] = dram_in(nc, "cwg", [128, NKH * 16])
    io["alog"] = dram_in(nc, "alog", [1, NVH])
    io["dtb"] = dram_in(nc, "dtb", [1, NVH])
    if mode == "a":
        st_out = nc.dram_tensor("st", [NVH, 128, 256], F32, kind="ExternalOutput").ap()
        oq_out = nc.dram_tensor("oq", [NVH, c.NT, 256], F32, kind="ExternalOutput").ap()
    else:
        io["normw"] = dram_in(nc, "normw", [1, 128])
        io["sin_pt"] = dram_in(nc, "sin_pt", [3, NVH, 128, 128])
        io["sin_s"] = dram_in(nc, "sin_s", [3, NVH, 128, 128])
        io["w_o_t"] = [dram_in(nc, "w_o_t%d" % gi, [c.D // 512, G // 4, 128, 2048]) for gi, G in enumerate(vgroups)]
        io["w_gu_t"] = dram_in(nc, "w_gu_t", [c.F // 128, 2, 128, KC * 128])
        io["w_dn_t"] = [dram_in(nc, "w_dn_t%d" % gi, [c.D // 512, G // 4, 128, 2048]) for gi, G in enumerate(c.fgroups)]
        io["ln"] = dram_in(nc, "ln", [4, c.D])
        xo = nc.dram_tensor("xo", [c.NT, c.D], F32, kind="ExternalOutput").ap()
        P = max(len(vgroups), len(c.fgroups))
        h_dram = [nc.dram_tensor("h%d" % p, [c.NT, c.D], F32, kind="Internal").ap() for p in range(P)]
        x1 = nc.dram_tensor("x1", [c.NT, c.D], F32, kind="Internal").ap()
        onT_d = nc.dram_tensor("onT_d", [NVH, 128, c.NT], BF16, kind="Internal").ap()
    with ExitStack() as es:
        k = K(nc, es)
        k.limit = LIMIT[0]
        env = Env(k, c, io, alloc_act=False, ws_slots=(6 if mode == "a" else 7))
        with env.phase():
            emit_halo(env, io["xh"])
            load_x_to_xT(env, io["x"])
        onbufs = [[Buf("on_%d_%d" % (h, tg)) for tg in range(c.NTG)] for h in range(NVH)]
        if dbg <= 1:
            k.finish()
            return nc

        with env.phase():
            def f32t(name, shape=(128, 128)):
                return k.sbuf(name, list(shape), F32)

            CN = []
            for i in range(8):
                t = f32t("cst%d" % i)
                k.dma("sp", out=t.t[:], in_=io["consts"][i], writes=[t], owner=t)
                CN.append(t)
            TRIBD, UU, STRICT, CAUSALT, BLK, ONES, CH0, CH1 = CN
            cwg = f32t("cwg_sb", (128, NKH * 16))
            k.dma("sp", out=cwg.t[:], in_=io["cwg"], writes=[cwg], owner=cwg)
            wba_f = f32t("wba_f", (128, KC * 2 * NVH))
            wba = k.sbuf("wba_b", [128, KC, 2 * NVH], BF16)
            k.dma("sp", out=wba_f.t[:], in_=io["w_ba"], writes=[wba_f], owner=wba_f)
            k.op("dve", lambda e: e.tensor_copy(out=wba.t[:].rearrange("p a b -> p (a b)"), in_=wba_f.t[:]),
                 reads=[wba_f], writes=[wba])
            alog = f32t("alog_sb", (128, NVH))
            dtb = f32t("dtb_sb", (128, NVH))
            k.dma("sp", out=alog.t[:], in_=io["alog"][0, :].partition_broadcast(128), writes=[alog], owner=alog)
            k.dma("sp", out=dtb.t[:], in_=io["dtb"][0, :].partition_broadcast(128), writes=[dtb], owner=dtb)
            negA = f32t("negA", (128, NVH))
            k.op("act", lambda e: e.activation(out=negA.t[:], in_=alog.t[:], func=AF.Exp), reads=[alog], writes=[negA])
            k.op("dve", lambda e: e.tensor_scalar(out=negA.t[:], in0=negA.t[:], scalar1=-1.0, scalar2=None, op0=ALU.mult),
                 reads=[negA], writes=[negA])
            if mode == "b":
                nwbc = f32t("nwbc")
                k.dma("sp", out=nwbc.t[:], in_=io["normw"][0, :].partition_broadcast(128), writes=[nwbc], owner=nwbc)
            shp = (128, c.NTT, NVH)
            BETA, AX, GRAW, GC, EG = [f32t(n, shp) for n in ("BETA", "AX", "GRAW", "GC", "EG")]
            EKT, BEG = AX, GC
            CD = f32t("CD", (128, c.NTT, 2 * NVH))

            for tt in range(c.NTT):
                g, o = tt // 4, (tt % 4) * 128
                ps = env.pnext()
                for kc in range(KC):
                    mm(env, ps.t[:, 0:2 * NVH], env.xT[g].t[:, kc, o:o + 128], wba.t[:, kc, :], kc == 0, kc == KC - 1,
                       [env.xT[g], wba], [ps])
                k.op("act", lambda e: e.activation(out=BETA.t[:, tt, :], in_=ps.t[:, 0:NVH], func=AF.Sigmoid),
                     reads=[ps], writes=[BETA])
                k.op("dve", lambda e: e.tensor_tensor(out=AX.t[:, tt, :], in0=ps.t[:, NVH:2 * NVH], in1=dtb.t[:], op=ALU.add),
                     reads=[ps, dtb], writes=[AX])
            k.op("act", lambda e: e.activation(out=AX.t[:], in_=AX.t[:], func=AF.Exp), reads=[AX], writes=[AX])
            k.op("act", lambda e: e.activation(out=AX.t[:], in_=AX.t[:], func=AF.Ln, bias=1.0), reads=[AX], writes=[AX])
            for tt in range(c.NTT):
                k.op("dve", lambda e: e.tensor_tensor(out=GRAW.t[:, tt, :], in0=AX.t[:, tt, :], in1=negA.t[:], op=ALU.mult),
                     reads=[AX, negA], writes=[GRAW])
            for tt in range(c.NTT):
                gm = env.ring("gm", [128, 2 * NVH], F32, 2)
                k.op("dve", lambda e: e.tensor_scalar(out=gm.t[:, 0:NVH], in0=GRAW.t[:, tt, :], scalar1=CH0.t[:, 0:1],
                                                      scalar2=None, op0=ALU.mult), reads=[GRAW, CH0], writes=[gm])
                k.op("dve", lambda e: e.tensor_scalar(out=gm.t[:, NVH:2 * NVH], in0=GRAW.t[:, tt, :], scalar1=CH1.t[:, 0:1],
                                                      scalar2=None, op0=ALU.mult), reads=[GRAW, CH1], writes=[gm])
                ps = env.pnext()
                mm(env, ps.t[:, 0:NVH], TRIBD.t[:], GRAW.t[:, tt, :], True, True, [TRIBD, GRAW], [ps])
                mm(env, ps.t[:, 64:64 + NVH], BLK.t[:], GRAW.t[:, tt, :], True, True, [BLK, GRAW], [ps])
                mm(env, ps.t[:, 128:128 + 2 * NVH], ONES.t[:], gm.t[:], True, True, [ONES, gm], [ps])
                k.op("dve", lambda e: e.tensor_copy(out=GC.t[:, tt, :], in_=ps.t[:, 0:NVH]), reads=[ps], writes=[GC])
                k.op("act", lambda e: e.activation(out=EG.t[:, tt, :], in_=ps.t[:, 0:NVH], func=AF.Exp), reads=[ps], writes=[EG])
                k.op("dve", lambda e: e.tensor_tensor(out=EKT.t[:, tt, :], in0=ps.t[:, 64:64 + NVH], in1=GC.t[:, tt, :],
                                                      op=ALU.subtract), reads=[ps, GC], writes=[EKT])
                k.op("act", lambda e: e.activation(out=EKT.t[:, tt, :], in_=EKT.t[:, tt, :], func=AF.Exp),
                     reads=[EKT], writes=[EKT])
                k.op("act", lambda e: e.activation(out=CD.t[:, tt, :], in_=ps.t[:, 128:128 + 2 * NVH], func=AF.Exp),
                     reads=[ps], writes=[CD])
                k.op("dve", lambda e: e.tensor_tensor(out=BEG.t[:, tt, :], in0=BETA.t[:, tt, :], in1=EG.t[:, tt, :], op=ALU.mult),
                     reads=[BETA, EG], writes=[BEG])

            if dbg <= 2:
                k.finish()
                return nc
            if mode == "a" and USE_PT_VIEWS:
                for pt_b in env.pT:
                    v_ = pt_b.view(pt_b.name + "_f32", pt_b.t[:].bitcast(F32))
                    v_.is_psum = True
                    env.pA.append(v_)
            Sf = [f32t("Sf%d" % i, (128, NV)) for i in range(2)]
            Sb = [k.sbuf("Sb%d" % i, [128, NV], BF16) for i in range(2)]
            ubuf = [[f32t("u%d_%d" % (i, j), (128, NV)) for j in range(3)] for i in range(2)]
            if NV == 256:
                for i in range(2):
                    for j in range(3):
                        k.op("pool", lambda e: e.memset(ubuf[i][j].t[:], 0.0), writes=[ubuf[i][j]])
            ucnt = [0, 0]
            carry = [f32t("carry%d" % ci, (128, 4)) for ci in range(4)]
            ws = env.ws
            for kh in range(NKH):
                for ci in range(nch):
                    ws.push((io["w_t"][kh, ci], KC * 128))

            for kh in range(NKH):
                wb = ws.take(nch)
                for hh in range(2):
                    h = 2 * kh + hh
                    if mode == "a":
                        k.op("pool", lambda e: e.memset(Sf[hh].t[:, 0:128], 0.0), writes=[Sf[hh]])
                        k.op("pool", lambda e: e.tensor_copy(out=Sf[hh].t[:, 128:256], in_=env.ident_f.t[:]),
                             reads=[env.ident_f], writes=[Sf[hh]])
                    else:
                        k.dma("sp", out=Sf[hh].t[:], in_=io["sin_s"][0, h], writes=[Sf[hh]], owner=Sf[hh])
                        for j in (1, 2):
                            pt_ = env.ring("foldp", [128, 128], F32, 2)
                            sl_ = env.ring("folds", [128, 128], F32, 2)
                            k.dma("sp", out=pt_.t[:], in_=io["sin_pt"][j, h], writes=[pt_], owner=pt_)
                            k.dma("sp", out=sl_.t[:], in_=io["sin_s"][j, h], writes=[sl_], owner=sl_)
                            ps = env.pnext()
                            mm(env, ps.t[:, 0:128], pt_.t[:], Sf[hh].t[:], True, True, [pt_, Sf[hh]], [ps])
                            k.op("dve", lambda e: e.tensor_tensor(out=Sf[hh].t[:], in0=ps.t[:, 0:128], in1=sl_.t[:], op=ALU.add),
                                 reads=[ps, sl_], writes=[Sf[hh]])
                    k.op("act", lambda e: e.activation(out=Sb[hh].t[:], in_=Sf[hh].t[:], func=AF.Copy),
                         reads=[Sf[hh]], writes=[Sb[hh]])
                psh = env.pnext()
                for ci in range(4):
                    for kc in range(KC):
                        mm(env, psh.t[:, ci * 4:ci * 4 + 4], wb[ci].t[:, kc * 128:(kc + 1) * 128], env.xhT.t[:, kc, :],
                           kc == 0, kc == KC - 1, [wb[ci], env.xhT], [psh])
                for ci in range(4):
                    k.op("dve", lambda e: e.tensor_copy(out=carry[ci].t[:, 0:3], in_=psh.t[:, ci * 4 + 1:ci * 4 + 4]),
                         reads=[psh], writes=[carry[ci]])
                TG = {}

                def proj(tg):
                    knT = env.ring("knT", [128, 512], F32, 2)
                    qnT = env.ring("qnT", [128, 512], F32, 2)
                    qb = env.ring("qb", [128, 512], BF16, 2)
                    vT = [env.ring("vT%d" % i, [128, 512], F32, 2) for i in range(2)]
                    sz = [env.ring("sz%d" % i, [128, 512], BF16, 2) for i in range(2)] if mode == "b" else None
                    TG[tg] = dict(knT=knT, qnT=qnT, qb=qb, vT=vT, sz=sz)
                    for ci in range(nch):
                        ps = env.pnext()
                        for kc in range(KC):
                            mm(env, ps.t[:, :], wb[ci].t[:, kc * 128:(kc + 1) * 128], env.xT[tg].t[:, kc, :],
                               kc == 0, kc == KC - 1, [wb[ci], env.xT[tg]], [ps])
                        if ci >= 4:
                            k.op("act", lambda e: e.activation(out=sz[ci - 4].t[:], in_=ps.t[:, :], func=AF.Silu),
                                 reads=[ps], writes=[sz[ci - 4]])
                            continue
                        pre = env.ring("pre", [128, 3 + 512], F32, 2)
                        k.op("dve", lambda e: e.tensor_copy(out=pre.t[:, 0:3], in_=carry[ci].t[:, 0:3]), reads=[carry[ci]], writes=[pre])
                        k.op("act", lambda e: e.activation(out=pre.t[:, 3:515], in_=ps.t[:, :], func=AF.Copy), reads=[ps], writes=[pre])
                        k.op("dve", lambda e: e.tensor_copy(out=carry[ci].t[:, 0:3], in_=pre.t[:, 512:515]), reads=[pre], writes=[carry[ci]])
                        cb = (kh * 4 + ci) * 4
                        ca = env.ring("cacc", [128, 512], F32, 2)
                        k.op("act", lambda e: e.activation(out=ca.t[:], in_=pre.t[:, 0:512], func=AF.Copy, scale=cwg.t[:, cb:cb + 1]),
                             reads=[pre, cwg], writes=[ca])
                        for tap in (1, 2, 3):
                            k.op("dve", lambda e: e.scalar_tensor_tensor(out=ca.t[:], in0=pre.t[:, tap:tap + 512],
                                                                         scalar=cwg.t[:, cb + tap:cb + tap + 1], in1=ca.t[:],
                                                                         op0=ALU.mult, op1=ALU.add), reads=[pre, cwg, ca], writes=[ca])
                        if ci >= 2:
                            k.op("act", lambda e: e.activation(out=vT[ci - 2].t[:], in_=ca.t[:], func=AF.Silu),
                                 reads=[ca], writes=[vT[ci - 2]])
                            continue
                        sl = env.ring("sl", [128, 512], F32, 1)
                        k.op("act", lambda e: e.activation(out=sl.t[:], in_=ca.t[:], func=AF.Silu), reads=[ca], writes=[sl])
                        k.op("pool", lambda e: e.tensor_tensor(out=ca.t[:], in0=sl.t[:], in1=sl.t[:], op=ALU.mult), reads=[sl], writes=[ca])
                        ps2 = env.pnext()
                        mm(env, ps2.t[:, :], ONES.t[:], ca.t[:], True, True, [ONES, ca], [ps2])
                        k.op("act", lambda e: e.activation(out=ca.t[:], in_=ps2.t[:, :], func=AF.Sqrt, bias=float(RMS_EPS)),
                             reads=[ps2], writes=[ca])
                        k.op("dve", lambda e: e.reciprocal(out=ca.t[:], in_=ca.t[:]), reads=[ca], writes=[ca])
                        if ci == 1:
                            k.op("dve", lambda e: e.tensor_tensor(out=knT.t[:], in0=sl.t[:], in1=ca.t[:], op=ALU.mult),
                                 reads=[sl, ca], writes=[knT])
                        else:
                            k.op("dve", lambda e: e.scalar_tensor_tensor(out=qnT.t[:], in0=sl.t[:], scalar=float(HD ** -0.5),
                                                                         in1=ca.t[:], op0=ALU.mult, op1=ALU.mult),
                                 reads=[sl, ca], writes=[qnT])
                            k.op("pool", lambda e: e.tensor_copy(out=qb.t[:], in_=qnT.t[:]), reads=[qnT], writes=[qb])

                tile = {}

                def prep(tt):
                    cols = slice((tt % 4) * 128, (tt % 4 + 1) * 128)
                    knT, qnT, vT = TG[tt // 4]["knT"], TG[tt // 4]["qnT"], TG[tt // 4]["vT"]
                    T = {}
                    psk = env.pnext()
                    k.op("pe", lambda e: e.transpose(psk.t[:, 0:128], knT.t[:, cols], env.ident_f.t[:]),
                         reads=[knT, env.ident_f], writes=[psk])
                    psv = env.pnext()
                    for hh in range(2):
                        k.op("pe", lambda e: e.transpose(psv.t[:, hh * 128:(hh + 1) * 128], vT[hh].t[:, cols], env.ident_f.t[:]),
                             reads=[vT[hh], env.ident_f], writes=[psv])
                    T["kbg"], T["ktl"], T["vb"] = [], [], []
                    for hh in range(2):
                        h = 2 * kh + hh
                        kbg = env.ring("kbg%d" % hh, [128, 128], F32, 3)
                        ktl = env.ring("ktl%d" % hh, [128, 128], BF16, 3)
                        vb = env.ring("vb%d" % hh, [128, 128], F32, 3)
                        k.op("act", lambda e: e.activation(out=kbg.t[:], in_=psk.t[:, 0:128], func=AF.Copy, scale=BEG.t[:, tt, h:h + 1]),
                             reads=[psk, BEG], writes=[kbg])
                        k.op("dve", lambda e: e.tensor_scalar(out=ktl.t[:], in0=psk.t[:, 0:128], scalar1=EKT.t[:, tt, h:h + 1],
                                                              scalar2=None, op0=ALU.mult), reads=[psk, EKT], writes=[ktl])
                        k.op("dve", lambda e: e.tensor_scalar(out=vb.t[:], in0=psv.t[:, hh * 128:(hh + 1) * 128],
                                                              scalar1=BETA.t[:, tt, h:h + 1], scalar2=None, op0=ALU.mult),
                             reads=[psv, BETA], writes=[vb])
                        T["kbg"].append(kbg); T["ktl"].append(ktl); T["vb"].append(vb)
                    yield
                    pskk = env.pnext()
                    mm(env, pskk.t[:, 0:128], knT.t[:, cols], knT.t[:, cols], True, True, [knT], [pskk])
                    mm(env, pskk.t[:, 128:256], knT.t[:, cols], qnT.t[:, cols], True, True, [knT, qnT], [pskk])
                    KKs = env.ring("KKs", [128, 128], F32, 3)
                    QKm = env.ring("QKm", [128, 128], F32, 3)
                    k.op("dve", lambda e: e.tensor_tensor(out=KKs.t[:], in0=pskk.t[:, 0:128], in1=STRICT.t[:], op=ALU.mult),
                         reads=[pskk, STRICT], writes=[KKs])
                    k.op("dve", lambda e: e.tensor_tensor(out=QKm.t[:], in0=pskk.t[:, 128:256], in1=CAUSALT.t[:], op=ALU.mult),
                         reads=[pskk, CAUSALT], writes=[QKm])
                    T["KKs"], T["QKm"] = KKs, QKm
                    T["u"], T["wT"], T["aqk"] = [None, None], [None, None], [None, None]
                    tile[tt] = T
                    yield

                def solve(tt, hh):
                    T = tile[tt]
                    h = 2 * kh + hh
                    G = env.ring("G%d_%d" % (hh, tt % 2), [128, 128], F32, 1)
                    k.op("act", lambda e: e.activation(out=G.t[:], in_=TRIBD.t[:], func=AF.Copy, scale=GRAW.t[:, tt, h:h + 1]),
                         reads=[TRIBD, GRAW], writes=[G])
                    ps = env.pnext()
                    mm(env, ps.t[:, 0:128], G.t[:], UU.t[:], True, True, [G, UU], [ps])
                    mm(env, ps.t[:, 128:256], UU.t[:], G.t[:], True, True, [G, UU], [ps])
                    Dec = env.ring("Dec%d_%d" % (hh, tt % 2), [128, 256], F32, 1)
                    k.op("act", lambda e: e.activation(out=Dec.t[:], in_=ps.t[:, 0:256], func=AF.Exp), reads=[ps], writes=[Dec])
                    yield
                    A = env.ring("A%d_%d" % (hh, tt % 2), [128, 128], F32, 2)
                    k.op("dve", lambda e: e.scalar_tensor_tensor(out=A.t[:], in0=T["KKs"].t[:], scalar=BETA.t[:, tt, h:h + 1],
                                                                 in1=Dec.t[:, 0:128], op0=ALU.mult, op1=ALU.mult),
                         reads=[T["KKs"], BETA, Dec], writes=[A])
                    aqk = env.ring("aqk%d" % hh, [128, 128], BF16, 3)
                    k.op("pool", lambda e: e.tensor_tensor(out=aqk.t[:], in0=T["QKm"].t[:], in1=Dec.t[:, 128:256], op=ALU.mult),
                         reads=[T["QKm"], Dec], writes=[aqk])
                    T["aqk"][hh] = aqk
                    ps = env.pnext()
                    k.op("pe", lambda e: e.transpose(ps.t[:, 0:128], A.t[:], env.ident_f.t[:]), reads=[A, env.ident_f], writes=[ps])
                    X = env.ring("X%d_%d" % (hh, tt % 2), [128, 128], F32, 2)
                    B = env.ring("B%d_%d" % (hh, tt % 2), [128, 128], F32, 2)
                    k.op("dve", lambda e: e.tensor_tensor(out=X.t[:], in0=env.ident_f.t[:], in1=ps.t[:, 0:128], op=ALU.subtract),
                         reads=[env.ident_f, ps], writes=[X])
                    k.op("act", lambda e: e.activation(out=B.t[:], in_=ps.t[:, 0:128], func=AF.Copy), reads=[ps], writes=[B])
                    yield
                    for m in (1, 2, 4, 8, 16):
                        ps = env.pnext()
                        mm(env, ps.t[:, 0:128], B.t[:], A.t[:], True, True, [A, B], [ps])
                        if m < 16:
                            mm(env, ps.t[:, 128:256], A.t[:], B.t[:], True, True, [A, B], [ps])
                        A2 = env.ring("A%d_%d" % (hh, tt % 2), [128, 128], F32, 2)
                        k.op("act", lambda e: e.activation(out=A2.t[:], in_=ps.t[:, 0:128], func=AF.Copy), reads=[ps], writes=[A2])
                        if m < 16:
                            B2 = env.ring("B%d_%d" % (hh, tt % 2), [128, 128], F32, 2)
                            k.op("dve", lambda e: e.tensor_copy(out=B2.t[:], in_=ps.t[:, 128:256]), reads=[ps], writes=[B2])
                        ps2 = env.pnext()
                        mm(env, ps2.t[:, 0:128], A2.t[:], X.t[:], True, True, [A2, X], [ps2])
                        X2 = env.ring("X%d_%d" % (hh, tt % 2), [128, 128], F32, 2)
                        k.op("dve", lambda e: e.tensor_tensor(out=X2.t[:], in0=X.t[:], in1=ps2.t[:, 0:128], op=ALU.add),
                             reads=[X, ps2], writes=[X2])
                        A, X = A2, X2
                        if m < 16:
                            B = B2
                        yield
                    ps = env.pnext()
                    mm(env, ps.t[:, 0:128], X.t[:], T["vb"][hh].t[:], True, True, [X, T["vb"][hh]], [ps])
                    mm(env, ps.t[:, 128:256], T["kbg"][hh].t[:], X.t[:], True, True, [X, T["kbg"][hh]], [ps])
                    u = ubuf[hh][ucnt[hh] % 3]
                    ucnt[hh] += 1
                    k.op("act", lambda e: e.activation(out=u.t[:, 0:128], in_=ps.t[:, 0:128], func=AF.Copy), reads=[ps], writes=[u])
                    wT = env.ring("wT%d" % hh, [128, 128], BF16, 3)
                    k.op("dve", lambda e: e.tensor_copy(out=wT.t[:], in_=ps.t[:, 128:256]), reads=[ps], writes=[wT])
                    T["u"][hh], T["wT"][hh] = u, wT
                    yield

                def recur(tt, hh):
                    T = tile[tt]
                    h = 2 * kh + hh
                    cols = slice((tt % 4) * 128, (tt % 4 + 1) * 128)
                    tg, o = tt // 4, (tt % 4) * 128
                    qb, sz = TG[tg]["qb"], TG[tg]["sz"]
                    u, wT, aqk, ktl = T["u"][hh], T["wT"][hh], T["aqk"][hh], T["ktl"][hh]
                    ot = env.ring("ot%d" % hh, [128, NV], F32, 2)
                    for cc in range(2):
                        r = slice(64 * cc, 64 * cc + 64)
                        ps = env.pnext()
                        mm(env, ps.t[r, 0:NV], wT.t[:, r], Sb[hh].t[:, :], True, True, [wT, Sb[hh]], [ps])
                        vn = env.ring("vn%d" % hh, [128, NV], BF16, 2)
                        k.op("dve", lambda e: e.tensor_tensor(out=vn.t[r, :], in0=u.t[r, :], in1=ps.t[r, 0:NV], op=ALU.subtract),
                             reads=[u, ps], writes=[vn])
                        yield
                        if True:
                            ps1 = env.pnext()
                            mm(env, ps1.t[r, 0:NV], qb.t[:, o + 64 * cc:o + 64 * cc + 64], Sb[hh].t[:, :], True, True,
                               [qb, Sb[hh]], [ps1])
                            ps2 = env.pnext()
                            mm(env, ps2.t[r, 0:NV], aqk.t[r, r], vn.t[r, :], True, True, [aqk, vn], [ps2])
                        ps3 = env.pnext()
                        mm(env, ps3.t[:, 0:NV], ktl.t[r, :], vn.t[r, :], True, True, [ktl, vn], [ps3])
                        if True:
                            o2 = env.ring("o2%d" % hh, [128, NV], F32, 1)
                            k.op("act", lambda e: e.activation(out=o2.t[r, :], in_=ps2.t[r, 0:NV], func=AF.Copy), reads=[ps2], writes=[o2])
                            k.op("dve", lambda e: e.scalar_tensor_tensor(out=ot.t[r, :], in0=ps1.t[r, 0:NV], scalar=EG.t[r, tt, h:h + 1],
                                                                         in1=o2.t[r, :], op0=ALU.mult, op1=ALU.add),
                                 reads=[ps1, EG, o2], writes=[ot])
                        k.op("dve", lambda e: e.scalar_tensor_tensor(out=Sf[hh].t[:], in0=Sf[hh].t[:],
                                                                     scalar=CD.t[:, tt, cc * NVH + h:cc * NVH + h + 1],
                                                                     in1=ps3.t[:, 0:NV], op0=ALU.mult, op1=ALU.add),
                             reads=[Sf[hh], CD, ps3], writes=[Sf[hh]])
                        k.op("act", lambda e: e.activation(out=Sb[hh].t[:], in_=Sf[hh].t[:], func=AF.Copy), reads=[Sf[hh]], writes=[Sb[hh]])
                        yield
                    if mode == "a":
                        k.dma("act", out=oq_out[h, tt * 128:(tt + 1) * 128, :], in_=ot.t[:], reads=[ot], writes=[], owner=ot,
                              is_output=True)
                    if mode == "b":
                        sq = env.ring("osq", [128, 128], F32, 2)
                        ss = env.ring("oss", [128, 2], F32, 4)
                        k.op("pool", lambda e: e.tensor_tensor(out=sq.t[:], in0=ot.t[:], in1=ot.t[:], op=ALU.mult), reads=[ot], writes=[sq])
                        k.op("dve", lambda e: e.reduce_sum(out=ss.t[:, 0:1], in_=sq.t[:], axis=mybir.AxisListType.X), reads=[sq], writes=[ss])
                        k.op("act", lambda e: e.activation(out=ss.t[:, 1:2], in_=ss.t[:, 0:1], func=AF.Sqrt, scale=1.0 / HD,
                                                           bias=float(RMS_EPS)), reads=[ss], writes=[ss])
                        k.op("dve", lambda e: e.reciprocal(out=ss.t[:, 1:2], in_=ss.t[:, 1:2]), reads=[ss], writes=[ss])
                        onb = env.ring("onb", [128, 128], BF16, 2)
                        k.op("dve", lambda e: e.scalar_tensor_tensor(out=onb.t[:], in0=ot.t[:], scalar=ss.t[:, 1:2], in1=nwbc.t[:],
                                                                     op0=ALU.mult, op1=ALU.mult), reads=[ot, ss, nwbc], writes=[onb])
                        pt = env.ptnext()
                        k.op("pe", lambda e: e.transpose(pt.t[:, 0:128], onb.t[:], env.ident_bf.t[:]), reads=[onb, env.ident_bf], writes=[pt])
                        if tt % 4 == 0:
                            tile["og%d" % hh] = env.ring("og%d" % hh, [128, 512], BF16, 2)
                        og = tile["og%d" % hh]
                        k.op("dve", lambda e: e.tensor_tensor(out=og.t[:, o:o + 128], in0=pt.t[:, 0:128], in1=sz[hh].t[:, cols], op=ALU.mult),
                             reads=[pt, sz[hh]], writes=[og])
                        if tt % 4 == 3:
                            k.dma("act", out=onT_d[h, :, tg * 512:(tg + 1) * 512], in_=og.t[:], reads=[og], writes=[onbufs[h][tg]], owner=og)
                        yield

                def tile_front(tt):
                    for _ in prep(tt):
                        yield
                    gens = [solve(tt, 0), solve(tt, 1)]
                    while gens:
                        nxt = []
                        for g_ in gens:
                            try:
                                next(g_)
                                nxt.append(g_)
                            except StopIteration:
                                pass
                        gens = nxt
                        yield

                def tile_back(tt):
                    gens = [recur(tt, 0), recur(tt, 1)]
                    while gens:
                        nxt = []
                        for g_ in gens:
                            try:
                                next(g_)
                                nxt.append(g_)
                            except StopIteration:
                                pass
                        gens = nxt
                        yield
                    del tile[tt]

                def start_front(t):
                    if t % 4 == 0:
                        proj(t // 4)
                    return tile_front(t)

                def step(g_):
                    try:
                        next(g_)
                        return True
                    except StopIteration:
                        return False

                roundrobin([start_front(0)])
                if dbg <= 4:
                    k.finish()
                    return nc
                carry_f = None
                for tt in range(c.NTT):
                    must = [tile_back(tt)]
                    if carry_f is None and tt + 1 < c.NTT:
                        must.append(start_front(tt + 1))
                    elif carry_f:
                        must.append(carry_f)
                    ahead = start_front(tt + 2) if tt + 2 < c.NTT else None
                    ahead_alive = ahead is not None
                    while must:
                        must = [g_ for g_ in must if step(g_)]
                        if ahead_alive:
                            ahead_alive = step(ahead)
                    carry_f = None if ahead is None else (ahead if ahead_alive else False)
                if mode == "a":
                    for hh in range(2):
                        k.dma("act", out=st_out[2 * kh + hh], in_=Sf[hh].t[:], reads=[Sf[hh]], writes=[], owner=Sf[hh], is_output=True)

        if mode == "b":
            gdn_tail(env, io, vgroups, onT_d, onbufs, h_dram, x1, xo, P)
        k.finish()
    return nc


def gdn_tail(env, io, vgroups, onT_d, onbufs, h_dram, x1, xo, P):
    k, c = env.k, env.cfg
    env.alloc_act()
    hbufs = [[Buf("h%d_%d" % (p, tt)) for tt in range(c.NTT)] for p in range(P)]
    x1bufs = [Buf("x1_%d" % tt) for tt in range(c.NTT)]
    xobufs = [Buf("xo_%d" % tt) for tt in range(c.NTT)]
    parts = []
    h0 = 0
    for gi, G in enumerate(vgroups):
        for tg in range(c.NTG):
            k.dma("sp", out=env.actT[tg].t[:, 0:G, :],
                  in_=onT_d[h0:h0 + G, :, tg * 512:(tg + 1) * 512].rearrange("h p t -> p h t"),
                  reads=[onbufs[h][tg] for h in range(h0, h0 + G)], writes=[env.actT[tg]], owner=env.actT[tg])
        slabs = [[io["w_o_t"][gi][s, pc] for pc in range(G // 4)] for s in range(c.D // 512)]
        type_b(env, G, slabs, hbufs[gi], h_dram[gi])
        parts.append((h_dram[gi], hbufs[gi]))
        h0 += G
    ln_phase(env, io["x"], None, parts, io["ln"][0, :], io["ln"][1, :], x1, x1bufs, True, False)
    ffn_and_ln(env, io, x1, x1bufs, h_dram, hbufs, xo, xobufs, True, False)


def build_gdn_c_program(cfg):
    c = cfg
    NVH, KC = c.NVH, c.KC
    vgroups = [min(16, NVH - g0) for g0 in range(0, NVH, 16)]
    nc = bass.Bass("TRN2", target_bir_lowering=False)
    io = {}
    io["x"] = dram_in(nc, "x", [c.NT, c.D])
    io["ident"] = dram_in(nc, "ident", [128, 128])
    io["oq"] = dram_in(nc, "oq", [NVH, c.NT, 256])
    io["w_z"] = dram_in(nc, "w_z", [NVH, 128, KC * 128])
    io["normw"] = dram_in(nc, "normw", [1, 128])
    io["sin_pt"] = dram_in(nc, "sin_pt", [3, NVH, 128, 128])
    io["sin_s"] = dram_in(nc, "sin_s", [3, NVH, 128, 128])
    io["w_o_t"] = [dram_in(nc, "w_o_t%d" % gi, [c.D // 512, G // 4, 128, 2048]) for gi, G in enumerate(vgroups)]
    io["w_gu_t"] = dram_in(nc, "w_gu_t", [c.F // 128, 2, 128, KC * 128])
    io["w_dn_t"] = [dram_in(nc, "w_dn_t%d" % gi, [c.D // 512, G // 4, 128, 2048]) for gi, G in enumerate(c.fgroups)]
    io["ln"] = dram_in(nc, "ln", [4, c.D])
    xo = nc.dram_tensor("xo", [c.NT, c.D], F32, kind="ExternalOutput").ap()
    P = max(len(vgroups), len(c.fgroups))
    h_dram = [nc.dram_tensor("h%d" % p, [c.NT, c.D], F32, kind="Internal").ap() for p in range(P)]
    x1 = nc.dram_tensor("x1", [c.NT, c.D], F32, kind="Internal").ap()
    onT_d = nc.dram_tensor("onT_d", [NVH, 128, c.NT], BF16, kind="Internal").ap()
    with ExitStack() as es:
        k = K(nc, es)
        env = Env(k, c, io, alloc_act=False, ws_slots=4)
        with env.phase():
            load_x_to_xT(env, io["x"])
        onbufs = [[Buf("on_%d_%d" % (h, tg)) for tg in range(c.NTG)] for h in range(NVH)]
        with env.phase():
            nwbc = k.sbuf("nwbc", [128, 128], F32)
            k.dma("sp", out=nwbc.t[:], in_=io["normw"][0, :].partition_broadcast(128), writes=[nwbc], owner=nwbc)
            ws = env.ws
            for h in range(NVH):
                ws.push((io["w_z"][h], KC * 128))
            for h in range(NVH):
                Sf = env.ring("Sf", [128, 128], F32, 2)
                Sb = env.ring("Sb", [128, 128], BF16, 2)
                k.dma("sp", out=Sf.t[:], in_=io["sin_s"][0, h], writes=[Sf], owner=Sf)
                for j in (1, 2):
                    pt_ = env.ring("foldp", [128, 128], F32, 2)
                    sl_ = env.ring("folds", [128, 128], F32, 2)
                    k.dma("sp", out=pt_.t[:], in_=io["sin_pt"][j, h], writes=[pt_], owner=pt_)
                    k.dma("sp", out=sl_.t[:], in_=io["sin_s"][j, h], writes=[sl_], owner=sl_)
                    ps = env.pnext()
                    mm(env, ps.t[:, 0:128], pt_.t[:], Sf.t[:], True, True, [pt_, Sf], [ps])
                    k.op("dve", lambda e: e.tensor_tensor(out=Sf.t[:], in0=ps.t[:, 0:128], in1=sl_.t[:], op=ALU.add),
                         reads=[ps, sl_], writes=[Sf])
                k.op("act", lambda e: e.activation(out=Sb.t[:], in_=Sf.t[:], func=AF.Copy), reads=[Sf], writes=[Sb])
                wb = ws.take(1)[0]
                for tg in range(c.NTG):
                    ps = env.pnext()
                    for kc in range(KC):
                        mm(env, ps.t[:, :], wb.t[:, kc * 128:(kc + 1) * 128], env.xT[tg].t[:, kc, :],
                           kc == 0, kc == KC - 1, [wb, env.xT[tg]], [ps])
                    sz = env.ring("sz", [128, 512], BF16, 2)
                    k.op("act", lambda e: e.activation(out=sz.t[:], in_=ps.t[:, :], func=AF.Silu), reads=[ps], writes=[sz])
                    og = env.ring("og", [128, 512], BF16, 2)
                    oq4 = env.ring("oq4", [128, 4, 256], F32, 2)
                    k.dma("sp", out=oq4.t[:], in_=io["oq"][h, tg * 512:(tg + 1) * 512, :].rearrange("(t p) c -> p t c", p=128),
                          writes=[oq4], owner=oq4)
                    pq = env.pnext()
                    for t4 in range(4):
                        k.op("pe", lambda e: e.transpose(pq.t[:, t4 * 128:(t4 + 1) * 128], oq4.t[:, t4, 128:256], env.ident_f.t[:]),
                             reads=[oq4, env.ident_f], writes=[pq], inc=(t4 == 3))
                    qT = env.ring("qT", [128, 512], BF16, 2)
                    k.op("act", lambda e: e.activation(out=qT.t[:], in_=pq.t[:, :], func=AF.Copy), reads=[pq], writes=[qT])
                    pc_ = env.pnext()
                    for t4 in range(4):
                        mm(env, pc_.t[:, t4 * 128:(t4 + 1) * 128], qT.t[:, t4 * 128:(t4 + 1) * 128], Sb.t[:], True, True,
                           [qT, Sb], [pc_], inc=(t4 == 3))
                    ot = env.ring("ot", [128, 4, 128], F32, 2)
                    k.op("dve", lambda e: e.tensor_tensor(out=ot.t[:], in0=pc_.t[:, :].rearrange("p (t c) -> p t c", c=128),
                                                          in1=oq4.t[:, :, 0:128], op=ALU.add), reads=[pc_, oq4], writes=[ot])
                    sq = env.ring("osq", [128, 4, 128], F32, 2)
                    ss = env.ring("oss", [128, 8], F32, 4)
                    k.op("pool", lambda e: e.tensor_tensor(out=sq.t[:], in0=ot.t[:], in1=ot.t[:], op=ALU.mult), reads=[ot], writes=[sq])
                    k.op("dve", lambda e: e.reduce_sum(out=ss.t[:, 0:4], in_=sq.t[:], axis=mybir.AxisListType.X), reads=[sq], writes=[ss])
                    k.op("act", lambda e: e.activation(out=ss.t[:, 4:8], in_=ss.t[:, 0:4], func=AF.Sqrt, scale=1.0 / HD,
                                                       bias=float(RMS_EPS)), reads=[ss], writes=[ss])
                    k.op("dve", lambda e: e.reciprocal(out=ss.t[:, 4:8], in_=ss.t[:, 4:8]), reads=[ss], writes=[ss])
                    onb = env.ring("onb", [128, 4, 128], BF16, 2)
                    for t4 in range(4):
                        k.op("dve", lambda e: e.scalar_tensor_tensor(out=onb.t[:, t4, :], in0=ot.t[:, t4, :], scalar=ss.t[:, 4 + t4:5 + t4],
                                                                     in1=nwbc.t[:], op0=ALU.mult, op1=ALU.mult),
                             reads=[ot, ss, nwbc], writes=[onb])
                    pt = env.ptnext()
                    for t4 in range(4):
                        k.op("pe", lambda e: e.transpose(pt.t[:, t4 * 128:(t4 + 1) * 128], onb.t[:, t4, :], env.ident_bf.t[:]),
                             reads=[onb, env.ident_bf], writes=[pt], inc=(t4 == 3))
                    k.op("dve", lambda e: e.tensor_tensor(out=og.t[:], in0=pt.t[:, 0:512], in1=sz.t[:], op=ALU.mult),
                         reads=[pt, sz], writes=[og])
                    k.dma("act", out=onT_d[h, :, tg * 512:(tg + 1) * 512], in_=og.t[:], reads=[og], writes=[onbufs[h][tg]], owner=og)
        gdn_tail(env, io, vgroups, onT_d, onbufs, h_dram, x1, xo, P)
        k.finish()
    return nc


def gdn_c_inputs(cfg, w_in, norm_w, w_out, w_gu, w_dn, ln4):
    c = cfg
    NKH, NVH, KC = c.NKH, c.NVH, c.KC
    KD, VD = NKH * 128, NVH * 128
    o1 = 2 * KD + VD
    d = {"ident": np.eye(128, dtype=np.float32)}
    d["w_z"] = tile_a(w_in, [o1 + h * 128 for h in range(NVH)])
    d["normw"] = np.ascontiguousarray(norm_w, dtype=np.float32).reshape(1, 128)
    g0 = 0
    gi = 0
    while g0 < NVH:
        G = min(16, NVH - g0)
        d["w_o_t%d" % gi] = tile_b(w_out, g0 * 128, G)
        g0 += G
        gi += 1
    d.update(ffn_inputs(cfg, w_gu, w_dn))
    d["ln"] = np.ascontiguousarray(ln4, dtype=np.float32)
    return d


def gdn_inputs(cfg, mode, w_in, conv_w, a_log, dt_bias, norm_w=None, w_out=None, w_gu=None, w_dn=None, ln4=None):
    c = cfg
    NKH, NVH, KC = c.NKH, c.NVH, c.KC
    KD, VD = NKH * 128, NVH * 128
    o1 = 2 * KD + VD
    o2 = o1 + VD
    d = {"ident": np.eye(128, dtype=np.float32), "consts": gdn_consts()}
    cols = []
    for kh in range(NKH):
        cols += [kh * 128, KD + kh * 128, 2 * KD + 2 * kh * 128, 2 * KD + (2 * kh + 1) * 128,
                 o1 + 2 * kh * 128, o1 + (2 * kh + 1) * 128]
    d["w_t"] = tile_a(w_in, cols).reshape(NKH, 6, 128, KC * 128)
    wba = w_in[:, o2:o2 + 2 * NVH].reshape(KC, 128, 2 * NVH).transpose(1, 0, 2)
    d["w_ba"] = np.ascontiguousarray(wba).reshape(128, KC * 2 * NVH)
    cwg = np.empty((128, NKH, 4, 4), np.float32)
    for kh in range(NKH):
        ch = [kh * 128, KD + kh * 128, 2 * KD + 2 * kh * 128, 2 * KD + (2 * kh + 1) * 128]
        for ci in range(4):
            cwg[:, kh, ci, :] = conv_w[:, ch[ci]:ch[ci] + 128].T
    d["cwg"] = cwg.reshape(128, NKH * 16)
    d["alog"] = np.ascontiguousarray(a_log, dtype=np.float32).reshape(1, NVH)
    d["dtb"] = np.ascontiguousarray(dt_bias, dtype=np.float32).reshape(1, NVH)
    if mode == "b":
        d["normw"] = np.ascontiguousarray(norm_w, dtype=np.float32).reshape(1, 128)
        g0 = 0
        gi = 0
        while g0 < NVH:
            G = min(16, NVH - g0)
            d["w_o_t%d" % gi] = tile_b(w_out, g0 * 128, G)
            g0 += G
            gi += 1
        d.update(ffn_inputs(cfg, w_gu, w_dn))
        d["ln"] = np.ascontiguousarray(ln4, dtype=np.float32)
    return d


_PROGS = {}


def _prog(name, cfg):
    if name not in _PROGS:
        if name == "sc":
            _PROGS[name] = build_sc_program(cfg)
        elif name == "ga":
            _PROGS[name] = build_gdn_program(cfg, "a")
        else:
            _PROGS[name] = build_gdn_c_program(cfg)
    return _PROGS[name]


def _halos(xs, cfg):
    out = []
    for cidx in range(N_CORES):
        if cidx % 4 == 0:
            out.append(np.zeros((4, cfg.D), np.float32))
        else:
            out.append(np.ascontiguousarray(xs[cidx - 1][-4:]))
    return out


def kernel(x, sc_w_in, sc_conv_w, sc_w_out, dn_w_in, dn_conv_w, dn_a_log, dn_dt_bias, dn_norm_w, dn_w_out,
           ffn_w_gate_up, ffn_w_down, ln_gain, ln_bias):
    cfg = Cfg()
    f = lambda a: np.ascontiguousarray(np.asarray(a), dtype=np.float32)
    x = f(x)
    NT = cfg.NT
    xs = [np.ascontiguousarray(x[ci // 4, (ci % 4) * NT:(ci % 4 + 1) * NT]) for ci in range(N_CORES)]
    cores = list(range(N_CORES))
    for layer in range(DEPTH):
        j = layer // 2
        ln4 = np.stack([f(ln_gain)[layer, 0], f(ln_bias)[layer, 0], f(ln_gain)[layer, 1], f(ln_bias)[layer, 1]])
        halos = _halos(xs, cfg)
        if layer % 2 == 0:
            base = sc_inputs(cfg, f(sc_w_in)[j], f(sc_conv_w)[j], f(sc_w_out)[j], f(ffn_w_gate_up)[layer],
                             f(ffn_w_down)[layer], ln4)
            maps = [dict(base, x=xs[ci], xh=halos[ci]) for ci in cores]
            res = run_bass_kernel_spmd(_prog("sc", cfg), maps, core_ids=cores)
            xs = [np.asarray(res.results[ci]["xo"]) for ci in cores]
        else:
            base_a = gdn_inputs(cfg, "a", f(dn_w_in)[j], f(dn_conv_w)[j], f(dn_a_log)[j], f(dn_dt_bias)[j])
            maps = [dict(base_a, x=xs[ci], xh=halos[ci]) for ci in cores]
            res = run_bass_kernel_spmd(_prog("ga", cfg), maps, core_ids=cores)
            sts = [np.asarray(res.results[ci]["st"]) for ci in cores]
            oqs = [np.asarray(res.results[ci]["oq"]) for ci in cores]
            base_b = gdn_c_inputs(cfg, f(dn_w_in)[j], f(dn_norm_w)[j], f(dn_w_out)[j], f(ffn_w_gate_up)[layer],
                                  f(ffn_w_down)[layer], ln4)
            eye = np.ascontiguousarray(np.broadcast_to(np.eye(128, dtype=np.float32), (cfg.NVH, 128, 128)))
            zer = np.zeros((cfg.NVH, 128, 128), np.float32)
            maps = []
            for ci in cores:
                q = ci % 4
                pts, ss = [eye] * (3 - q), [zer] * (3 - q)
                for p in range(ci - q, ci):
                    pts.append(np.ascontiguousarray(sts[p][:, :, 128:].transpose(0, 2, 1)))
                    ss.append(np.ascontiguousarray(sts[p][:, :, :128]))
                maps.append(dict(base_b, x=xs[ci], oq=oqs[ci], sin_pt=np.stack(pts), sin_s=np.stack(ss)))
            res = run_bass_kernel_spmd(_prog("gc", cfg), maps, core_ids=cores)
            xs = [np.asarray(res.results[ci]["xo"]) for ci in cores]
    out = np.empty((BATCH, SEQ, D_MODEL), np.float32)
    for ci in cores:
        out[ci // 4, (ci % 4) * NT:(ci % 4 + 1) * NT] = xs[ci]
    return out
```

```python
import numpy as np
from contextlib import ExitStack
import concourse.bass as bass
import concourse.mybir as mybir
from concourse.bass_utils import run_bass_kernel_spmd

F32 = mybir.dt.float32
BF16 = mybir.dt.bfloat16
F32R = mybir.dt.float32r
AF = mybir.ActivationFunctionType
ALU = mybir.AluOpType

D_MODEL = 2048
BATCH = 2
SEQ = 8192
DEPTH = 4
N_CORES = 8
FFN_HIDDEN = 5632
ALPHA = (2 * DEPTH) ** 0.25
LN_EPS = 1e-5
RMS_EPS = 1e-6
HD = 128


class Buf:
    def __init__(self, name, t=None):
        self.name = name
        self.t = t
        self.w = None
        self.r = {}
        self.dsem = None
        self.dcnt = 0
        self.parent = None
        self.children = []
        self.is_psum = False

    def view(self, name, ap):
        c = Buf(name, ap)
        c.parent = self
        self.children.append(c)
        return c


class K:
    def __init__(self, nc, es):
        self.nc = nc
        self.es = es
        self.eng = {"pe": nc.tensor, "act": nc.scalar, "dve": nc.vector,
                    "pool": nc.gpsimd, "sp": nc.sync}
        self.sem = {e: es.enter_context(nc.semaphore("s_" + e)) for e in self.eng}
        self.cnt = {e: 0 for e in self.eng}
        self.waited = {e: {} for e in self.eng}
        self.semobj = {}
        self.n_dsem = 0
        self.uid = 0
        self.out_stamps = []
        self.allbufs = []
        self.es_cur = None
        self.names = {}

    def sbuf(self, name, shape, dt):
        es = self.es_cur if getattr(self, "es_cur", None) is not None else self.es
        n = self.names.get(name, 0)
        self.names[name] = n + 1
        if n:
            name = "%s_v%d" % (name, n)
        t = es.enter_context(self.nc.sbuf_tensor(name, list(shape), dt))
        b = Buf(name, t)
        self.allbufs.append(b)
        return b

    def barrier(self):
        tgt = {}
        for e in self.eng:
            if self.cnt[e] > 0:
                tgt[id(self.sem[e])] = (self.sem[e], self.cnt[e])
        for b in self.allbufs:
            if b.dsem is not None and b.dcnt > 0:
                tgt[id(b.dsem)] = (b.dsem, 16 * b.dcnt)
        for e in self.eng:
            w = self.waited[e]
            for key, (sm, v) in tgt.items():
                if w.get(key, 0) < v:
                    self.eng[e].wait_ge(sm, v)
                    w[key] = v

    def psum(self, name, shape, dt):
        t = self.es.enter_context(self.nc.psum_tensor(name, list(shape), dt))
        b = Buf(name, t)
        b.is_psum = True
        return b

    def dsem_for(self, b):
        if b.dsem is None:
            b.dsem = self.es.enter_context(self.nc.semaphore("d%d" % self.n_dsem))
            self.n_dsem += 1
        return b.dsem

    def _wait(self, e, reads, writes):
        need = {}

        def add(st):
            if st is None:
                return
            s, v = st
            key = id(s)
            self.semobj[key] = s
            if need.get(key, 0) < v:
                need[key] = v

        def addr(b):
            for key, v in b.r.items():
                if need.get(key, 0) < v:
                    need[key] = v

        for b in reads:
            add(b.w)
            if b.is_psum:
                addr(b)
            if b.parent is not None:
                add(b.parent.w)
            for ch in b.children:
                add(ch.w)
        for b in writes:
            add(b.w)
            addr(b)
            if b.parent is not None:
                add(b.parent.w)
                addr(b.parent)
            for ch in b.children:
                add(ch.w)
                addr(ch)
        w = self.waited[e]
        own = id(self.sem[e])
        for key, v in need.items():
            if key == own and (e == "pe" or v > self.cnt[e]):
                continue
            if w.get(key, 0) < v:
                self.eng[e].wait_ge(self.semobj[key], v)
                w[key] = v

    def _stamp(self, st, reads, writes):
        key = id(st[0])
        self.semobj[key] = st[0]
        for b in reads:
            if b.r.get(key, 0) < st[1]:
                b.r[key] = st[1]
        for b in writes:
            b.w = st
            b.r = {}

    def op(self, e, fn, reads=(), writes=(), inc=True):
        self.nops = getattr(self, "nops", 0) + 1
        if self.nops > getattr(self, "limit", 1 << 60):
            return None
        self._wait(e, reads, writes)
        ins = fn(self.eng[e])
        if inc:
            ins.then_inc(self.sem[e], 1)
            self.cnt[e] += 1
            st = (self.sem[e], self.cnt[e])
        else:
            st = (self.sem[e], self.cnt[e] + 1)
        self._stamp(st, reads, writes)
        return ins

    def dma(self, e, out, in_, reads=(), writes=(), owner=None, is_output=False):
        self.nops = getattr(self, "nops", 0) + 1
        if self.nops > getattr(self, "limit", 1 << 60):
            return None
        self._wait(e, reads, writes)
        sem = self.dsem_for(owner)
        ins = self.eng[e].dma_start(out=out, in_=in_)
        ins.then_inc(sem, 16)
        owner.dcnt += 1
        st = (sem, 16 * owner.dcnt)
        self._stamp(st, reads, writes)
        if is_output:
            self.out_stamps.append(st)
        return ins

    def finish(self):
        need = {}
        for s, v in self.out_stamps:
            key = id(s)
            self.semobj[key] = s
            need[key] = max(need.get(key, 0), v)
        for key, v in need.items():
            self.eng["sp"].wait_ge(self.semobj[key], v)


class WStream:
    def __init__(self, k, name, ch_elems, n_stage, n_bf, cast_engines=("pool",)):
        self.k = k
        self.ch = ch_elems
        self.stage = [k.sbuf("%s_st%d" % (name, i), [128, ch_elems], F32) for i in range(n_stage)]
        self.bf = [k.sbuf("%s_bf%d" % (name, i), [128, ch_elems], BF16) for i in range(n_bf)]
        self.n = 0
        self.cast_engines = cast_engines
        self.queue = []
        self.ready = []

    def push(self, dram_ap):
        self.queue.append(dram_ap)

    def _fetch(self, item):
        k = self.k
        dram_ap, n = item
        st = self.stage[self.n % len(self.stage)]
        bf = self.bf[self.n % len(self.bf)]
        ce = self.cast_engines[self.n % len(self.cast_engines)]
        self.n += 1
        k.dma("sp", out=st.t[:, 0:n], in_=dram_ap, writes=[st], owner=st)
        if ce == "act":
            k.op("act", lambda e: e.activation(out=bf.t[:, 0:n], in_=st.t[:, 0:n], func=AF.Copy),
                 reads=[st], writes=[bf])
        else:
            k.op(ce, lambda e: e.tensor_copy(out=bf.t[:, 0:n], in_=st.t[:, 0:n]), reads=[st], writes=[bf])
        return bf

    def prefetch(self, depth):
        while self.queue and len(self.ready) < depth:
            self.ready.append(self._fetch(self.queue.pop(0)))

    def take(self, n):
        self.prefetch(n)
        out = [self.ready.pop(0) for _ in range(n)]
        self.prefetch(len(self.bf) - n)
        return out


class Cfg:
    def __init__(self, D=D_MODEL, NT=2048, F=FFN_HIDDEN, fgroups=(16, 16, 12), NKH=None):
        self.D = D
        self.NT = NT
        self.F = F
        self.KC = D // 128
        self.NTT = NT // 128
        self.NTG = NT // 512
        self.fgroups = tuple(fgroups)
        assert sum(fgroups) * 128 == F
        self.NKH = NKH if NKH is not None else D // HD
        self.NVH = 2 * self.NKH
        self.CH = max(self.KC * 128, 2048)


class Env:
    def __init__(self, k, cfg, consts, alloc_act=True, ws_slots=6):
        self.k = k
        self.cfg = cfg
        c = cfg
        nc = k.nc
        self.xT = [k.sbuf("xT%d" % g, [128, c.KC, 512], BF16) for g in range(c.NTG)]
        self.xhT = k.sbuf("xhT", [128, c.KC, 4], BF16)
        self.AK = max(c.KC, max(c.fgroups), min(16, c.NVH))
        self.ws = WStream(k, "w", c.CH, 2, ws_slots)
        self.lnviews = {}
        self.rings = {}
        self.pA = [k.psum("pA%d" % i, [128, 512], F32) for i in range(6)]
        self.pT = [k.psum("pT%d" % i, [128, 1024], BF16) for i in range(2)]
        self.pi = 0
        self.pti = 0
        self.ident_bf = k.sbuf("ident_bf", [128, 128], BF16)
        self.ident_f = k.sbuf("ident_f", [128, 128], F32)
        k.dma("sp", out=self.ident_f.t[:], in_=consts["ident"], writes=[self.ident_f], owner=self.ident_f)
        k.op("dve", lambda e: e.tensor_copy(out=self.ident_bf.t[:], in_=self.ident_f.t[:]),
             reads=[self.ident_f], writes=[self.ident_bf])
        if alloc_act:
            self.alloc_act()

    def alloc_act(self):
        k, c = self.k, self.cfg
        self.actT = [k.sbuf("actT%d" % g, [128, self.AK, 512], BF16) for g in range(c.NTG)]
        fl = [a.t[:].rearrange("p a b -> p (a b)") for a in self.actT]
        nb = self.AK * 512
        self.lnviews = {}

        def carve(g, off_bf, n_bf, dt, name):
            ap = fl[g][:, off_bf:off_bf + n_bf]
            if dt == F32:
                ap = ap.bitcast(F32)
            return self.actT[g].view(name, ap)

        if 2 * c.D * 2 <= nb and c.NTG >= 3:
            self.lnviews["lnr"] = [carve(0, 0, 2 * c.D, F32, "lnr0"), carve(0, 2 * c.D, 2 * c.D, F32, "lnr1")]
            self.lnviews["lnh"] = [carve(1, 0, 2 * c.D, F32, "lnh0")]
            if c.NTG >= 4:
                self.lnviews["lnh"] += [carve(3, 0, 2 * c.D, F32, "lnh1"), carve(3, 2 * c.D, 2 * c.D, F32, "lnh2")]
            self.lnviews["xb"] = [carve(1, 2 * c.D, c.D, BF16, "xb0"), carve(1, 3 * c.D, c.D, BF16, "xb1")]
            self.lnviews["gain"] = [carve(2, 0, 2 * c.D, F32, "gain")]
            self.lnviews["bias"] = [carve(2, 2 * c.D, 2 * c.D, F32, "bias")]

    def phase(self):
        env = self

        class _P:
            def __enter__(p):
                p.before = set(env.rings)
                p.es2 = ExitStack()
                p.es2.__enter__()
                p.prev = env.k.es_cur
                env.k.es_cur = p.es2
                return p

            def __exit__(p, *a):
                if a[0] is None:
                    env.k.barrier()
                env.k.es_cur = p.prev
                p.es2.__exit__(*a)
                for n in list(env.rings):
                    if n not in p.before:
                        del env.rings[n]
                return False

        return _P()

    def pnext(self):
        b = self.pA[self.pi % len(self.pA)]
        self.pi += 1
        return b

    def ptnext(self):
        b = self.pT[self.pti % len(self.pT)]
        self.pti += 1
        return b

    def ring(self, name, shape, dt, n):
        if name in self.lnviews and name not in self.rings:
            self.rings[name] = [self.lnviews[name], 0]
        if name not in self.rings:
            self.rings[name] = [[self.k.sbuf("%s%d" % (name, i), shape, dt) for i in range(n)], 0]
        r = self.rings[name]
        b = r[0][r[1] % len(r[0])]
        r[1] += 1
        return b


def f32v(b):
    return b.t[:].bitcast(F32)


def mm(env, ps_ap, lhsT, rhs, start, stop, reads, writes, inc=None):
    env.k.op("pe", lambda e: e.matmul(ps_ap, lhsT=lhsT, rhs=rhs, start=start, stop=stop),
             reads=reads, writes=writes, inc=(stop if inc is None else inc))


def emit_xT_tile(env, xf, tt):
    k, c = env.k, env.cfg
    xb = env.ring("xb", [128, c.D], BF16, 2)
    k.op("act", lambda e: e.activation(out=xb.t[:], in_=xf.t[:], func=AF.Copy), reads=[xf], writes=[xb])
    g, o = tt // 4, (tt % 4) * 128
    for k0 in range(0, c.KC, 8):
        n = min(8, c.KC - k0)
        pt = env.ptnext()
        for i in range(n):
            kc = k0 + i
            k.op("pe", lambda e: e.transpose(pt.t[:, i * 128:(i + 1) * 128], xb.t[:, kc * 128:(kc + 1) * 128],
                                             env.ident_bf.t[:]),
                 reads=[xb, env.ident_bf], writes=[pt], inc=(i == n - 1))
        k.op("dve", lambda e: e.tensor_copy(out=env.xT[g].t[:, k0:k0 + n, o:o + 128],
                                            in_=pt.t[:, 0:n * 128].rearrange("p (a b) -> p a b", b=128)),
             reads=[pt], writes=[env.xT[g]])


def emit_halo(env, xh_dram):
    k, c = env.k, env.cfg
    hf = k.sbuf("halo_f", [4, c.D], F32)
    hb = k.sbuf("halo_b", [4, c.D], BF16)
    k.dma("sp", out=hf.t[:], in_=xh_dram, writes=[hf], owner=hf)
    k.op("act", lambda e: e.activation(out=hb.t[:], in_=hf.t[:], func=AF.Copy), reads=[hf], writes=[hb])
    pt = env.ptnext()
    for kc in range(c.KC):
        k.op("pe", lambda e: e.transpose(pt.t[:, kc * 4:(kc + 1) * 4], hb.t[:, kc * 128:(kc + 1) * 128],
                                         env.ident_bf.t[0:4, 0:4]),
             reads=[hb, env.ident_bf], writes=[pt], inc=(kc == c.KC - 1))
    k.op("dve", lambda e: e.tensor_copy(out=env.xhT.t[:],
                                        in_=pt.t[:, 0:c.KC * 4].rearrange("p (a b) -> p a b", b=4)),
         reads=[pt], writes=[env.xhT])


def load_x_to_xT(env, x_dram):
    k, c = env.k, env.cfg
    for tt in range(c.NTT):
        xf = env.ring("lnr", [128, c.D], F32, 2)
        k.dma("sp", out=xf.t[:], in_=x_dram[tt * 128:(tt + 1) * 128, :], writes=[xf], owner=xf)
        emit_xT_tile(env, xf, tt)


def type_a(env, units, n_per_unit, epilogue, halo_fn=None):
    k, c = env.k, env.cfg
    ws = env.ws
    for u in units:
        for ap in u:
            ws.push((ap, c.KC * 128))
    for ui, u in enumerate(units):
        wb = ws.take(n_per_unit)
        if halo_fn is not None:
            halo_fn(ui, wb)
        for tg in range(c.NTG):
            ps = [env.pnext() for _ in range(n_per_unit)]
            for ci in range(n_per_unit):
                for kc in range(c.KC):
                    mm(env, ps[ci].t[:, :], wb[ci].t[:, kc * 128:(kc + 1) * 128], env.xT[tg].t[:, kc, :],
                       kc == 0, kc == c.KC - 1, [wb[ci], env.xT[tg]], [ps[ci]])
            epilogue(ui, tg, ps)


def type_b(env, KCb, slab_aps, hbufs, h_dram):
    k, c = env.k, env.cfg
    ws = env.ws
    npc = KCb // 4
    for s in range(c.D // 512):
        for ap in slab_aps[s]:
            ws.push((ap, 2048))
    for s in range(c.D // 512):
        pcs = ws.take(npc)
        for tt in range(c.NTT):
            g, o = tt // 4, (tt % 4) * 128
            ps = env.pnext()
            for kc in range(KCb):
                pc = pcs[kc // 4]
                mm(env, ps.t[:, :], env.actT[g].t[:, kc, o:o + 128], pc.t[:, (kc % 4) * 512:(kc % 4 + 1) * 512],
                   kc == 0, kc == KCb - 1, [env.actT[g], pc], [ps])
            ob = env.ring("ob", [128, 512], F32, 3)
            k.op("act", lambda e: e.activation(out=ob.t[:], in_=ps.t[:, :], func=AF.Copy), reads=[ps], writes=[ob])
            k.dma("act", out=h_dram[tt * 128:(tt + 1) * 128, s * 512:(s + 1) * 512], in_=ob.t[:],
                  reads=[ob], writes=[hbufs[tt]], owner=ob)


def ln_phase(env, x_dram, xbufs, parts, gain_dram, bias_dram, out_dram, obufs, make_xT, final):
    k, c = env.k, env.cfg
    gb = env.ring("gain", [128, c.D], F32, 1)
    bb = env.ring("bias", [128, c.D], F32, 1)
    k.dma("sp", out=gb.t[:], in_=gain_dram.partition_broadcast(128), writes=[gb], owner=gb)
    k.dma("sp", out=bb.t[:], in_=bias_dram.partition_broadcast(128), writes=[bb], owner=bb)
    nch = c.D // 512
    for tt in range(c.NTT):
        r = env.ring("lnr", [128, c.D], F32, 2)
        rows = slice(tt * 128, (tt + 1) * 128)
        k.dma("sp", out=r.t[:], in_=x_dram[rows, :], reads=[xbufs[tt]] if xbufs else [], writes=[r], owner=r)
        for pi, (hd, hb) in enumerate(parts):
            hin = env.ring("lnh", [128, c.D], F32, 2)
            k.dma("sp", out=hin.t[:], in_=hd[rows, :], reads=[hb[tt]], writes=[hin], owner=hin)
            if pi == 0:
                k.op("act", lambda e: e.activation(out=r.t[:], in_=r.t[:], func=AF.Copy, scale=float(ALPHA)),
                     reads=[r], writes=[r])
            k.op("pool", lambda e: e.tensor_tensor(out=r.t[:], in0=r.t[:], in1=hin.t[:], op=ALU.add),
                 reads=[r, hin], writes=[r])
        st = env.ring("lnst", [128, nch * 6], F32, 2)
        for i in range(nch):
            k.op("dve", lambda e: e.bn_stats(out=st.t[:, i * 6:(i + 1) * 6], in_=r.t[:, i * 512:(i + 1) * 512]),
                 reads=[r], writes=[st])
        mv = env.ring("lnmv", [128, 4], F32, 2)
        k.op("dve", lambda e: e.bn_aggr(out=mv.t[:, 0:2], in_=st.t[:]), reads=[st], writes=[mv])
        k.op("act", lambda e: e.activation(out=mv.t[:, 2:3], in_=mv.t[:, 1:2], func=AF.Sqrt, bias=float(LN_EPS)),
             reads=[mv], writes=[mv])
        k.op("dve", lambda e: e.reciprocal(out=mv.t[:, 2:3], in_=mv.t[:, 2:3]), reads=[mv], writes=[mv])
        k.op("dve", lambda e: e.tensor_scalar(out=r.t[:], in0=r.t[:], scalar1=mv.t[:, 0:1], scalar2=mv.t[:, 2:3],
                                              op0=ALU.subtract, op1=ALU.mult), reads=[r, mv], writes=[r])
        k.op("pool", lambda e: e.tensor_tensor(out=r.t[:], in0=r.t[:], in1=gb.t[:], op=ALU.mult),
             reads=[r, gb], writes=[r])
        k.op("pool", lambda e: e.tensor_tensor(out=r.t[:], in0=r.t[:], in1=bb.t[:], op=ALU.add),
             reads=[r, bb], writes=[r])
        k.dma("act", out=out_dram[rows, :], in_=r.t[:], reads=[r], writes=[obufs[tt]], owner=r, is_output=final)
        if make_xT:
            emit_xT_tile(env, r, tt)


def tile_a(W, col_starts):
    K_ = W.shape[0]
    KC = K_ // 128
    Wr = W.reshape(KC, 128, W.shape[1])
    out = np.empty((len(col_starts), 128, KC * 128), np.float32)
    for i, c0 in enumerate(col_starts):
        out[i] = Wr[:, :, c0:c0 + 128].transpose(1, 0, 2).reshape(128, KC * 128)
    return out


def tile_b(W, r0, KCb):
    D = W.shape[1]
    Wr = W[r0:r0 + KCb * 128].reshape(KCb // 4, 4, 128, D // 512, 512)
    return np.ascontiguousarray(Wr.transpose(3, 0, 2, 1, 4)).reshape(D // 512, KCb // 4, 128, 2048)


def dram_in(nc, name, shape):
    return nc.dram_tensor(name, list(shape), F32, kind="ExternalInput").ap()


def ffn_and_ln(env, io, x_dram, xbufs, h_dram, hbufs, out_dram, obufs, final, make_xT):
    k, c = env.k, env.cfg
    hc0 = 0
    parts = []
    for gi, G in enumerate(c.fgroups):
        units = [[io["w_gu_t"][hc0 + j, 0], io["w_gu_t"][hc0 + j, 1]] for j in range(G)]

        def epi(ui, tg, ps):
            sg = env.ring("sg", [128, 512], F32, 2)
            k.op("act", lambda e: e.activation(out=sg.t[:], in_=ps[0].t[:, :], func=AF.Silu), reads=[ps[0]], writes=[sg])
            k.op("dve", lambda e: e.tensor_tensor(out=env.actT[tg].t[:, ui, :], in0=ps[1].t[:, :], in1=sg.t[:], op=ALU.mult),
                 reads=[ps[1], sg], writes=[env.actT[tg]])

        type_a(env, units, 2, epi)
        slabs = [[io["w_dn_t"][gi][s, pc] for pc in range(G // 4)] for s in range(c.D // 512)]
        type_b(env, G, slabs, hbufs[gi], h_dram[gi])
        parts.append((h_dram[gi], hbufs[gi]))
        hc0 += G
    ln_phase(env, x_dram, xbufs, parts, io["ln"][2, :], io["ln"][3, :], out_dram, obufs, make_xT, final)


def build_sc_program(cfg, stages=9):
    c = cfg
    nc = bass.Bass("TRN2", target_bir_lowering=False)
    io = {}
    io["x"] = dram_in(nc, "x", [c.NT, c.D])
    io["xh"] = dram_in(nc, "xh", [4, c.D])
    io["ident"] = dram_in(nc, "ident", [128, 128])
    io["w_in_t"] = dram_in(nc, "w_in_t", [c.KC, 3, 128, c.KC * 128])
    io["cw"] = dram_in(nc, "cw", [128, c.KC * 3])
    io["w_out_t"] = dram_in(nc, "w_out_t", [c.D // 512, c.KC // 4, 128, 2048])
    io["w_gu_t"] = dram_in(nc, "w_gu_t", [c.F // 128, 2, 128, c.KC * 128])
    io["w_dn_t"] = [dram_in(nc, "w_dn_t%d" % gi, [c.D // 512, G // 4, 128, 2048]) for gi, G in enumerate(c.fgroups)]
    io["ln"] = dram_in(nc, "ln", [4, c.D])
    xo = nc.dram_tensor("xo", [c.NT, c.D], F32, kind="ExternalOutput").ap()
    P = max(1, len(c.fgroups))
    h_dram = [nc.dram_tensor("h%d" % p, [c.NT, c.D], F32, kind="Internal").ap() for p in range(P)]
    x1 = nc.dram_tensor("x1", [c.NT, c.D], F32, kind="Internal").ap()
    with ExitStack() as es:
        k = K(nc, es)
        env = Env(k, c, io)
        hbufs = [[Buf("h%d_%d" % (p, tt)) for tt in range(c.NTT)] for p in range(P)]
        x1bufs = [Buf("x1_%d" % tt) for tt in range(c.NTT)]
        xobufs = [Buf("xo_%d" % tt) for tt in range(c.NTT)]
        cw = k.sbuf("cw_sb", [128, c.KC * 3], F32)
        k.dma("sp", out=cw.t[:], in_=io["cw"], writes=[cw], owner=cw)
        emit_halo(env, io["xh"])
        load_x_to_xT(env, io["x"])

        units = [[io["w_in_t"][j, 0], io["w_in_t"][j, 1], io["w_in_t"][j, 2]] for j in range(c.KC)]
        state = {}

        def halo_fn(j, wb):
            ps = env.pnext()
            for ci in (1, 2):
                for kc in range(c.KC):
                    mm(env, ps.t[:, (ci - 1) * 4:(ci - 1) * 4 + 4], wb[ci].t[:, kc * 128:(kc + 1) * 128],
                       env.xhT.t[:, kc, :], kc == 0, kc == c.KC - 1, [wb[ci], env.xhT], [ps],
                       inc=(kc == c.KC - 1))
            hh = env.ring("hh", [128, 4], F32, 2)
            k.op("act", lambda e: e.activation(out=hh.t[:], in_=ps.t[:, 4:8], func=AF.Copy), reads=[ps], writes=[hh])
            u = env.ring("u", [128, 2 + 512], F32, 3)
            k.op("dve", lambda e: e.tensor_tensor(out=u.t[:, 0:2], in0=ps.t[:, 2:4], in1=hh.t[:, 2:4], op=ALU.mult),
                 reads=[ps, hh], writes=[u])
            state["u"] = u

        def epi(j, tg, ps):
            hs = env.ring("hs", [128, 512], F32, 2)
            k.op("act", lambda e: e.activation(out=hs.t[:], in_=ps[2].t[:, :], func=AF.Copy), reads=[ps[2]], writes=[hs])
            if tg == 0:
                u = state["u"]
            else:
                u = env.ring("u", [128, 2 + 512], F32, 3)
                up = state["u"]
                k.op("dve", lambda e: e.tensor_copy(out=u.t[:, 0:2], in_=up.t[:, 512:514]), reads=[up], writes=[u])
                state["u"] = u
            k.op("dve", lambda e: e.tensor_tensor(out=u.t[:, 2:514], in0=ps[1].t[:, :], in1=hs.t[:], op=ALU.mult),
                 reads=[ps[1], hs], writes=[u])
            y = env.ring("y", [128, 512], F32, 2)
            k.op("act", lambda e: e.activation(out=y.t[:], in_=u.t[:, 0:512], func=AF.Copy, scale=cw.t[:, j * 3:j * 3 + 1]),
                 reads=[u, cw], writes=[y])
            k.op("dve", lambda e: e.scalar_tensor_tensor(out=y.t[:], in0=u.t[:, 1:513], scalar=cw.t[:, j * 3 + 1:j * 3 + 2],
                                                         in1=y.t[:], op0=ALU.mult, op1=ALU.add), reads=[u, cw, y], writes=[y])
            k.op("dve", lambda e: e.scalar_tensor_tensor(out=y.t[:], in0=u.t[:, 2:514], scalar=cw.t[:, j * 3 + 2:j * 3 + 3],
                                                         in1=y.t[:], op0=ALU.mult, op1=ALU.add), reads=[u, cw, y], writes=[y])
            k.op("dve", lambda e: e.tensor_tensor(out=env.actT[tg].t[:, j, :], in0=ps[0].t[:, :], in1=y.t[:], op=ALU.mult),
                 reads=[ps[0], y], writes=[env.actT[tg]])

        type_a(env, units, 3, epi, halo_fn)
        slabs = [[io["w_out_t"][s, pc] for pc in range(c.KC // 4)] for s in range(c.D // 512)]
        if stages >= 2:
            type_b(env, c.KC, slabs, hbufs[0], h_dram[0])
        if stages >= 3:
            ln_phase(env, io["x"], None, [(h_dram[0], hbufs[0])], io["ln"][0, :], io["ln"][1, :], x1, x1bufs, True, False)
        if stages >= 4:
            ffn_and_ln(env, io, x1, x1bufs, h_dram, hbufs, xo, xobufs, True, False)
        k.finish()
    return nc


def sc_inputs(cfg, w_in, conv_w, w_out, w_gu, w_dn, ln4):
    c = cfg
    D, F = c.D, c.F
    d = {}
    d["ident"] = np.eye(128, dtype=np.float32)
    cols = []
    for j in range(c.KC):
        cols += [j * 128, D + j * 128, 2 * D + j * 128]
    d["w_in_t"] = tile_a(w_in, cols).reshape(c.KC, 3, 128, c.KC * 128)
    d["cw"] = np.ascontiguousarray(conv_w.reshape(3, c.KC, 128).transpose(2, 1, 0)).reshape(128, c.KC * 3)
    d["w_out_t"] = tile_b(w_out, 0, c.KC)
    d.update(ffn_inputs(cfg, w_gu, w_dn))
    d["ln"] = np.ascontiguousarray(ln4, dtype=np.float32)
    return d


def ffn_inputs(cfg, w_gu, w_dn):
    c = cfg
    d = {}
    cols = []
    for j in range(c.F // 128):
        cols += [j * 128, c.F + j * 128]
    d["w_gu_t"] = tile_a(w_gu, cols).reshape(c.F // 128, 2, 128, c.KC * 128)
    r0 = 0
    for gi, G in enumerate(c.fgroups):
        d["w_dn_t%d" % gi] = tile_b(w_dn, r0, G)
        r0 += G * 128
    return d


def gdn_consts():
    i = np.arange(128)
    same = (i[:, None] // 64) == (i[None, :] // 64)
    c = np.zeros((8, 128, 128), np.float32)
    c[0] = ((i[:, None] <= i[None, :]) & same)
    c[1] = (i[:, None] > i[None, :])
    c[2] = ((i[:, None] > i[None, :]) & same)
    c[3] = ((i[None, :] >= i[:, None]) & same)
    c[4] = same
    c[5] = 1.0
    c[6] = (i[:, None] < 64) * np.ones((1, 128))
    c[7] = (i[:, None] >= 64) * np.ones((1, 128))
    return c


def roundrobin(gens):
    gens = list(gens)
    while gens:
        nxt = []
        for g in gens:
            try:
                next(g)
                nxt.append(g)
            except StopIteration:
                pass
        gens = nxt


LIMIT = [1 << 60]
USE_PT_VIEWS = False


def build_gdn_program(cfg, mode, dbg=99):
    c = cfg
    NVH, NKH, KC = c.NVH, c.NKH, c.KC
    NV = 256 if mode == "a" else 128
    nch = 4 if mode == "a" else 6
    vgroups = [min(16, NVH - g0) for g0 in range(0, NVH, 16)]
    nc = bass.Bass("TRN2", target_bir_lowering=False)
    io = {}
    io["x"] = dram_in(nc, "x", [c.NT, c.D])
    io["xh"] = dram_in(nc, "xh", [4, c.D])
    io["ident"] = dram_in(nc, "ident", [128, 128])
    io["consts"] = dram_in(nc, "consts", [8, 128, 128])
    io["w_t"] = dram_in(nc, "w_t", [NKH, 6, 128, KC * 128])
    io["w_ba"] = dram_in(nc, "w_ba", [128, KC * 2 * NVH])
    io["cwg"] = dram_in(nc, "cwg", [128, NKH * 16])
    io["alog"] = dram_in(nc, "alog", [1, NVH])
    io["dtb"] = dram_in(nc, "dtb", [1, NVH])
    if mode == "a":
        st_out = nc.dram_tensor("st", [NVH, 128, 256], F32, kind="ExternalOutput").ap()
        oq_out = nc.dram_tensor("oq", [NVH, c.NT, 256], F32, kind="ExternalOutput").ap()
    else:
        io["normw"] = dram_in(nc, "normw", [1, 128])
        io["sin_pt"] = dram_in(nc, "sin_pt", [3, NVH, 128, 128])
        io["sin_s"] = dram_in(nc, "sin_s", [3, NVH, 128, 128])
        io["w_o_t"] = [dram_in(nc, "w_o_t%d" % gi, [c.D // 512, G // 4, 128, 2048]) for gi, G in enumerate(vgroups)]
        io["w_gu_t"] = dram_in(nc, "w_gu_t", [c.F // 128, 2, 128, KC * 128])
        io["w_dn_t"] = [dram_in(nc, "w_dn_t%d" % gi, [c.D // 512, G // 4, 128, 2048]) for gi, G in enumerate(c.fgroups)]
        io["ln"] = dram_in(nc, "ln", [4, c.D])
        xo = nc.dram_tensor("xo", [c.NT, c.D], F32, kind="ExternalOutput").ap()
        P = max(len(vgroups), len(c.fgroups))
        h_dram = [nc.dram_tensor("h%d" % p, [c.NT, c.D], F32, kind="Internal").ap() for p in range(P)]
        x1 = nc.dram_tensor("x1", [c.NT, c.D], F32, kind="Internal").ap()
        onT_d = nc.dram_tensor("onT_d", [NVH, 128, c.NT], BF16, kind="Internal").ap()
    with ExitStack() as es:
        k = K(nc, es)
        k.limit = LIMIT[0]
        env = Env(k, c, io, alloc_act=False, ws_slots=(5 if mode == "a" else 7))
        with env.phase():
            emit_halo(env, io["xh"])
            load_x_to_xT(env, io["x"])
        onbufs = [[Buf("on_%d_%d" % (h, tg)) for tg in range(c.NTG)] for h in range(NVH)]
        if dbg <= 1:
            k.finish()
            return nc

        with env.phase():
            def f32t(name, shape=(128, 128)):
                return k.sbuf(name, list(shape), F32)

            CN = []
            for i in range(8):
                t = f32t("cst%d" % i)
                k.dma("sp", out=t.t[:], in_=io["consts"][i], writes=[t], owner=t)
                CN.append(t)
            TRIBD, UU, STRICT, CAUSALT, BLK, ONES, CH0, CH1 = CN
            UUr = k.sbuf("UUr", [128, 128], F32R)
            k.op("dve", lambda e: e.tensor_copy(out=UUr.t[:], in_=UU.t[:]), reads=[UU], writes=[UUr])
            cwg = f32t("cwg_sb", (128, NKH * 16))
            k.dma("sp", out=cwg.t[:], in_=io["cwg"], writes=[cwg], owner=cwg)
            wba_f = f32t("wba_f", (128, KC * 2 * NVH))
            wba = k.sbuf("wba_b", [128, KC, 2 * NVH], BF16)
            k.dma("sp", out=wba_f.t[:], in_=io["w_ba"], writes=[wba_f], owner=wba_f)
            k.op("dve", lambda e: e.tensor_copy(out=wba.t[:].rearrange("p a b -> p (a b)"), in_=wba_f.t[:]),
                 reads=[wba_f], writes=[wba])
            alog = f32t("alog_sb", (128, NVH))
            dtb = f32t("dtb_sb", (128, NVH))
            k.dma("sp", out=alog.t[:], in_=io["alog"][0, :].partition_broadcast(128), writes=[alog], owner=alog)
            k.dma("sp", out=dtb.t[:], in_=io["dtb"][0, :].partition_broadcast(128), writes=[dtb], owner=dtb)
            negA = f32t("negA", (128, NVH))
            k.op("act", lambda e: e.activation(out=negA.t[:], in_=alog.t[:], func=AF.Exp), reads=[alog], writes=[negA])
            k.op("dve", lambda e: e.tensor_scalar(out=negA.t[:], in0=negA.t[:], scalar1=-1.0, scalar2=None, op0=ALU.mult),
                 reads=[negA], writes=[negA])
            if mode == "b":
                nwbc = f32t("nwbc")
                k.dma("sp", out=nwbc.t[:], in_=io["normw"][0, :].partition_broadcast(128), writes=[nwbc], owner=nwbc)
            shp = (128, c.NTT, NVH)
            BETA, AX, GRAW, GC, EG = [f32t(n, shp) for n in ("BETA", "AX", "GRAW", "GC", "EG")]
            EKT, BEG = AX, GC
            CD = f32t("CD", (128, c.NTT, 2 * NVH))

            for tt in range(c.NTT):
                g, o = tt // 4, (tt % 4) * 128
                ps = env.pnext()
                for kc in range(KC):
                    mm(env, ps.t[:, 0:2 * NVH], env.xT[g].t[:, kc, o:o + 128], wba.t[:, kc, :], kc == 0, kc == KC - 1,
                       [env.xT[g], wba], [ps])
                k.op("act", lambda e: e.activation(out=BETA.t[:, tt, :], in_=ps.t[:, 0:NVH], func=AF.Sigmoid),
                     reads=[ps], writes=[BETA])
                k.op("dve", lambda e: e.tensor_tensor(out=AX.t[:, tt, :], in0=ps.t[:, NVH:2 * NVH], in1=dtb.t[:], op=ALU.add),
                     reads=[ps, dtb], writes=[AX])
            k.op("act", lambda e: e.activation(out=AX.t[:], in_=AX.t[:], func=AF.Exp), reads=[AX], writes=[AX])
            k.op("act", lambda e: e.activation(out=AX.t[:], in_=AX.t[:], func=AF.Ln, bias=1.0), reads=[AX], writes=[AX])
            for tt in range(c.NTT):
                k.op("dve", lambda e: e.tensor_tensor(out=GRAW.t[:, tt, :], in0=AX.t[:, tt, :], in1=negA.t[:], op=ALU.mult),
                     reads=[AX, negA], writes=[GRAW])
            for tt in range(c.NTT):
                gm = env.ring("gm", [128, 2 * NVH], F32, 2)
                k.op("dve", lambda e: e.tensor_scalar(out=gm.t[:, 0:NVH], in0=GRAW.t[:, tt, :], scalar1=CH0.t[:, 0:1],
                                                      scalar2=None, op0=ALU.mult), reads=[GRAW, CH0], writes=[gm])
                k.op("dve", lambda e: e.tensor_scalar(out=gm.t[:, NVH:2 * NVH], in0=GRAW.t[:, tt, :], scalar1=CH1.t[:, 0:1],
                                                      scalar2=None, op0=ALU.mult), reads=[GRAW, CH1], writes=[gm])
                ps = env.pnext()
                mm(env, ps.t[:, 0:NVH], TRIBD.t[:], GRAW.t[:, tt, :], True, True, [TRIBD, GRAW], [ps])
                mm(env, ps.t[:, 64:64 + NVH], BLK.t[:], GRAW.t[:, tt, :], True, True, [BLK, GRAW], [ps])
                mm(env, ps.t[:, 128:128 + 2 * NVH], ONES.t[:], gm.t[:], True, True, [ONES, gm], [ps])
                k.op("dve", lambda e: e.tensor_copy(out=GC.t[:, tt, :], in_=ps.t[:, 0:NVH]), reads=[ps], writes=[GC])
                k.op("act", lambda e: e.activation(out=EG.t[:, tt, :], in_=ps.t[:, 0:NVH], func=AF.Exp), reads=[ps], writes=[EG])
                k.op("dve", lambda e: e.tensor_tensor(out=EKT.t[:, tt, :], in0=ps.t[:, 64:64 + NVH], in1=GC.t[:, tt, :],
                                                      op=ALU.subtract), reads=[ps, GC], writes=[EKT])
                k.op("act", lambda e: e.activation(out=EKT.t[:, tt, :], in_=EKT.t[:, tt, :], func=AF.Exp),
                     reads=[EKT], writes=[EKT])
                k.op("act", lambda e: e.activation(out=CD.t[:, tt, :], in_=ps.t[:, 128:128 + 2 * NVH], func=AF.Exp),
                     reads=[ps], writes=[CD])
                k.op("dve", lambda e: e.tensor_tensor(out=BEG.t[:, tt, :], in0=BETA.t[:, tt, :], in1=EG.t[:, tt, :], op=ALU.mult),
                     reads=[BETA, EG], writes=[BEG])

            if dbg <= 2:
                k.finish()
                return nc
            if mode == "a" and USE_PT_VIEWS:
                for pt_b in env.pT:
                    v_ = pt_b.view(pt_b.name + "_f32", pt_b.t[:].bitcast(F32))
                    v_.is_psum = True
                    env.pA.append(v_)
            Sf = [f32t("Sf%d" % i, (128, NV)) for i in range(2)]
            Sb = [k.sbuf("Sb%d" % i, [128, NV], BF16) for i in range(2)]
            ubuf = [[f32t("u%d_%d" % (i, j), (128, NV)) for j in range(3)] for i in range(2)]
            if NV == 256:
                for i in range(2):
                    for j in range(3):
                        k.op("pool", lambda e: e.memset(ubuf[i][j].t[:], 0.0), writes=[ubuf[i][j]])
            ucnt = [0, 0]
            carry = [f32t("carry%d" % ci, (128, 4)) for ci in range(4)]
            ws = env.ws
            for kh in range(NKH):
                for ci in range(nch):
                    ws.push((io["w_t"][kh, ci], KC * 128))

            for kh in range(NKH):
                wb = ws.take(nch)
                for hh in range(2):
                    h = 2 * kh + hh
                    if mode == "a":
                        k.op("pool", lambda e: e.memset(Sf[hh].t[:, 0:128], 0.0), writes=[Sf[hh]])
                        k.op("pool", lambda e: e.tensor_copy(out=Sf[hh].t[:, 128:256], in_=env.ident_f.t[:]),
                             reads=[env.ident_f], writes=[Sf[hh]])
                    else:
                        k.dma("sp", out=Sf[hh].t[:], in_=io["sin_s"][0, h], writes=[Sf[hh]], owner=Sf[hh])
                        for j in (1, 2):
                            pt_ = env.ring("foldp", [128, 128], F32, 2)
                            sl_ = env.ring("folds", [128, 128], F32, 2)
                            k.dma("sp", out=pt_.t[:], in_=io["sin_pt"][j, h], writes=[pt_], owner=pt_)
                            k.dma("sp", out=sl_.t[:], in_=io["sin_s"][j, h], writes=[sl_], owner=sl_)
                            ps = env.pnext()
                            mm(env, ps.t[:, 0:128], pt_.t[:], Sf[hh].t[:], True, True, [pt_, Sf[hh]], [ps])
                            k.op("dve", lambda e: e.tensor_tensor(out=Sf[hh].t[:], in0=ps.t[:, 0:128], in1=sl_.t[:], op=ALU.add),
                                 reads=[ps, sl_], writes=[Sf[hh]])
                    k.op("act", lambda e: e.activation(out=Sb[hh].t[:], in_=Sf[hh].t[:], func=AF.Copy),
                         reads=[Sf[hh]], writes=[Sb[hh]])
                psh = env.pnext()
                for ci in range(4):
                    for kc in range(KC):
                        mm(env, psh.t[:, ci * 4:ci * 4 + 4], wb[ci].t[:, kc * 128:(kc + 1) * 128], env.xhT.t[:, kc, :],
                           kc == 0, kc == KC - 1, [wb[ci], env.xhT], [psh])
                for ci in range(4):
                    k.op("dve", lambda e: e.tensor_copy(out=carry[ci].t[:, 0:3], in_=psh.t[:, ci * 4 + 1:ci * 4 + 4]),
                         reads=[psh], writes=[carry[ci]])
                TG = {}

                def proj(tg):
                    knT = env.ring("knT", [128, 512], F32, 2)
                    qnT = env.ring("qnT", [128, 512], F32, 2)
                    qb = env.ring("qb", [128, 512], BF16, 2)
                    vT = [env.ring("vT%d" % i, [128, 512], F32, 2) for i in range(2)]
                    sz = [env.ring("sz%d" % i, [128, 512], BF16, 2) for i in range(2)] if mode == "b" else None
                    TG[tg] = dict(knT=knT, qnT=qnT, qb=qb, vT=vT, sz=sz)
                    for ci in range(nch):
                        ps = env.pnext()
                        for kc in range(KC):
                            mm(env, ps.t[:, :], wb[ci].t[:, kc * 128:(kc + 1) * 128], env.xT[tg].t[:, kc, :],
                               kc == 0, kc == KC - 1, [wb[ci], env.xT[tg]], [ps])
                        if ci >= 4:
                            k.op("act", lambda e: e.activation(out=sz[ci - 4].t[:], in_=ps.t[:, :], func=AF.Silu),
                                 reads=[ps], writes=[sz[ci - 4]])
                            continue
                        pre = env.ring("pre", [128, 3 + 512], F32, 2)
                        k.op("dve", lambda e: e.tensor_copy(out=pre.t[:, 0:3], in_=carry[ci].t[:, 0:3]), reads=[carry[ci]], writes=[pre])
                        k.op("act", lambda e: e.activation(out=pre.t[:, 3:515], in_=ps.t[:, :], func=AF.Copy), reads=[ps], writes=[pre])
                        k.op("dve", lambda e: e.tensor_copy(out=carry[ci].t[:, 0:3], in_=pre.t[:, 512:515]), reads=[pre], writes=[carry[ci]])
                        cb = (kh * 4 + ci) * 4
                        ca = env.ring("cacc", [128, 512], F32, 2)
                        k.op("act", lambda e: e.activation(out=ca.t[:], in_=pre.t[:, 0:512], func=AF.Copy, scale=cwg.t[:, cb:cb + 1]),
                             reads=[pre, cwg], writes=[ca])
                        for tap in (1, 2, 3):
                            k.op("dve", lambda e: e.scalar_tensor_tensor(out=ca.t[:], in0=pre.t[:, tap:tap + 512],
                                                                         scalar=cwg.t[:, cb + tap:cb + tap + 1], in1=ca.t[:],
                                                                         op0=ALU.mult, op1=ALU.add), reads=[pre, cwg, ca], writes=[ca])
                        if ci >= 2:
                            k.op("act", lambda e: e.activation(out=vT[ci - 2].t[:], in_=ca.t[:], func=AF.Silu),
                                 reads=[ca], writes=[vT[ci - 2]])
                            continue
                        sl = env.ring("sl", [128, 512], F32, 1)
                        k.op("act", lambda e: e.activation(out=sl.t[:], in_=ca.t[:], func=AF.Silu), reads=[ca], writes=[sl])
                        k.op("pool", lambda e: e.tensor_tensor(out=ca.t[:], in0=sl.t[:], in1=sl.t[:], op=ALU.mult), reads=[sl], writes=[ca])
                        ps2 = env.pnext()
                        mm(env, ps2.t[:, :], ONES.t[:], ca.t[:], True, True, [ONES, ca], [ps2])
                        k.op("act", lambda e: e.activation(out=ca.t[:], in_=ps2.t[:, :], func=AF.Sqrt, bias=float(RMS_EPS)),
                             reads=[ps2], writes=[ca])
                        k.op("dve", lambda e: e.reciprocal(out=ca.t[:], in_=ca.t[:]), reads=[ca], writes=[ca])
                        if ci == 1:
                            k.op("dve", lambda e: e.tensor_tensor(out=knT.t[:], in0=sl.t[:], in1=ca.t[:], op=ALU.mult),
                                 reads=[sl, ca], writes=[knT])
                        else:
                            k.op("dve", lambda e: e.scalar_tensor_tensor(out=qnT.t[:], in0=sl.t[:], scalar=float(HD ** -0.5),
                                                                         in1=ca.t[:], op0=ALU.mult, op1=ALU.mult),
                                 reads=[sl, ca], writes=[qnT])
                            k.op("pool", lambda e: e.tensor_copy(out=qb.t[:], in_=qnT.t[:]), reads=[qnT], writes=[qb])

                tile = {}

                def prep(tt):
                    cols = slice((tt % 4) * 128, (tt % 4 + 1) * 128)
                    knT, qnT, vT = TG[tt // 4]["knT"], TG[tt // 4]["qnT"], TG[tt // 4]["vT"]
                    T = {}
                    psk = env.pnext()
                    k.op("pe", lambda e: e.transpose(psk.t[:, 0:128], knT.t[:, cols], env.ident_f.t[:]),
                         reads=[knT, env.ident_f], writes=[psk])
                    psv = env.pnext()
                    for hh in range(2):
                        k.op("pe", lambda e: e.transpose(psv.t[:, hh * 128:(hh + 1) * 128], vT[hh].t[:, cols], env.ident_f.t[:]),
                             reads=[vT[hh], env.ident_f], writes=[psv])
                    T["kbg"], T["ktl"], T["vb"] = [], [], []
                    for hh in range(2):
                        h = 2 * kh + hh
                        kbg = env.ring("kbg%d" % hh, [128, 128], F32R, 3)
                        ktl = env.ring("ktl%d" % hh, [128, 128], BF16, 3)
                        vb = env.ring("vb%d" % hh, [128, 128], F32R, 3)
                        k.op("act", lambda e: e.activation(out=kbg.t[:], in_=psk.t[:, 0:128], func=AF.Copy, scale=BEG.t[:, tt, h:h + 1]),
                             reads=[psk, BEG], writes=[kbg])
                        k.op("dve", lambda e: e.tensor_scalar(out=ktl.t[:], in0=psk.t[:, 0:128], scalar1=EKT.t[:, tt, h:h + 1],
                                                              scalar2=None, op0=ALU.mult), reads=[psk, EKT], writes=[ktl])
                        k.op("dve", lambda e: e.tensor_scalar(out=vb.t[:], in0=psv.t[:, hh * 128:(hh + 1) * 128],
                                                              scalar1=BETA.t[:, tt, h:h + 1], scalar2=None, op0=ALU.mult),
                             reads=[psv, BETA], writes=[vb])
                        T["kbg"].append(kbg); T["ktl"].append(ktl); T["vb"].append(vb)
                    yield
                    pskk = env.pnext()
                    mm(env, pskk.t[:, 0:128], knT.t[:, cols], knT.t[:, cols], True, True, [knT], [pskk])
                    mm(env, pskk.t[:, 128:256], knT.t[:, cols], qnT.t[:, cols], True, True, [knT, qnT], [pskk])
                    KKs = env.ring("KKs", [128, 128], F32, 3)
                    QKm = env.ring("QKm", [128, 128], F32, 3)
                    k.op("dve", lambda e: e.tensor_tensor(out=KKs.t[:], in0=pskk.t[:, 0:128], in1=STRICT.t[:], op=ALU.mult),
                         reads=[pskk, STRICT], writes=[KKs])
                    k.op("dve", lambda e: e.tensor_tensor(out=QKm.t[:], in0=pskk.t[:, 128:256], in1=CAUSALT.t[:], op=ALU.mult),
                         reads=[pskk, CAUSALT], writes=[QKm])
                    T["KKs"], T["QKm"] = KKs, QKm
                    T["u"], T["wT"], T["aqk"] = [None, None], [None, None], [None, None]
                    tile[tt] = T
                    yield

                def solve(tt, hh):
                    T = tile[tt]
                    h = 2 * kh + hh
                    G = env.ring("G%d_%d" % (hh, tt % 2), [128, 128], F32R, 1)
                    k.op("act", lambda e: e.activation(out=G.t[:], in_=TRIBD.t[:], func=AF.Copy, scale=GRAW.t[:, tt, h:h + 1]),
                         reads=[TRIBD, GRAW], writes=[G])
                    yield
                    ps = env.pnext()
                    mm(env, ps.t[:, 0:128], G.t[:], UUr.t[:], True, True, [G, UUr], [ps])
                    mm(env, ps.t[:, 128:256], UUr.t[:], G.t[:], True, True, [G, UUr], [ps])
                    Dec = env.ring("Dec%d_%d" % (hh, tt % 2), [128, 256], F32, 1)
                    k.op("act", lambda e: e.activation(out=Dec.t[:], in_=ps.t[:, 0:256], func=AF.Exp), reads=[ps], writes=[Dec])
                    yield
                    A = env.ring("A%d_%d" % (hh, tt % 2), [128, 128], F32R, 2)
                    k.op("dve", lambda e: e.scalar_tensor_tensor(out=A.t[:], in0=T["KKs"].t[:], scalar=BETA.t[:, tt, h:h + 1],
                                                                 in1=Dec.t[:, 0:128], op0=ALU.mult, op1=ALU.mult),
                         reads=[T["KKs"], BETA, Dec], writes=[A])
                    aqk = env.ring("aqk%d" % hh, [128, 128], BF16, 3)
                    k.op("pool", lambda e: e.tensor_tensor(out=aqk.t[:], in0=T["QKm"].t[:], in1=Dec.t[:, 128:256], op=ALU.mult),
                         reads=[T["QKm"], Dec], writes=[aqk])
                    T["aqk"][hh] = aqk
                    yield
                    ps = env.pnext()
                    k.op("pe", lambda e: e.transpose(ps.t[:, 0:128], f32v(A), env.ident_f.t[:]), reads=[A, env.ident_f], writes=[ps])
                    X = env.ring("X%d_%d" % (hh, tt % 2), [128, 128], F32R, 2)
                    B = env.ring("B%d_%d" % (hh, tt % 2), [128, 128], F32R, 2)
                    k.op("dve", lambda e: e.tensor_tensor(out=X.t[:], in0=env.ident_f.t[:], in1=ps.t[:, 0:128], op=ALU.subtract),
                         reads=[env.ident_f, ps], writes=[X])
                    k.op("act", lambda e: e.activation(out=B.t[:], in_=ps.t[:, 0:128], func=AF.Copy), reads=[ps], writes=[B])
                    yield
                    for m in (1, 2, 4, 8, 16):
                        ps = env.pnext()
                        mm(env, ps.t[:, 0:128], B.t[:], A.t[:], True, True, [A, B], [ps])
                        if m < 16:
                            mm(env, ps.t[:, 128:256], A.t[:], B.t[:], True, True, [A, B], [ps])
                        A2 = env.ring("A%d_%d" % (hh, tt % 2), [128, 128], F32R, 2)
                        k.op("act", lambda e: e.activation(out=A2.t[:], in_=ps.t[:, 0:128], func=AF.Copy), reads=[ps], writes=[A2])
                        if m < 16:
                            B2 = env.ring("B%d_%d" % (hh, tt % 2), [128, 128], F32R, 2)
                            k.op("dve", lambda e: e.tensor_copy(out=B2.t[:], in_=ps.t[:, 128:256]), reads=[ps], writes=[B2])
                        yield
                        ps2 = env.pnext()
                        mm(env, ps2.t[:, 0:128], A2.t[:], X.t[:], True, True, [A2, X], [ps2])
                        X2 = env.ring("X%d_%d" % (hh, tt % 2), [128, 128], F32R, 2)
                        k.op("dve", lambda e: e.tensor_tensor(out=X2.t[:], in0=f32v(X), in1=ps2.t[:, 0:128], op=ALU.add),
                             reads=[X, ps2], writes=[X2])
                        A, X = A2, X2
                        if m < 16:
                            B = B2
                        yield
                    ps = env.pnext()
                    mm(env, ps.t[:, 0:128], X.t[:], T["vb"][hh].t[:], True, True, [X, T["vb"][hh]], [ps])
                    mm(env, ps.t[:, 128:256], T["kbg"][hh].t[:], X.t[:], True, True, [X, T["kbg"][hh]], [ps])
                    u = ubuf[hh][ucnt[hh] % 3]
                    ucnt[hh] += 1
                    k.op("act", lambda e: e.activation(out=u.t[:, 0:128], in_=ps.t[:, 0:128], func=AF.Copy), reads=[ps], writes=[u])
                    wT = env.ring("wT%d" % hh, [128, 128], BF16, 3)
                    k.op("dve", lambda e: e.tensor_copy(out=wT.t[:], in_=ps.t[:, 128:256]), reads=[ps], writes=[wT])
                    T["u"][hh], T["wT"][hh] = u, wT
                    yield

                def recur(tt, hh):
                    T = tile[tt]
                    h = 2 * kh + hh
                    cols = slice((tt % 4) * 128, (tt % 4 + 1) * 128)
                    tg, o = tt // 4, (tt % 4) * 128
                    qb, sz = TG[tg]["qb"], TG[tg]["sz"]
                    u, wT, aqk, ktl = T["u"][hh], T["wT"][hh], T["aqk"][hh], T["ktl"][hh]
                    ot = env.ring("ot%d" % hh, [128, NV], F32, 2)
                    for cc in range(2):
                        r = slice(64 * cc, 64 * cc + 64)
                        ps = env.pnext()
                        mm(env, ps.t[r, 0:NV], wT.t[:, r], Sb[hh].t[:, :], True, True, [wT, Sb[hh]], [ps])
                        vn = env.ring("vn%d" % hh, [128, NV], BF16, 2)
                        k.op("dve", lambda e: e.tensor_tensor(out=vn.t[r, :], in0=u.t[r, :], in1=ps.t[r, 0:NV], op=ALU.subtract),
                             reads=[u, ps], writes=[vn])
                        yield
                        if True:
                            ps1 = env.pnext()
                            mm(env, ps1.t[r, 0:NV], qb.t[:, o + 64 * cc:o + 64 * cc + 64], Sb[hh].t[:, :], True, True,
                               [qb, Sb[hh]], [ps1])
                            ps2 = env.pnext()
                            mm(env, ps2.t[r, 0:NV], aqk.t[r, r], vn.t[r, :], True, True, [aqk, vn], [ps2])
                        ps3 = env.pnext()
                        mm(env, ps3.t[:, 0:NV], ktl.t[r, :], vn.t[r, :], True, True, [ktl, vn], [ps3])
                        if True:
                            o2 = env.ring("o2%d" % hh, [128, NV], F32, 1)
                            k.op("act", lambda e: e.activation(out=o2.t[r, :], in_=ps2.t[r, 0:NV], func=AF.Copy), reads=[ps2], writes=[o2])
                            k.op("dve", lambda e: e.scalar_tensor_tensor(out=ot.t[r, :], in0=ps1.t[r, 0:NV], scalar=EG.t[r, tt, h:h + 1],
                                                                         in1=o2.t[r, :], op0=ALU.mult, op1=ALU.add),
                                 reads=[ps1, EG, o2], writes=[ot])
                        k.op("dve", lambda e: e.scalar_tensor_tensor(out=Sf[hh].t[:], in0=Sf[hh].t[:],
                                                                     scalar=CD.t[:, tt, cc * NVH + h:cc * NVH + h + 1],
                                                                     in1=ps3.t[:, 0:NV], op0=ALU.mult, op1=ALU.add),
                             reads=[Sf[hh], CD, ps3], writes=[Sf[hh]])
                        k.op("act", lambda e: e.activation(out=Sb[hh].t[:], in_=Sf[hh].t[:], func=AF.Copy), reads=[Sf[hh]], writes=[Sb[hh]])
                        yield
                    if mode == "a":
                        k.dma("act", out=oq_out[h, tt * 128:(tt + 1) * 128, :], in_=ot.t[:], reads=[ot], writes=[], owner=ot,
                              is_output=True)
                    if mode == "b":
                        sq = env.ring("osq", [128, 128], F32, 2)
                        ss = env.ring("oss", [128, 2], F32, 4)
                        k.op("pool", lambda e: e.tensor_tensor(out=sq.t[:], in0=ot.t[:], in1=ot.t[:], op=ALU.mult), reads=[ot], writes=[sq])
                        k.op("dve", lambda e: e.reduce_sum(out=ss.t[:, 0:1], in_=sq.t[:], axis=mybir.AxisListType.X), reads=[sq], writes=[ss])
                        k.op("act", lambda e: e.activation(out=ss.t[:, 1:2], in_=ss.t[:, 0:1], func=AF.Sqrt, scale=1.0 / HD,
                                                           bias=float(RMS_EPS)), reads=[ss], writes=[ss])
                        k.op("dve", lambda e: e.reciprocal(out=ss.t[:, 1:2], in_=ss.t[:, 1:2]), reads=[ss], writes=[ss])
                        onb = env.ring("onb", [128, 128], BF16, 2)
                        k.op("dve", lambda e: e.scalar_tensor_tensor(out=onb.t[:], in0=ot.t[:], scalar=ss.t[:, 1:2], in1=nwbc.t[:],
                                                                     op0=ALU.mult, op1=ALU.mult), reads=[ot, ss, nwbc], writes=[onb])
                        pt = env.ptnext()
                        k.op("pe", lambda e: e.transpose(pt.t[:, 0:128], onb.t[:], env.ident_bf.t[:]), reads=[onb, env.ident_bf], writes=[pt])
                        if tt % 4 == 0:
                            tile["og%d" % hh] = env.ring("og%d" % hh, [128, 512], BF16, 2)
                        og = tile["og%d" % hh]
                        k.op("dve", lambda e: e.tensor_tensor(out=og.t[:, o:o + 128], in0=pt.t[:, 0:128], in1=sz[hh].t[:, cols], op=ALU.mult),
                             reads=[pt, sz[hh]], writes=[og])
                        if tt % 4 == 3:
                            k.dma("act", out=onT_d[h, :, tg * 512:(tg + 1) * 512], in_=og.t[:], reads=[og], writes=[onbufs[h][tg]], owner=og)
                        yield

                def tile_front(tt):
                    for _ in prep(tt):
                        yield
                    gens = [solve(tt, 0), solve(tt, 1)]
                    while gens:
                        nxt = []
                        for g_ in gens:
                            try:
                                next(g_)
                                nxt.append(g_)
                            except StopIteration:
                                pass
                        gens = nxt
                        yield

                def tile_back(tt):
                    gens = [recur(tt, 0), recur(tt, 1)]
                    while gens:
                        nxt = []
                        for g_ in gens:
                            try:
                                next(g_)
                                nxt.append(g_)
                            except StopIteration:
                                pass
                        gens = nxt
                        yield
                    del tile[tt]

                def start_front(t):
                    if t % 4 == 0:
                        proj(t // 4)
                    return tile_front(t)

                def step(g_):
                    try:
                        next(g_)
                        return True
                    except StopIteration:
                        return False

                roundrobin([start_front(0)])
                if dbg <= 4:
                    k.finish()
                    return nc
                carry_f = None
                for tt in range(c.NTT):
                    must = [tile_back(tt)]
                    if carry_f is None and tt + 1 < c.NTT:
                        must.append(start_front(tt + 1))
                    elif carry_f:
                        must.append(carry_f)
                    ahead = start_front(tt + 2) if tt + 2 < c.NTT else None
                    ahead_alive = ahead is not None
                    while must:
                        must = [g_ for g_ in must if step(g_)]
                        if ahead_alive:
                            ahead_alive = step(ahead)
                    carry_f = None if ahead is None else (ahead if ahead_alive else False)
                if mode == "a":
                    for hh in range(2):
                        k.dma("act", out=st_out[2 * kh + hh], in_=Sf[hh].t[:], reads=[Sf[hh]], writes=[], owner=Sf[hh], is_output=True)

        if mode == "b":
            gdn_tail(env, io, vgroups, onT_d, onbufs, h_dram, x1, xo, P)
        k.finish()
    return nc


def gdn_tail(env, io, vgroups, onT_d, onbufs, h_dram, x1, xo, P):
    k, c = env.k, env.cfg
    env.alloc_act()
    hbufs = [[Buf("h%d_%d" % (p, tt)) for tt in range(c.NTT)] for p in range(P)]
    x1bufs = [Buf("x1_%d" % tt) for tt in range(c.NTT)]
    xobufs = [Buf("xo_%d" % tt) for tt in range(c.NTT)]
    parts = []
    h0 = 0
    for gi, G in enumerate(vgroups):
        for tg in range(c.NTG):
            k.dma("sp", out=env.actT[tg].t[:, 0:G, :],
                  in_=onT_d[h0:h0 + G, :, tg * 512:(tg + 1) * 512].rearrange("h p t -> p h t"),
                  reads=[onbufs[h][tg] for h in range(h0, h0 + G)], writes=[env.actT[tg]], owner=env.actT[tg])
        slabs = [[io["w_o_t"][gi][s, pc] for pc in range(G // 4)] for s in range(c.D // 512)]
        type_b(env, G, slabs, hbufs[gi], h_dram[gi])
        parts.append((h_dram[gi], hbufs[gi]))
        h0 += G
    ln_phase(env, io["x"], None, parts, io["ln"][0, :], io["ln"][1, :], x1, x1bufs, True, False)
    ffn_and_ln(env, io, x1, x1bufs, h_dram, hbufs, xo, xobufs, True, False)


def build_gdn_c_program(cfg):
    c = cfg
    NVH, KC = c.NVH, c.KC
    vgroups = [min(16, NVH - g0) for g0 in range(0, NVH, 16)]
    nc = bass.Bass("TRN2", target_bir_lowering=False)
    io = {}
    io["x"] = dram_in(nc, "x", [c.NT, c.D])
    io["ident"] = dram_in(nc, "ident", [128, 128])
    io["oq"] = dram_in(nc, "oq", [NVH, c.NT, 256])
    io["w_z"] = dram_in(nc, "w_z", [NVH, 128, KC * 128])
    io["normw"] = dram_in(nc, "normw", [1, 128])
    io["sin_pt"] = dram_in(nc, "sin_pt", [3, NVH, 128, 128])
    io["sin_s"] = dram_in(nc, "sin_s", [3, NVH, 128, 128])
    io["w_o_t"] = [dram_in(nc, "w_o_t%d" % gi, [c.D // 512, G // 4, 128, 2048]) for gi, G in enumerate(vgroups)]
    io["w_gu_t"] = dram_in(nc, "w_gu_t", [c.F // 128, 2, 128, KC * 128])
    io["w_dn_t"] = [dram_in(nc, "w_dn_t%d" % gi, [c.D // 512, G // 4, 128, 2048]) for gi, G in enumerate(c.fgroups)]
    io["ln"] = dram_in(nc, "ln", [4, c.D])
    xo = nc.dram_tensor("xo", [c.NT, c.D], F32, kind="ExternalOutput").ap()
    P = max(len(vgroups), len(c.fgroups))
    h_dram = [nc.dram_tensor("h%d" % p, [c.NT, c.D], F32, kind="Internal").ap() for p in range(P)]
    x1 = nc.dram_tensor("x1", [c.NT, c.D], F32, kind="Internal").ap()
    onT_d = nc.dram_tensor("onT_d", [NVH, 128, c.NT], BF16, kind="Internal").ap()
    with ExitStack() as es:
        k = K(nc, es)
        env = Env(k, c, io, alloc_act=False, ws_slots=4)
        with env.phase():
            load_x_to_xT(env, io["x"])
        onbufs = [[Buf("on_%d_%d" % (h, tg)) for tg in range(c.NTG)] for h in range(NVH)]
        with env.phase():
            nwbc = k.sbuf("nwbc", [128, 128], F32)
            k.dma("sp", out=nwbc.t[:], in_=io["normw"][0, :].partition_broadcast(128), writes=[nwbc], owner=nwbc)
            ws = env.ws
            for h in range(NVH):
                ws.push((io["w_z"][h], KC * 128))
            pending = []

            def stepg(g_):
                try:
                    next(g_)
                    return True
                except StopIteration:
                    return False

            def zproj(tg, wb):
                ps = env.pnext()
                for kc in range(KC):
                    mm(env, ps.t[:, :], wb.t[:, kc * 128:(kc + 1) * 128], env.xT[tg].t[:, kc, :],
                       kc == 0, kc == KC - 1, [wb, env.xT[tg]], [ps])
                sz = env.ring("sz", [128, 512], BF16, 5)
                k.op("act", lambda e: e.activation(out=sz.t[:], in_=ps.t[:, :], func=AF.Silu), reads=[ps], writes=[sz])
                return sz

            def item(h, tg, sz, Sb):
                og = env.ring("og", [128, 512], BF16, 4)
                oq4 = env.ring("oq4", [128, 4, 256], F32, 4)
                k.dma("sp", out=oq4.t[:], in_=io["oq"][h, tg * 512:(tg + 1) * 512, :].rearrange("(t p) c -> p t c", p=128),
                      writes=[oq4], owner=oq4)
                yield
                pq = env.pnext()
                for t4 in range(4):
                    k.op("pe", lambda e: e.transpose(pq.t[:, t4 * 128:(t4 + 1) * 128], oq4.t[:, t4, 128:256], env.ident_f.t[:]),
                         reads=[oq4, env.ident_f], writes=[pq], inc=(t4 == 3))
                qT = env.ring("qT", [128, 512], BF16, 4)
                k.op("act", lambda e: e.activation(out=qT.t[:], in_=pq.t[:, :], func=AF.Copy), reads=[pq], writes=[qT])
                yield
                pc_ = env.pnext()
                for t4 in range(4):
                    mm(env, pc_.t[:, t4 * 128:(t4 + 1) * 128], qT.t[:, t4 * 128:(t4 + 1) * 128], Sb.t[:], True, True,
                       [qT, Sb], [pc_], inc=(t4 == 3))
                ot = env.ring("ot", [128, 4, 128], F32, 4)
                k.op("dve", lambda e: e.tensor_tensor(out=ot.t[:], in0=pc_.t[:, :].rearrange("p (t c) -> p t c", c=128),
                                                      in1=oq4.t[:, :, 0:128], op=ALU.add), reads=[pc_, oq4], writes=[ot])
                yield
                sq = env.ring("osq", [128, 4, 128], F32, 4)
                ss = env.ring("oss", [128, 8], F32, 4)
                k.op("pool", lambda e: e.tensor_tensor(out=sq.t[:], in0=ot.t[:], in1=ot.t[:], op=ALU.mult), reads=[ot], writes=[sq])
                yield
                k.op("dve", lambda e: e.reduce_sum(out=ss.t[:, 0:4], in_=sq.t[:], axis=mybir.AxisListType.X), reads=[sq], writes=[ss])
                yield
                k.op("act", lambda e: e.activation(out=ss.t[:, 4:8], in_=ss.t[:, 0:4], func=AF.Sqrt, scale=1.0 / HD,
                                                   bias=float(RMS_EPS)), reads=[ss], writes=[ss])
                yield
                k.op("dve", lambda e: e.reciprocal(out=ss.t[:, 4:8], in_=ss.t[:, 4:8]), reads=[ss], writes=[ss])
                onb = env.ring("onb", [128, 4, 128], BF16, 4)
                for t4 in range(4):
                    k.op("dve", lambda e: e.scalar_tensor_tensor(out=onb.t[:, t4, :], in0=ot.t[:, t4, :], scalar=ss.t[:, 4 + t4:5 + t4],
                                                                 in1=nwbc.t[:], op0=ALU.mult, op1=ALU.mult),
                         reads=[ot, ss, nwbc], writes=[onb])
                yield
                pt = env.ptnext()
                for t4 in range(4):
                    k.op("pe", lambda e: e.transpose(pt.t[:, t4 * 128:(t4 + 1) * 128], onb.t[:, t4, :], env.ident_bf.t[:]),
                         reads=[onb, env.ident_bf], writes=[pt], inc=(t4 == 3))
                k.op("dve", lambda e: e.tensor_tensor(out=og.t[:], in0=pt.t[:, 0:512], in1=sz.t[:], op=ALU.mult),
                     reads=[pt, sz], writes=[og])
                k.dma("act", out=onT_d[h, :, tg * 512:(tg + 1) * 512], in_=og.t[:], reads=[og], writes=[onbufs[h][tg]], owner=og)

            for h in range(NVH):
                Sf = env.ring("Sf", [128, 128], F32, 3)
                Sb = env.ring("Sb", [128, 128], BF16, 3)
                k.dma("sp", out=Sf.t[:], in_=io["sin_s"][0, h], writes=[Sf], owner=Sf)
                for j in (1, 2):
                    pt_ = env.ring("foldp", [128, 128], F32, 2)
                    sl_ = env.ring("folds", [128, 128], F32, 2)
                    k.dma("sp", out=pt_.t[:], in_=io["sin_pt"][j, h], writes=[pt_], owner=pt_)
                    k.dma("sp", out=sl_.t[:], in_=io["sin_s"][j, h], writes=[sl_], owner=sl_)
                    ps = env.pnext()
                    mm(env, ps.t[:, 0:128], pt_.t[:], Sf.t[:], True, True, [pt_, Sf], [ps])
                    k.op("dve", lambda e: e.tensor_tensor(out=Sf.t[:], in0=ps.t[:, 0:128], in1=sl_.t[:], op=ALU.add),
                         reads=[ps, sl_], writes=[Sf])
                k.op("act", lambda e: e.activation(out=Sb.t[:], in_=Sf.t[:], func=AF.Copy), reads=[Sf], writes=[Sb])
                wb = ws.take(1)[0]
                for tg in range(c.NTG):
                    pending.append(item(h, tg, zproj(tg, wb), Sb))
                    while len(pending) >= 3:
                        pending[:] = [g_ for g_ in pending if stepg(g_)]
            while pending:
                pending[:] = [g_ for g_ in pending if stepg(g_)]
        gdn_tail(env, io, vgroups, onT_d, onbufs, h_dram, x1, xo, P)
        k.finish()
    return nc


def gdn_c_inputs(cfg, w_in, norm_w, w_out, w_gu, w_dn, ln4):
    c = cfg
    NKH, NVH, KC = c.NKH, c.NVH, c.KC
    KD, VD = NKH * 128, NVH * 128
    o1 = 2 * KD + VD
    d = {"ident": np.eye(128, dtype=np.float32)}
    d["w_z"] = tile_a(w_in, [o1 + h * 128 for h in range(NVH)])
    d["normw"] = np.ascontiguousarray(norm_w, dtype=np.float32).reshape(1, 128)
    g0 = 0
    gi = 0
    while g0 < NVH:
        G = min(16, NVH - g0)
        d["w_o_t%d" % gi] = tile_b(w_out, g0 * 128, G)
        g0 += G
        gi += 1
    d.update(ffn_inputs(cfg, w_gu, w_dn))
    d["ln"] = np.ascontiguousarray(ln4, dtype=np.float32)
    return d


def gdn_inputs(cfg, mode, w_in, conv_w, a_log, dt_bias, norm_w=None, w_out=None, w_gu=None, w_dn=None, ln4=None):
    c = cfg
    NKH, NVH, KC = c.NKH, c.NVH, c.KC
    KD, VD = NKH * 128, NVH * 128
    o1 = 2 * KD + VD
    o2 = o1 + VD
    d = {"ident": np.eye(128, dtype=np.float32), "consts": gdn_consts()}
    cols = []
    for kh in range(NKH):
        cols += [kh * 128, KD + kh * 128, 2 * KD + 2 * kh * 128, 2 * KD + (2 * kh + 1) * 128,
                 o1 + 2 * kh * 128, o1 + (2 * kh + 1) * 128]
    d["w_t"] = tile_a(w_in, cols).reshape(NKH, 6, 128, KC * 128)
    wba = w_in[:, o2:o2 + 2 * NVH].reshape(KC, 128, 2 * NVH).transpose(1, 0, 2)
    d["w_ba"] = np.ascontiguousarray(wba).reshape(128, KC * 2 * NVH)
    cwg = np.empty((128, NKH, 4, 4), np.float32)
    for kh in range(NKH):
        ch = [kh * 128, KD + kh * 128, 2 * KD + 2 * kh * 128, 2 * KD + (2 * kh + 1) * 128]
        for ci in range(4):
            cwg[:, kh, ci, :] = conv_w[:, ch[ci]:ch[ci] + 128].T
    d["cwg"] = cwg.reshape(128, NKH * 16)
    d["alog"] = np.ascontiguousarray(a_log, dtype=np.float32).reshape(1, NVH)
    d["dtb"] = np.ascontiguousarray(dt_bias, dtype=np.float32).reshape(1, NVH)
    if mode == "b":
        d["normw"] = np.ascontiguousarray(norm_w, dtype=np.float32).reshape(1, 128)
        g0 = 0
        gi = 0
        while g0 < NVH:
            G = min(16, NVH - g0)
            d["w_o_t%d" % gi] = tile_b(w_out, g0 * 128, G)
            g0 += G
            gi += 1
        d.update(ffn_inputs(cfg, w_gu, w_dn))
        d["ln"] = np.ascontiguousarray(ln4, dtype=np.float32)
    return d


_PROGS = {}


def _prog(name, cfg):
    if name not in _PROGS:
        if name == "sc":
            _PROGS[name] = build_sc_program(cfg)
        elif name == "ga":
            _PROGS[name] = build_gdn_program(cfg, "a")
        else:
            _PROGS[name] = build_gdn_c_program(cfg)
    return _PROGS[name]


def _halos(xs, cfg):
    out = []
    for cidx in range(N_CORES):
        if cidx % 4 == 0:
            out.append(np.zeros((4, cfg.D), np.float32))
        else:
            out.append(np.ascontiguousarray(xs[cidx - 1][-4:]))
    return out


def kernel(x, sc_w_in, sc_conv_w, sc_w_out, dn_w_in, dn_conv_w, dn_a_log, dn_dt_bias, dn_norm_w, dn_w_out,
           ffn_w_gate_up, ffn_w_down, ln_gain, ln_bias):
    cfg = Cfg()
    f = lambda a: np.ascontiguousarray(np.asarray(a), dtype=np.float32)
    x = f(x)
    NT = cfg.NT
    xs = [np.ascontiguousarray(x[ci // 4, (ci % 4) * NT:(ci % 4 + 1) * NT]) for ci in range(N_CORES)]
    cores = list(range(N_CORES))
    for layer in range(DEPTH):
        j = layer // 2
        ln4 = np.stack([f(ln_gain)[layer, 0], f(ln_bias)[layer, 0], f(ln_gain)[layer, 1], f(ln_bias)[layer, 1]])
        halos = _halos(xs, cfg)
        if layer % 2 == 0:
            base = sc_inputs(cfg, f(sc_w_in)[j], f(sc_conv_w)[j], f(sc_w_out)[j], f(ffn_w_gate_up)[layer],
                             f(ffn_w_down)[layer], ln4)
            maps = [dict(base, x=xs[ci], xh=halos[ci]) for ci in cores]
            res = run_bass_kernel_spmd(_prog("sc", cfg), maps, core_ids=cores)
            xs = [np.asarray(res.results[ci]["xo"]) for ci in cores]
        else:
            base_a = gdn_inputs(cfg, "a", f(dn_w_in)[j], f(dn_conv_w)[j], f(dn_a_log)[j], f(dn_dt_bias)[j])
            maps = [dict(base_a, x=xs[ci], xh=halos[ci]) for ci in cores]
            res = run_bass_kernel_spmd(_prog("ga", cfg), maps, core_ids=cores)
            sts = [np.asarray(res.results[ci]["st"]) for ci in cores]
            oqs = [np.asarray(res.results[ci]["oq"]) for ci in cores]
            base_b = gdn_c_inputs(cfg, f(dn_w_in)[j], f(dn_norm_w)[j], f(dn_w_out)[j], f(ffn_w_gate_up)[layer],
                                  f(ffn_w_down)[layer], ln4)
            eye = np.ascontiguousarray(np.broadcast_to(np.eye(128, dtype=np.float32), (cfg.NVH, 128, 128)))
            zer = np.zeros((cfg.NVH, 128, 128), np.float32)
            maps = []
            for ci in cores:
                q = ci % 4
                pts, ss = [eye] * (3 - q), [zer] * (3 - q)
                for p in range(ci - q, ci):
                    pts.append(np.ascontiguousarray(sts[p][:, :, 128:].transpose(0, 2, 1)))
                    ss.append(np.ascontiguousarray(sts[p][:, :, :128]))
                maps.append(dict(base_b, x=xs[ci], oq=oqs[ci], sin_pt=np.stack(pts), sin_s=np.stack(ss)))
            res = run_bass_kernel_spmd(_prog("gc", cfg), maps, core_ids=cores)
            xs = [np.asarray(res.results[ci]["xo"]) for ci in cores]
    out = np.empty((BATCH, SEQ, D_MODEL), np.float32)
    for ci in cores:
        out[ci // 4, (ci % 4) * NT:(ci % 4 + 1) * NT] = xs[ci]
    return out
```

```python
import numpy as np
from contextlib import ExitStack
import concourse.bass as bass
import concourse.mybir as mybir
from concourse.bass_utils import run_bass_kernel_spmd

F32 = mybir.dt.float32
BF16 = mybir.dt.bfloat16
F32R = mybir.dt.float32r
AF = mybir.ActivationFunctionType
ALU = mybir.AluOpType

D_MODEL = 2048
BATCH = 2
SEQ = 8192
DEPTH = 4
N_CORES = 8
FFN_HIDDEN = 5632
ALPHA = (2 * DEPTH) ** 0.25
LN_EPS = 1e-5
RMS_EPS = 1e-6
HD = 128


class Buf:
    def __init__(self, name, t=None):
        self.name = name
        self.t = t
        self.w = None
        self.r = {}
        self.dsem = None
        self.dcnt = 0
        self.parent = None
        self.children = []
        self.is_psum = False

    def view(self, name, ap):
        c = Buf(name, ap)
        c.parent = self
        self.children.append(c)
        return c


class K:
    def __init__(self, nc, es):
        self.nc = nc
        self.es = es
        self.eng = {"pe": nc.tensor, "act": nc.scalar, "dve": nc.vector,
                    "pool": nc.gpsimd, "sp": nc.sync}
        self.sem = {e: es.enter_context(nc.semaphore("s_" + e)) for e in self.eng}
        self.cnt = {e: 0 for e in self.eng}
        self.waited = {e: {} for e in self.eng}
        self.semobj = {}
        self.n_dsem = 0
        self.uid = 0
        self.out_stamps = []
        self.allbufs = []
        self.es_cur = None
        self.names = {}

    def sbuf(self, name, shape, dt):
        es = self.es_cur if getattr(self, "es_cur", None) is not None else self.es
        n = self.names.get(name, 0)
        self.names[name] = n + 1
        if n:
            name = "%s_v%d" % (name, n)
        t = es.enter_context(self.nc.sbuf_tensor(name, list(shape), dt))
        b = Buf(name, t)
        self.allbufs.append(b)
        return b

    def barrier(self):
        tgt = {}
        for e in self.eng:
            if self.cnt[e] > 0:
                tgt[id(self.sem[e])] = (self.sem[e], self.cnt[e])
        for b in self.allbufs:
            if b.dsem is not None and b.dcnt > 0:
                tgt[id(b.dsem)] = (b.dsem, 16 * b.dcnt)
        for e in self.eng:
            w = self.waited[e]
            for key, (sm, v) in tgt.items():
                if w.get(key, 0) < v:
                    self.eng[e].wait_ge(sm, v)
                    w[key] = v

    def psum(self, name, shape, dt):
        t = self.es.enter_context(self.nc.psum_tensor(name, list(shape), dt))
        b = Buf(name, t)
        b.is_psum = True
        return b

    def dsem_for(self, b):
        if b.dsem is None:
            b.dsem = self.es.enter_context(self.nc.semaphore("d%d" % self.n_dsem))
            self.n_dsem += 1
        return b.dsem

    def _wait(self, e, reads, writes):
        need = {}

        def add(st):
            if st is None:
                return
            s, v = st
            key = id(s)
            self.semobj[key] = s
            if need.get(key, 0) < v:
                need[key] = v

        def addr(b):
            for key, v in b.r.items():
                if need.get(key, 0) < v:
                    need[key] = v

        for b in reads:
            add(b.w)
            if b.is_psum:
                addr(b)
            if b.parent is not None:
                add(b.parent.w)
            for ch in b.children:
                add(ch.w)
        for b in writes:
            add(b.w)
            addr(b)
            if b.parent is not None:
                add(b.parent.w)
                addr(b.parent)
            for ch in b.children:
                add(ch.w)
                addr(ch)
        w = self.waited[e]
        own = id(self.sem[e])
        for key, v in need.items():
            if key == own and (e == "pe" or v > self.cnt[e]):
                continue
            if w.get(key, 0) < v:
                self.eng[e].wait_ge(self.semobj[key], v)
                w[key] = v

    def _stamp(self, st, reads, writes):
        key = id(st[0])
        self.semobj[key] = st[0]
        for b in reads:
            if b.r.get(key, 0) < st[1]:
                b.r[key] = st[1]
        for b in writes:
            b.w = st
            b.r = {}

    def op(self, e, fn, reads=(), writes=(), inc=True):
        self.nops = getattr(self, "nops", 0) + 1
        if self.nops > getattr(self, "limit", 1 << 60):
            return None
        self._wait(e, reads, writes)
        ins = fn(self.eng[e])
        if inc:
            ins.then_inc(self.sem[e], 1)
            self.cnt[e] += 1
            st = (self.sem[e], self.cnt[e])
        else:
            st = (self.sem[e], self.cnt[e] + 1)
        self._stamp(st, reads, writes)
        return ins

    def dma(self, e, out, in_, reads=(), writes=(), owner=None, is_output=False):
        self.nops = getattr(self, "nops", 0) + 1
        if self.nops > getattr(self, "limit", 1 << 60):
            return None
        self._wait(e, reads, writes)
        sem = self.dsem_for(owner)
        ins = self.eng[e].dma_start(out=out, in_=in_)
        ins.then_inc(sem, 16)
        owner.dcnt += 1
        st = (sem, 16 * owner.dcnt)
        self._stamp(st, reads, writes)
        if is_output:
            self.out_stamps.append(st)
        return ins

    def finish(self):
        need = {}
        for s, v in self.out_stamps:
            key = id(s)
            self.semobj[key] = s
            need[key] = max(need.get(key, 0), v)
        for key, v in need.items():
            self.eng["sp"].wait_ge(self.semobj[key], v)


class WStream:
    def __init__(self, k, name, ch_elems, n_stage, n_bf, cast_engines=("pool",)):
        self.k = k
        self.ch = ch_elems
        self.stage = [k.sbuf("%s_st%d" % (name, i), [128, ch_elems], F32) for i in range(n_stage)]
        self.bf = [k.sbuf("%s_bf%d" % (name, i), [128, ch_elems], BF16) for i in range(n_bf)]
        self.n = 0
        self.cast_engines = cast_engines
        self.queue = []
        self.ready = []

    def push(self, dram_ap):
        self.queue.append(dram_ap)

    def _fetch(self, item):
        k = self.k
        dram_ap, n = item
        st = self.stage[self.n % len(self.stage)]
        bf = self.bf[self.n % len(self.bf)]
        ce = self.cast_engines[self.n % len(self.cast_engines)]
        self.n += 1
        k.dma("sp", out=st.t[:, 0:n], in_=dram_ap, writes=[st], owner=st)
        if ce == "act":
            k.op("act", lambda e: e.activation(out=bf.t[:, 0:n], in_=st.t[:, 0:n], func=AF.Copy),
                 reads=[st], writes=[bf])
        else:
            k.op(ce, lambda e: e.tensor_copy(out=bf.t[:, 0:n], in_=st.t[:, 0:n]), reads=[st], writes=[bf])
        return bf

    def prefetch(self, depth):
        while self.queue and len(self.ready) < depth:
            self.ready.append(self._fetch(self.queue.pop(0)))

    def take(self, n):
        self.prefetch(n)
        out = [self.ready.pop(0) for _ in range(n)]
        self.prefetch(len(self.bf) - n)
        return out


class Cfg:
    def __init__(self, D=D_MODEL, NT=2048, F=FFN_HIDDEN, fgroups=(16, 16, 12), NKH=None):
        self.D = D
        self.NT = NT
        self.F = F
        self.KC = D // 128
        self.NTT = NT // 128
        self.NTG = NT // 512
        self.fgroups = tuple(fgroups)
        assert sum(fgroups) * 128 == F
        self.NKH = NKH if NKH is not None else D // HD
        self.NVH = 2 * self.NKH
        self.CH = max(self.KC * 128, 2048)


class Env:
    def __init__(self, k, cfg, consts, alloc_act=True, ws_slots=6):
        self.k = k
        self.cfg = cfg
        c = cfg
        nc = k.nc
        self.xT = [k.sbuf("xT%d" % g, [128, c.KC, 512], BF16) for g in range(c.NTG)]
        self.xhT = k.sbuf("xhT", [128, c.KC, 4], BF16)
        self.AK = max(c.KC, max(c.fgroups), min(16, c.NVH))
        self.ws = WStream(k, "w", c.CH, 2, ws_slots)
        self.lnviews = {}
        self.rings = {}
        self.pA = [k.psum("pA%d" % i, [128, 512], F32) for i in range(6)]
        self.pT = [k.psum("pT%d" % i, [128, 1024], BF16) for i in range(2)]
        self.pi = 0
        self.pti = 0
        self.ident_bf = k.sbuf("ident_bf", [128, 128], BF16)
        self.ident_f = k.sbuf("ident_f", [128, 128], F32)
        k.dma("sp", out=self.ident_f.t[:], in_=consts["ident"], writes=[self.ident_f], owner=self.ident_f)
        k.op("dve", lambda e: e.tensor_copy(out=self.ident_bf.t[:], in_=self.ident_f.t[:]),
             reads=[self.ident_f], writes=[self.ident_bf])
        if alloc_act:
            self.alloc_act()

    def alloc_act(self):
        k, c = self.k, self.cfg
        self.actT = [k.sbuf("actT%d" % g, [128, self.AK, 512], BF16) for g in range(c.NTG)]
        fl = [a.t[:].rearrange("p a b -> p (a b)") for a in self.actT]
        nb = self.AK * 512
        self.lnviews = {}

        def carve(g, off_bf, n_bf, dt, name):
            ap = fl[g][:, off_bf:off_bf + n_bf]
            if dt == F32:
                ap = ap.bitcast(F32)
            return self.actT[g].view(name, ap)

        if 2 * c.D * 2 <= nb and c.NTG >= 3:
            self.lnviews["lnr"] = [carve(0, 0, 2 * c.D, F32, "lnr0"), carve(0, 2 * c.D, 2 * c.D, F32, "lnr1")]
            self.lnviews["lnh"] = [carve(1, 0, 2 * c.D, F32, "lnh0")]
            if c.NTG >= 4:
                self.lnviews["lnh"] += [carve(3, 0, 2 * c.D, F32, "lnh1"), carve(3, 2 * c.D, 2 * c.D, F32, "lnh2")]
            self.lnviews["xb"] = [carve(1, 2 * c.D, c.D, BF16, "xb0"), carve(1, 3 * c.D, c.D, BF16, "xb1")]
            self.lnviews["gain"] = [carve(2, 0, 2 * c.D, F32, "gain")]
            self.lnviews["bias"] = [carve(2, 2 * c.D, 2 * c.D, F32, "bias")]

    def phase(self):
        env = self

        class _P:
            def __enter__(p):
                p.before = set(env.rings)
                p.es2 = ExitStack()
                p.es2.__enter__()
                p.prev = env.k.es_cur
                env.k.es_cur = p.es2
                return p

            def __exit__(p, *a):
                if a[0] is None:
                    env.k.barrier()
                env.k.es_cur = p.prev
                p.es2.__exit__(*a)
                for n in list(env.rings):
                    if n not in p.before:
                        del env.rings[n]
                return False

        return _P()

    def pnext(self):
        b = self.pA[self.pi % len(self.pA)]
        self.pi += 1
        return b

    def ptnext(self):
        b = self.pT[self.pti % len(self.pT)]
        self.pti += 1
        return b

    def ring(self, name, shape, dt, n):
        if name in self.lnviews and name not in self.rings:
            self.rings[name] = [self.lnviews[name], 0]
        if name not in self.rings:
            self.rings[name] = [[self.k.sbuf("%s%d" % (name, i), shape, dt) for i in range(n)], 0]
        r = self.rings[name]
        b = r[0][r[1] % len(r[0])]
        r[1] += 1
        return b


def f32v(b):
    return b.t[:].bitcast(F32)


def mm(env, ps_ap, lhsT, rhs, start, stop, reads, writes, inc=None):
    env.k.op("pe", lambda e: e.matmul(ps_ap, lhsT=lhsT, rhs=rhs, start=start, stop=stop),
             reads=reads, writes=writes, inc=(stop if inc is None else inc))


def emit_xT_tile(env, xf, tt):
    k, c = env.k, env.cfg
    xb = env.ring("xb", [128, c.D], BF16, 2)
    k.op("act", lambda e: e.activation(out=xb.t[:], in_=xf.t[:], func=AF.Copy), reads=[xf], writes=[xb])
    g, o = tt // 4, (tt % 4) * 128
    for k0 in range(0, c.KC, 8):
        n = min(8, c.KC - k0)
        pt = env.ptnext()
        for i in range(n):
            kc = k0 + i
            k.op("pe", lambda e: e.transpose(pt.t[:, i * 128:(i + 1) * 128], xb.t[:, kc * 128:(kc + 1) * 128],
                                             env.ident_bf.t[:]),
                 reads=[xb, env.ident_bf], writes=[pt], inc=(i == n - 1))
        k.op("dve", lambda e: e.tensor_copy(out=env.xT[g].t[:, k0:k0 + n, o:o + 128],
                                            in_=pt.t[:, 0:n * 128].rearrange("p (a b) -> p a b", b=128)),
             reads=[pt], writes=[env.xT[g]])


def emit_halo(env, xh_dram):
    k, c = env.k, env.cfg
    hf = k.sbuf("halo_f", [4, c.D], F32)
    hb = k.sbuf("halo_b", [4, c.D], BF16)
    k.dma("sp", out=hf.t[:], in_=xh_dram, writes=[hf], owner=hf)
    k.op("act", lambda e: e.activation(out=hb.t[:], in_=hf.t[:], func=AF.Copy), reads=[hf], writes=[hb])
    pt = env.ptnext()
    for kc in range(c.KC):
        k.op("pe", lambda e: e.transpose(pt.t[:, kc * 4:(kc + 1) * 4], hb.t[:, kc * 128:(kc + 1) * 128],
                                         env.ident_bf.t[0:4, 0:4]),
             reads=[hb, env.ident_bf], writes=[pt], inc=(kc == c.KC - 1))
    k.op("dve", lambda e: e.tensor_copy(out=env.xhT.t[:],
                                        in_=pt.t[:, 0:c.KC * 4].rearrange("p (a b) -> p a b", b=4)),
         reads=[pt], writes=[env.xhT])


def load_x_to_xT(env, x_dram):
    k, c = env.k, env.cfg
    for tt in range(c.NTT):
        xf = env.ring("lnr", [128, c.D], F32, 2)
        k.dma("sp", out=xf.t[:], in_=x_dram[tt * 128:(tt + 1) * 128, :], writes=[xf], owner=xf)
        emit_xT_tile(env, xf, tt)


def type_a(env, units, n_per_unit, epilogue, halo_fn=None):
    k, c = env.k, env.cfg
    ws = env.ws
    for u in units:
        for ap in u:
            ws.push((ap, c.KC * 128))
    for ui, u in enumerate(units):
        wb = ws.take(n_per_unit)
        if halo_fn is not None:
            halo_fn(ui, wb)
        for tg in range(c.NTG):
            ps = [env.pnext() for _ in range(n_per_unit)]
            for ci in range(n_per_unit):
                for kc in range(c.KC):
                    mm(env, ps[ci].t[:, :], wb[ci].t[:, kc * 128:(kc + 1) * 128], env.xT[tg].t[:, kc, :],
                       kc == 0, kc == c.KC - 1, [wb[ci], env.xT[tg]], [ps[ci]])
            epilogue(ui, tg, ps)


def type_b(env, KCb, slab_aps, hbufs, h_dram):
    k, c = env.k, env.cfg
    ws = env.ws
    npc = KCb // 4
    for s in range(c.D // 512):
        for ap in slab_aps[s]:
            ws.push((ap, 2048))
    for s in range(c.D // 512):
        pcs = ws.take(npc)
        for tt in range(c.NTT):
            g, o = tt // 4, (tt % 4) * 128
            ps = env.pnext()
            for kc in range(KCb):
                pc = pcs[kc // 4]
                mm(env, ps.t[:, :], env.actT[g].t[:, kc, o:o + 128], pc.t[:, (kc % 4) * 512:(kc % 4 + 1) * 512],
                   kc == 0, kc == KCb - 1, [env.actT[g], pc], [ps])
            ob = env.ring("ob", [128, 512], F32, 3)
            k.op("act", lambda e: e.activation(out=ob.t[:], in_=ps.t[:, :], func=AF.Copy), reads=[ps], writes=[ob])
            k.dma("act", out=h_dram[tt * 128:(tt + 1) * 128, s * 512:(s + 1) * 512], in_=ob.t[:],
                  reads=[ob], writes=[hbufs[tt]], owner=ob)


def ln_phase(env, x_dram, xbufs, parts, gain_dram, bias_dram, out_dram, obufs, make_xT, final):
    k, c = env.k, env.cfg
    gb = env.ring("gain", [128, c.D], F32, 1)
    bb = env.ring("bias", [128, c.D], F32, 1)
    k.dma("sp", out=gb.t[:], in_=gain_dram.partition_broadcast(128), writes=[gb], owner=gb)
    k.dma("sp", out=bb.t[:], in_=bias_dram.partition_broadcast(128), writes=[bb], owner=bb)
    nch = c.D // 512
    for tt in range(c.NTT):
        r = env.ring("lnr", [128, c.D], F32, 2)
        rows = slice(tt * 128, (tt + 1) * 128)
        k.dma("sp", out=r.t[:], in_=x_dram[rows, :], reads=[xbufs[tt]] if xbufs else [], writes=[r], owner=r)
        for pi, (hd, hb) in enumerate(parts):
            hin = env.ring("lnh", [128, c.D], F32, 2)
            k.dma("sp", out=hin.t[:], in_=hd[rows, :], reads=[hb[tt]], writes=[hin], owner=hin)
            if pi == 0:
                k.op("act", lambda e: e.activation(out=r.t[:], in_=r.t[:], func=AF.Copy, scale=float(ALPHA)),
                     reads=[r], writes=[r])
            k.op("pool", lambda e: e.tensor_tensor(out=r.t[:], in0=r.t[:], in1=hin.t[:], op=ALU.add),
                 reads=[r, hin], writes=[r])
        st = env.ring("lnst", [128, nch * 6], F32, 2)
        for i in range(nch):
            k.op("dve", lambda e: e.bn_stats(out=st.t[:, i * 6:(i + 1) * 6], in_=r.t[:, i * 512:(i + 1) * 512]),
                 reads=[r], writes=[st])
        mv = env.ring("lnmv", [128, 4], F32, 2)
        k.op("dve", lambda e: e.bn_aggr(out=mv.t[:, 0:2], in_=st.t[:]), reads=[st], writes=[mv])
        k.op("act", lambda e: e.activation(out=mv.t[:, 2:3], in_=mv.t[:, 1:2], func=AF.Sqrt, bias=float(LN_EPS)),
             reads=[mv], writes=[mv])
        k.op("dve", lambda e: e.reciprocal(out=mv.t[:, 2:3], in_=mv.t[:, 2:3]), reads=[mv], writes=[mv])
        k.op("dve", lambda e: e.tensor_scalar(out=r.t[:], in0=r.t[:], scalar1=mv.t[:, 0:1], scalar2=mv.t[:, 2:3],
                                              op0=ALU.subtract, op1=ALU.mult), reads=[r, mv], writes=[r])
        k.op("pool", lambda e: e.tensor_tensor(out=r.t[:], in0=r.t[:], in1=gb.t[:], op=ALU.mult),
             reads=[r, gb], writes=[r])
        k.op("pool", lambda e: e.tensor_tensor(out=r.t[:], in0=r.t[:], in1=bb.t[:], op=ALU.add),
             reads=[r, bb], writes=[r])
        k.dma("act", out=out_dram[rows, :], in_=r.t[:], reads=[r], writes=[obufs[tt]], owner=r, is_output=final)
        if make_xT:
            emit_xT_tile(env, r, tt)


def tile_a(W, col_starts):
    K_ = W.shape[0]
    KC = K_ // 128
    Wr = W.reshape(KC, 128, W.shape[1])
    out = np.empty((len(col_starts), 128, KC * 128), np.float32)
    for i, c0 in enumerate(col_starts):
        out[i] = Wr[:, :, c0:c0 + 128].transpose(1, 0, 2).reshape(128, KC * 128)
    return out


def tile_b(W, r0, KCb):
    D = W.shape[1]
    Wr = W[r0:r0 + KCb * 128].reshape(KCb // 4, 4, 128, D // 512, 512)
    return np.ascontiguousarray(Wr.transpose(3, 0, 2, 1, 4)).reshape(D // 512, KCb // 4, 128, 2048)


def dram_in(nc, name, shape):
    return nc.dram_tensor(name, list(shape), F32, kind="ExternalInput").ap()


def ffn_and_ln(env, io, x_dram, xbufs, h_dram, hbufs, out_dram, obufs, final, make_xT):
    k, c = env.k, env.cfg
    hc0 = 0
    parts = []
    for gi, G in enumerate(c.fgroups):
        units = [[io["w_gu_t"][hc0 + j, 0], io["w_gu_t"][hc0 + j, 1]] for j in range(G)]

        def epi(ui, tg, ps):
            sg = env.ring("sg", [128, 512], F32, 2)
            k.op("act", lambda e: e.activation(out=sg.t[:], in_=ps[0].t[:, :], func=AF.Silu), reads=[ps[0]], writes=[sg])
            k.op("dve", lambda e: e.tensor_tensor(out=env.actT[tg].t[:, ui, :], in0=ps[1].t[:, :], in1=sg.t[:], op=ALU.mult),
                 reads=[ps[1], sg], writes=[env.actT[tg]])

        type_a(env, units, 2, epi)
        slabs = [[io["w_dn_t"][gi][s, pc] for pc in range(G // 4)] for s in range(c.D // 512)]
        type_b(env, G, slabs, hbufs[gi], h_dram[gi])
        parts.append((h_dram[gi], hbufs[gi]))
        hc0 += G
    ln_phase(env, x_dram, xbufs, parts, io["ln"][2, :], io["ln"][3, :], out_dram, obufs, make_xT, final)


def build_sc_program(cfg, stages=9):
    c = cfg
    nc = bass.Bass("TRN2", target_bir_lowering=False)
    io = {}
    io["x"] = dram_in(nc, "x", [c.NT, c.D])
    io["xh"] = dram_in(nc, "xh", [4, c.D])
    io["ident"] = dram_in(nc, "ident", [128, 128])
    io["w_in_t"] = dram_in(nc, "w_in_t", [c.KC, 3, 128, c.KC * 128])
    io["cw"] = dram_in(nc, "cw", [128, c.KC * 3])
    io["w_out_t"] = dram_in(nc, "w_out_t", [c.D // 512, c.KC // 4, 128, 2048])
    io["w_gu_t"] = dram_in(nc, "w_gu_t", [c.F // 128, 2, 128, c.KC * 128])
    io["w_dn_t"] = [dram_in(nc, "w_dn_t%d" % gi, [c.D // 512, G // 4, 128, 2048]) for gi, G in enumerate(c.fgroups)]
    io["ln"] = dram_in(nc, "ln", [4, c.D])
    xo = nc.dram_tensor("xo", [c.NT, c.D], F32, kind="ExternalOutput").ap()
    P = max(1, len(c.fgroups))
    h_dram = [nc.dram_tensor("h%d" % p, [c.NT, c.D], F32, kind="Internal").ap() for p in range(P)]
    x1 = nc.dram_tensor("x1", [c.NT, c.D], F32, kind="Internal").ap()
    with ExitStack() as es:
        k = K(nc, es)
        env = Env(k, c, io)
        hbufs = [[Buf("h%d_%d" % (p, tt)) for tt in range(c.NTT)] for p in range(P)]
        x1bufs = [Buf("x1_%d" % tt) for tt in range(c.NTT)]
        xobufs = [Buf("xo_%d" % tt) for tt in range(c.NTT)]
        cw = k.sbuf("cw_sb", [128, c.KC * 3], F32)
        k.dma("sp", out=cw.t[:], in_=io["cw"], writes=[cw], owner=cw)
        emit_halo(env, io["xh"])
        load_x_to_xT(env, io["x"])

        units = [[io["w_in_t"][j, 0], io["w_in_t"][j, 1], io["w_in_t"][j, 2]] for j in range(c.KC)]
        state = {}

        def halo_fn(j, wb):
            ps = env.pnext()
            for ci in (1, 2):
                for kc in range(c.KC):
                    mm(env, ps.t[:, (ci - 1) * 4:(ci - 1) * 4 + 4], wb[ci].t[:, kc * 128:(kc + 1) * 128],
                       env.xhT.t[:, kc, :], kc == 0, kc == c.KC - 1, [wb[ci], env.xhT], [ps],
                       inc=(kc == c.KC - 1))
            hh = env.ring("hh", [128, 4], F32, 2)
            k.op("act", lambda e: e.activation(out=hh.t[:], in_=ps.t[:, 4:8], func=AF.Copy), reads=[ps], writes=[hh])
            u = env.ring("u", [128, 2 + 512], F32, 3)
            k.op("dve", lambda e: e.tensor_tensor(out=u.t[:, 0:2], in0=ps.t[:, 2:4], in1=hh.t[:, 2:4], op=ALU.mult),
                 reads=[ps, hh], writes=[u])
            state["u"] = u

        def epi(j, tg, ps):
            hs = env.ring("hs", [128, 512], F32, 2)
            k.op("act", lambda e: e.activation(out=hs.t[:], in_=ps[2].t[:, :], func=AF.Copy), reads=[ps[2]], writes=[hs])
            if tg == 0:
                u = state["u"]
            else:
                u = env.ring("u", [128, 2 + 512], F32, 3)
                up = state["u"]
                k.op("dve", lambda e: e.tensor_copy(out=u.t[:, 0:2], in_=up.t[:, 512:514]), reads=[up], writes=[u])
                state["u"] = u
            k.op("dve", lambda e: e.tensor_tensor(out=u.t[:, 2:514], in0=ps[1].t[:, :], in1=hs.t[:], op=ALU.mult),
                 reads=[ps[1], hs], writes=[u])
            y = env.ring("y", [128, 512], F32, 2)
            k.op("act", lambda e: e.activation(out=y.t[:], in_=u.t[:, 0:512], func=AF.Copy, scale=cw.t[:, j * 3:j * 3 + 1]),
                 reads=[u, cw], writes=[y])
            k.op("dve", lambda e: e.scalar_tensor_tensor(out=y.t[:], in0=u.t[:, 1:513], scalar=cw.t[:, j * 3 + 1:j * 3 + 2],
                                                         in1=y.t[:], op0=ALU.mult, op1=ALU.add), reads=[u, cw, y], writes=[y])
            k.op("dve", lambda e: e.scalar_tensor_tensor(out=y.t[:], in0=u.t[:, 2:514], scalar=cw.t[:, j * 3 + 2:j * 3 + 3],
                                                         in1=y.t[:], op0=ALU.mult, op1=ALU.add), reads=[u, cw, y], writes=[y])
            k.op("dve", lambda e: e.tensor_tensor(out=env.actT[tg].t[:, j, :], in0=ps[0].t[:, :], in1=y.t[:], op=ALU.mult),
                 reads=[ps[0], y], writes=[env.actT[tg]])

        type_a(env, units, 3, epi, halo_fn)
        slabs = [[io["w_out_t"][s, pc] for pc in range(c.KC // 4)] for s in range(c.D // 512)]
        if stages >= 2:
            type_b(env, c.KC, slabs, hbufs[0], h_dram[0])
        if stages >= 3:
            ln_phase(env, io["x"], None, [(h_dram[0], hbufs[0])], io["ln"][0, :], io["ln"][1, :], x1, x1bufs, True, False)
        if stages >= 4:
            ffn_and_ln(env, io, x1, x1bufs, h_dram, hbufs, xo, xobufs, True, False)
        k.finish()
    return nc


def sc_inputs(cfg, w_in, conv_w, w_out, w_gu, w_dn, ln4):
    c = cfg
    D, F = c.D, c.F
    d = {}
    d["ident"] = np.eye(128, dtype=np.float32)
    cols = []
    for j in range(c.KC):
        cols += [j * 128, D + j * 128, 2 * D + j * 128]
    d["w_in_t"] = tile_a(w_in, cols).reshape(c.KC, 3, 128, c.KC * 128)
    d["cw"] = np.ascontiguousarray(conv_w.reshape(3, c.KC, 128).transpose(2, 1, 0)).reshape(128, c.KC * 3)
    d["w_out_t"] = tile_b(w_out, 0, c.KC)
    d.update(ffn_inputs(cfg, w_gu, w_dn))
    d["ln"] = np.ascontiguousarray(ln4, dtype=np.float32)
    return d


def ffn_inputs(cfg, w_gu, w_dn):
    c = cfg
    d = {}
    cols = []
    for j in range(c.F // 128):
        cols += [j * 128, c.F + j * 128]
    d["w_gu_t"] = tile_a(w_gu, cols).reshape(c.F // 128, 2, 128, c.KC * 128)
    r0 = 0
    for gi, G in enumerate(c.fgroups):
        d["w_dn_t%d" % gi] = tile_b(w_dn, r0, G)
        r0 += G * 128
    return d


def gdn_consts():
    i = np.arange(128)
    same = (i[:, None] // 64) == (i[None, :] // 64)
    c = np.zeros((8, 128, 128), np.float32)
    c[0] = ((i[:, None] <= i[None, :]) & same)
    c[1] = (i[:, None] > i[None, :])
    c[2] = ((i[:, None] > i[None, :]) & same)
    c[3] = ((i[None, :] >= i[:, None]) & same)
    c[4] = same
    c[5] = 1.0
    c[6] = (i[:, None] < 64) * np.ones((1, 128))
    c[7] = (i[:, None] >= 64) * np.ones((1, 128))
    return c


def roundrobin(gens):
    gens = list(gens)
    while gens:
        nxt = []
        for g in gens:
            try:
                next(g)
                nxt.append(g)
            except StopIteration:
                pass
        gens = nxt


LIMIT = [1 << 60]
USE_PT_VIEWS = False


def build_gdn_program(cfg, mode, dbg=99):
    c = cfg
    NVH, NKH, KC = c.NVH, c.NKH, c.KC
    NV = 256 if mode == "a" else 128
    nch = 4 if mode == "a" else 6
    vgroups = [min(16, NVH - g0) for g0 in range(0, NVH, 16)]
    nc = bass.Bass("TRN2", target_bir_lowering=False)
    io = {}
    io["x"] = dram_in(nc, "x", [c.NT, c.D])
    io["xh"] = dram_in(nc, "xh", [4, c.D])
    io["ident"] = dram_in(nc, "ident", [128, 128])
    io["consts"] = dram_in(nc, "consts", [8, 128, 128])
    io["w_t"] = dram_in(nc, "w_t", [NKH, 6, 128, KC * 128])
    io["w_ba"] = dram_in(nc, "w_ba", [128, KC * 2 * NVH])
    io["cwg"] = dram_in(nc, "cwg", [128, NKH * 16])
    io["alog"] = dram_in(nc, "alog", [1, NVH])
    io["dtb"] = dram_in(nc, "dtb", [1, NVH])
    if mode == "a":
        st_out = nc.dram_tensor("st", [NVH, 128, 256], F32, kind="ExternalOutput").ap()
        oq_out = nc.dram_tensor("oq", [NVH, c.NT, 256], F32, kind="ExternalOutput").ap()
    else:
        io["normw"] = dram_in(nc, "normw", [1, 128])
        io["sin_pt"] = dram_in(nc, "sin_pt", [3, NVH, 128, 128])
        io["sin_s"] = dram_in(nc, "sin_s", [3, NVH, 128, 128])
        io["w_o_t"] = [dram_in(nc, "w_o_t%d" % gi, [c.D // 512, G // 4, 128, 2048]) for gi, G in enumerate(vgroups)]
        io["w_gu_t"] = dram_in(nc, "w_gu_t", [c.F // 128, 2, 128, KC * 128])
        io["w_dn_t"] = [dram_in(nc, "w_dn_t%d" % gi, [c.D // 512, G // 4, 128, 2048]) for gi, G in enumerate(c.fgroups)]
        io["ln"] = dram_in(nc, "ln", [4, c.D])
        xo = nc.dram_tensor("xo", [c.NT, c.D], F32, kind="ExternalOutput").ap()
        P = max(len(vgroups), len(c.fgroups))
        h_dram = [nc.dram_tensor("h%d" % p, [c.NT, c.D], F32, kind="Internal").ap() for p in range(P)]
        x1 = nc.dram_tensor("x1", [c.NT, c.D], F32, kind="Internal").ap()
        onT_d = nc.dram_tensor("onT_d", [NVH, 128, c.NT], BF16, kind="Internal").ap()
    with ExitStack() as es:
        k = K(nc, es)
        k.limit = LIMIT[0]
        env = Env(k, c, io, alloc_act=False, ws_slots=(5 if mode == "a" else 7))
        with env.phase():
            emit_halo(env, io["xh"])
            load_x_to_xT(env, io["x"])
        onbufs = [[Buf("on_%d_%d" % (h, tg)) for tg in range(c.NTG)] for h in range(NVH)]
        if dbg <= 1:
            k.finish()
            return nc

        with env.phase():
            def f32t(name, shape=(128, 128)):
                return k.sbuf(name, list(shape), F32)

            CN = []
            for i in range(8):
                t = f32t("cst%d" % i)
                k.dma("sp", out=t.t[:], in_=io["consts"][i], writes=[t], owner=t)
                CN.append(t)
            TRIBD, UU, STRICT, CAUSALT, BLK, ONES, CH0, CH1 = CN
            UUr = k.sbuf("UUr", [128, 128], F32R)
            k.op("dve", lambda e: e.tensor_copy(out=UUr.t[:], in_=UU.t[:]), reads=[UU], writes=[UUr])
            cwg = f32t("cwg_sb", (128, NKH * 16))
            k.dma("sp", out=cwg.t[:], in_=io["cwg"], writes=[cwg], owner=cwg)
            wba_f = f32t("wba_f", (128, KC * 2 * NVH))
            wba = k.sbuf("wba_b", [128, KC, 2 * NVH], BF16)
            k.dma("sp", out=wba_f.t[:], in_=io["w_ba"], writes=[wba_f], owner=wba_f)
            k.op("dve", lambda e: e.tensor_copy(out=wba.t[:].rearrange("p a b -> p (a b)"), in_=wba_f.t[:]),
                 reads=[wba_f], writes=[wba])
            alog = f32t("alog_sb", (128, NVH))
            dtb = f32t("dtb_sb", (128, NVH))
            k.dma("sp", out=alog.t[:], in_=io["alog"][0, :].partition_broadcast(128), writes=[alog], owner=alog)
            k.dma("sp", out=dtb.t[:], in_=io["dtb"][0, :].partition_broadcast(128), writes=[dtb], owner=dtb)
            negA = f32t("negA", (128, NVH))
            k.op("act", lambda e: e.activation(out=negA.t[:], in_=alog.t[:], func=AF.Exp), reads=[alog], writes=[negA])
            k.op("dve", lambda e: e.tensor_scalar(out=negA.t[:], in0=negA.t[:], scalar1=-1.0, scalar2=None, op0=ALU.mult),
                 reads=[negA], writes=[negA])
            if mode == "b":
                nwbc = f32t("nwbc")
                k.dma("sp", out=nwbc.t[:], in_=io["normw"][0, :].partition_broadcast(128), writes=[nwbc], owner=nwbc)
            shp = (128, c.NTT, NVH)
            BETA, AX, GRAW, GC, EG = [f32t(n, shp) for n in ("BETA", "AX", "GRAW", "GC", "EG")]
            EKT, BEG = AX, GC
            CD = f32t("CD", (128, c.NTT, 2 * NVH))

            for tt in range(c.NTT):
                g, o = tt // 4, (tt % 4) * 128
                ps = env.pnext()
                for kc in range(KC):
                    mm(env, ps.t[:, 0:2 * NVH], env.xT[g].t[:, kc, o:o + 128], wba.t[:, kc, :], kc == 0, kc == KC - 1,
                       [env.xT[g], wba], [ps])
                k.op("act", lambda e: e.activation(out=BETA.t[:, tt, :], in_=ps.t[:, 0:NVH], func=AF.Sigmoid),
                     reads=[ps], writes=[BETA])
                k.op("dve", lambda e: e.tensor_tensor(out=AX.t[:, tt, :], in0=ps.t[:, NVH:2 * NVH], in1=dtb.t[:], op=ALU.add),
                     reads=[ps, dtb], writes=[AX])
            k.op("act", lambda e: e.activation(out=AX.t[:], in_=AX.t[:], func=AF.Exp), reads=[AX], writes=[AX])
            k.op("act", lambda e: e.activation(out=AX.t[:], in_=AX.t[:], func=AF.Ln, bias=1.0), reads=[AX], writes=[AX])
            for tt in range(c.NTT):
                k.op("dve", lambda e: e.tensor_tensor(out=GRAW.t[:, tt, :], in0=AX.t[:, tt, :], in1=negA.t[:], op=ALU.mult),
                     reads=[AX, negA], writes=[GRAW])
            for tt in range(c.NTT):
                gm = env.ring("gm", [128, 2 * NVH], F32, 2)
                k.op("dve", lambda e: e.tensor_scalar(out=gm.t[:, 0:NVH], in0=GRAW.t[:, tt, :], scalar1=CH0.t[:, 0:1],
                                                      scalar2=None, op0=ALU.mult), reads=[GRAW, CH0], writes=[gm])
                k.op("dve", lambda e: e.tensor_scalar(out=gm.t[:, NVH:2 * NVH], in0=GRAW.t[:, tt, :], scalar1=CH1.t[:, 0:1],
                                                      scalar2=None, op0=ALU.mult), reads=[GRAW, CH1], writes=[gm])
                ps = env.pnext()
                mm(env, ps.t[:, 0:NVH], TRIBD.t[:], GRAW.t[:, tt, :], True, True, [TRIBD, GRAW], [ps])
                mm(env, ps.t[:, 64:64 + NVH], BLK.t[:], GRAW.t[:, tt, :], True, True, [BLK, GRAW], [ps])
                mm(env, ps.t[:, 128:128 + 2 * NVH], ONES.t[:], gm.t[:], True, True, [ONES, gm], [ps])
                k.op("dve", lambda e: e.tensor_copy(out=GC.t[:, tt, :], in_=ps.t[:, 0:NVH]), reads=[ps], writes=[GC])
                k.op("act", lambda e: e.activation(out=EG.t[:, tt, :], in_=ps.t[:, 0:NVH], func=AF.Exp), reads=[ps], writes=[EG])
                k.op("dve", lambda e: e.tensor_tensor(out=EKT.t[:, tt, :], in0=ps.t[:, 64:64 + NVH], in1=GC.t[:, tt, :],
                                                      op=ALU.subtract), reads=[ps, GC], writes=[EKT])
                k.op("act", lambda e: e.activation(out=EKT.t[:, tt, :], in_=EKT.t[:, tt, :], func=AF.Exp),
                     reads=[EKT], writes=[EKT])
                k.op("act", lambda e: e.activation(out=CD.t[:, tt, :], in_=ps.t[:, 128:128 + 2 * NVH], func=AF.Exp),
                     reads=[ps], writes=[CD])
                k.op("dve", lambda e: e.tensor_tensor(out=BEG.t[:, tt, :], in0=BETA.t[:, tt, :], in1=EG.t[:, tt, :], op=ALU.mult),
                     reads=[BETA, EG], writes=[BEG])

            if dbg <= 2:
                k.finish()
                return nc
            if mode == "a" and USE_PT_VIEWS:
                for pt_b in env.pT:
                    v_ = pt_b.view(pt_b.name + "_f32", pt_b.t[:].bitcast(F32))
                    v_.is_psum = True
                    env.pA.append(v_)
            Sf = [f32t("Sf%d" % i, (128, NV)) for i in range(2)]
            Sb = [k.sbuf("Sb%d" % i, [128, NV], BF16) for i in range(2)]
            ubuf = [[f32t("u%d_%d" % (i, j), (128, NV)) for j in range(3)] for i in range(2)]
            if NV == 256:
                for i in range(2):
                    for j in range(3):
                        k.op("pool", lambda e: e.memset(ubuf[i][j].t[:], 0.0), writes=[ubuf[i][j]])
            ucnt = [0, 0]
            carry = [f32t("carry%d" % ci, (128, 4)) for ci in range(4)]
            ws = env.ws
            for kh in range(NKH):
                for ci in range(nch):
                    ws.push((io["w_t"][kh, ci], KC * 128))

            for kh in range(NKH):
                wb = ws.take(nch)
                for hh in range(2):
                    h = 2 * kh + hh
                    if mode == "a":
                        k.op("pool", lambda e: e.memset(Sf[hh].t[:, 0:128], 0.0), writes=[Sf[hh]])
                        k.op("pool", lambda e: e.tensor_copy(out=Sf[hh].t[:, 128:256], in_=env.ident_f.t[:]),
                             reads=[env.ident_f], writes=[Sf[hh]])
                    else:
                        k.dma("sp", out=Sf[hh].t[:], in_=io["sin_s"][0, h], writes=[Sf[hh]], owner=Sf[hh])
                        for j in (1, 2):
                            pt_ = env.ring("foldp", [128, 128], F32, 2)
                            sl_ = env.ring("folds", [128, 128], F32, 2)
                            k.dma("sp", out=pt_.t[:], in_=io["sin_pt"][j, h], writes=[pt_], owner=pt_)
                            k.dma("sp", out=sl_.t[:], in_=io["sin_s"][j, h], writes=[sl_], owner=sl_)
                            ps = env.pnext()
                            mm(env, ps.t[:, 0:128], pt_.t[:], Sf[hh].t[:], True, True, [pt_, Sf[hh]], [ps])
                            k.op("dve", lambda e: e.tensor_tensor(out=Sf[hh].t[:], in0=ps.t[:, 0:128], in1=sl_.t[:], op=ALU.add),
                                 reads=[ps, sl_], writes=[Sf[hh]])
                    k.op("act", lambda e: e.activation(out=Sb[hh].t[:], in_=Sf[hh].t[:], func=AF.Copy),
                         reads=[Sf[hh]], writes=[Sb[hh]])
                psh = env.pnext()
                for ci in range(4):
                    for kc in range(KC):
                        mm(env, psh.t[:, ci * 4:ci * 4 + 4], wb[ci].t[:, kc * 128:(kc + 1) * 128], env.xhT.t[:, kc, :],
                           kc == 0, kc == KC - 1, [wb[ci], env.xhT], [psh])
                for ci in range(4):
                    k.op("dve", lambda e: e.tensor_copy(out=carry[ci].t[:, 0:3], in_=psh.t[:, ci * 4 + 1:ci * 4 + 4]),
                         reads=[psh], writes=[carry[ci]])
                TG = {}

                def proj(tg):
                    knT = env.ring("knT", [128, 512], F32R, 2)
                    qnT = env.ring("qnT", [128, 512], F32R, 2)
                    qb = env.ring("qb", [128, 512], BF16, 2)
                    vT = [env.ring("vT%d" % i, [128, 512], F32, 2) for i in range(2)]
                    sz = [env.ring("sz%d" % i, [128, 512], BF16, 2) for i in range(2)] if mode == "b" else None
                    TG[tg] = dict(knT=knT, qnT=qnT, qb=qb, vT=vT, sz=sz)
                    for ci in range(nch):
                        ps = env.pnext()
                        for kc in range(KC):
                            mm(env, ps.t[:, :], wb[ci].t[:, kc * 128:(kc + 1) * 128], env.xT[tg].t[:, kc, :],
                               kc == 0, kc == KC - 1, [wb[ci], env.xT[tg]], [ps])
                        if ci >= 4:
                            k.op("act", lambda e: e.activation(out=sz[ci - 4].t[:], in_=ps.t[:, :], func=AF.Silu),
                                 reads=[ps], writes=[sz[ci - 4]])
                            continue
                        pre = env.ring("pre", [128, 3 + 512], F32, 2)
                        k.op("dve", lambda e: e.tensor_copy(out=pre.t[:, 0:3], in_=carry[ci].t[:, 0:3]), reads=[carry[ci]], writes=[pre])
                        k.op("act", lambda e: e.activation(out=pre.t[:, 3:515], in_=ps.t[:, :], func=AF.Copy), reads=[ps], writes=[pre])
                        k.op("dve", lambda e: e.tensor_copy(out=carry[ci].t[:, 0:3], in_=pre.t[:, 512:515]), reads=[pre], writes=[carry[ci]])
                        cb = (kh * 4 + ci) * 4
                        ca = env.ring("cacc", [128, 512], F32, 2)
                        k.op("act", lambda e: e.activation(out=ca.t[:], in_=pre.t[:, 0:512], func=AF.Copy, scale=cwg.t[:, cb:cb + 1]),
                             reads=[pre, cwg], writes=[ca])
                        for tap in (1, 2, 3):
                            k.op("dve", lambda e: e.scalar_tensor_tensor(out=ca.t[:], in0=pre.t[:, tap:tap + 512],
                                                                         scalar=cwg.t[:, cb + tap:cb + tap + 1], in1=ca.t[:],
                                                                         op0=ALU.mult, op1=ALU.add), reads=[pre, cwg, ca], writes=[ca])
                        if ci >= 2:
                            k.op("act", lambda e: e.activation(out=vT[ci - 2].t[:], in_=ca.t[:], func=AF.Silu),
                                 reads=[ca], writes=[vT[ci - 2]])
                            continue
                        sl = env.ring("sl", [128, 512], F32, 1)
                        k.op("act", lambda e: e.activation(out=sl.t[:], in_=ca.t[:], func=AF.Silu), reads=[ca], writes=[sl])
                        k.op("pool", lambda e: e.tensor_tensor(out=ca.t[:], in0=sl.t[:], in1=sl.t[:], op=ALU.mult), reads=[sl], writes=[ca])
                        ps2 = env.pnext()
                        mm(env, ps2.t[:, :], ONES.t[:], ca.t[:], True, True, [ONES, ca], [ps2])
                        k.op("act", lambda e: e.activation(out=ca.t[:], in_=ps2.t[:, :], func=AF.Sqrt, bias=float(RMS_EPS)),
                             reads=[ps2], writes=[ca])
                        k.op("dve", lambda e: e.reciprocal(out=ca.t[:], in_=ca.t[:]), reads=[ca], writes=[ca])
                        if ci == 1:
                            k.op("dve", lambda e: e.tensor_tensor(out=knT.t[:], in0=sl.t[:], in1=ca.t[:], op=ALU.mult),
                                 reads=[sl, ca], writes=[knT])
                        else:
                            k.op("dve", lambda e: e.scalar_tensor_tensor(out=qnT.t[:], in0=sl.t[:], scalar=float(HD ** -0.5),
                                                                         in1=ca.t[:], op0=ALU.mult, op1=ALU.mult),
                                 reads=[sl, ca], writes=[qnT])
                            k.op("pool", lambda e: e.tensor_copy(out=qb.t[:], in_=f32v(qnT)), reads=[qnT], writes=[qb])

                tile = {}

                def prep(tt):
                    cols = slice((tt % 4) * 128, (tt % 4 + 1) * 128)
                    knT, qnT, vT = TG[tt // 4]["knT"], TG[tt // 4]["qnT"], TG[tt // 4]["vT"]
                    T = {}
                    psk = env.pnext()
                    k.op("pe", lambda e: e.transpose(psk.t[:, 0:128], f32v(knT)[:, cols], env.ident_f.t[:]),
                         reads=[knT, env.ident_f], writes=[psk])
                    psv = env.pnext()
                    for hh in range(2):
                        k.op("pe", lambda e: e.transpose(psv.t[:, hh * 128:(hh + 1) * 128], vT[hh].t[:, cols], env.ident_f.t[:]),
                             reads=[vT[hh], env.ident_f], writes=[psv])
                    T["kbg"], T["ktl"], T["vb"] = [], [], []
                    for hh in range(2):
                        h = 2 * kh + hh
                        kbg = env.ring("kbg%d" % hh, [128, 128], F32R, 3)
                        ktl = env.ring("ktl%d" % hh, [128, 128], BF16, 3)
                        vb = env.ring("vb%d" % hh, [128, 128], F32R, 3)
                        k.op("act", lambda e: e.activation(out=kbg.t[:], in_=psk.t[:, 0:128], func=AF.Copy, scale=BEG.t[:, tt, h:h + 1]),
                             reads=[psk, BEG], writes=[kbg])
                        k.op("dve", lambda e: e.tensor_scalar(out=ktl.t[:], in0=psk.t[:, 0:128], scalar1=EKT.t[:, tt, h:h + 1],
                                                              scalar2=None, op0=ALU.mult), reads=[psk, EKT], writes=[ktl])
                        k.op("dve", lambda e: e.tensor_scalar(out=vb.t[:], in0=psv.t[:, hh * 128:(hh + 1) * 128],
                                                              scalar1=BETA.t[:, tt, h:h + 1], scalar2=None, op0=ALU.mult),
                             reads=[psv, BETA], writes=[vb])
                        T["kbg"].append(kbg); T["ktl"].append(ktl); T["vb"].append(vb)
                    yield
                    pskk = env.pnext()
                    mm(env, pskk.t[:, 0:128], knT.t[:, cols], knT.t[:, cols], True, True, [knT], [pskk])
                    mm(env, pskk.t[:, 128:256], knT.t[:, cols], qnT.t[:, cols], True, True, [knT, qnT], [pskk])
                    KKs = env.ring("KKs", [128, 128], F32, 3)
                    QKm = env.ring("QKm", [128, 128], F32, 3)
                    k.op("dve", lambda e: e.tensor_tensor(out=KKs.t[:], in0=pskk.t[:, 0:128], in1=STRICT.t[:], op=ALU.mult),
                         reads=[pskk, STRICT], writes=[KKs])
                    k.op("dve", lambda e: e.tensor_tensor(out=QKm.t[:], in0=pskk.t[:, 128:256], in1=CAUSALT.t[:], op=ALU.mult),
                         reads=[pskk, CAUSALT], writes=[QKm])
                    T["KKs"], T["QKm"] = KKs, QKm
                    T["u"], T["wT"], T["aqk"] = [None, None], [None, None], [None, None]
                    tile[tt] = T
                    yield

                def solve(tt, hh):
                    T = tile[tt]
                    h = 2 * kh + hh
                    G = env.ring("G%d_%d" % (hh, tt % 2), [128, 128], F32R, 1)
                    k.op("act", lambda e: e.activation(out=G.t[:], in_=TRIBD.t[:], func=AF.Copy, scale=GRAW.t[:, tt, h:h + 1]),
                         reads=[TRIBD, GRAW], writes=[G])
                    yield
                    ps = env.pnext()
                    mm(env, ps.t[:, 0:128], G.t[:], UUr.t[:], True, True, [G, UUr], [ps])
                    mm(env, ps.t[:, 128:256], UUr.t[:], G.t[:], True, True, [G, UUr], [ps])
                    Dec = env.ring("Dec%d_%d" % (hh, tt % 2), [128, 256], F32, 1)
                    k.op("act", lambda e: e.activation(out=Dec.t[:], in_=ps.t[:, 0:256], func=AF.Exp), reads=[ps], writes=[Dec])
                    yield
                    A = env.ring("A%d_%d" % (hh, tt % 2), [128, 128], F32R, 2)
                    k.op("dve", lambda e: e.scalar_tensor_tensor(out=A.t[:], in0=T["KKs"].t[:], scalar=BETA.t[:, tt, h:h + 1],
                                                                 in1=Dec.t[:, 0:128], op0=ALU.mult, op1=ALU.mult),
                         reads=[T["KKs"], BETA, Dec], writes=[A])
                    aqk = env.ring("aqk%d" % hh, [128, 128], BF16, 3)
                    k.op("pool", lambda e: e.tensor_tensor(out=aqk.t[:], in0=T["QKm"].t[:], in1=Dec.t[:, 128:256], op=ALU.mult),
                         reads=[T["QKm"], Dec], writes=[aqk])
                    T["aqk"][hh] = aqk
                    yield
                    ps = env.pnext()
                    k.op("pe", lambda e: e.transpose(ps.t[:, 0:128], f32v(A), env.ident_f.t[:]), reads=[A, env.ident_f], writes=[ps])
                    X = env.ring("X%d_%d" % (hh, tt % 2), [128, 128], F32R, 2)
                    B = env.ring("B%d_%d" % (hh, tt % 2), [128, 128], F32R, 2)
                    k.op("dve", lambda e: e.tensor_tensor(out=X.t[:], in0=env.ident_f.t[:], in1=ps.t[:, 0:128], op=ALU.subtract),
                         reads=[env.ident_f, ps], writes=[X])
                    k.op("act", lambda e: e.activation(out=B.t[:], in_=ps.t[:, 0:128], func=AF.Copy), reads=[ps], writes=[B])
                    yield
                    for m in (1, 2, 4, 8, 16):
                        ps = env.pnext()
                        mm(env, ps.t[:, 0:128], B.t[:], A.t[:], True, True, [A, B], [ps])
                        if m < 16:
                            mm(env, ps.t[:, 128:256], A.t[:], B.t[:], True, True, [A, B], [ps])
                        A2 = env.ring("A%d_%d" % (hh, tt % 2), [128, 128], F32R, 2)
                        k.op("act", lambda e: e.activation(out=A2.t[:], in_=ps.t[:, 0:128], func=AF.Copy), reads=[ps], writes=[A2])
                        if m < 16:
                            B2 = env.ring("B%d_%d" % (hh, tt % 2), [128, 128], F32R, 2)
                            k.op("dve", lambda e: e.tensor_copy(out=B2.t[:], in_=ps.t[:, 128:256]), reads=[ps], writes=[B2])
                        yield
                        ps2 = env.pnext()
                        mm(env, ps2.t[:, 0:128], A2.t[:], X.t[:], True, True, [A2, X], [ps2])
                        X2 = env.ring("X%d_%d" % (hh, tt % 2), [128, 128], F32R, 2)
                        k.op("dve", lambda e: e.tensor_tensor(out=X2.t[:], in0=f32v(X), in1=ps2.t[:, 0:128], op=ALU.add),
                             reads=[X, ps2], writes=[X2])
                        A, X = A2, X2
                        if m < 16:
                            B = B2
                        yield
                    ps = env.pnext()
                    mm(env, ps.t[:, 0:128], X.t[:], T["vb"][hh].t[:], True, True, [X, T["vb"][hh]], [ps])
                    mm(env, ps.t[:, 128:256], T["kbg"][hh].t[:], X.t[:], True, True, [X, T["kbg"][hh]], [ps])
                    u = ubuf[hh][ucnt[hh] % 3]
                    ucnt[hh] += 1
                    k.op("act", lambda e: e.activation(out=u.t[:, 0:128], in_=ps.t[:, 0:128], func=AF.Copy), reads=[ps], writes=[u])
                    wT = env.ring("wT%d" % hh, [128, 128], BF16, 3)
                    k.op("dve", lambda e: e.tensor_copy(out=wT.t[:], in_=ps.t[:, 128:256]), reads=[ps], writes=[wT])
                    T["u"][hh], T["wT"][hh] = u, wT
                    yield

                def recur(tt, hh):
                    T = tile[tt]
                    h = 2 * kh + hh
                    cols = slice((tt % 4) * 128, (tt % 4 + 1) * 128)
                    tg, o = tt // 4, (tt % 4) * 128
                    qb, sz = TG[tg]["qb"], TG[tg]["sz"]
                    u, wT, aqk, ktl = T["u"][hh], T["wT"][hh], T["aqk"][hh], T["ktl"][hh]
                    ot = env.ring("ot%d" % hh, [128, NV], F32, 2)
                    for cc in range(2):
                        r = slice(64 * cc, 64 * cc + 64)
                        ps = env.pnext()
                        mm(env, ps.t[r, 0:NV], wT.t[:, r], Sb[hh].t[:, :], True, True, [wT, Sb[hh]], [ps])
                        vn = env.ring("vn%d" % hh, [128, NV], BF16, 2)
                        k.op("dve", lambda e: e.tensor_tensor(out=vn.t[r, :], in0=u.t[r, :], in1=ps.t[r, 0:NV], op=ALU.subtract),
                             reads=[u, ps], writes=[vn])
                        yield
                        if True:
                            ps1 = env.pnext()
                            mm(env, ps1.t[r, 0:NV], qb.t[:, o + 64 * cc:o + 64 * cc + 64], Sb[hh].t[:, :], True, True,
                               [qb, Sb[hh]], [ps1])
                            ps2 = env.pnext()
                            mm(env, ps2.t[r, 0:NV], aqk.t[r, r], vn.t[r, :], True, True, [aqk, vn], [ps2])
                        ps3 = env.pnext()
                        mm(env, ps3.t[:, 0:NV], ktl.t[r, :], vn.t[r, :], True, True, [ktl, vn], [ps3])
                        if True:
                            o2 = env.ring("o2%d" % hh, [128, NV], F32, 1)
                            k.op("act", lambda e: e.activation(out=o2.t[r, :], in_=ps2.t[r, 0:NV], func=AF.Copy), reads=[ps2], writes=[o2])
                            k.op("dve", lambda e: e.scalar_tensor_tensor(out=ot.t[r, :], in0=ps1.t[r, 0:NV], scalar=EG.t[r, tt, h:h + 1],
                                                                         in1=o2.t[r, :], op0=ALU.mult, op1=ALU.add),
                                 reads=[ps1, EG, o2], writes=[ot])
                        k.op("dve", lambda e: e.scalar_tensor_tensor(out=Sb[hh].t[:], in0=Sf[hh].t[:],
                                                                     scalar=CD.t[:, tt, cc * NVH + h:cc * NVH + h + 1],
                                                                     in1=ps3.t[:, 0:NV], op0=ALU.mult, op1=ALU.add),
                             reads=[Sf[hh], CD, ps3], writes=[Sb[hh]])
                        k.op("dve", lambda e: e.scalar_tensor_tensor(out=Sf[hh].t[:], in0=Sf[hh].t[:],
                                                                     scalar=CD.t[:, tt, cc * NVH + h:cc * NVH + h + 1],
                                                                     in1=ps3.t[:, 0:NV], op0=ALU.mult, op1=ALU.add),
                             reads=[Sf[hh], CD, ps3], writes=[Sf[hh]])
                        yield
                    if mode == "a":
                        k.dma("act", out=oq_out[h, tt * 128:(tt + 1) * 128, :], in_=ot.t[:], reads=[ot], writes=[], owner=ot,
                              is_output=True)
                    if mode == "b":
                        sq = env.ring("osq", [128, 128], F32, 2)
                        ss = env.ring("oss", [128, 2], F32, 4)
                        k.op("pool", lambda e: e.tensor_tensor(out=sq.t[:], in0=ot.t[:], in1=ot.t[:], op=ALU.mult), reads=[ot], writes=[sq])
                        k.op("dve", lambda e: e.reduce_sum(out=ss.t[:, 0:1], in_=sq.t[:], axis=mybir.AxisListType.X), reads=[sq], writes=[ss])
                        k.op("act", lambda e: e.activation(out=ss.t[:, 1:2], in_=ss.t[:, 0:1], func=AF.Sqrt, scale=1.0 / HD,
                                                           bias=float(RMS_EPS)), reads=[ss], writes=[ss])
                        k.op("dve", lambda e: e.reciprocal(out=ss.t[:, 1:2], in_=ss.t[:, 1:2]), reads=[ss], writes=[ss])
                        onb = env.ring("onb", [128, 128], BF16, 2)
                        k.op("dve", lambda e: e.scalar_tensor_tensor(out=onb.t[:], in0=ot.t[:], scalar=ss.t[:, 1:2], in1=nwbc.t[:],
                                                                     op0=ALU.mult, op1=ALU.mult), reads=[ot, ss, nwbc], writes=[onb])
                        pt = env.ptnext()
                        k.op("pe", lambda e: e.transpose(pt.t[:, 0:128], onb.t[:], env.ident_bf.t[:]), reads=[onb, env.ident_bf], writes=[pt])
                        if tt % 4 == 0:
                            tile["og%d" % hh] = env.ring("og%d" % hh, [128, 512], BF16, 2)
                        og = tile["og%d" % hh]
                        k.op("dve", lambda e: e.tensor_tensor(out=og.t[:, o:o + 128], in0=pt.t[:, 0:128], in1=sz[hh].t[:, cols], op=ALU.mult),
                             reads=[pt, sz[hh]], writes=[og])
                        if tt % 4 == 3:
                            k.dma("act", out=onT_d[h, :, tg * 512:(tg + 1) * 512], in_=og.t[:], reads=[og], writes=[onbufs[h][tg]], owner=og)
                        yield

                def tile_front(tt):
                    for _ in prep(tt):
                        yield
                    gens = [solve(tt, 0), solve(tt, 1)]
                    while gens:
                        nxt = []
                        for g_ in gens:
                            try:
                                next(g_)
                                nxt.append(g_)
                            except StopIteration:
                                pass
                        gens = nxt
                        yield

                def tile_back(tt):
                    gens = [recur(tt, 0), recur(tt, 1)]
                    while gens:
                        nxt = []
                        for g_ in gens:
                            try:
                                next(g_)
                                nxt.append(g_)
                            except StopIteration:
                                pass
                        gens = nxt
                        yield
                    del tile[tt]

                def start_front(t):
                    if t % 4 == 0:
                        proj(t // 4)
                    return tile_front(t)

                def step(g_):
                    try:
                        next(g_)
                        return True
                    except StopIteration:
                        return False

                roundrobin([start_front(0)])
                if dbg <= 4:
                    k.finish()
                    return nc
                carry_f = None
                for tt in range(c.NTT):
                    must = [tile_back(tt)]
                    if carry_f is None and tt + 1 < c.NTT:
                        must.append(start_front(tt + 1))
                    elif carry_f:
                        must.append(carry_f)
                    ahead = start_front(tt + 2) if tt + 2 < c.NTT else None
                    ahead_alive = ahead is not None
                    while must:
                        must = [g_ for g_ in must if step(g_)]
                        if ahead_alive:
                            ahead_alive = step(ahead)
                    carry_f = None if ahead is None else (ahead if ahead_alive else False)
                if mode == "a":
                    for hh in range(2):
                        k.dma("act", out=st_out[2 * kh + hh], in_=Sf[hh].t[:], reads=[Sf[hh]], writes=[], owner=Sf[hh], is_output=True)

        if mode == "b":
            gdn_tail(env, io, vgroups, onT_d, onbufs, h_dram, x1, xo, P)
        k.finish()
    return nc


def gdn_tail(env, io, vgroups, onT_d, onbufs, h_dram, x1, xo, P):
    k, c = env.k, env.cfg
    env.alloc_act()
    hbufs = [[Buf("h%d_%d" % (p, tt)) for tt in range(c.NTT)] for p in range(P)]
    x1bufs = [Buf("x1_%d" % tt) for tt in range(c.NTT)]
    xobufs = [Buf("xo_%d" % tt) for tt in range(c.NTT)]
    parts = []
    h0 = 0
    for gi, G in enumerate(vgroups):
        for tg in range(c.NTG):
            k.dma("sp", out=env.actT[tg].t[:, 0:G, :],
                  in_=onT_d[h0:h0 + G, :, tg * 512:(tg + 1) * 512].rearrange("h p t -> p h t"),
                  reads=[onbufs[h][tg] for h in range(h0, h0 + G)], writes=[env.actT[tg]], owner=env.actT[tg])
        slabs = [[io["w_o_t"][gi][s, pc] for pc in range(G // 4)] for s in range(c.D // 512)]
        type_b(env, G, slabs, hbufs[gi], h_dram[gi])
        parts.append((h_dram[gi], hbufs[gi]))
        h0 += G
    ln_phase(env, io["x"], None, parts, io["ln"][0, :], io["ln"][1, :], x1, x1bufs, True, False)
    ffn_and_ln(env, io, x1, x1bufs, h_dram, hbufs, xo, xobufs, True, False)


def build_gdn_c_program(cfg):
    c = cfg
    NVH, KC = c.NVH, c.KC
    vgroups = [min(16, NVH - g0) for g0 in range(0, NVH, 16)]
    nc = bass.Bass("TRN2", target_bir_lowering=False)
    io = {}
    io["x"] = dram_in(nc, "x", [c.NT, c.D])
    io["ident"] = dram_in(nc, "ident", [128, 128])
    io["oq"] = dram_in(nc, "oq", [NVH, c.NT, 256])
    io["w_z"] = dram_in(nc, "w_z", [NVH, 128, KC * 128])
    io["normw"] = dram_in(nc, "normw", [1, 128])
    io["sin_pt"] = dram_in(nc, "sin_pt", [3, NVH, 128, 128])
    io["sin_s"] = dram_in(nc, "sin_s", [3, NVH, 128, 128])
    io["w_o_t"] = [dram_in(nc, "w_o_t%d" % gi, [c.D // 512, G // 4, 128, 2048]) for gi, G in enumerate(vgroups)]
    io["w_gu_t"] = dram_in(nc, "w_gu_t", [c.F // 128, 2, 128, KC * 128])
    io["w_dn_t"] = [dram_in(nc, "w_dn_t%d" % gi, [c.D // 512, G // 4, 128, 2048]) for gi, G in enumerate(c.fgroups)]
    io["ln"] = dram_in(nc, "ln", [4, c.D])
    xo = nc.dram_tensor("xo", [c.NT, c.D], F32, kind="ExternalOutput").ap()
    P = max(len(vgroups), len(c.fgroups))
    h_dram = [nc.dram_tensor("h%d" % p, [c.NT, c.D], F32, kind="Internal").ap() for p in range(P)]
    x1 = nc.dram_tensor("x1", [c.NT, c.D], F32, kind="Internal").ap()
    onT_d = nc.dram_tensor("onT_d", [NVH, 128, c.NT], BF16, kind="Internal").ap()
    with ExitStack() as es:
        k = K(nc, es)
        env = Env(k, c, io, alloc_act=False, ws_slots=4)
        with env.phase():
            load_x_to_xT(env, io["x"])
        onbufs = [[Buf("on_%d_%d" % (h, tg)) for tg in range(c.NTG)] for h in range(NVH)]
        with env.phase():
            nwbc = k.sbuf("nwbc", [128, 128], F32)
            k.dma("sp", out=nwbc.t[:], in_=io["normw"][0, :].partition_broadcast(128), writes=[nwbc], owner=nwbc)
            ws = env.ws
            for h in range(NVH):
                ws.push((io["w_z"][h], KC * 128))
            pending = []

            def stepg(g_):
                try:
                    next(g_)
                    return True
                except StopIteration:
                    return False

            def zproj(tg, wb):
                ps = env.pnext()
                for kc in range(KC):
                    mm(env, ps.t[:, :], wb.t[:, kc * 128:(kc + 1) * 128], env.xT[tg].t[:, kc, :],
                       kc == 0, kc == KC - 1, [wb, env.xT[tg]], [ps])
                sz = env.ring("sz", [128, 512], BF16, 5)
                k.op("act", lambda e: e.activation(out=sz.t[:], in_=ps.t[:, :], func=AF.Silu), reads=[ps], writes=[sz])
                return sz

            def item(h, tg, sz, Sb):
                og = env.ring("og", [128, 512], BF16, 4)
                oq4 = env.ring("oq4", [128, 4, 256], F32, 4)
                k.dma("sp", out=oq4.t[:], in_=io["oq"][h, tg * 512:(tg + 1) * 512, :].rearrange("(t p) c -> p t c", p=128),
                      writes=[oq4], owner=oq4)
                yield
                pq = env.pnext()
                for t4 in range(4):
                    k.op("pe", lambda e: e.transpose(pq.t[:, t4 * 128:(t4 + 1) * 128], oq4.t[:, t4, 128:256], env.ident_f.t[:]),
                         reads=[oq4, env.ident_f], writes=[pq], inc=(t4 == 3))
                qT = env.ring("qT", [128, 512], BF16, 4)
                k.op("act", lambda e: e.activation(out=qT.t[:], in_=pq.t[:, :], func=AF.Copy), reads=[pq], writes=[qT])
                yield
                pc_ = env.pnext()
                for t4 in range(4):
                    mm(env, pc_.t[:, t4 * 128:(t4 + 1) * 128], qT.t[:, t4 * 128:(t4 + 1) * 128], Sb.t[:], True, True,
                       [qT, Sb], [pc_], inc=(t4 == 3))
                ot = env.ring("ot", [128, 4, 128], F32, 4)
                k.op("dve", lambda e: e.tensor_tensor(out=ot.t[:], in0=pc_.t[:, :].rearrange("p (t c) -> p t c", c=128),
                                                      in1=oq4.t[:, :, 0:128], op=ALU.add), reads=[pc_, oq4], writes=[ot])
                yield
                sq = env.ring("osq", [128, 4, 128], F32, 4)
                ss = env.ring("oss", [128, 8], F32, 4)
                k.op("pool", lambda e: e.tensor_tensor(out=sq.t[:], in0=ot.t[:], in1=ot.t[:], op=ALU.mult), reads=[ot], writes=[sq])
                yield
                k.op("dve", lambda e: e.reduce_sum(out=ss.t[:, 0:4], in_=sq.t[:], axis=mybir.AxisListType.X), reads=[sq], writes=[ss])
                yield
                k.op("act", lambda e: e.activation(out=ss.t[:, 4:8], in_=ss.t[:, 0:4], func=AF.Sqrt, scale=1.0 / HD,
                                                   bias=float(RMS_EPS)), reads=[ss], writes=[ss])
                yield
                k.op("dve", lambda e: e.reciprocal(out=ss.t[:, 4:8], in_=ss.t[:, 4:8]), reads=[ss], writes=[ss])
                onb = env.ring("onb", [128, 4, 128], BF16, 4)
                for t4 in range(4):
                    k.op("dve", lambda e: e.scalar_tensor_tensor(out=onb.t[:, t4, :], in0=ot.t[:, t4, :], scalar=ss.t[:, 4 + t4:5 + t4],
                                                                 in1=nwbc.t[:], op0=ALU.mult, op1=ALU.mult),
                         reads=[ot, ss, nwbc], writes=[onb])
                yield
                pt = env.ptnext()
                for t4 in range(4):
                    k.op("pe", lambda e: e.transpose(pt.t[:, t4 * 128:(t4 + 1) * 128], onb.t[:, t4, :], env.ident_bf.t[:]),
                         reads=[onb, env.ident_bf], writes=[pt], inc=(t4 == 3))
                k.op("dve", lambda e: e.tensor_tensor(out=og.t[:], in0=pt.t[:, 0:512], in1=sz.t[:], op=ALU.mult),
                     reads=[pt, sz], writes=[og])
                k.dma("act", out=onT_d[h, :, tg * 512:(tg + 1) * 512], in_=og.t[:], reads=[og], writes=[onbufs[h][tg]], owner=og)

            for h in range(NVH):
                Sf = env.ring("Sf", [128, 128], F32, 3)
                Sb = env.ring("Sb", [128, 128], BF16, 3)
                k.dma("sp", out=Sf.t[:], in_=io["sin_s"][0, h], writes=[Sf], owner=Sf)
                for j in (1, 2):
                    pt_ = env.ring("foldp", [128, 128], F32, 2)
                    sl_ = env.ring("folds", [128, 128], F32, 2)
                    k.dma("sp", out=pt_.t[:], in_=io["sin_pt"][j, h], writes=[pt_], owner=pt_)
                    k.dma("sp", out=sl_.t[:], in_=io["sin_s"][j, h], writes=[sl_], owner=sl_)
                    ps = env.pnext()
                    mm(env, ps.t[:, 0:128], pt_.t[:], Sf.t[:], True, True, [pt_, Sf], [ps])
                    k.op("dve", lambda e: e.tensor_tensor(out=Sf.t[:], in0=ps.t[:, 0:128], in1=sl_.t[:], op=ALU.add),
                         reads=[ps, sl_], writes=[Sf])
                k.op("act", lambda e: e.activation(out=Sb.t[:], in_=Sf.t[:], func=AF.Copy), reads=[Sf], writes=[Sb])
                wb = ws.take(1)[0]
                for tg in range(c.NTG):
                    pending.append(item(h, tg, zproj(tg, wb), Sb))
                    while len(pending) >= 3:
                        pending[:] = [g_ for g_ in pending if stepg(g_)]
            while pending:
                pending[:] = [g_ for g_ in pending if stepg(g_)]
        gdn_tail(env, io, vgroups, onT_d, onbufs, h_dram, x1, xo, P)
        k.finish()
    return nc


def gdn_c_inputs(cfg, w_in, norm_w, w_out, w_gu, w_dn, ln4):
    c = cfg
    NKH, NVH, KC = c.NKH, c.NVH, c.KC
    KD, VD = NKH * 128, NVH * 128
    o1 = 2 * KD + VD
    d = {"ident": np.eye(128, dtype=np.float32)}
    d["w_z"] = tile_a(w_in, [o1 + h * 128 for h in range(NVH)])
    d["normw"] = np.ascontiguousarray(norm_w, dtype=np.float32).reshape(1, 128)
    g0 = 0
    gi = 0
    while g0 < NVH:
        G = min(16, NVH - g0)
        d["w_o_t%d" % gi] = tile_b(w_out, g0 * 128, G)
        g0 += G
        gi += 1
    d.update(ffn_inputs(cfg, w_gu, w_dn))
    d["ln"] = np.ascontiguousarray(ln4, dtype=np.float32)
    return d


def gdn_inputs(cfg, mode, w_in, conv_w, a_log, dt_bias, norm_w=None, w_out=None, w_gu=None, w_dn=None, ln4=None):
    c = cfg
    NKH, NVH, KC = c.NKH, c.NVH, c.KC
    KD, VD = NKH * 128, NVH * 128
    o1 = 2 * KD + VD
    o2 = o1 + VD
    d = {"ident": np.eye(128, dtype=np.float32), "consts": gdn_consts()}
    cols = []
    for kh in range(NKH):
        cols += [kh * 128, KD + kh * 128, 2 * KD + 2 * kh * 128, 2 * KD + (2 * kh + 1) * 128,
                 o1 + 2 * kh * 128, o1 + (2 * kh + 1) * 128]
    d["w_t"] = tile_a(w_in, cols).reshape(NKH, 6, 128, KC * 128)
    wba = w_in[:, o2:o2 + 2 * NVH].reshape(KC, 128, 2 * NVH).transpose(1, 0, 2)
    d["w_ba"] = np.ascontiguousarray(wba).reshape(128, KC * 2 * NVH)
    cwg = np.empty((128, NKH, 4, 4), np.float32)
    for kh in range(NKH):
        ch = [kh * 128, KD + kh * 128, 2 * KD + 2 * kh * 128, 2 * KD + (2 * kh + 1) * 128]
        for ci in range(4):
            cwg[:, kh, ci, :] = conv_w[:, ch[ci]:ch[ci] + 128].T
    d["cwg"] = cwg.reshape(128, NKH * 16)
    d["alog"] = np.ascontiguousarray(a_log, dtype=np.float32).reshape(1, NVH)
    d["dtb"] = np.ascontiguousarray(dt_bias, dtype=np.float32).reshape(1, NVH)
    if mode == "b":
        d["normw"] = np.ascontiguousarray(norm_w, dtype=np.float32).reshape(1, 128)
        g0 = 0
        gi = 0
        while g0 < NVH:
            G = min(16, NVH - g0)
            d["w_o_t%d" % gi] = tile_b(w_out, g0 * 128, G)
            g0 += G
            gi += 1
        d.update(ffn_inputs(cfg, w_gu, w_dn))
        d["ln"] = np.ascontiguousarray(ln4, dtype=np.float32)
    return d


_PROGS = {}


def _prog(name, cfg):
    if name not in _PROGS:
        if name == "sc":
            _PROGS[name] = build_sc_program(cfg)
        elif name == "ga":
            _PROGS[name] = build_gdn_program(cfg, "a")
        else:
            _PROGS[name] = build_gdn_c_program(cfg)
    return _PROGS[name]


def _halos(xs, cfg):
    out = []
    for cidx in range(N_CORES):
        if cidx % 4 == 0:
            out.append(np.zeros((4, cfg.D), np.float32))
        else:
            out.append(np.ascontiguousarray(xs[cidx - 1][-4:]))
    return out


def kernel(x, sc_w_in, sc_conv_w, sc_w_out, dn_w_in, dn_conv_w, dn_a_log, dn_dt_bias, dn_norm_w, dn_w_out,
           ffn_w_gate_up, ffn_w_down, ln_gain, ln_bias):
    cfg = Cfg()
    f = lambda a: np.ascontiguousarray(np.asarray(a), dtype=np.float32)
    x = f(x)
    NT = cfg.NT
    xs = [np.ascontiguousarray(x[ci // 4, (ci % 4) * NT:(ci % 4 + 1) * NT]) for ci in range(N_CORES)]
    cores = list(range(N_CORES))
    for layer in range(DEPTH):
        j = layer // 2
        ln4 = np.stack([f(ln_gain)[layer, 0], f(ln_bias)[layer, 0], f(ln_gain)[layer, 1], f(ln_bias)[layer, 1]])
        halos = _halos(xs, cfg)
        if layer % 2 == 0:
            base = sc_inputs(cfg, f(sc_w_in)[j], f(sc_conv_w)[j], f(sc_w_out)[j], f(ffn_w_gate_up)[layer],
                             f(ffn_w_down)[layer], ln4)
            maps = [dict(base, x=xs[ci], xh=halos[ci]) for ci in cores]
            res = run_bass_kernel_spmd(_prog("sc", cfg), maps, core_ids=cores)
            xs = [np.asarray(res.results[ci]["xo"]) for ci in cores]
        else:
            base_a = gdn_inputs(cfg, "a", f(dn_w_in)[j], f(dn_conv_w)[j], f(dn_a_log)[j], f(dn_dt_bias)[j])
            maps = [dict(base_a, x=xs[ci], xh=halos[ci]) for ci in cores]
            res = run_bass_kernel_spmd(_prog("ga", cfg), maps, core_ids=cores)
            sts = [np.asarray(res.results[ci]["st"]) for ci in cores]
            oqs = [np.asarray(res.results[ci]["oq"]) for ci in cores]
            base_b = gdn_c_inputs(cfg, f(dn_w_in)[j], f(dn_norm_w)[j], f(dn_w_out)[j], f(ffn_w_gate_up)[layer],
                                  f(ffn_w_down)[layer], ln4)
            eye = np.ascontiguousarray(np.broadcast_to(np.eye(128, dtype=np.float32), (cfg.NVH, 128, 128)))
            zer = np.zeros((cfg.NVH, 128, 128), np.float32)
            maps = []
            for ci in cores:
                q = ci % 4
                pts, ss = [eye] * (3 - q), [zer] * (3 - q)
                for p in range(ci - q, ci):
                    pts.append(np.ascontiguousarray(sts[p][:, :, 128:].transpose(0, 2, 1)))
                    ss.append(np.ascontiguousarray(sts[p][:, :, :128]))
                maps.append(dict(base_b, x=xs[ci], oq=oqs[ci], sin_pt=np.stack(pts), sin_s=np.stack(ss)))
            res = run_bass_kernel_spmd(_prog("gc", cfg), maps, core_ids=cores)
            xs = [np.asarray(res.results[ci]["xo"]) for ci in cores]
    out = np.empty((BATCH, SEQ, D_MODEL), np.float32)
    for ci in cores:
        out[ci // 4, (ci % 4) * NT:(ci % 4 + 1) * NT] = xs[ci]
    return out
```

```python
import numpy as np
from contextlib import ExitStack
import concourse.bass as bass
import concourse.mybir as mybir
from concourse.bass_utils import run_bass_kernel_spmd

F32 = mybir.dt.float32
BF16 = mybir.dt.bfloat16
F32R = mybir.dt.float32r
AF = mybir.ActivationFunctionType
ALU = mybir.AluOpType

D_MODEL = 2048
BATCH = 2
SEQ = 8192
DEPTH = 4
N_CORES = 8
FFN_HIDDEN = 5632
ALPHA = (2 * DEPTH) ** 0.25
LN_EPS = 1e-5
RMS_EPS = 1e-6
HD = 128


class Buf:
    def __init__(self, name, t=None):
        self.name = name
        self.t = t
        self.w = None
        self.r = {}
        self.dsem = None
        self.dcnt = 0
        self.parent = None
        self.children = []
        self.is_psum = False

    def view(self, name, ap):
        c = Buf(name, ap)
        c.parent = self
        self.children.append(c)
        return c


class K:
    def __init__(self, nc, es):
        self.nc = nc
        self.es = es
        self.eng = {"pe": nc.tensor, "act": nc.scalar, "dve": nc.vector,
                    "pool": nc.gpsimd, "sp": nc.sync}
        self.sem = {e: es.enter_context(nc.semaphore("s_" + e)) for e in self.eng}
        self.cnt = {e: 0 for e in self.eng}
        self.waited = {e: {} for e in self.eng}
        self.semobj = {}
        self.n_dsem = 0
        self.uid = 0
        self.out_stamps = []
        self.allbufs = []
        self.es_cur = None
        self.names = {}

    def sbuf(self, name, shape, dt):
        es = self.es_cur if getattr(self, "es_cur", None) is not None else self.es
        n = self.names.get(name, 0)
        self.names[name] = n + 1
        if n:
            name = "%s_v%d" % (name, n)
        t = es.enter_context(self.nc.sbuf_tensor(name, list(shape), dt))
        b = Buf(name, t)
        self.allbufs.append(b)
        return b

    def barrier(self):
        tgt = {}
        for e in self.eng:
            if self.cnt[e] > 0:
                tgt[id(self.sem[e])] = (self.sem[e], self.cnt[e])
        for b in self.allbufs:
            if b.dsem is not None and b.dcnt > 0:
                tgt[id(b.dsem)] = (b.dsem, 16 * b.dcnt)
        for e in self.eng:
            w = self.waited[e]
            for key, (sm, v) in tgt.items():
                if w.get(key, 0) < v:
                    self.eng[e].wait_ge(sm, v)
                    w[key] = v

    def psum(self, name, shape, dt):
        t = self.es.enter_context(self.nc.psum_tensor(name, list(shape), dt))
        b = Buf(name, t)
        b.is_psum = True
        return b

    def dsem_for(self, b):
        if b.dsem is None:
            b.dsem = self.es.enter_context(self.nc.semaphore("d%d" % self.n_dsem))
            self.n_dsem += 1
        return b.dsem

    def _wait(self, e, reads, writes):
        need = {}

        def add(st):
            if st is None:
                return
            s, v = st
            key = id(s)
            self.semobj[key] = s
            if need.get(key, 0) < v:
                need[key] = v

        def addr(b):
            for key, v in b.r.items():
                if need.get(key, 0) < v:
                    need[key] = v

        for b in reads:
            add(b.w)
            if b.is_psum:
                addr(b)
            if b.parent is not None:
                add(b.parent.w)
            for ch in b.children:
                add(ch.w)
        for b in writes:
            add(b.w)
            addr(b)
            if b.parent is not None:
                add(b.parent.w)
                addr(b.parent)
            for ch in b.children:
                add(ch.w)
                addr(ch)
        w = self.waited[e]
        own = id(self.sem[e])
        for key, v in need.items():
            if key == own and (e == "pe" or v > self.cnt[e]):
                continue
            if w.get(key, 0) < v:
                self.eng[e].wait_ge(self.semobj[key], v)
                w[key] = v

    def _stamp(self, st, reads, writes):
        key = id(st[0])
        self.semobj[key] = st[0]
        for b in reads:
            if b.r.get(key, 0) < st[1]:
                b.r[key] = st[1]
        for b in writes:
            b.w = st
            b.r = {}

    def op(self, e, fn, reads=(), writes=(), inc=True):
        self.nops = getattr(self, "nops", 0) + 1
        if self.nops > getattr(self, "limit", 1 << 60):
            return None
        self._wait(e, reads, writes)
        ins = fn(self.eng[e])
        if inc:
            ins.then_inc(self.sem[e], 1)
            self.cnt[e] += 1
            st = (self.sem[e], self.cnt[e])
        else:
            st = (self.sem[e], self.cnt[e] + 1)
        self._stamp(st, reads, writes)
        return ins

    def dma(self, e, out, in_, reads=(), writes=(), owner=None, is_output=False):
        self.nops = getattr(self, "nops", 0) + 1
        if self.nops > getattr(self, "limit", 1 << 60):
            return None
        self._wait(e, reads, writes)
        sem = self.dsem_for(owner)
        ins = self.eng[e].dma_start(out=out, in_=in_)
        ins.then_inc(sem, 16)
        owner.dcnt += 1
        st = (sem, 16 * owner.dcnt)
        self._stamp(st, reads, writes)
        if is_output:
            self.out_stamps.append(st)
        return ins

    def finish(self):
        need = {}
        for s, v in self.out_stamps:
            key = id(s)
            self.semobj[key] = s
            need[key] = max(need.get(key, 0), v)
        for key, v in need.items():
            self.eng["sp"].wait_ge(self.semobj[key], v)


class WStream:
    def __init__(self, k, name, ch_elems, n_stage, n_bf, cast_engines=("pool",)):
        self.k = k
        self.ch = ch_elems
        self.stage = [k.sbuf("%s_st%d" % (name, i), [128, ch_elems], F32) for i in range(n_stage)]
        self.bf = [k.sbuf("%s_bf%d" % (name, i), [128, ch_elems], BF16) for i in range(n_bf)]
        self.n = 0
        self.cast_engines = cast_engines
        self.queue = []
        self.ready = []

    def push(self, dram_ap):
        self.queue.append(dram_ap)

    def _fetch(self, item):
        k = self.k
        dram_ap, n = item
        st = self.stage[self.n % len(self.stage)]
        bf = self.bf[self.n % len(self.bf)]
        ce = self.cast_engines[self.n % len(self.cast_engines)]
        self.n += 1
        k.dma("sp", out=st.t[:, 0:n], in_=dram_ap, writes=[st], owner=st)
        if ce == "act":
            k.op("act", lambda e: e.activation(out=bf.t[:, 0:n], in_=st.t[:, 0:n], func=AF.Copy),
                 reads=[st], writes=[bf])
        else:
            k.op(ce, lambda e: e.tensor_copy(out=bf.t[:, 0:n], in_=st.t[:, 0:n]), reads=[st], writes=[bf])
        return bf

    def prefetch(self, depth):
        while self.queue and len(self.ready) < depth:
            self.ready.append(self._fetch(self.queue.pop(0)))

    def take(self, n):
        self.prefetch(n)
        out = [self.ready.pop(0) for _ in range(n)]
        self.prefetch(len(self.bf) - n)
        return out


class Cfg:
    def __init__(self, D=D_MODEL, NT=2048, F=FFN_HIDDEN, fgroups=(16, 16, 12), NKH=None):
        self.D = D
        self.NT = NT
        self.F = F
        self.KC = D // 128
        self.NTT = NT // 128
        self.NTG = NT // 512
        self.fgroups = tuple(fgroups)
        assert sum(fgroups) * 128 == F
        self.NKH = NKH if NKH is not None else D // HD
        self.NVH = 2 * self.NKH
        self.CH = max(self.KC * 128, 2048)


class Env:
    def __init__(self, k, cfg, consts, alloc_act=True, ws_slots=6):
        self.k = k
        self.cfg = cfg
        c = cfg
        nc = k.nc
        self.xT = [k.sbuf("xT%d" % g, [128, c.KC, 512], BF16) for g in range(c.NTG)]
        self.xhT = k.sbuf("xhT", [128, c.KC, 4], BF16)
        self.AK = max(c.KC, max(c.fgroups), min(16, c.NVH))
        self.ws = WStream(k, "w", c.CH, 2, ws_slots)
        self.lnviews = {}
        self.rings = {}
        self.pA = [k.psum("pA%d" % i, [128, 512], F32) for i in range(6)]
        self.pT = [k.psum("pT%d" % i, [128, 1024], BF16) for i in range(2)]
        self.pi = 0
        self.pti = 0
        self.ident_bf = k.sbuf("ident_bf", [128, 128], BF16)
        self.ident_f = k.sbuf("ident_f", [128, 128], F32)
        k.dma("sp", out=self.ident_f.t[:], in_=consts["ident"], writes=[self.ident_f], owner=self.ident_f)
        k.op("dve", lambda e: e.tensor_copy(out=self.ident_bf.t[:], in_=self.ident_f.t[:]),
             reads=[self.ident_f], writes=[self.ident_bf])
        if alloc_act:
            self.alloc_act()

    def alloc_act(self):
        k, c = self.k, self.cfg
        self.actT = [k.sbuf("actT%d" % g, [128, self.AK, 512], BF16) for g in range(c.NTG)]
        fl = [a.t[:].rearrange("p a b -> p (a b)") for a in self.actT]
        nb = self.AK * 512
        self.lnviews = {}

        def carve(g, off_bf, n_bf, dt, name):
            ap = fl[g][:, off_bf:off_bf + n_bf]
            if dt == F32:
                ap = ap.bitcast(F32)
            return self.actT[g].view(name, ap)

        if 2 * c.D * 2 <= nb and c.NTG >= 3:
            self.lnviews["lnr"] = [carve(0, 0, 2 * c.D, F32, "lnr0"), carve(0, 2 * c.D, 2 * c.D, F32, "lnr1")]
            self.lnviews["lnh"] = [carve(1, 0, 2 * c.D, F32, "lnh0")]
            if c.NTG >= 4:
                self.lnviews["lnh"] += [carve(3, 0, 2 * c.D, F32, "lnh1"), carve(3, 2 * c.D, 2 * c.D, F32, "lnh2")]
            self.lnviews["xb"] = [carve(1, 2 * c.D, c.D, BF16, "xb0"), carve(1, 3 * c.D, c.D, BF16, "xb1")]
            self.lnviews["gain"] = [carve(2, 0, 2 * c.D, F32, "gain")]
            self.lnviews["bias"] = [carve(2, 2 * c.D, 2 * c.D, F32, "bias")]

    def phase(self):
        env = self

        class _P:
            def __enter__(p):
                p.before = set(env.rings)
                p.es2 = ExitStack()
                p.es2.__enter__()
                p.prev = env.k.es_cur
                env.k.es_cur = p.es2
                return p

            def __exit__(p, *a):
                if a[0] is None:
                    env.k.barrier()
                env.k.es_cur = p.prev
                p.es2.__exit__(*a)
                for n in list(env.rings):
                    if n not in p.before:
                        del env.rings[n]
                return False

        return _P()

    def pnext(self):
        b = self.pA[self.pi % len(self.pA)]
        self.pi += 1
        return b

    def ptnext(self):
        b = self.pT[self.pti % len(self.pT)]
        self.pti += 1
        return b

    def ring(self, name, shape, dt, n):
        if name in self.lnviews and name not in self.rings:
            self.rings[name] = [self.lnviews[name], 0]
        if name not in self.rings:
            self.rings[name] = [[self.k.sbuf("%s%d" % (name, i), shape, dt) for i in range(n)], 0]
        r = self.rings[name]
        b = r[0][r[1] % len(r[0])]
        r[1] += 1
        return b


def f32v(b):
    return b.t[:].bitcast(F32)


def mm(env, ps_ap, lhsT, rhs, start, stop, reads, writes, inc=None):
    env.k.op("pe", lambda e: e.matmul(ps_ap, lhsT=lhsT, rhs=rhs, start=start, stop=stop),
             reads=reads, writes=writes, inc=(stop if inc is None else inc))


def emit_xT_tile(env, xf, tt):
    k, c = env.k, env.cfg
    xb = env.ring("xb", [128, c.D], BF16, 2)
    k.op("act", lambda e: e.activation(out=xb.t[:], in_=xf.t[:], func=AF.Copy), reads=[xf], writes=[xb])
    g, o = tt // 4, (tt % 4) * 128
    for k0 in range(0, c.KC, 8):
        n = min(8, c.KC - k0)
        pt = env.ptnext()
        for i in range(n):
            kc = k0 + i
            k.op("pe", lambda e: e.transpose(pt.t[:, i * 128:(i + 1) * 128], xb.t[:, kc * 128:(kc + 1) * 128],
                                             env.ident_bf.t[:]),
                 reads=[xb, env.ident_bf], writes=[pt], inc=(i == n - 1))
        k.op("dve", lambda e: e.tensor_copy(out=env.xT[g].t[:, k0:k0 + n, o:o + 128],
                                            in_=pt.t[:, 0:n * 128].rearrange("p (a b) -> p a b", b=128)),
             reads=[pt], writes=[env.xT[g]])


def emit_halo(env, xh_dram):
    k, c = env.k, env.cfg
    hf = k.sbuf("halo_f", [4, c.D], F32)
    hb = k.sbuf("halo_b", [4, c.D], BF16)
    k.dma("sp", out=hf.t[:], in_=xh_dram, writes=[hf], owner=hf)
    k.op("act", lambda e: e.activation(out=hb.t[:], in_=hf.t[:], func=AF.Copy), reads=[hf], writes=[hb])
    pt = env.ptnext()
    for kc in range(c.KC):
        k.op("pe", lambda e: e.transpose(pt.t[:, kc * 4:(kc + 1) * 4], hb.t[:, kc * 128:(kc + 1) * 128],
                                         env.ident_bf.t[0:4, 0:4]),
             reads=[hb, env.ident_bf], writes=[pt], inc=(kc == c.KC - 1))
    k.op("dve", lambda e: e.tensor_copy(out=env.xhT.t[:],
                                        in_=pt.t[:, 0:c.KC * 4].rearrange("p (a b) -> p a b", b=4)),
         reads=[pt], writes=[env.xhT])


def load_x_to_xT(env, x_dram):
    k, c = env.k, env.cfg
    for tt in range(c.NTT):
        xf = env.ring("lnr", [128, c.D], F32, 2)
        k.dma("sp", out=xf.t[:], in_=x_dram[tt * 128:(tt + 1) * 128, :], writes=[xf], owner=xf)
        emit_xT_tile(env, xf, tt)


def type_a(env, units, n_per_unit, epilogue, halo_fn=None):
    k, c = env.k, env.cfg
    ws = env.ws
    for u in units:
        for ap in u:
            ws.push((ap, c.KC * 128))
    for ui, u in enumerate(units):
        wb = ws.take(n_per_unit)
        if halo_fn is not None:
            halo_fn(ui, wb)
        for tg in range(c.NTG):
            ps = [env.pnext() for _ in range(n_per_unit)]
            for ci in range(n_per_unit):
                for kc in range(c.KC):
                    mm(env, ps[ci].t[:, :], wb[ci].t[:, kc * 128:(kc + 1) * 128], env.xT[tg].t[:, kc, :],
                       kc == 0, kc == c.KC - 1, [wb[ci], env.xT[tg]], [ps[ci]])
            epilogue(ui, tg, ps)


def type_b(env, KCb, slab_aps, hbufs, h_dram):
    k, c = env.k, env.cfg
    ws = env.ws
    npc = KCb // 4
    for s in range(c.D // 512):
        for ap in slab_aps[s]:
            ws.push((ap, 2048))
    for s in range(c.D // 512):
        pcs = ws.take(npc)
        for tt in range(c.NTT):
            g, o = tt // 4, (tt % 4) * 128
            ps = env.pnext()
            for kc in range(KCb):
                pc = pcs[kc // 4]
                mm(env, ps.t[:, :], env.actT[g].t[:, kc, o:o + 128], pc.t[:, (kc % 4) * 512:(kc % 4 + 1) * 512],
                   kc == 0, kc == KCb - 1, [env.actT[g], pc], [ps])
            ob = env.ring("ob", [128, 512], F32, 3)
            k.op("act", lambda e: e.activation(out=ob.t[:], in_=ps.t[:, :], func=AF.Copy), reads=[ps], writes=[ob])
            k.dma("act", out=h_dram[tt * 128:(tt + 1) * 128, s * 512:(s + 1) * 512], in_=ob.t[:],
                  reads=[ob], writes=[hbufs[tt]], owner=ob)


def ln_phase(env, x_dram, xbufs, parts, gain_dram, bias_dram, out_dram, obufs, make_xT, final):
    k, c = env.k, env.cfg
    gb = env.ring("gain", [128, c.D], F32, 1)
    bb = env.ring("bias", [128, c.D], F32, 1)
    k.dma("sp", out=gb.t[:], in_=gain_dram.partition_broadcast(128), writes=[gb], owner=gb)
    k.dma("sp", out=bb.t[:], in_=bias_dram.partition_broadcast(128), writes=[bb], owner=bb)
    nch = c.D // 512
    for tt in range(c.NTT):
        r = env.ring("lnr", [128, c.D], F32, 2)
        rows = slice(tt * 128, (tt + 1) * 128)
        k.dma("sp", out=r.t[:], in_=x_dram[rows, :], reads=[xbufs[tt]] if xbufs else [], writes=[r], owner=r)
        for pi, (hd, hb) in enumerate(parts):
            hin = env.ring("lnh", [128, c.D], F32, 2)
            k.dma("sp", out=hin.t[:], in_=hd[rows, :], reads=[hb[tt]], writes=[hin], owner=hin)
            if pi == 0:
                k.op("act", lambda e: e.activation(out=r.t[:], in_=r.t[:], func=AF.Copy, scale=float(ALPHA)),
                     reads=[r], writes=[r])
            k.op("pool", lambda e: e.tensor_tensor(out=r.t[:], in0=r.t[:], in1=hin.t[:], op=ALU.add),
                 reads=[r, hin], writes=[r])
        st = env.ring("lnst", [128, nch * 6], F32, 2)
        for i in range(nch):
            k.op("dve", lambda e: e.bn_stats(out=st.t[:, i * 6:(i + 1) * 6], in_=r.t[:, i * 512:(i + 1) * 512]),
                 reads=[r], writes=[st])
        mv = env.ring("lnmv", [128, 4], F32, 2)
        k.op("dve", lambda e: e.bn_aggr(out=mv.t[:, 0:2], in_=st.t[:]), reads=[st], writes=[mv])
        k.op("act", lambda e: e.activation(out=mv.t[:, 2:3], in_=mv.t[:, 1:2], func=AF.Sqrt, bias=float(LN_EPS)),
             reads=[mv], writes=[mv])
        k.op("dve", lambda e: e.reciprocal(out=mv.t[:, 2:3], in_=mv.t[:, 2:3]), reads=[mv], writes=[mv])
        k.op("dve", lambda e: e.scalar_tensor_tensor(out=r.t[:], in0=r.t[:], scalar=mv.t[:, 0:1], in1=gb.t[:],
                                                     op0=ALU.subtract, op1=ALU.mult), reads=[r, mv, gb], writes=[r])
        k.op("dve", lambda e: e.scalar_tensor_tensor(out=r.t[:], in0=r.t[:], scalar=mv.t[:, 2:3], in1=bb.t[:],
                                                     op0=ALU.mult, op1=ALU.add), reads=[r, mv, bb], writes=[r])
        k.dma("act", out=out_dram[rows, :], in_=r.t[:], reads=[r], writes=[obufs[tt]], owner=r, is_output=final)
        if make_xT:
            emit_xT_tile(env, r, tt)


def tile_a(W, col_starts):
    K_ = W.shape[0]
    KC = K_ // 128
    Wr = W.reshape(KC, 128, W.shape[1])
    out = np.empty((len(col_starts), 128, KC * 128), np.float32)
    for i, c0 in enumerate(col_starts):
        out[i] = Wr[:, :, c0:c0 + 128].transpose(1, 0, 2).reshape(128, KC * 128)
    return out


def tile_b(W, r0, KCb):
    D = W.shape[1]
    Wr = W[r0:r0 + KCb * 128].reshape(KCb // 4, 4, 128, D // 512, 512)
    return np.ascontiguousarray(Wr.transpose(3, 0, 2, 1, 4)).reshape(D // 512, KCb // 4, 128, 2048)


def dram_in(nc, name, shape):
    return nc.dram_tensor(name, list(shape), F32, kind="ExternalInput").ap()


def ffn_and_ln(env, io, x_dram, xbufs, h_dram, hbufs, out_dram, obufs, final, make_xT):
    k, c = env.k, env.cfg
    hc0 = 0
    parts = []
    for gi, G in enumerate(c.fgroups):
        units = [[io["w_gu_t"][hc0 + j, 0], io["w_gu_t"][hc0 + j, 1]] for j in range(G)]

        def epi(ui, tg, ps):
            sg = env.ring("sg", [128, 512], F32, 2)
            k.op("act", lambda e: e.activation(out=sg.t[:], in_=ps[0].t[:, :], func=AF.Silu), reads=[ps[0]], writes=[sg])
            k.op("dve", lambda e: e.tensor_tensor(out=env.actT[tg].t[:, ui, :], in0=ps[1].t[:, :], in1=sg.t[:], op=ALU.mult),
                 reads=[ps[1], sg], writes=[env.actT[tg]])

        type_a(env, units, 2, epi)
        slabs = [[io["w_dn_t"][gi][s, pc] for pc in range(G // 4)] for s in range(c.D // 512)]
        type_b(env, G, slabs, hbufs[gi], h_dram[gi])
        parts.append((h_dram[gi], hbufs[gi]))
        hc0 += G
    ln_phase(env, x_dram, xbufs, parts, io["ln"][2, :], io["ln"][3, :], out_dram, obufs, make_xT, final)


def build_sc_program(cfg, stages=9):
    c = cfg
    nc = bass.Bass("TRN2", target_bir_lowering=False)
    io = {}
    io["x"] = dram_in(nc, "x", [c.NT, c.D])
    io["xh"] = dram_in(nc, "xh", [4, c.D])
    io["ident"] = dram_in(nc, "ident", [128, 128])
    io["w_in_t"] = dram_in(nc, "w_in_t", [c.KC, 3, 128, c.KC * 128])
    io["cw"] = dram_in(nc, "cw", [128, c.KC * 3])
    io["w_out_t"] = dram_in(nc, "w_out_t", [c.D // 512, c.KC // 4, 128, 2048])
    io["w_gu_t"] = dram_in(nc, "w_gu_t", [c.F // 128, 2, 128, c.KC * 128])
    io["w_dn_t"] = [dram_in(nc, "w_dn_t%d" % gi, [c.D // 512, G // 4, 128, 2048]) for gi, G in enumerate(c.fgroups)]
    io["ln"] = dram_in(nc, "ln", [4, c.D])
    xo = nc.dram_tensor("xo", [c.NT, c.D], F32, kind="ExternalOutput").ap()
    P = max(1, len(c.fgroups))
    h_dram = [nc.dram_tensor("h%d" % p, [c.NT, c.D], F32, kind="Internal").ap() for p in range(P)]
    x1 = nc.dram_tensor("x1", [c.NT, c.D], F32, kind="Internal").ap()
    with ExitStack() as es:
        k = K(nc, es)
        env = Env(k, c, io)
        hbufs = [[Buf("h%d_%d" % (p, tt)) for tt in range(c.NTT)] for p in range(P)]
        x1bufs = [Buf("x1_%d" % tt) for tt in range(c.NTT)]
        xobufs = [Buf("xo_%d" % tt) for tt in range(c.NTT)]
        cw = k.sbuf("cw_sb", [128, c.KC * 3], F32)
        k.dma("sp", out=cw.t[:], in_=io["cw"], writes=[cw], owner=cw)
        emit_halo(env, io["xh"])
        load_x_to_xT(env, io["x"])

        units = [[io["w_in_t"][j, 0], io["w_in_t"][j, 1], io["w_in_t"][j, 2]] for j in range(c.KC)]
        state = {}

        def halo_fn(j, wb):
            ps = env.pnext()
            for ci in (1, 2):
                for kc in range(c.KC):
                    mm(env, ps.t[:, (ci - 1) * 4:(ci - 1) * 4 + 4], wb[ci].t[:, kc * 128:(kc + 1) * 128],
                       env.xhT.t[:, kc, :], kc == 0, kc == c.KC - 1, [wb[ci], env.xhT], [ps],
                       inc=(kc == c.KC - 1))
            hh = env.ring("hh", [128, 4], F32, 2)
            k.op("act", lambda e: e.activation(out=hh.t[:], in_=ps.t[:, 4:8], func=AF.Copy), reads=[ps], writes=[hh])
            u = env.ring("u", [128, 2 + 512], F32, 3)
            k.op("dve", lambda e: e.tensor_tensor(out=u.t[:, 0:2], in0=ps.t[:, 2:4], in1=hh.t[:, 2:4], op=ALU.mult),
                 reads=[ps, hh], writes=[u])
            state["u"] = u

        def epi(j, tg, ps):
            hs = env.ring("hs", [128, 512], F32, 2)
            k.op("act", lambda e: e.activation(out=hs.t[:], in_=ps[2].t[:, :], func=AF.Copy), reads=[ps[2]], writes=[hs])
            if tg == 0:
                u = state["u"]
            else:
                u = env.ring("u", [128, 2 + 512], F32, 3)
                up = state["u"]
                k.op("dve", lambda e: e.tensor_copy(out=u.t[:, 0:2], in_=up.t[:, 512:514]), reads=[up], writes=[u])
                state["u"] = u
            k.op("dve", lambda e: e.tensor_tensor(out=u.t[:, 2:514], in0=ps[1].t[:, :], in1=hs.t[:], op=ALU.mult),
                 reads=[ps[1], hs], writes=[u])
            y = env.ring("y", [128, 512], F32, 2)
            k.op("act", lambda e: e.activation(out=y.t[:], in_=u.t[:, 0:512], func=AF.Copy, scale=cw.t[:, j * 3:j * 3 + 1]),
                 reads=[u, cw], writes=[y])
            k.op("dve", lambda e: e.scalar_tensor_tensor(out=y.t[:], in0=u.t[:, 1:513], scalar=cw.t[:, j * 3 + 1:j * 3 + 2],
                                                         in1=y.t[:], op0=ALU.mult, op1=ALU.add), reads=[u, cw, y], writes=[y])
            k.op("dve", lambda e: e.scalar_tensor_tensor(out=y.t[:], in0=u.t[:, 2:514], scalar=cw.t[:, j * 3 + 2:j * 3 + 3],
                                                         in1=y.t[:], op0=ALU.mult, op1=ALU.add), reads=[u, cw, y], writes=[y])
            k.op("dve", lambda e: e.tensor_tensor(out=env.actT[tg].t[:, j, :], in0=ps[0].t[:, :], in1=y.t[:], op=ALU.mult),
                 reads=[ps[0], y], writes=[env.actT[tg]])

        type_a(env, units, 3, epi, halo_fn)
        slabs = [[io["w_out_t"][s, pc] for pc in range(c.KC // 4)] for s in range(c.D // 512)]
        if stages >= 2:
            type_b(env, c.KC, slabs, hbufs[0], h_dram[0])
        if stages >= 3:
            ln_phase(env, io["x"], None, [(h_dram[0], hbufs[0])], io["ln"][0, :], io["ln"][1, :], x1, x1bufs, True, False)
        if stages >= 4:
            ffn_and_ln(env, io, x1, x1bufs, h_dram, hbufs, xo, xobufs, True, False)
        k.finish()
    return nc


def sc_inputs(cfg, w_in, conv_w, w_out, w_gu, w_dn, ln4):
    c = cfg
    D, F = c.D, c.F
    d = {}
    d["ident"] = np.eye(128, dtype=np.float32)
    cols = []
    for j in range(c.KC):
        cols += [j * 128, D + j * 128, 2 * D + j * 128]
    d["w_in_t"] = tile_a(w_in, cols).reshape(c.KC, 3, 128, c.KC * 128)
    d["cw"] = np.ascontiguousarray(conv_w.reshape(3, c.KC, 128).transpose(2, 1, 0)).reshape(128, c.KC * 3)
    d["w_out_t"] = tile_b(w_out, 0, c.KC)
    d.update(ffn_inputs(cfg, w_gu, w_dn))
    d["ln"] = np.ascontiguousarray(ln4, dtype=np.float32)
    return d


def ffn_inputs(cfg, w_gu, w_dn):
    c = cfg
    d = {}
    cols = []
    for j in range(c.F // 128):
        cols += [j * 128, c.F + j * 128]
    d["w_gu_t"] = tile_a(w_gu, cols).reshape(c.F // 128, 2, 128, c.KC * 128)
    r0 = 0
    for gi, G in enumerate(c.fgroups):
        d["w_dn_t%d" % gi] = tile_b(w_dn, r0, G)
        r0 += G * 128
    return d


def gdn_consts():
    i = np.arange(128)
    same = (i[:, None] // 64) == (i[None, :] // 64)
    c = np.zeros((8, 128, 128), np.float32)
    c[0] = ((i[:, None] <= i[None, :]) & same)
    c[1] = (i[:, None] > i[None, :])
    c[2] = ((i[:, None] > i[None, :]) & same)
    c[3] = ((i[None, :] >= i[:, None]) & same)
    c[4] = same
    c[5] = 1.0
    c[6] = (i[:, None] < 64) * np.ones((1, 128))
    c[7] = (i[:, None] >= 64) * np.ones((1, 128))
    return c


def roundrobin(gens):
    gens = list(gens)
    while gens:
        nxt = []
        for g in gens:
            try:
                next(g)
                nxt.append(g)
            except StopIteration:
                pass
        gens = nxt


LIMIT = [1 << 60]
USE_PT_VIEWS = False


def build_gdn_program(cfg, mode, dbg=99):
    c = cfg
    NVH, NKH, KC = c.NVH, c.NKH, c.KC
    NV = 256 if mode == "a" else 128
    nch = 4 if mode == "a" else 6
    vgroups = [min(16, NVH - g0) for g0 in range(0, NVH, 16)]
    nc = bass.Bass("TRN2", target_bir_lowering=False)
    io = {}
    io["x"] = dram_in(nc, "x", [c.NT, c.D])
    io["xh"] = dram_in(nc, "xh", [4, c.D])
    io["ident"] = dram_in(nc, "ident", [128, 128])
    io["consts"] = dram_in(nc, "consts", [8, 128, 128])
    io["w_t"] = dram_in(nc, "w_t", [NKH, 6, 128, KC * 128])
    io["w_ba"] = dram_in(nc, "w_ba", [128, KC * 2 * NVH])
    io["cwg"] = dram_in(nc, "cwg", [128, NKH * 16])
    io["alog"] = dram_in(nc, "alog", [1, NVH])
    io["dtb"] = dram_in(nc, "dtb", [1, NVH])
    if mode == "a":
        st_out = nc.dram_tensor("st", [NVH, 128, 256], F32, kind="ExternalOutput").ap()
        oq_out = nc.dram_tensor("oq", [NVH, c.NT, 256], F32, kind="ExternalOutput").ap()
    else:
        io["normw"] = dram_in(nc, "normw", [1, 128])
        io["sin_pt"] = dram_in(nc, "sin_pt", [3, NVH, 128, 128])
        io["sin_s"] = dram_in(nc, "sin_s", [3, NVH, 128, 128])
        io["w_o_t"] = [dram_in(nc, "w_o_t%d" % gi, [c.D // 512, G // 4, 128, 2048]) for gi, G in enumerate(vgroups)]
        io["w_gu_t"] = dram_in(nc, "w_gu_t", [c.F // 128, 2, 128, KC * 128])
        io["w_dn_t"] = [dram_in(nc, "w_dn_t%d" % gi, [c.D // 512, G // 4, 128, 2048]) for gi, G in enumerate(c.fgroups)]
        io["ln"] = dram_in(nc, "ln", [4, c.D])
        xo = nc.dram_tensor("xo", [c.NT, c.D], F32, kind="ExternalOutput").ap()
        P = max(len(vgroups), len(c.fgroups))
        h_dram = [nc.dram_tensor("h%d" % p, [c.NT, c.D], F32, kind="Internal").ap() for p in range(P)]
        x1 = nc.dram_tensor("x1", [c.NT, c.D], F32, kind="Internal").ap()
        onT_d = nc.dram_tensor("onT_d", [NVH, 128, c.NT], BF16, kind="Internal").ap()
    with ExitStack() as es:
        k = K(nc, es)
        k.limit = LIMIT[0]
        env = Env(k, c, io, alloc_act=False, ws_slots=(5 if mode == "a" else 7))
        with env.phase():
            emit_halo(env, io["xh"])
            load_x_to_xT(env, io["x"])
        onbufs = [[Buf("on_%d_%d" % (h, tg)) for tg in range(c.NTG)] for h in range(NVH)]
        if dbg <= 1:
            k.finish()
            return nc

        with env.phase():
            def f32t(name, shape=(128, 128)):
                return k.sbuf(name, list(shape), F32)

            CN = []
            for i in range(8):
                t = f32t("cst%d" % i)
                k.dma("sp", out=t.t[:], in_=io["consts"][i], writes=[t], owner=t)
                CN.append(t)
            TRIBD, UU, STRICT, CAUSALT, BLK, ONES, CH0, CH1 = CN
            UUr = k.sbuf("UUr", [128, 128], F32R)
            k.op("dve", lambda e: e.tensor_copy(out=UUr.t[:], in_=UU.t[:]), reads=[UU], writes=[UUr])
            cwg = f32t("cwg_sb", (128, NKH * 16))
            k.dma("sp", out=cwg.t[:], in_=io["cwg"], writes=[cwg], owner=cwg)
            wba_f = f32t("wba_f", (128, KC * 2 * NVH))
            wba = k.sbuf("wba_b", [128, KC, 2 * NVH], BF16)
            k.dma("sp", out=wba_f.t[:], in_=io["w_ba"], writes=[wba_f], owner=wba_f)
            k.op("dve", lambda e: e.tensor_copy(out=wba.t[:].rearrange("p a b -> p (a b)"), in_=wba_f.t[:]),
                 reads=[wba_f], writes=[wba])
            alog = f32t("alog_sb", (128, NVH))
            dtb = f32t("dtb_sb", (128, NVH))
            k.dma("sp", out=alog.t[:], in_=io["alog"][0, :].partition_broadcast(128), writes=[alog], owner=alog)
            k.dma("sp", out=dtb.t[:], in_=io["dtb"][0, :].partition_broadcast(128), writes=[dtb], owner=dtb)
            negA = f32t("negA", (128, NVH))
            k.op("act", lambda e: e.activation(out=negA.t[:], in_=alog.t[:], func=AF.Exp), reads=[alog], writes=[negA])
            k.op("dve", lambda e: e.tensor_scalar(out=negA.t[:], in0=negA.t[:], scalar1=-1.0, scalar2=None, op0=ALU.mult),
                 reads=[negA], writes=[negA])
            if mode == "b":
                nwbc = f32t("nwbc")
                k.dma("sp", out=nwbc.t[:], in_=io["normw"][0, :].partition_broadcast(128), writes=[nwbc], owner=nwbc)
            shp = (128, c.NTT, NVH)
            BETA, AX, GRAW, GC, EG = [f32t(n, shp) for n in ("BETA", "AX", "GRAW", "GC", "EG")]
            EKT, BEG = AX, GC
            CD = f32t("CD", (128, c.NTT, 2 * NVH))

            for tt in range(c.NTT):
                g, o = tt // 4, (tt % 4) * 128
                ps = env.pnext()
                for kc in range(KC):
                    mm(env, ps.t[:, 0:2 * NVH], env.xT[g].t[:, kc, o:o + 128], wba.t[:, kc, :], kc == 0, kc == KC - 1,
                       [env.xT[g], wba], [ps])
                k.op("act", lambda e: e.activation(out=BETA.t[:, tt, :], in_=ps.t[:, 0:NVH], func=AF.Sigmoid),
                     reads=[ps], writes=[BETA])
                k.op("dve", lambda e: e.tensor_tensor(out=AX.t[:, tt, :], in0=ps.t[:, NVH:2 * NVH], in1=dtb.t[:], op=ALU.add),
                     reads=[ps, dtb], writes=[AX])
            k.op("act", lambda e: e.activation(out=AX.t[:], in_=AX.t[:], func=AF.Exp), reads=[AX], writes=[AX])
            k.op("act", lambda e: e.activation(out=AX.t[:], in_=AX.t[:], func=AF.Ln, bias=1.0), reads=[AX], writes=[AX])
            for tt in range(c.NTT):
                k.op("dve", lambda e: e.tensor_tensor(out=GRAW.t[:, tt, :], in0=AX.t[:, tt, :], in1=negA.t[:], op=ALU.mult),
                     reads=[AX, negA], writes=[GRAW])
            for tt in range(c.NTT):
                gm = env.ring("gm", [128, 2 * NVH], F32, 2)
                k.op("dve", lambda e: e.tensor_scalar(out=gm.t[:, 0:NVH], in0=GRAW.t[:, tt, :], scalar1=CH0.t[:, 0:1],
                                                      scalar2=None, op0=ALU.mult), reads=[GRAW, CH0], writes=[gm])
                k.op("dve", lambda e: e.tensor_scalar(out=gm.t[:, NVH:2 * NVH], in0=GRAW.t[:, tt, :], scalar1=CH1.t[:, 0:1],
                                                      scalar2=None, op0=ALU.mult), reads=[GRAW, CH1], writes=[gm])
                ps = env.pnext()
                mm(env, ps.t[:, 0:NVH], TRIBD.t[:], GRAW.t[:, tt, :], True, True, [TRIBD, GRAW], [ps])
                mm(env, ps.t[:, 64:64 + NVH], BLK.t[:], GRAW.t[:, tt, :], True, True, [BLK, GRAW], [ps])
                mm(env, ps.t[:, 128:128 + 2 * NVH], ONES.t[:], gm.t[:], True, True, [ONES, gm], [ps])
                k.op("dve", lambda e: e.tensor_copy(out=GC.t[:, tt, :], in_=ps.t[:, 0:NVH]), reads=[ps], writes=[GC])
                k.op("act", lambda e: e.activation(out=EG.t[:, tt, :], in_=ps.t[:, 0:NVH], func=AF.Exp), reads=[ps], writes=[EG])
                k.op("dve", lambda e: e.tensor_tensor(out=EKT.t[:, tt, :], in0=ps.t[:, 64:64 + NVH], in1=GC.t[:, tt, :],
                                                      op=ALU.subtract), reads=[ps, GC], writes=[EKT])
                k.op("act", lambda e: e.activation(out=EKT.t[:, tt, :], in_=EKT.t[:, tt, :], func=AF.Exp),
                     reads=[EKT], writes=[EKT])
                k.op("act", lambda e: e.activation(out=CD.t[:, tt, :], in_=ps.t[:, 128:128 + 2 * NVH], func=AF.Exp),
                     reads=[ps], writes=[CD])
                k.op("dve", lambda e: e.tensor_tensor(out=BEG.t[:, tt, :], in0=BETA.t[:, tt, :], in1=EG.t[:, tt, :], op=ALU.mult),
                     reads=[BETA, EG], writes=[BEG])

            if dbg <= 2:
                k.finish()
                return nc
            if mode == "a" and USE_PT_VIEWS:
                for pt_b in env.pT:
                    v_ = pt_b.view(pt_b.name + "_f32", pt_b.t[:].bitcast(F32))
                    v_.is_psum = True
                    env.pA.append(v_)
            Sf = [f32t("Sf%d" % i, (128, NV)) for i in range(2)]
            Sb = [k.sbuf("Sb%d" % i, [128, NV], BF16) for i in range(2)]
            ubuf = [[f32t("u%d_%d" % (i, j), (128, NV)) for j in range(3)] for i in range(2)]
            if NV == 256:
                for i in range(2):
                    for j in range(3):
                        k.op("pool", lambda e: e.memset(ubuf[i][j].t[:], 0.0), writes=[ubuf[i][j]])
            ucnt = [0, 0]
            carry = [f32t("carry%d" % ci, (128, 4)) for ci in range(4)]
            ws = env.ws
            for kh in range(NKH):
                for ci in range(nch):
                    ws.push((io["w_t"][kh, ci], KC * 128))

            for kh in range(NKH):
                wb = ws.take(nch)
                for hh in range(2):
                    h = 2 * kh + hh
                    if mode == "a":
                        k.op("pool", lambda e: e.memset(Sf[hh].t[:, 0:128], 0.0), writes=[Sf[hh]])
                        k.op("pool", lambda e: e.tensor_copy(out=Sf[hh].t[:, 128:256], in_=env.ident_f.t[:]),
                             reads=[env.ident_f], writes=[Sf[hh]])
                    else:
                        k.dma("sp", out=Sf[hh].t[:], in_=io["sin_s"][0, h], writes=[Sf[hh]], owner=Sf[hh])
                        for j in (1, 2):
                            pt_ = env.ring("foldp", [128, 128], F32, 2)
                            sl_ = env.ring("folds", [128, 128], F32, 2)
                            k.dma("sp", out=pt_.t[:], in_=io["sin_pt"][j, h], writes=[pt_], owner=pt_)
                            k.dma("sp", out=sl_.t[:], in_=io["sin_s"][j, h], writes=[sl_], owner=sl_)
                            ps = env.pnext()
                            mm(env, ps.t[:, 0:128], pt_.t[:], Sf[hh].t[:], True, True, [pt_, Sf[hh]], [ps])
                            k.op("dve", lambda e: e.tensor_tensor(out=Sf[hh].t[:], in0=ps.t[:, 0:128], in1=sl_.t[:], op=ALU.add),
                                 reads=[ps, sl_], writes=[Sf[hh]])
                    k.op("act", lambda e: e.activation(out=Sb[hh].t[:], in_=Sf[hh].t[:], func=AF.Copy),
                         reads=[Sf[hh]], writes=[Sb[hh]])
                psh = env.pnext()
                for ci in range(4):
                    for kc in range(KC):
                        mm(env, psh.t[:, ci * 4:ci * 4 + 4], wb[ci].t[:, kc * 128:(kc + 1) * 128], env.xhT.t[:, kc, :],
                           kc == 0, kc == KC - 1, [wb[ci], env.xhT], [psh])
                for ci in range(4):
                    k.op("dve", lambda e: e.tensor_copy(out=carry[ci].t[:, 0:3], in_=psh.t[:, ci * 4 + 1:ci * 4 + 4]),
                         reads=[psh], writes=[carry[ci]])
                TG = {}

                def proj(tg):
                    knT = env.ring("knT", [128, 512], F32R, 2)
                    qnT = env.ring("qnT", [128, 512], F32R, 2)
                    qb = env.ring("qb", [128, 512], BF16, 2)
                    vT = [env.ring("vT%d" % i, [128, 512], F32, 2) for i in range(2)]
                    sz = [env.ring("sz%d" % i, [128, 512], BF16, 2) for i in range(2)] if mode == "b" else None
                    TG[tg] = dict(knT=knT, qnT=qnT, qb=qb, vT=vT, sz=sz)
                    for ci in range(nch):
                        ps = env.pnext()
                        for kc in range(KC):
                            mm(env, ps.t[:, :], wb[ci].t[:, kc * 128:(kc + 1) * 128], env.xT[tg].t[:, kc, :],
                               kc == 0, kc == KC - 1, [wb[ci], env.xT[tg]], [ps])
                        if ci >= 4:
                            k.op("act", lambda e: e.activation(out=sz[ci - 4].t[:], in_=ps.t[:, :], func=AF.Silu),
                                 reads=[ps], writes=[sz[ci - 4]])
                            continue
                        pre = env.ring("pre", [128, 3 + 512], F32, 2)
                        k.op("dve", lambda e: e.tensor_copy(out=pre.t[:, 0:3], in_=carry[ci].t[:, 0:3]), reads=[carry[ci]], writes=[pre])
                        k.op("act", lambda e: e.activation(out=pre.t[:, 3:515], in_=ps.t[:, :], func=AF.Copy), reads=[ps], writes=[pre])
                        k.op("dve", lambda e: e.tensor_copy(out=carry[ci].t[:, 0:3], in_=pre.t[:, 512:515]), reads=[pre], writes=[carry[ci]])
                        cb = (kh * 4 + ci) * 4
                        ca = env.ring("cacc", [128, 512], F32, 2)
                        k.op("act", lambda e: e.activation(out=ca.t[:], in_=pre.t[:, 0:512], func=AF.Copy, scale=cwg.t[:, cb:cb + 1]),
                             reads=[pre, cwg], writes=[ca])
                        for tap in (1, 2, 3):
                            k.op("dve", lambda e: e.scalar_tensor_tensor(out=ca.t[:], in0=pre.t[:, tap:tap + 512],
                                                                         scalar=cwg.t[:, cb + tap:cb + tap + 1], in1=ca.t[:],
                                                                         op0=ALU.mult, op1=ALU.add), reads=[pre, cwg, ca], writes=[ca])
                        if ci >= 2:
                            k.op("act", lambda e: e.activation(out=vT[ci - 2].t[:], in_=ca.t[:], func=AF.Silu),
                                 reads=[ca], writes=[vT[ci - 2]])
                            continue
                        sl = env.ring("sl", [128, 512], F32, 1)
                        k.op("act", lambda e: e.activation(out=sl.t[:], in_=ca.t[:], func=AF.Silu), reads=[ca], writes=[sl])
                        k.op("pool", lambda e: e.tensor_tensor(out=ca.t[:], in0=sl.t[:], in1=sl.t[:], op=ALU.mult), reads=[sl], writes=[ca])
                        ps2 = env.pnext()
                        mm(env, ps2.t[:, :], ONES.t[:], ca.t[:], True, True, [ONES, ca], [ps2])
                        k.op("act", lambda e: e.activation(out=ca.t[:], in_=ps2.t[:, :], func=AF.Sqrt, bias=float(RMS_EPS)),
                             reads=[ps2], writes=[ca])
                        k.op("dve", lambda e: e.reciprocal(out=ca.t[:], in_=ca.t[:]), reads=[ca], writes=[ca])
                        if ci == 1:
                            k.op("dve", lambda e: e.tensor_tensor(out=knT.t[:], in0=sl.t[:], in1=ca.t[:], op=ALU.mult),
                                 reads=[sl, ca], writes=[knT])
                        else:
                            k.op("dve", lambda e: e.scalar_tensor_tensor(out=qnT.t[:], in0=sl.t[:], scalar=float(HD ** -0.5),
                                                                         in1=ca.t[:], op0=ALU.mult, op1=ALU.mult),
                                 reads=[sl, ca], writes=[qnT])
                            k.op("pool", lambda e: e.tensor_copy(out=qb.t[:], in_=f32v(qnT)), reads=[qnT], writes=[qb])

                tile = {}

                def prep(tt):
                    cols = slice((tt % 4) * 128, (tt % 4 + 1) * 128)
                    knT, qnT, vT = TG[tt // 4]["knT"], TG[tt // 4]["qnT"], TG[tt // 4]["vT"]
                    T = {}
                    psk = env.pnext()
                    k.op("pe", lambda e: e.transpose(psk.t[:, 0:128], f32v(knT)[:, cols], env.ident_f.t[:]),
                         reads=[knT, env.ident_f], writes=[psk])
                    psv = env.pnext()
                    for hh in range(2):
                        k.op("pe", lambda e: e.transpose(psv.t[:, hh * 128:(hh + 1) * 128], vT[hh].t[:, cols], env.ident_f.t[:]),
                             reads=[vT[hh], env.ident_f], writes=[psv])
                    T["kbg"], T["ktl"], T["vb"] = [], [], []
                    for hh in range(2):
                        h = 2 * kh + hh
                        kbg = env.ring("kbg%d" % hh, [128, 128], F32R, 3)
                        ktl = env.ring("ktl%d" % hh, [128, 128], BF16, 3)
                        vb = env.ring("vb%d" % hh, [128, 128], F32R, 3)
                        k.op("act", lambda e: e.activation(out=kbg.t[:], in_=psk.t[:, 0:128], func=AF.Copy, scale=BEG.t[:, tt, h:h + 1]),
                             reads=[psk, BEG], writes=[kbg])
                        k.op("dve", lambda e: e.tensor_scalar(out=ktl.t[:], in0=psk.t[:, 0:128], scalar1=EKT.t[:, tt, h:h + 1],
                                                              scalar2=None, op0=ALU.mult), reads=[psk, EKT], writes=[ktl])
                        k.op("dve", lambda e: e.tensor_scalar(out=vb.t[:], in0=psv.t[:, hh * 128:(hh + 1) * 128],
                                                              scalar1=BETA.t[:, tt, h:h + 1], scalar2=None, op0=ALU.mult),
                             reads=[psv, BETA], writes=[vb])
                        T["kbg"].append(kbg); T["ktl"].append(ktl); T["vb"].append(vb)
                    yield
                    pskk = env.pnext()
                    mm(env, pskk.t[:, 0:128], knT.t[:, cols], knT.t[:, cols], True, True, [knT], [pskk])
                    mm(env, pskk.t[:, 128:256], knT.t[:, cols], qnT.t[:, cols], True, True, [knT, qnT], [pskk])
                    KKs = env.ring("KKs", [128, 128], F32, 3)
                    QKm = env.ring("QKm", [128, 128], F32, 3)
                    k.op("dve", lambda e: e.tensor_tensor(out=KKs.t[:], in0=pskk.t[:, 0:128], in1=STRICT.t[:], op=ALU.mult),
                         reads=[pskk, STRICT], writes=[KKs])
                    k.op("dve", lambda e: e.tensor_tensor(out=QKm.t[:], in0=pskk.t[:, 128:256], in1=CAUSALT.t[:], op=ALU.mult),
                         reads=[pskk, CAUSALT], writes=[QKm])
                    T["KKs"], T["QKm"] = KKs, QKm
                    T["u"], T["wT"], T["aqk"] = [None, None], [None, None], [None, None]
                    tile[tt] = T
                    yield

                def solve(tt, hh):
                    T = tile[tt]
                    h = 2 * kh + hh
                    G = env.ring("G%d_%d" % (hh, tt % 2), [128, 128], F32R, 1)
                    k.op("act", lambda e: e.activation(out=G.t[:], in_=TRIBD.t[:], func=AF.Copy, scale=GRAW.t[:, tt, h:h + 1]),
                         reads=[TRIBD, GRAW], writes=[G])
                    yield
                    ps = env.pnext()
                    mm(env, ps.t[:, 0:128], G.t[:], UUr.t[:], True, True, [G, UUr], [ps])
                    mm(env, ps.t[:, 128:256], UUr.t[:], G.t[:], True, True, [G, UUr], [ps])
                    Dec = env.ring("Dec%d_%d" % (hh, tt % 2), [128, 256], F32, 1)
                    k.op("act", lambda e: e.activation(out=Dec.t[:], in_=ps.t[:, 0:256], func=AF.Exp), reads=[ps], writes=[Dec])
                    yield
                    A = env.ring("A%d_%d" % (hh, tt % 2), [128, 128], F32R, 2)
                    k.op("dve", lambda e: e.scalar_tensor_tensor(out=A.t[:], in0=T["KKs"].t[:], scalar=BETA.t[:, tt, h:h + 1],
                                                                 in1=Dec.t[:, 0:128], op0=ALU.mult, op1=ALU.mult),
                         reads=[T["KKs"], BETA, Dec], writes=[A])
                    aqk = env.ring("aqk%d" % hh, [128, 128], BF16, 3)
                    k.op("pool", lambda e: e.tensor_tensor(out=aqk.t[:], in0=T["QKm"].t[:], in1=Dec.t[:, 128:256], op=ALU.mult),
                         reads=[T["QKm"], Dec], writes=[aqk])
                    T["aqk"][hh] = aqk
                    yield
                    ps = env.pnext()
                    k.op("pe", lambda e: e.transpose(ps.t[:, 0:128], f32v(A), env.ident_f.t[:]), reads=[A, env.ident_f], writes=[ps])
                    X = env.ring("X%d_%d" % (hh, tt % 2), [128, 128], F32R, 2)
                    B = env.ring("B%d_%d" % (hh, tt % 2), [128, 128], F32R, 2)
                    k.op("dve", lambda e: e.tensor_tensor(out=X.t[:], in0=env.ident_f.t[:], in1=ps.t[:, 0:128], op=ALU.subtract),
                         reads=[env.ident_f, ps], writes=[X])
                    k.op("act", lambda e: e.activation(out=B.t[:], in_=ps.t[:, 0:128], func=AF.Copy), reads=[ps], writes=[B])
                    yield
                    for m in (1, 2, 4, 8, 16):
                        ps = env.pnext()
                        mm(env, ps.t[:, 0:128], B.t[:], A.t[:], True, True, [A, B], [ps])
                        if m < 16:
                            mm(env, ps.t[:, 128:256], A.t[:], B.t[:], True, True, [A, B], [ps])
                        A2 = env.ring("A%d_%d" % (hh, tt % 2), [128, 128], F32R, 2)
                        k.op("act", lambda e: e.activation(out=A2.t[:], in_=ps.t[:, 0:128], func=AF.Copy), reads=[ps], writes=[A2])
                        if m < 16:
                            B2 = env.ring("B%d_%d" % (hh, tt % 2), [128, 128], F32R, 2)
                            k.op("dve", lambda e: e.tensor_copy(out=B2.t[:], in_=ps.t[:, 128:256]), reads=[ps], writes=[B2])
                        yield
                        ps2 = env.pnext()
                        mm(env, ps2.t[:, 0:128], A2.t[:], X.t[:], True, True, [A2, X], [ps2])
                        X2 = env.ring("X%d_%d" % (hh, tt % 2), [128, 128], F32R, 2)
                        k.op("dve", lambda e: e.tensor_tensor(out=X2.t[:], in0=f32v(X), in1=ps2.t[:, 0:128], op=ALU.add),
                             reads=[X, ps2], writes=[X2])
                        A, X = A2, X2
                        if m < 16:
                            B = B2
                        yield
                    ps = env.pnext()
                    mm(env, ps.t[:, 0:128], X.t[:], T["vb"][hh].t[:], True, True, [X, T["vb"][hh]], [ps])
                    mm(env, ps.t[:, 128:256], T["kbg"][hh].t[:], X.t[:], True, True, [X, T["kbg"][hh]], [ps])
                    u = ubuf[hh][ucnt[hh] % 3]
                    ucnt[hh] += 1
                    k.op("act", lambda e: e.activation(out=u.t[:, 0:128], in_=ps.t[:, 0:128], func=AF.Copy), reads=[ps], writes=[u])
                    wT = env.ring("wT%d" % hh, [128, 128], BF16, 3)
                    k.op("dve", lambda e: e.tensor_copy(out=wT.t[:], in_=ps.t[:, 128:256]), reads=[ps], writes=[wT])
                    T["u"][hh], T["wT"][hh] = u, wT
                    yield

                def recur(tt, hh):
                    T = tile[tt]
                    h = 2 * kh + hh
                    cols = slice((tt % 4) * 128, (tt % 4 + 1) * 128)
                    tg, o = tt // 4, (tt % 4) * 128
                    qb, sz = TG[tg]["qb"], TG[tg]["sz"]
                    u, wT, aqk, ktl = T["u"][hh], T["wT"][hh], T["aqk"][hh], T["ktl"][hh]
                    ot = env.ring("ot%d" % hh, [128, NV], F32, 2)
                    for cc in range(2):
                        r = slice(64 * cc, 64 * cc + 64)
                        ps = env.pnext()
                        mm(env, ps.t[r, 0:NV], wT.t[:, r], Sb[hh].t[:, :], True, True, [wT, Sb[hh]], [ps])
                        vn = env.ring("vn%d" % hh, [128, NV], BF16, 2)
                        k.op("dve", lambda e: e.tensor_tensor(out=vn.t[r, :], in0=u.t[r, :], in1=ps.t[r, 0:NV], op=ALU.subtract),
                             reads=[u, ps], writes=[vn])
                        yield
                        if True:
                            ps1 = env.pnext()
                            mm(env, ps1.t[r, 0:NV], qb.t[:, o + 64 * cc:o + 64 * cc + 64], Sb[hh].t[:, :], True, True,
                               [qb, Sb[hh]], [ps1])
                            ps2 = env.pnext()
                            mm(env, ps2.t[r, 0:NV], aqk.t[r, r], vn.t[r, :], True, True, [aqk, vn], [ps2])
                        ps3 = env.pnext()
                        mm(env, ps3.t[:, 0:NV], ktl.t[r, :], vn.t[r, :], True, True, [ktl, vn], [ps3])
                        if True:
                            o2 = env.ring("o2%d" % hh, [128, NV], F32, 1)
                            k.op("act", lambda e: e.activation(out=o2.t[r, :], in_=ps2.t[r, 0:NV], func=AF.Copy), reads=[ps2], writes=[o2])
                            k.op("dve", lambda e: e.scalar_tensor_tensor(out=ot.t[r, :], in0=ps1.t[r, 0:NV], scalar=EG.t[r, tt, h:h + 1],
                                                                         in1=o2.t[r, :], op0=ALU.mult, op1=ALU.add),
                                 reads=[ps1, EG, o2], writes=[ot])
                        k.op("dve", lambda e: e.scalar_tensor_tensor(out=Sb[hh].t[:], in0=Sf[hh].t[:],
                                                                     scalar=CD.t[:, tt, cc * NVH + h:cc * NVH + h + 1],
                                                                     in1=ps3.t[:, 0:NV], op0=ALU.mult, op1=ALU.add),
                             reads=[Sf[hh], CD, ps3], writes=[Sb[hh]])
                        k.op("dve", lambda e: e.scalar_tensor_tensor(out=Sf[hh].t[:], in0=Sf[hh].t[:],
                                                                     scalar=CD.t[:, tt, cc * NVH + h:cc * NVH + h + 1],
                                                                     in1=ps3.t[:, 0:NV], op0=ALU.mult, op1=ALU.add),
                             reads=[Sf[hh], CD, ps3], writes=[Sf[hh]])
                        yield
                    if mode == "a":
                        k.dma("act", out=oq_out[h, tt * 128:(tt + 1) * 128, :], in_=ot.t[:], reads=[ot], writes=[], owner=ot,
                              is_output=True)
                    if mode == "b":
                        sq = env.ring("osq", [128, 128], F32, 2)
                        ss = env.ring("oss", [128, 2], F32, 4)
                        k.op("pool", lambda e: e.tensor_tensor(out=sq.t[:], in0=ot.t[:], in1=ot.t[:], op=ALU.mult), reads=[ot], writes=[sq])
                        k.op("dve", lambda e: e.reduce_sum(out=ss.t[:, 0:1], in_=sq.t[:], axis=mybir.AxisListType.X), reads=[sq], writes=[ss])
                        k.op("act", lambda e: e.activation(out=ss.t[:, 1:2], in_=ss.t[:, 0:1], func=AF.Sqrt, scale=1.0 / HD,
                                                           bias=float(RMS_EPS)), reads=[ss], writes=[ss])
                        k.op("dve", lambda e: e.reciprocal(out=ss.t[:, 1:2], in_=ss.t[:, 1:2]), reads=[ss], writes=[ss])
                        onb = env.ring("onb", [128, 128], BF16, 2)
                        k.op("dve", lambda e: e.scalar_tensor_tensor(out=onb.t[:], in0=ot.t[:], scalar=ss.t[:, 1:2], in1=nwbc.t[:],
                                                                     op0=ALU.mult, op1=ALU.mult), reads=[ot, ss, nwbc], writes=[onb])
                        pt = env.ptnext()
                        k.op("pe", lambda e: e.transpose(pt.t[:, 0:128], onb.t[:], env.ident_bf.t[:]), reads=[onb, env.ident_bf], writes=[pt])
                        if tt % 4 == 0:
                            tile["og%d" % hh] = env.ring("og%d" % hh, [128, 512], BF16, 2)
                        og = tile["og%d" % hh]
                        k.op("dve", lambda e: e.tensor_tensor(out=og.t[:, o:o + 128], in0=pt.t[:, 0:128], in1=sz[hh].t[:, cols], op=ALU.mult),
                             reads=[pt, sz[hh]], writes=[og])
                        if tt % 4 == 3:
                            k.dma("act", out=onT_d[h, :, tg * 512:(tg + 1) * 512], in_=og.t[:], reads=[og], writes=[onbufs[h][tg]], owner=og)
                        yield

                def tile_front(tt):
                    for _ in prep(tt):
                        yield
                    gens = [solve(tt, 0), solve(tt, 1)]
                    while gens:
                        nxt = []
                        for g_ in gens:
                            try:
                                next(g_)
                                nxt.append(g_)
                            except StopIteration:
                                pass
                        gens = nxt
                        yield

                def tile_back(tt):
                    gens = [recur(tt, 0), recur(tt, 1)]
                    while gens:
                        nxt = []
                        for g_ in gens:
                            try:
                                next(g_)
                                nxt.append(g_)
                            except StopIteration:
                                pass
                        gens = nxt
                        yield
                    del tile[tt]

                def start_front(t):
                    if t % 4 == 0:
                        proj(t // 4)
                    return tile_front(t)

                def step(g_):
                    try:
                        next(g_)
                        return True
                    except StopIteration:
                        return False

                roundrobin([start_front(0)])
                if dbg <= 4:
                    k.finish()
                    return nc
                carry_f = None
                for tt in range(c.NTT):
                    must = [tile_back(tt)]
                    if carry_f is None and tt + 1 < c.NTT:
                        must.append(start_front(tt + 1))
                    elif carry_f:
                        must.append(carry_f)
                    ahead = start_front(tt + 2) if tt + 2 < c.NTT else None
                    ahead_alive = ahead is not None
                    while must:
                        must = [g_ for g_ in must if step(g_)]
                        if ahead_alive:
                            ahead_alive = step(ahead)
                    carry_f = None if ahead is None else (ahead if ahead_alive else False)
                if mode == "a":
                    for hh in range(2):
                        k.dma("act", out=st_out[2 * kh + hh], in_=Sf[hh].t[:], reads=[Sf[hh]], writes=[], owner=Sf[hh], is_output=True)

        if mode == "b":
            gdn_tail(env, io, vgroups, onT_d, onbufs, h_dram, x1, xo, P)
        k.finish()
    return nc


def gdn_tail(env, io, vgroups, onT_d, onbufs, h_dram, x1, xo, P):
    k, c = env.k, env.cfg
    env.alloc_act()
    hbufs = [[Buf("h%d_%d" % (p, tt)) for tt in range(c.NTT)] for p in range(P)]
    x1bufs = [Buf("x1_%d" % tt) for tt in range(c.NTT)]
    xobufs = [Buf("xo_%d" % tt) for tt in range(c.NTT)]
    parts = []
    h0 = 0
    for gi, G in enumerate(vgroups):
        for tg in range(c.NTG):
            k.dma("sp", out=env.actT[tg].t[:, 0:G, :],
                  in_=onT_d[h0:h0 + G, :, tg * 512:(tg + 1) * 512].rearrange("h p t -> p h t"),
                  reads=[onbufs[h][tg] for h in range(h0, h0 + G)], writes=[env.actT[tg]], owner=env.actT[tg])
        slabs = [[io["w_o_t"][gi][s, pc] for pc in range(G // 4)] for s in range(c.D // 512)]
        type_b(env, G, slabs, hbufs[gi], h_dram[gi])
        parts.append((h_dram[gi], hbufs[gi]))
        h0 += G
    ln_phase(env, io["x"], None, parts, io["ln"][0, :], io["ln"][1, :], x1, x1bufs, True, False)
    ffn_and_ln(env, io, x1, x1bufs, h_dram, hbufs, xo, xobufs, True, False)


def build_gdn_c_program(cfg):
    c = cfg
    NVH, KC = c.NVH, c.KC
    vgroups = [min(16, NVH - g0) for g0 in range(0, NVH, 16)]
    nc = bass.Bass("TRN2", target_bir_lowering=False)
    io = {}
    io["x"] = dram_in(nc, "x", [c.NT, c.D])
    io["ident"] = dram_in(nc, "ident", [128, 128])
    io["oq"] = dram_in(nc, "oq", [NVH, c.NT, 256])
    io["w_z"] = dram_in(nc, "w_z", [NVH, 128, KC * 128])
    io["normw"] = dram_in(nc, "normw", [1, 128])
    io["sin_pt"] = dram_in(nc, "sin_pt", [3, NVH, 128, 128])
    io["sin_s"] = dram_in(nc, "sin_s", [3, NVH, 128, 128])
    io["w_o_t"] = [dram_in(nc, "w_o_t%d" % gi, [c.D // 512, G // 4, 128, 2048]) for gi, G in enumerate(vgroups)]
    io["w_gu_t"] = dram_in(nc, "w_gu_t", [c.F // 128, 2, 128, KC * 128])
    io["w_dn_t"] = [dram_in(nc, "w_dn_t%d" % gi, [c.D // 512, G // 4, 128, 2048]) for gi, G in enumerate(c.fgroups)]
    io["ln"] = dram_in(nc, "ln", [4, c.D])
    xo = nc.dram_tensor("xo", [c.NT, c.D], F32, kind="ExternalOutput").ap()
    P = max(len(vgroups), len(c.fgroups))
    h_dram = [nc.dram_tensor("h%d" % p, [c.NT, c.D], F32, kind="Internal").ap() for p in range(P)]
    x1 = nc.dram_tensor("x1", [c.NT, c.D], F32, kind="Internal").ap()
    onT_d = nc.dram_tensor("onT_d", [NVH, 128, c.NT], BF16, kind="Internal").ap()
    with ExitStack() as es:
        k = K(nc, es)
        env = Env(k, c, io, alloc_act=False, ws_slots=4)
        with env.phase():
            load_x_to_xT(env, io["x"])
        onbufs = [[Buf("on_%d_%d" % (h, tg)) for tg in range(c.NTG)] for h in range(NVH)]
        with env.phase():
            nwbc = k.sbuf("nwbc", [128, 128], F32)
            k.dma("sp", out=nwbc.t[:], in_=io["normw"][0, :].partition_broadcast(128), writes=[nwbc], owner=nwbc)
            ws = env.ws
            for h in range(NVH):
                ws.push((io["w_z"][h], KC * 128))
            pending = []

            def stepg(g_):
                try:
                    next(g_)
                    return True
                except StopIteration:
                    return False

            def zproj(tg, wb):
                ps = env.pnext()
                for kc in range(KC):
                    mm(env, ps.t[:, :], wb.t[:, kc * 128:(kc + 1) * 128], env.xT[tg].t[:, kc, :],
                       kc == 0, kc == KC - 1, [wb, env.xT[tg]], [ps])
                sz = env.ring("sz", [128, 512], BF16, 5)
                k.op("act", lambda e: e.activation(out=sz.t[:], in_=ps.t[:, :], func=AF.Silu), reads=[ps], writes=[sz])
                return sz

            def item(h, tg, sz, Sb):
                og = env.ring("og", [128, 512], BF16, 4)
                oq4 = env.ring("oq4", [128, 4, 256], F32, 4)
                k.dma("sp", out=oq4.t[:], in_=io["oq"][h, tg * 512:(tg + 1) * 512, :].rearrange("(t p) c -> p t c", p=128),
                      writes=[oq4], owner=oq4)
                yield
                pq = env.pnext()
                for t4 in range(4):
                    k.op("pe", lambda e: e.transpose(pq.t[:, t4 * 128:(t4 + 1) * 128], oq4.t[:, t4, 128:256], env.ident_f.t[:]),
                         reads=[oq4, env.ident_f], writes=[pq], inc=(t4 == 3))
                qT = env.ring("qT", [128, 512], BF16, 4)
                k.op("act", lambda e: e.activation(out=qT.t[:], in_=pq.t[:, :], func=AF.Copy), reads=[pq], writes=[qT])
                yield
                pc_ = env.pnext()
                for t4 in range(4):
                    mm(env, pc_.t[:, t4 * 128:(t4 + 1) * 128], qT.t[:, t4 * 128:(t4 + 1) * 128], Sb.t[:], True, True,
                       [qT, Sb], [pc_], inc=(t4 == 3))
                ot = env.ring("ot", [128, 4, 128], F32, 4)
                k.op("dve", lambda e: e.tensor_tensor(out=ot.t[:], in0=pc_.t[:, :].rearrange("p (t c) -> p t c", c=128),
                                                      in1=oq4.t[:, :, 0:128], op=ALU.add), reads=[pc_, oq4], writes=[ot])
                yield
                sq = env.ring("osq", [128, 4, 128], F32, 4)
                ss = env.ring("oss", [128, 8], F32, 4)
                k.op("pool", lambda e: e.tensor_tensor(out=sq.t[:], in0=ot.t[:], in1=ot.t[:], op=ALU.mult), reads=[ot], writes=[sq])
                yield
                k.op("dve", lambda e: e.reduce_sum(out=ss.t[:, 0:4], in_=sq.t[:], axis=mybir.AxisListType.X), reads=[sq], writes=[ss])
                yield
                k.op("act", lambda e: e.activation(out=ss.t[:, 4:8], in_=ss.t[:, 0:4], func=AF.Sqrt, scale=1.0 / HD,
                                                   bias=float(RMS_EPS)), reads=[ss], writes=[ss])
                yield
                k.op("dve", lambda e: e.reciprocal(out=ss.t[:, 4:8], in_=ss.t[:, 4:8]), reads=[ss], writes=[ss])
                onb = env.ring("onb", [128, 4, 128], BF16, 4)
                for t4 in range(4):
                    k.op("dve", lambda e: e.scalar_tensor_tensor(out=onb.t[:, t4, :], in0=ot.t[:, t4, :], scalar=ss.t[:, 4 + t4:5 + t4],
                                                                 in1=nwbc.t[:], op0=ALU.mult, op1=ALU.mult),
                         reads=[ot, ss, nwbc], writes=[onb])
                yield
                pt = env.ptnext()
                for t4 in range(4):
                    k.op("pe", lambda e: e.transpose(pt.t[:, t4 * 128:(t4 + 1) * 128], onb.t[:, t4, :], env.ident_bf.t[:]),
                         reads=[onb, env.ident_bf], writes=[pt], inc=(t4 == 3))
                k.op("dve", lambda e: e.tensor_tensor(out=og.t[:], in0=pt.t[:, 0:512], in1=sz.t[:], op=ALU.mult),
                     reads=[pt, sz], writes=[og])
                k.dma("act", out=onT_d[h, :, tg * 512:(tg + 1) * 512], in_=og.t[:], reads=[og], writes=[onbufs[h][tg]], owner=og)

            for h in range(NVH):
                Sf = env.ring("Sf", [128, 128], F32, 3)
                Sb = env.ring("Sb", [128, 128], BF16, 3)
                k.dma("sp", out=Sf.t[:], in_=io["sin_s"][0, h], writes=[Sf], owner=Sf)
                for j in (1, 2):
                    pt_ = env.ring("foldp", [128, 128], F32, 2)
                    sl_ = env.ring("folds", [128, 128], F32, 2)
                    k.dma("sp", out=pt_.t[:], in_=io["sin_pt"][j, h], writes=[pt_], owner=pt_)
                    k.dma("sp", out=sl_.t[:], in_=io["sin_s"][j, h], writes=[sl_], owner=sl_)
                    ps = env.pnext()
                    mm(env, ps.t[:, 0:128], pt_.t[:], Sf.t[:], True, True, [pt_, Sf], [ps])
                    k.op("dve", lambda e: e.tensor_tensor(out=Sf.t[:], in0=ps.t[:, 0:128], in1=sl_.t[:], op=ALU.add),
                         reads=[ps, sl_], writes=[Sf])
                k.op("act", lambda e: e.activation(out=Sb.t[:], in_=Sf.t[:], func=AF.Copy), reads=[Sf], writes=[Sb])
                wb = ws.take(1)[0]
                for tg in range(c.NTG):
                    pending.append(item(h, tg, zproj(tg, wb), Sb))
                    while len(pending) >= 3:
                        pending[:] = [g_ for g_ in pending if stepg(g_)]
            while pending:
                pending[:] = [g_ for g_ in pending if stepg(g_)]
        gdn_tail(env, io, vgroups, onT_d, onbufs, h_dram, x1, xo, P)
        k.finish()
    return nc


def gdn_c_inputs(cfg, w_in, norm_w, w_out, w_gu, w_dn, ln4):
    c = cfg
    NKH, NVH, KC = c.NKH, c.NVH, c.KC
    KD, VD = NKH * 128, NVH * 128
    o1 = 2 * KD + VD
    d = {"ident": np.eye(128, dtype=np.float32)}
    d["w_z"] = tile_a(w_in, [o1 + h * 128 for h in range(NVH)])
    d["normw"] = np.ascontiguousarray(norm_w, dtype=np.float32).reshape(1, 128)
    g0 = 0
    gi = 0
    while g0 < NVH:
        G = min(16, NVH - g0)
        d["w_o_t%d" % gi] = tile_b(w_out, g0 * 128, G)
        g0 += G
        gi += 1
    d.update(ffn_inputs(cfg, w_gu, w_dn))
    d["ln"] = np.ascontiguousarray(ln4, dtype=np.float32)
    return d


def gdn_inputs(cfg, mode, w_in, conv_w, a_log, dt_bias, norm_w=None, w_out=None, w_gu=None, w_dn=None, ln4=None):
    c = cfg
    NKH, NVH, KC = c.NKH, c.NVH, c.KC
    KD, VD = NKH * 128, NVH * 128
    o1 = 2 * KD + VD
    o2 = o1 + VD
    d = {"ident": np.eye(128, dtype=np.float32), "consts": gdn_consts()}
    cols = []
    for kh in range(NKH):
        cols += [kh * 128, KD + kh * 128, 2 * KD + 2 * kh * 128, 2 * KD + (2 * kh + 1) * 128,
                 o1 + 2 * kh * 128, o1 + (2 * kh + 1) * 128]
    d["w_t"] = tile_a(w_in, cols).reshape(NKH, 6, 128, KC * 128)
    wba = w_in[:, o2:o2 + 2 * NVH].reshape(KC, 128, 2 * NVH).transpose(1, 0, 2)
    d["w_ba"] = np.ascontiguousarray(wba).reshape(128, KC * 2 * NVH)
    cwg = np.empty((128, NKH, 4, 4), np.float32)
    for kh in range(NKH):
        ch = [kh * 128, KD + kh * 128, 2 * KD + 2 * kh * 128, 2 * KD + (2 * kh + 1) * 128]
        for ci in range(4):
            cwg[:, kh, ci, :] = conv_w[:, ch[ci]:ch[ci] + 128].T
    d["cwg"] = cwg.reshape(128, NKH * 16)
    d["alog"] = np.ascontiguousarray(a_log, dtype=np.float32).reshape(1, NVH)
    d["dtb"] = np.ascontiguousarray(dt_bias, dtype=np.float32).reshape(1, NVH)
    if mode == "b":
        d["normw"] = np.ascontiguousarray(norm_w, dtype=np.float32).reshape(1, 128)
        g0 = 0
        gi = 0
        while g0 < NVH:
            G = min(16, NVH - g0)
            d["w_o_t%d" % gi] = tile_b(w_out, g0 * 128, G)
            g0 += G
            gi += 1
        d.update(ffn_inputs(cfg, w_gu, w_dn))
        d["ln"] = np.ascontiguousarray(ln4, dtype=np.float32)
    return d


_PROGS = {}


def _prog(name, cfg):
    if name not in _PROGS:
        if name == "sc":
            _PROGS[name] = build_sc_program(cfg)
        elif name == "ga":
            _PROGS[name] = build_gdn_program(cfg, "a")
        else:
            _PROGS[name] = build_gdn_c_program(cfg)
    return _PROGS[name]


def _halos(xs, cfg):
    out = []
    for cidx in range(N_CORES):
        if cidx % 4 == 0:
            out.append(np.zeros((4, cfg.D), np.float32))
        else:
            out.append(np.ascontiguousarray(xs[cidx - 1][-4:]))
    return out


def kernel(x, sc_w_in, sc_conv_w, sc_w_out, dn_w_in, dn_conv_w, dn_a_log, dn_dt_bias, dn_norm_w, dn_w_out,
           ffn_w_gate_up, ffn_w_down, ln_gain, ln_bias):
    cfg = Cfg()
    f = lambda a: np.ascontiguousarray(np.asarray(a), dtype=np.float32)
    x = f(x)
    NT = cfg.NT
    xs = [np.ascontiguousarray(x[ci // 4, (ci % 4) * NT:(ci % 4 + 1) * NT]) for ci in range(N_CORES)]
    cores = list(range(N_CORES))
    for layer in range(DEPTH):
        j = layer // 2
        ln4 = np.stack([f(ln_gain)[layer, 0], f(ln_bias)[layer, 0], f(ln_gain)[layer, 1], f(ln_bias)[layer, 1]])
        halos = _halos(xs, cfg)
        if layer % 2 == 0:
            base = sc_inputs(cfg, f(sc_w_in)[j], f(sc_conv_w)[j], f(sc_w_out)[j], f(ffn_w_gate_up)[layer],
                             f(ffn_w_down)[layer], ln4)
            maps = [dict(base, x=xs[ci], xh=halos[ci]) for ci in cores]
            res = run_bass_kernel_spmd(_prog("sc", cfg), maps, core_ids=cores)
            xs = [np.asarray(res.results[ci]["xo"]) for ci in cores]
        else:
            base_a = gdn_inputs(cfg, "a", f(dn_w_in)[j], f(dn_conv_w)[j], f(dn_a_log)[j], f(dn_dt_bias)[j])
            maps = [dict(base_a, x=xs[ci], xh=halos[ci]) for ci in cores]
            res = run_bass_kernel_spmd(_prog("ga", cfg), maps, core_ids=cores)
            sts = [np.asarray(res.results[ci]["st"]) for ci in cores]
            oqs = [np.asarray(res.results[ci]["oq"]) for ci in cores]
            base_b = gdn_c_inputs(cfg, f(dn_w_in)[j], f(dn_norm_w)[j], f(dn_w_out)[j], f(ffn_w_gate_up)[layer],
                                  f(ffn_w_down)[layer], ln4)
            eye = np.ascontiguousarray(np.broadcast_to(np.eye(128, dtype=np.float32), (cfg.NVH, 128, 128)))
            zer = np.zeros((cfg.NVH, 128, 128), np.float32)
            maps = []
            for ci in cores:
                q = ci % 4
                pts, ss = [eye] * (3 - q), [zer] * (3 - q)
                for p in range(ci - q, ci):
                    pts.append(np.ascontiguousarray(sts[p][:, :, 128:].transpose(0, 2, 1)))
                    ss.append(np.ascontiguousarray(sts[p][:, :, :128]))
                maps.append(dict(base_b, x=xs[ci], oq=oqs[ci], sin_pt=np.stack(pts), sin_s=np.stack(ss)))
            res = run_bass_kernel_spmd(_prog("gc", cfg), maps, core_ids=cores)
            xs = [np.asarray(res.results[ci]["xo"]) for ci in cores]
    out = np.empty((BATCH, SEQ, D_MODEL), np.float32)
    for ci in cores:
        out[ci // 4, (ci % 4) * NT:(ci % 4 + 1) * NT] = xs[ci]
    return out
```

```python
import numpy as np
from contextlib import ExitStack
import concourse.bass as bass
import concourse.mybir as mybir
from concourse.bass_utils import run_bass_kernel_spmd

F32 = mybir.dt.float32
BF16 = mybir.dt.bfloat16
F32R = mybir.dt.float32r
AF = mybir.ActivationFunctionType
ALU = mybir.AluOpType

D_MODEL = 2048
BATCH = 2
SEQ = 8192
DEPTH = 4
N_CORES = 8
FFN_HIDDEN = 5632
ALPHA = (2 * DEPTH) ** 0.25
LN_EPS = 1e-5
RMS_EPS = 1e-6
HD = 128


class Buf:
    def __init__(self, name, t=None):
        self.name = name
        self.t = t
        self.w = None
        self.r = {}
        self.dsem = None
        self.dcnt = 0
        self.parent = None
        self.children = []
        self.is_psum = False

    def view(self, name, ap):
        c = Buf(name, ap)
        c.parent = self
        self.children.append(c)
        return c


class K:
    def __init__(self, nc, es):
        self.nc = nc
        self.es = es
        self.eng = {"pe": nc.tensor, "act": nc.scalar, "dve": nc.vector,
                    "pool": nc.gpsimd, "sp": nc.sync}
        self.sem = {e: es.enter_context(nc.semaphore("s_" + e)) for e in self.eng}
        self.cnt = {e: 0 for e in self.eng}
        self.waited = {e: {} for e in self.eng}
        self.semobj = {}
        self.n_dsem = 0
        self.uid = 0
        self.out_stamps = []
        self.allbufs = []
        self.es_cur = None
        self.names = {}

    def sbuf(self, name, shape, dt):
        es = self.es_cur if getattr(self, "es_cur", None) is not None else self.es
        n = self.names.get(name, 0)
        self.names[name] = n + 1
        if n:
            name = "%s_v%d" % (name, n)
        t = es.enter_context(self.nc.sbuf_tensor(name, list(shape), dt))
        b = Buf(name, t)
        self.allbufs.append(b)
        return b

    def barrier(self):
        tgt = {}
        for e in self.eng:
            if self.cnt[e] > 0:
                tgt[id(self.sem[e])] = (self.sem[e], self.cnt[e])
        for b in self.allbufs:
            if b.dsem is not None and b.dcnt > 0:
                tgt[id(b.dsem)] = (b.dsem, 16 * b.dcnt)
        for e in self.eng:
            w = self.waited[e]
            for key, (sm, v) in tgt.items():
                if w.get(key, 0) < v:
                    self.eng[e].wait_ge(sm, v)
                    w[key] = v

    def psum(self, name, shape, dt):
        t = self.es.enter_context(self.nc.psum_tensor(name, list(shape), dt))
        b = Buf(name, t)
        b.is_psum = True
        return b

    def dsem_for(self, b):
        if b.dsem is None:
            b.dsem = self.es.enter_context(self.nc.semaphore("d%d" % self.n_dsem))
            self.n_dsem += 1
        return b.dsem

    def _wait(self, e, reads, writes):
        need = {}

        def add(st):
            if st is None:
                return
            s, v = st
            key = id(s)
            self.semobj[key] = s
            if need.get(key, 0) < v:
                need[key] = v

        def addr(b):
            for key, v in b.r.items():
                if need.get(key, 0) < v:
                    need[key] = v

        for b in reads:
            add(b.w)
            if b.is_psum:
                addr(b)
            if b.parent is not None:
                add(b.parent.w)
            for ch in b.children:
                add(ch.w)
        for b in writes:
            add(b.w)
            addr(b)
            if b.parent is not None:
                add(b.parent.w)
                addr(b.parent)
            for ch in b.children:
                add(ch.w)
                addr(ch)
        w = self.waited[e]
        own = id(self.sem[e])
        for key, v in need.items():
            if key == own and (e == "pe" or v > self.cnt[e]):
                continue
            if w.get(key, 0) < v:
                self.eng[e].wait_ge(self.semobj[key], v)
                w[key] = v

    def _stamp(self, st, reads, writes):
        key = id(st[0])
        self.semobj[key] = st[0]
        for b in reads:
            if b.r.get(key, 0) < st[1]:
                b.r[key] = st[1]
        for b in writes:
            b.w = st
            b.r = {}

    def op(self, e, fn, reads=(), writes=(), inc=True):
        self.nops = getattr(self, "nops", 0) + 1
        if self.nops > getattr(self, "limit", 1 << 60):
            return None
        self._wait(e, reads, writes)
        ins = fn(self.eng[e])
        if inc:
            ins.then_inc(self.sem[e], 1)
            self.cnt[e] += 1
            st = (self.sem[e], self.cnt[e])
        else:
            st = (self.sem[e], self.cnt[e] + 1)
        self._stamp(st, reads, writes)
        return ins

    def dma(self, e, out, in_, reads=(), writes=(), owner=None, is_output=False):
        self.nops = getattr(self, "nops", 0) + 1
        if self.nops > getattr(self, "limit", 1 << 60):
            return None
        self._wait(e, reads, writes)
        sem = self.dsem_for(owner)
        ins = self.eng[e].dma_start(out=out, in_=in_)
        ins.then_inc(sem, 16)
        owner.dcnt += 1
        st = (sem, 16 * owner.dcnt)
        self._stamp(st, reads, writes)
        if is_output:
            self.out_stamps.append(st)
        return ins

    def finish(self):
        need = {}
        for s, v in self.out_stamps:
            key = id(s)
            self.semobj[key] = s
            need[key] = max(need.get(key, 0), v)
        for key, v in need.items():
            self.eng["sp"].wait_ge(self.semobj[key], v)


class WStream:
    def __init__(self, k, name, ch_elems, n_stage, n_bf, cast_engines=("pool",)):
        self.k = k
        self.ch = ch_elems
        self.stage = [k.sbuf("%s_st%d" % (name, i), [128, ch_elems], F32) for i in range(n_stage)]
        self.bf = [k.sbuf("%s_bf%d" % (name, i), [128, ch_elems], BF16) for i in range(n_bf)]
        self.n = 0
        self.cast_engines = cast_engines
        self.queue = []
        self.ready = []

    def push(self, dram_ap):
        self.queue.append(dram_ap)

    def _fetch(self, item):
        k = self.k
        dram_ap, n = item
        st = self.stage[self.n % len(self.stage)]
        bf = self.bf[self.n % len(self.bf)]
        ce = self.cast_engines[self.n % len(self.cast_engines)]
        self.n += 1
        k.dma("sp", out=st.t[:, 0:n], in_=dram_ap, writes=[st], owner=st)
        if ce == "act":
            k.op("act", lambda e: e.activation(out=bf.t[:, 0:n], in_=st.t[:, 0:n], func=AF.Copy),
                 reads=[st], writes=[bf])
        else:
            k.op(ce, lambda e: e.tensor_copy(out=bf.t[:, 0:n], in_=st.t[:, 0:n]), reads=[st], writes=[bf])
        return bf

    def prefetch(self, depth):
        while self.queue and len(self.ready) < depth:
            self.ready.append(self._fetch(self.queue.pop(0)))

    def take(self, n):
        self.prefetch(n)
        out = [self.ready.pop(0) for _ in range(n)]
        self.prefetch(len(self.bf) - n)
        return out


class Cfg:
    def __init__(self, D=D_MODEL, NT=2048, F=FFN_HIDDEN, fgroups=(16, 16, 12), NKH=None):
        self.D = D
        self.NT = NT
        self.F = F
        self.KC = D // 128
        self.NTT = NT // 128
        self.NTG = NT // 512
        self.fgroups = tuple(fgroups)
        assert sum(fgroups) * 128 == F
        self.NKH = NKH if NKH is not None else D // HD
        self.NVH = 2 * self.NKH
        self.CH = max(self.KC * 128, 2048)


class Env:
    def __init__(self, k, cfg, consts, alloc_act=True, ws_slots=6):
        self.k = k
        self.cfg = cfg
        c = cfg
        nc = k.nc
        self.xT = [k.sbuf("xT%d" % g, [128, c.KC, 512], BF16) for g in range(c.NTG)]
        self.xhT = k.sbuf("xhT", [128, c.KC, 4], BF16)
        self.AK = max(c.KC, max(c.fgroups), min(16, c.NVH))
        self.ws = WStream(k, "w", c.CH, 2, ws_slots)
        self.lnviews = {}
        self.rings = {}
        self.pA = [k.psum("pA%d" % i, [128, 512], F32) for i in range(6)]
        self.pT = [k.psum("pT%d" % i, [128, 1024], BF16) for i in range(2)]
        self.pi = 0
        self.pti = 0
        self.ident_bf = k.sbuf("ident_bf", [128, 128], BF16)
        self.ident_f = k.sbuf("ident_f", [128, 128], F32)
        k.dma("sp", out=self.ident_f.t[:], in_=consts["ident"], writes=[self.ident_f], owner=self.ident_f)
        k.op("dve", lambda e: e.tensor_copy(out=self.ident_bf.t[:], in_=self.ident_f.t[:]),
             reads=[self.ident_f], writes=[self.ident_bf])
        if alloc_act:
            self.alloc_act()

    def alloc_act(self):
        k, c = self.k, self.cfg
        self.actT = [k.sbuf("actT%d" % g, [128, self.AK, 512], BF16) for g in range(c.NTG)]
        fl = [a.t[:].rearrange("p a b -> p (a b)") for a in self.actT]
        nb = self.AK * 512
        self.lnviews = {}

        def carve(g, off_bf, n_bf, dt, name):
            ap = fl[g][:, off_bf:off_bf + n_bf]
            if dt == F32:
                ap = ap.bitcast(F32)
            return self.actT[g].view(name, ap)

        if 2 * c.D * 2 <= nb and c.NTG >= 3:
            self.lnviews["lnr"] = [carve(0, 0, 2 * c.D, F32, "lnr0"), carve(0, 2 * c.D, 2 * c.D, F32, "lnr1")]
            self.lnviews["lnh"] = [carve(1, 0, 2 * c.D, F32, "lnh0")]
            if c.NTG >= 4:
                self.lnviews["lnh"] += [carve(3, 0, 2 * c.D, F32, "lnh1"), carve(3, 2 * c.D, 2 * c.D, F32, "lnh2")]
            self.lnviews["xb"] = [carve(1, 2 * c.D, c.D, BF16, "xb0"), carve(1, 3 * c.D, c.D, BF16, "xb1")]
            self.lnviews["gain"] = [carve(2, 0, 2 * c.D, F32, "gain")]
            self.lnviews["bias"] = [carve(2, 2 * c.D, 2 * c.D, F32, "bias")]

    def phase(self):
        env = self

        class _P:
            def __enter__(p):
                p.before = set(env.rings)
                p.es2 = ExitStack()
                p.es2.__enter__()
                p.prev = env.k.es_cur
                env.k.es_cur = p.es2
                return p

            def __exit__(p, *a):
                if a[0] is None:
                    env.k.barrier()
                env.k.es_cur = p.prev
                p.es2.__exit__(*a)
                for n in list(env.rings):
                    if n not in p.before:
                        del env.rings[n]
                return False

        return _P()

    def pnext(self):
        b = self.pA[self.pi % len(self.pA)]
        self.pi += 1
        return b

    def ptnext(self):
        b = self.pT[self.pti % len(self.pT)]
        self.pti += 1
        return b

    def ring(self, name, shape, dt, n):
        if name in self.lnviews and name not in self.rings:
            self.rings[name] = [self.lnviews[name], 0]
        if name not in self.rings:
            self.rings[name] = [[self.k.sbuf("%s%d" % (name, i), shape, dt) for i in range(n)], 0]
        r = self.rings[name]
        b = r[0][r[1] % len(r[0])]
        r[1] += 1
        return b


def f32v(b):
    return b.t[:].bitcast(F32)


def mm(env, ps_ap, lhsT, rhs, start, stop, reads, writes, inc=None):
    env.k.op("pe", lambda e: e.matmul(ps_ap, lhsT=lhsT, rhs=rhs, start=start, stop=stop),
             reads=reads, writes=writes, inc=(stop if inc is None else inc))


def emit_xT_tile(env, xf, tt):
    k, c = env.k, env.cfg
    xb = env.ring("xb", [128, c.D], BF16, 2)
    k.op("act", lambda e: e.activation(out=xb.t[:], in_=xf.t[:], func=AF.Copy), reads=[xf], writes=[xb])
    g, o = tt // 4, (tt % 4) * 128
    for k0 in range(0, c.KC, 8):
        n = min(8, c.KC - k0)
        pt = env.ptnext()
        for i in range(n):
            kc = k0 + i
            k.op("pe", lambda e: e.transpose(pt.t[:, i * 128:(i + 1) * 128], xb.t[:, kc * 128:(kc + 1) * 128],
                                             env.ident_bf.t[:]),
                 reads=[xb, env.ident_bf], writes=[pt], inc=(i == n - 1))
        k.op("dve", lambda e: e.tensor_copy(out=env.xT[g].t[:, k0:k0 + n, o:o + 128],
                                            in_=pt.t[:, 0:n * 128].rearrange("p (a b) -> p a b", b=128)),
             reads=[pt], writes=[env.xT[g]])


def emit_halo(env, xh_dram):
    k, c = env.k, env.cfg
    hf = k.sbuf("halo_f", [4, c.D], F32)
    hb = k.sbuf("halo_b", [4, c.D], BF16)
    k.dma("sp", out=hf.t[:], in_=xh_dram, writes=[hf], owner=hf)
    k.op("act", lambda e: e.activation(out=hb.t[:], in_=hf.t[:], func=AF.Copy), reads=[hf], writes=[hb])
    pt = env.ptnext()
    for kc in range(c.KC):
        k.op("pe", lambda e: e.transpose(pt.t[:, kc * 4:(kc + 1) * 4], hb.t[:, kc * 128:(kc + 1) * 128],
                                         env.ident_bf.t[0:4, 0:4]),
             reads=[hb, env.ident_bf], writes=[pt], inc=(kc == c.KC - 1))
    k.op("dve", lambda e: e.tensor_copy(out=env.xhT.t[:],
                                        in_=pt.t[:, 0:c.KC * 4].rearrange("p (a b) -> p a b", b=4)),
         reads=[pt], writes=[env.xhT])


def load_x_to_xT(env, x_dram):
    k, c = env.k, env.cfg
    for tt in range(c.NTT):
        xf = env.ring("lnr", [128, c.D], F32, 2)
        k.dma("sp", out=xf.t[:], in_=x_dram[tt * 128:(tt + 1) * 128, :], writes=[xf], owner=xf)
        emit_xT_tile(env, xf, tt)


def type_a(env, units, n_per_unit, epilogue, halo_fn=None):
    k, c = env.k, env.cfg
    ws = env.ws
    for u in units:
        for ap in u:
            ws.push((ap, c.KC * 128))
    for ui, u in enumerate(units):
        wb = ws.take(n_per_unit)
        if halo_fn is not None:
            halo_fn(ui, wb)
        for tg in range(c.NTG):
            ps = [env.pnext() for _ in range(n_per_unit)]
            for ci in range(n_per_unit):
                for kc in range(c.KC):
                    mm(env, ps[ci].t[:, :], wb[ci].t[:, kc * 128:(kc + 1) * 128], env.xT[tg].t[:, kc, :],
                       kc == 0, kc == c.KC - 1, [wb[ci], env.xT[tg]], [ps[ci]])
            epilogue(ui, tg, ps)


def type_b(env, KCb, slab_aps, hbufs, h_dram):
    k, c = env.k, env.cfg
    ws = env.ws
    npc = KCb // 4
    for s in range(c.D // 512):
        for ap in slab_aps[s]:
            ws.push((ap, 2048))
    for s in range(c.D // 512):
        pcs = ws.take(npc)
        for tt in range(c.NTT):
            g, o = tt // 4, (tt % 4) * 128
            ps = env.pnext()
            for kc in range(KCb):
                pc = pcs[kc // 4]
                mm(env, ps.t[:, :], env.actT[g].t[:, kc, o:o + 128], pc.t[:, (kc % 4) * 512:(kc % 4 + 1) * 512],
                   kc == 0, kc == KCb - 1, [env.actT[g], pc], [ps])
            ob = env.ring("ob", [128, 512], F32, 3)
            k.op("act", lambda e: e.activation(out=ob.t[:], in_=ps.t[:, :], func=AF.Copy), reads=[ps], writes=[ob])
            k.dma("act", out=h_dram[tt * 128:(tt + 1) * 128, s * 512:(s + 1) * 512], in_=ob.t[:],
                  reads=[ob], writes=[hbufs[tt]], owner=ob)


def ln_phase(env, x_dram, xbufs, parts, gain_dram, bias_dram, out_dram, obufs, make_xT, final):
    k, c = env.k, env.cfg
    gb = env.ring("gain", [128, c.D], F32, 1)
    bb = env.ring("bias", [128, c.D], F32, 1)
    k.dma("sp", out=gb.t[:], in_=gain_dram.partition_broadcast(128), writes=[gb], owner=gb)
    k.dma("sp", out=bb.t[:], in_=bias_dram.partition_broadcast(128), writes=[bb], owner=bb)
    nch = c.D // 512
    for tt in range(c.NTT):
        r = env.ring("lnr", [128, c.D], F32, 2)
        rows = slice(tt * 128, (tt + 1) * 128)
        k.dma("sp", out=r.t[:], in_=x_dram[rows, :], reads=[xbufs[tt]] if xbufs else [], writes=[r], owner=r)
        for pi, (hd, hb) in enumerate(parts):
            hin = env.ring("lnh", [128, c.D], F32, 2)
            k.dma("sp", out=hin.t[:], in_=hd[rows, :], reads=[hb[tt]], writes=[hin], owner=hin)
            if pi == 0:
                k.op("dve", lambda e: e.scalar_tensor_tensor(out=r.t[:], in0=r.t[:], scalar=float(ALPHA), in1=hin.t[:],
                                                             op0=ALU.mult, op1=ALU.add), reads=[r, hin], writes=[r])
            else:
                k.op("pool", lambda e: e.tensor_tensor(out=r.t[:], in0=r.t[:], in1=hin.t[:], op=ALU.add),
                     reads=[r, hin], writes=[r])
        st = env.ring("lnst", [128, nch * 6], F32, 2)
        for i in range(nch):
            k.op("dve", lambda e: e.bn_stats(out=st.t[:, i * 6:(i + 1) * 6], in_=r.t[:, i * 512:(i + 1) * 512]),
                 reads=[r], writes=[st])
        mv = env.ring("lnmv", [128, 4], F32, 2)
        k.op("dve", lambda e: e.bn_aggr(out=mv.t[:, 0:2], in_=st.t[:]), reads=[st], writes=[mv])
        k.op("act", lambda e: e.activation(out=mv.t[:, 2:3], in_=mv.t[:, 1:2], func=AF.Sqrt, bias=float(LN_EPS)),
             reads=[mv], writes=[mv])
        k.op("dve", lambda e: e.reciprocal(out=mv.t[:, 2:3], in_=mv.t[:, 2:3]), reads=[mv], writes=[mv])
        k.op("dve", lambda e: e.scalar_tensor_tensor(out=r.t[:], in0=r.t[:], scalar=mv.t[:, 0:1], in1=gb.t[:],
                                                     op0=ALU.subtract, op1=ALU.mult), reads=[r, mv, gb], writes=[r])
        k.op("dve", lambda e: e.scalar_tensor_tensor(out=r.t[:], in0=r.t[:], scalar=mv.t[:, 2:3], in1=bb.t[:],
                                                     op0=ALU.mult, op1=ALU.add), reads=[r, mv, bb], writes=[r])
        k.dma("act", out=out_dram[rows, :], in_=r.t[:], reads=[r], writes=[obufs[tt]], owner=r, is_output=final)
        if make_xT:
            emit_xT_tile(env, r, tt)


def tile_a(W, col_starts):
    K_ = W.shape[0]
    KC = K_ // 128
    Wr = W.reshape(KC, 128, W.shape[1])
    out = np.empty((len(col_starts), 128, KC * 128), np.float32)
    for i, c0 in enumerate(col_starts):
        out[i] = Wr[:, :, c0:c0 + 128].transpose(1, 0, 2).reshape(128, KC * 128)
    return out


def tile_b(W, r0, KCb):
    D = W.shape[1]
    Wr = W[r0:r0 + KCb * 128].reshape(KCb // 4, 4, 128, D // 512, 512)
    return np.ascontiguousarray(Wr.transpose(3, 0, 2, 1, 4)).reshape(D // 512, KCb // 4, 128, 2048)


def dram_in(nc, name, shape):
    return nc.dram_tensor(name, list(shape), F32, kind="ExternalInput").ap()


def ffn_and_ln(env, io, x_dram, xbufs, h_dram, hbufs, out_dram, obufs, final, make_xT):
    k, c = env.k, env.cfg
    hc0 = 0
    parts = []
    for gi, G in enumerate(c.fgroups):
        units = [[io["w_gu_t"][hc0 + j, 0], io["w_gu_t"][hc0 + j, 1]] for j in range(G)]

        def epi(ui, tg, ps):
            sg = env.ring("sg", [128, 512], F32, 2)
            k.op("act", lambda e: e.activation(out=sg.t[:], in_=ps[0].t[:, :], func=AF.Silu), reads=[ps[0]], writes=[sg])
            k.op("dve", lambda e: e.tensor_tensor(out=env.actT[tg].t[:, ui, :], in0=ps[1].t[:, :], in1=sg.t[:], op=ALU.mult),
                 reads=[ps[1], sg], writes=[env.actT[tg]])

        type_a(env, units, 2, epi)
        slabs = [[io["w_dn_t"][gi][s, pc] for pc in range(G // 4)] for s in range(c.D // 512)]
        type_b(env, G, slabs, hbufs[gi], h_dram[gi])
        parts.append((h_dram[gi], hbufs[gi]))
        hc0 += G
    ln_phase(env, x_dram, xbufs, parts, io["ln"][2, :], io["ln"][3, :], out_dram, obufs, make_xT, final)


def build_sc_program(cfg, stages=9):
    c = cfg
    nc = bass.Bass("TRN2", target_bir_lowering=False)
    io = {}
    io["x"] = dram_in(nc, "x", [c.NT, c.D])
    io["xh"] = dram_in(nc, "xh", [4, c.D])
    io["ident"] = dram_in(nc, "ident", [128, 128])
    io["w_in_t"] = dram_in(nc, "w_in_t", [c.KC, 3, 128, c.KC * 128])
    io["cw"] = dram_in(nc, "cw", [128, c.KC * 3])
    io["w_out_t"] = dram_in(nc, "w_out_t", [c.D // 512, c.KC // 4, 128, 2048])
    io["w_gu_t"] = dram_in(nc, "w_gu_t", [c.F // 128, 2, 128, c.KC * 128])
    io["w_dn_t"] = [dram_in(nc, "w_dn_t%d" % gi, [c.D // 512, G // 4, 128, 2048]) for gi, G in enumerate(c.fgroups)]
    io["ln"] = dram_in(nc, "ln", [4, c.D])
    xo = nc.dram_tensor("xo", [c.NT, c.D], F32, kind="ExternalOutput").ap()
    P = max(1, len(c.fgroups))
    h_dram = [nc.dram_tensor("h%d" % p, [c.NT, c.D], F32, kind="Internal").ap() for p in range(P)]
    x1 = nc.dram_tensor("x1", [c.NT, c.D], F32, kind="Internal").ap()
    with ExitStack() as es:
        k = K(nc, es)
        env = Env(k, c, io)
        hbufs = [[Buf("h%d_%d" % (p, tt)) for tt in range(c.NTT)] for p in range(P)]
        x1bufs = [Buf("x1_%d" % tt) for tt in range(c.NTT)]
        xobufs = [Buf("xo_%d" % tt) for tt in range(c.NTT)]
        cw = k.sbuf("cw_sb", [128, c.KC * 3], F32)
        k.dma("sp", out=cw.t[:], in_=io["cw"], writes=[cw], owner=cw)
        emit_halo(env, io["xh"])
        load_x_to_xT(env, io["x"])

        units = [[io["w_in_t"][j, 0], io["w_in_t"][j, 1], io["w_in_t"][j, 2]] for j in range(c.KC)]
        state = {}

        def halo_fn(j, wb):
            ps = env.pnext()
            for ci in (1, 2):
                for kc in range(c.KC):
                    mm(env, ps.t[:, (ci - 1) * 4:(ci - 1) * 4 + 4], wb[ci].t[:, kc * 128:(kc + 1) * 128],
                       env.xhT.t[:, kc, :], kc == 0, kc == c.KC - 1, [wb[ci], env.xhT], [ps],
                       inc=(kc == c.KC - 1))
            hh = env.ring("hh", [128, 4], F32, 2)
            k.op("act", lambda e: e.activation(out=hh.t[:], in_=ps.t[:, 4:8], func=AF.Copy), reads=[ps], writes=[hh])
            u = env.ring("u", [128, 2 + 512], F32, 3)
            k.op("dve", lambda e: e.tensor_tensor(out=u.t[:, 0:2], in0=ps.t[:, 2:4], in1=hh.t[:, 2:4], op=ALU.mult),
                 reads=[ps, hh], writes=[u])
            state["u"] = u

        def epi(j, tg, ps):
            hs = env.ring("hs", [128, 512], F32, 2)
            k.op("act", lambda e: e.activation(out=hs.t[:], in_=ps[2].t[:, :], func=AF.Copy), reads=[ps[2]], writes=[hs])
            if tg == 0:
                u = state["u"]
            else:
                u = env.ring("u", [128, 2 + 512], F32, 3)
                up = state["u"]
                k.op("dve", lambda e: e.tensor_copy(out=u.t[:, 0:2], in_=up.t[:, 512:514]), reads=[up], writes=[u])
                state["u"] = u
            k.op("dve", lambda e: e.tensor_tensor(out=u.t[:, 2:514], in0=ps[1].t[:, :], in1=hs.t[:], op=ALU.mult),
                 reads=[ps[1], hs], writes=[u])
            y = env.ring("y", [128, 512], F32, 2)
            k.op("act", lambda e: e.activation(out=y.t[:], in_=u.t[:, 0:512], func=AF.Copy, scale=cw.t[:, j * 3:j * 3 + 1]),
                 reads=[u, cw], writes=[y])
            k.op("dve", lambda e: e.scalar_tensor_tensor(out=y.t[:], in0=u.t[:, 1:513], scalar=cw.t[:, j * 3 + 1:j * 3 + 2],
                                                         in1=y.t[:], op0=ALU.mult, op1=ALU.add), reads=[u, cw, y], writes=[y])
            k.op("dve", lambda e: e.scalar_tensor_tensor(out=y.t[:], in0=u.t[:, 2:514], scalar=cw.t[:, j * 3 + 2:j * 3 + 3],
                                                         in1=y.t[:], op0=ALU.mult, op1=ALU.add), reads=[u, cw, y], writes=[y])
            k.op("dve", lambda e: e.tensor_tensor(out=env.actT[tg].t[:, j, :], in0=ps[0].t[:, :], in1=y.t[:], op=ALU.mult),
                 reads=[ps[0], y], writes=[env.actT[tg]])

        type_a(env, units, 3, epi, halo_fn)
        slabs = [[io["w_out_t"][s, pc] for pc in range(c.KC // 4)] for s in range(c.D // 512)]
        if stages >= 2:
            type_b(env, c.KC, slabs, hbufs[0], h_dram[0])
        if stages >= 3:
            ln_phase(env, io["x"], None, [(h_dram[0], hbufs[0])], io["ln"][0, :], io["ln"][1, :], x1, x1bufs, True, False)
        if stages >= 4:
            ffn_and_ln(env, io, x1, x1bufs, h_dram, hbufs, xo, xobufs, True, False)
        k.finish()
    return nc


def sc_inputs(cfg, w_in, conv_w, w_out, w_gu, w_dn, ln4):
    c = cfg
    D, F = c.D, c.F
    d = {}
    d["ident"] = np.eye(128, dtype=np.float32)
    cols = []
    for j in range(c.KC):
        cols += [j * 128, D + j * 128, 2 * D + j * 128]
    d["w_in_t"] = tile_a(w_in, cols).reshape(c.KC, 3, 128, c.KC * 128)
    d["cw"] = np.ascontiguousarray(conv_w.reshape(3, c.KC, 128).transpose(2, 1, 0)).reshape(128, c.KC * 3)
    d["w_out_t"] = tile_b(w_out, 0, c.KC)
    d.update(ffn_inputs(cfg, w_gu, w_dn))
    d["ln"] = np.ascontiguousarray(ln4, dtype=np.float32)
    return d


def ffn_inputs(cfg, w_gu, w_dn):
    c = cfg
    d = {}
    cols = []
    for j in range(c.F // 128):
        cols += [j * 128, c.F + j * 128]
    d["w_gu_t"] = tile_a(w_gu, cols).reshape(c.F // 128, 2, 128, c.KC * 128)
    r0 = 0
    for gi, G in enumerate(c.fgroups):
        d["w_dn_t%d" % gi] = tile_b(w_dn, r0, G)
        r0 += G * 128
    return d


def gdn_consts():
    i = np.arange(128)
    same = (i[:, None] // 64) == (i[None, :] // 64)
    c = np.zeros((8, 128, 128), np.float32)
    c[0] = ((i[:, None] <= i[None, :]) & same)
    c[1] = (i[:, None] > i[None, :])
    c[2] = ((i[:, None] > i[None, :]) & same)
    c[3] = ((i[None, :] >= i[:, None]) & same)
    c[4] = same
    c[5] = 1.0
    c[6] = (i[:, None] < 64) * np.ones((1, 128))
    c[7] = (i[:, None] >= 64) * np.ones((1, 128))
    return c


def roundrobin(gens):
    gens = list(gens)
    while gens:
        nxt = []
        for g in gens:
            try:
                next(g)
                nxt.append(g)
            except StopIteration:
                pass
        gens = nxt


LIMIT = [1 << 60]
USE_PT_VIEWS = False


def build_gdn_program(cfg, mode, dbg=99):
    c = cfg
    NVH, NKH, KC = c.NVH, c.NKH, c.KC
    NV = 256 if mode == "a" else 128
    nch = 4 if mode == "a" else 6
    vgroups = [min(16, NVH - g0) for g0 in range(0, NVH, 16)]
    nc = bass.Bass("TRN2", target_bir_lowering=False)
    io = {}
    io["x"] = dram_in(nc, "x", [c.NT, c.D])
    io["xh"] = dram_in(nc, "xh", [4, c.D])
    io["ident"] = dram_in(nc, "ident", [128, 128])
    io["consts"] = dram_in(nc, "consts", [8, 128, 128])
    io["w_t"] = dram_in(nc, "w_t", [NKH, 6, 128, KC * 128])
    io["w_ba"] = dram_in(nc, "w_ba", [128, KC * 2 * NVH])
    io["cwg"] = dram_in(nc, "cwg", [128, NKH * 16])
    io["alog"] = dram_in(nc, "alog", [1, NVH])
    io["dtb"] = dram_in(nc, "dtb", [1, NVH])
    if mode == "a":
        st_out = nc.dram_tensor("st", [NVH, 128, 256], F32, kind="ExternalOutput").ap()
        oq_out = nc.dram_tensor("oq", [NVH, c.NT, 256], F32, kind="ExternalOutput").ap()
    else:
        io["normw"] = dram_in(nc, "normw", [1, 128])
        io["sin_pt"] = dram_in(nc, "sin_pt", [3, NVH, 128, 128])
        io["sin_s"] = dram_in(nc, "sin_s", [3, NVH, 128, 128])
        io["w_o_t"] = [dram_in(nc, "w_o_t%d" % gi, [c.D // 512, G // 4, 128, 2048]) for gi, G in enumerate(vgroups)]
        io["w_gu_t"] = dram_in(nc, "w_gu_t", [c.F // 128, 2, 128, KC * 128])
        io["w_dn_t"] = [dram_in(nc, "w_dn_t%d" % gi, [c.D // 512, G // 4, 128, 2048]) for gi, G in enumerate(c.fgroups)]
        io["ln"] = dram_in(nc, "ln", [4, c.D])
        xo = nc.dram_tensor("xo", [c.NT, c.D], F32, kind="ExternalOutput").ap()
        P = max(len(vgroups), len(c.fgroups))
        h_dram = [nc.dram_tensor("h%d" % p, [c.NT, c.D], F32, kind="Internal").ap() for p in range(P)]
        x1 = nc.dram_tensor("x1", [c.NT, c.D], F32, kind="Internal").ap()
        onT_d = nc.dram_tensor("onT_d", [NVH, 128, c.NT], BF16, kind="Internal").ap()
    with ExitStack() as es:
        k = K(nc, es)
        k.limit = LIMIT[0]
        env = Env(k, c, io, alloc_act=False, ws_slots=(5 if mode == "a" else 7))
        with env.phase():
            emit_halo(env, io["xh"])
            load_x_to_xT(env, io["x"])
        onbufs = [[Buf("on_%d_%d" % (h, tg)) for tg in range(c.NTG)] for h in range(NVH)]
        if dbg <= 1:
            k.finish()
            return nc

        with env.phase():
            def f32t(name, shape=(128, 128)):
                return k.sbuf(name, list(shape), F32)

            CN = []
            for i in range(8):
                t = f32t("cst%d" % i)
                k.dma("sp", out=t.t[:], in_=io["consts"][i], writes=[t], owner=t)
                CN.append(t)
            TRIBD, UU, STRICT, CAUSALT, BLK, ONES, CH0, CH1 = CN
            UUr = k.sbuf("UUr", [128, 128], F32R)
            k.op("dve", lambda e: e.tensor_copy(out=UUr.t[:], in_=UU.t[:]), reads=[UU], writes=[UUr])
            cwg = f32t("cwg_sb", (128, NKH * 16))
            k.dma("sp", out=cwg.t[:], in_=io["cwg"], writes=[cwg], owner=cwg)
            wba_f = f32t("wba_f", (128, KC * 2 * NVH))
            wba = k.sbuf("wba_b", [128, KC, 2 * NVH], BF16)
            k.dma("sp", out=wba_f.t[:], in_=io["w_ba"], writes=[wba_f], owner=wba_f)
            k.op("dve", lambda e: e.tensor_copy(out=wba.t[:].rearrange("p a b -> p (a b)"), in_=wba_f.t[:]),
                 reads=[wba_f], writes=[wba])
            alog = f32t("alog_sb", (128, NVH))
            dtb = f32t("dtb_sb", (128, NVH))
            k.dma("sp", out=alog.t[:], in_=io["alog"][0, :].partition_broadcast(128), writes=[alog], owner=alog)
            k.dma("sp", out=dtb.t[:], in_=io["dtb"][0, :].partition_broadcast(128), writes=[dtb], owner=dtb)
            negA = f32t("negA", (128, NVH))
            k.op("act", lambda e: e.activation(out=negA.t[:], in_=alog.t[:], func=AF.Exp), reads=[alog], writes=[negA])
            k.op("dve", lambda e: e.tensor_scalar(out=negA.t[:], in0=negA.t[:], scalar1=-1.0, scalar2=None, op0=ALU.mult),
                 reads=[negA], writes=[negA])
            if mode == "b":
                nwbc = f32t("nwbc")
                k.dma("sp", out=nwbc.t[:], in_=io["normw"][0, :].partition_broadcast(128), writes=[nwbc], owner=nwbc)
            shp = (128, c.NTT, NVH)
            BETA, AX, GRAW, GC, EG = [f32t(n, shp) for n in ("BETA", "AX", "GRAW", "GC", "EG")]
            EKT, BEG = AX, GC
            CD = f32t("CD", (128, c.NTT, 2 * NVH))

            for tt in range(c.NTT):
                g, o = tt // 4, (tt % 4) * 128
                ps = env.pnext()
                for kc in range(KC):
                    mm(env, ps.t[:, 0:2 * NVH], env.xT[g].t[:, kc, o:o + 128], wba.t[:, kc, :], kc == 0, kc == KC - 1,
                       [env.xT[g], wba], [ps])
                k.op("act", lambda e: e.activation(out=BETA.t[:, tt, :], in_=ps.t[:, 0:NVH], func=AF.Sigmoid),
                     reads=[ps], writes=[BETA])
                k.op("dve", lambda e: e.tensor_tensor(out=AX.t[:, tt, :], in0=ps.t[:, NVH:2 * NVH], in1=dtb.t[:], op=ALU.add),
                     reads=[ps, dtb], writes=[AX])
            k.op("act", lambda e: e.activation(out=AX.t[:], in_=AX.t[:], func=AF.Exp), reads=[AX], writes=[AX])
            k.op("act", lambda e: e.activation(out=AX.t[:], in_=AX.t[:], func=AF.Ln, bias=1.0), reads=[AX], writes=[AX])
            for tt in range(c.NTT):
                k.op("dve", lambda e: e.tensor_tensor(out=GRAW.t[:, tt, :], in0=AX.t[:, tt, :], in1=negA.t[:], op=ALU.mult),
                     reads=[AX, negA], writes=[GRAW])
            for tt in range(c.NTT):
                gm = env.ring("gm", [128, 2 * NVH], F32, 2)
                k.op("dve", lambda e: e.tensor_scalar(out=gm.t[:, 0:NVH], in0=GRAW.t[:, tt, :], scalar1=CH0.t[:, 0:1],
                                                      scalar2=None, op0=ALU.mult), reads=[GRAW, CH0], writes=[gm])
                k.op("dve", lambda e: e.tensor_scalar(out=gm.t[:, NVH:2 * NVH], in0=GRAW.t[:, tt, :], scalar1=CH1.t[:, 0:1],
                                                      scalar2=None, op0=ALU.mult), reads=[GRAW, CH1], writes=[gm])
                ps = env.pnext()
                mm(env, ps.t[:, 0:NVH], TRIBD.t[:], GRAW.t[:, tt, :], True, True, [TRIBD, GRAW], [ps])
                mm(env, ps.t[:, 64:64 + NVH], BLK.t[:], GRAW.t[:, tt, :], True, True, [BLK, GRAW], [ps])
                mm(env, ps.t[:, 128:128 + 2 * NVH], ONES.t[:], gm.t[:], True, True, [ONES, gm], [ps])
                k.op("dve", lambda e: e.tensor_copy(out=GC.t[:, tt, :], in_=ps.t[:, 0:NVH]), reads=[ps], writes=[GC])
                k.op("act", lambda e: e.activation(out=EG.t[:, tt, :], in_=ps.t[:, 0:NVH], func=AF.Exp), reads=[ps], writes=[EG])
                k.op("dve", lambda e: e.tensor_tensor(out=EKT.t[:, tt, :], in0=ps.t[:, 64:64 + NVH], in1=GC.t[:, tt, :],
                                                      op=ALU.subtract), reads=[ps, GC], writes=[EKT])
                k.op("act", lambda e: e.activation(out=EKT.t[:, tt, :], in_=EKT.t[:, tt, :], func=AF.Exp),
                     reads=[EKT], writes=[EKT])
                k.op("act", lambda e: e.activation(out=CD.t[:, tt, :], in_=ps.t[:, 128:128 + 2 * NVH], func=AF.Exp),
                     reads=[ps], writes=[CD])
                k.op("dve", lambda e: e.tensor_tensor(out=BEG.t[:, tt, :], in0=BETA.t[:, tt, :], in1=EG.t[:, tt, :], op=ALU.mult),
                     reads=[BETA, EG], writes=[BEG])

            if dbg <= 2:
                k.finish()
                return nc
            if mode == "a" and USE_PT_VIEWS:
                for pt_b in env.pT:
                    v_ = pt_b.view(pt_b.name + "_f32", pt_b.t[:].bitcast(F32))
                    v_.is_psum = True
                    env.pA.append(v_)
            Sf = [f32t("Sf%d" % i, (128, NV)) for i in range(2)]
            Sb = [k.sbuf("Sb%d" % i, [128, NV], BF16) for i in range(2)]
            ubuf = [[f32t("u%d_%d" % (i, j), (128, NV)) for j in range(3)] for i in range(2)]
            if NV == 256:
                for i in range(2):
                    for j in range(3):
                        k.op("pool", lambda e: e.memset(ubuf[i][j].t[:], 0.0), writes=[ubuf[i][j]])
            ucnt = [0, 0]
            carry = [f32t("carry%d" % ci, (128, 4)) for ci in range(4)]
            ws = env.ws
            for kh in range(NKH):
                for ci in range(nch):
                    ws.push((io["w_t"][kh, ci], KC * 128))

            for kh in range(NKH):
                wb = ws.take(nch)
                for hh in range(2):
                    h = 2 * kh + hh
                    if mode == "a":
                        k.op("pool", lambda e: e.memset(Sf[hh].t[:, 0:128], 0.0), writes=[Sf[hh]])
                        k.op("pool", lambda e: e.tensor_copy(out=Sf[hh].t[:, 128:256], in_=env.ident_f.t[:]),
                             reads=[env.ident_f], writes=[Sf[hh]])
                    else:
                        k.dma("sp", out=Sf[hh].t[:], in_=io["sin_s"][0, h], writes=[Sf[hh]], owner=Sf[hh])
                        for j in (1, 2):
                            pt_ = env.ring("foldp", [128, 128], F32, 2)
                            sl_ = env.ring("folds", [128, 128], F32, 2)
                            k.dma("sp", out=pt_.t[:], in_=io["sin_pt"][j, h], writes=[pt_], owner=pt_)
                            k.dma("sp", out=sl_.t[:], in_=io["sin_s"][j, h], writes=[sl_], owner=sl_)
                            ps = env.pnext()
                            mm(env, ps.t[:, 0:128], pt_.t[:], Sf[hh].t[:], True, True, [pt_, Sf[hh]], [ps])
                            k.op("dve", lambda e: e.tensor_tensor(out=Sf[hh].t[:], in0=ps.t[:, 0:128], in1=sl_.t[:], op=ALU.add),
                                 reads=[ps, sl_], writes=[Sf[hh]])
                    k.op("act", lambda e: e.activation(out=Sb[hh].t[:], in_=Sf[hh].t[:], func=AF.Copy),
                         reads=[Sf[hh]], writes=[Sb[hh]])
                psh = env.pnext()
                for ci in range(4):
                    for kc in range(KC):
                        mm(env, psh.t[:, ci * 4:ci * 4 + 4], wb[ci].t[:, kc * 128:(kc + 1) * 128], env.xhT.t[:, kc, :],
                           kc == 0, kc == KC - 1, [wb[ci], env.xhT], [psh])
                for ci in range(4):
                    k.op("dve", lambda e: e.tensor_copy(out=carry[ci].t[:, 0:3], in_=psh.t[:, ci * 4 + 1:ci * 4 + 4]),
                         reads=[psh], writes=[carry[ci]])
                TG = {}

                def proj(tg):
                    knT = env.ring("knT", [128, 512], F32R, 2)
                    qnT = env.ring("qnT", [128, 512], F32R, 2)
                    qb = env.ring("qb", [128, 512], BF16, 2)
                    vT = [env.ring("vT%d" % i, [128, 512], F32, 2) for i in range(2)]
                    sz = [env.ring("sz%d" % i, [128, 512], BF16, 2) for i in range(2)] if mode == "b" else None
                    TG[tg] = dict(knT=knT, qnT=qnT, qb=qb, vT=vT, sz=sz)
                    for ci in range(nch):
                        ps = env.pnext()
                        for kc in range(KC):
                            mm(env, ps.t[:, :], wb[ci].t[:, kc * 128:(kc + 1) * 128], env.xT[tg].t[:, kc, :],
                               kc == 0, kc == KC - 1, [wb[ci], env.xT[tg]], [ps])
                        if ci >= 4:
                            k.op("act", lambda e: e.activation(out=sz[ci - 4].t[:], in_=ps.t[:, :], func=AF.Silu),
                                 reads=[ps], writes=[sz[ci - 4]])
                            continue
                        pre = env.ring("pre", [128, 3 + 512], F32, 2)
                        k.op("dve", lambda e: e.tensor_copy(out=pre.t[:, 0:3], in_=carry[ci].t[:, 0:3]), reads=[carry[ci]], writes=[pre])
                        k.op("act", lambda e: e.activation(out=pre.t[:, 3:515], in_=ps.t[:, :], func=AF.Copy), reads=[ps], writes=[pre])
                        k.op("dve", lambda e: e.tensor_copy(out=carry[ci].t[:, 0:3], in_=pre.t[:, 512:515]), reads=[pre], writes=[carry[ci]])
                        cb = (kh * 4 + ci) * 4
                        ca = env.ring("cacc", [128, 512], F32, 2)
                        k.op("act", lambda e: e.activation(out=ca.t[:], in_=pre.t[:, 0:512], func=AF.Copy, scale=cwg.t[:, cb:cb + 1]),
                             reads=[pre, cwg], writes=[ca])
                        for tap in (1, 2, 3):
                            k.op("dve", lambda e: e.scalar_tensor_tensor(out=ca.t[:], in0=pre.t[:, tap:tap + 512],
                                                                         scalar=cwg.t[:, cb + tap:cb + tap + 1], in1=ca.t[:],
                                                                         op0=ALU.mult, op1=ALU.add), reads=[pre, cwg, ca], writes=[ca])
                        if ci >= 2:
                            k.op("act", lambda e: e.activation(out=vT[ci - 2].t[:], in_=ca.t[:], func=AF.Silu),
                                 reads=[ca], writes=[vT[ci - 2]])
                            continue
                        sl = env.ring("sl", [128, 512], F32, 1)
                        k.op("act", lambda e: e.activation(out=sl.t[:], in_=ca.t[:], func=AF.Silu), reads=[ca], writes=[sl])
                        k.op("pool", lambda e: e.tensor_tensor(out=ca.t[:], in0=sl.t[:], in1=sl.t[:], op=ALU.mult), reads=[sl], writes=[ca])
                        ps2 = env.pnext()
                        mm(env, ps2.t[:, :], ONES.t[:], ca.t[:], True, True, [ONES, ca], [ps2])
                        k.op("act", lambda e: e.activation(out=ca.t[:], in_=ps2.t[:, :], func=AF.Sqrt, bias=float(RMS_EPS)),
                             reads=[ps2], writes=[ca])
                        k.op("dve", lambda e: e.reciprocal(out=ca.t[:], in_=ca.t[:]), reads=[ca], writes=[ca])
                        if ci == 1:
                            k.op("dve", lambda e: e.tensor_tensor(out=knT.t[:], in0=sl.t[:], in1=ca.t[:], op=ALU.mult),
                                 reads=[sl, ca], writes=[knT])
                        else:
                            k.op("dve", lambda e: e.scalar_tensor_tensor(out=qnT.t[:], in0=sl.t[:], scalar=float(HD ** -0.5),
                                                                         in1=ca.t[:], op0=ALU.mult, op1=ALU.mult),
                                 reads=[sl, ca], writes=[qnT])
                            k.op("pool", lambda e: e.tensor_copy(out=qb.t[:], in_=f32v(qnT)), reads=[qnT], writes=[qb])

                tile = {}

                def prep(tt):
                    cols = slice((tt % 4) * 128, (tt % 4 + 1) * 128)
                    knT, qnT, vT = TG[tt // 4]["knT"], TG[tt // 4]["qnT"], TG[tt // 4]["vT"]
                    T = {}
                    psk = env.pnext()
                    k.op("pe", lambda e: e.transpose(psk.t[:, 0:128], f32v(knT)[:, cols], env.ident_f.t[:]),
                         reads=[knT, env.ident_f], writes=[psk])
                    psv = env.pnext()
                    for hh in range(2):
                        k.op("pe", lambda e: e.transpose(psv.t[:, hh * 128:(hh + 1) * 128], vT[hh].t[:, cols], env.ident_f.t[:]),
                             reads=[vT[hh], env.ident_f], writes=[psv])
                    T["kbg"], T["ktl"], T["vb"] = [], [], []
                    for hh in range(2):
                        h = 2 * kh + hh
                        kbg = env.ring("kbg%d" % hh, [128, 128], F32R, 3)
                        ktl = env.ring("ktl%d" % hh, [128, 128], BF16, 3)
                        vb = env.ring("vb%d" % hh, [128, 128], F32R, 3)
                        k.op("act", lambda e: e.activation(out=kbg.t[:], in_=psk.t[:, 0:128], func=AF.Copy, scale=BEG.t[:, tt, h:h + 1]),
                             reads=[psk, BEG], writes=[kbg])
                        k.op("dve", lambda e: e.tensor_scalar(out=ktl.t[:], in0=psk.t[:, 0:128], scalar1=EKT.t[:, tt, h:h + 1],
                                                              scalar2=None, op0=ALU.mult), reads=[psk, EKT], writes=[ktl])
                        k.op("dve", lambda e: e.tensor_scalar(out=vb.t[:], in0=psv.t[:, hh * 128:(hh + 1) * 128],
                                                              scalar1=BETA.t[:, tt, h:h + 1], scalar2=None, op0=ALU.mult),
                             reads=[psv, BETA], writes=[vb])
                        T["kbg"].append(kbg); T["ktl"].append(ktl); T["vb"].append(vb)
                    yield
                    pskk = env.pnext()
                    mm(env, pskk.t[:, 0:128], knT.t[:, cols], knT.t[:, cols], True, True, [knT], [pskk])
                    mm(env, pskk.t[:, 128:256], knT.t[:, cols], qnT.t[:, cols], True, True, [knT, qnT], [pskk])
                    KKs = env.ring("KKs", [128, 128], F32, 3)
                    QKm = env.ring("QKm", [128, 128], F32, 3)
                    k.op("dve", lambda e: e.tensor_tensor(out=KKs.t[:], in0=pskk.t[:, 0:128], in1=STRICT.t[:], op=ALU.mult),
                         reads=[pskk, STRICT], writes=[KKs])
                    k.op("dve", lambda e: e.tensor_tensor(out=QKm.t[:], in0=pskk.t[:, 128:256], in1=CAUSALT.t[:], op=ALU.mult),
                         reads=[pskk, CAUSALT], writes=[QKm])
                    T["KKs"], T["QKm"] = KKs, QKm
                    T["u"], T["wT"], T["aqk"] = [None, None], [None, None], [None, None]
                    tile[tt] = T
                    yield

                def solve(tt, hh):
                    T = tile[tt]
                    h = 2 * kh + hh
                    G = env.ring("G%d_%d" % (hh, tt % 2), [128, 128], F32R, 1)
                    k.op("act", lambda e: e.activation(out=G.t[:], in_=TRIBD.t[:], func=AF.Copy, scale=GRAW.t[:, tt, h:h + 1]),
                         reads=[TRIBD, GRAW], writes=[G])
                    yield
                    ps = env.pnext()
                    mm(env, ps.t[:, 0:128], G.t[:], UUr.t[:], True, True, [G, UUr], [ps])
                    mm(env, ps.t[:, 128:256], UUr.t[:], G.t[:], True, True, [G, UUr], [ps])
                    Dec = env.ring("Dec%d_%d" % (hh, tt % 2), [128, 256], F32, 1)
                    k.op("act", lambda e: e.activation(out=Dec.t[:], in_=ps.t[:, 0:256], func=AF.Exp), reads=[ps], writes=[Dec])
                    yield
                    A = env.ring("A%d_%d" % (hh, tt % 2), [128, 128], F32R, 2)
                    k.op("dve", lambda e: e.scalar_tensor_tensor(out=A.t[:], in0=T["KKs"].t[:], scalar=BETA.t[:, tt, h:h + 1],
                                                                 in1=Dec.t[:, 0:128], op0=ALU.mult, op1=ALU.mult),
                         reads=[T["KKs"], BETA, Dec], writes=[A])
                    aqk = env.ring("aqk%d" % hh, [128, 128], BF16, 3)
                    k.op("pool", lambda e: e.tensor_tensor(out=aqk.t[:], in0=T["QKm"].t[:], in1=Dec.t[:, 128:256], op=ALU.mult),
                         reads=[T["QKm"], Dec], writes=[aqk])
                    T["aqk"][hh] = aqk
                    yield
                    ps = env.pnext()
                    k.op("pe", lambda e: e.transpose(ps.t[:, 0:128], f32v(A), env.ident_f.t[:]), reads=[A, env.ident_f], writes=[ps])
                    X = env.ring("X%d_%d" % (hh, tt % 2), [128, 128], F32R, 2)
                    B = env.ring("B%d_%d" % (hh, tt % 2), [128, 128], F32R, 2)
                    k.op("dve", lambda e: e.tensor_tensor(out=X.t[:], in0=env.ident_f.t[:], in1=ps.t[:, 0:128], op=ALU.subtract),
                         reads=[env.ident_f, ps], writes=[X])
                    k.op("act", lambda e: e.activation(out=B.t[:], in_=ps.t[:, 0:128], func=AF.Copy), reads=[ps], writes=[B])
                    yield
                    for m in (1, 2, 4, 8, 16):
                        ps = env.pnext()
                        mm(env, ps.t[:, 0:128], B.t[:], A.t[:], True, True, [A, B], [ps])
                        if m < 16:
                            mm(env, ps.t[:, 128:256], A.t[:], B.t[:], True, True, [A, B], [ps])
                        A2 = env.ring("A%d_%d" % (hh, tt % 2), [128, 128], F32R, 2)
                        k.op("act", lambda e: e.activation(out=A2.t[:], in_=ps.t[:, 0:128], func=AF.Copy), reads=[ps], writes=[A2])
                        if m < 16:
                            B2 = env.ring("B%d_%d" % (hh, tt % 2), [128, 128], F32R, 2)
                            k.op("dve", lambda e: e.tensor_copy(out=B2.t[:], in_=ps.t[:, 128:256]), reads=[ps], writes=[B2])
                        yield
                        ps2 = env.pnext()
                        mm(env, ps2.t[:, 0:128], A2.t[:], X.t[:], True, True, [A2, X], [ps2])
                        X2 = env.ring("X%d_%d" % (hh, tt % 2), [128, 128], F32R, 2)
                        k.op("dve", lambda e: e.tensor_tensor(out=X2.t[:], in0=f32v(X), in1=ps2.t[:, 0:128], op=ALU.add),
                             reads=[X, ps2], writes=[X2])
                        A, X = A2, X2
                        if m < 16:
                            B = B2
                        yield
                    ps = env.pnext()
                    mm(env, ps.t[:, 0:128], X.t[:], T["vb"][hh].t[:], True, True, [X, T["vb"][hh]], [ps])
                    mm(env, ps.t[:, 128:256], T["kbg"][hh].t[:], X.t[:], True, True, [X, T["kbg"][hh]], [ps])
                    u = ubuf[hh][ucnt[hh] % 3]
                    ucnt[hh] += 1
                    k.op("act", lambda e: e.activation(out=u.t[:, 0:128], in_=ps.t[:, 0:128], func=AF.Copy), reads=[ps], writes=[u])
                    wT = env.ring("wT%d" % hh, [128, 128], BF16, 3)
                    k.op("dve", lambda e: e.tensor_copy(out=wT.t[:], in_=ps.t[:, 128:256]), reads=[ps], writes=[wT])
                    T["u"][hh], T["wT"][hh] = u, wT
                    yield

                def recur(tt, hh):
                    T = tile[tt]
                    h = 2 * kh + hh
                    cols = slice((tt % 4) * 128, (tt % 4 + 1) * 128)
                    tg, o = tt // 4, (tt % 4) * 128
                    qb, sz = TG[tg]["qb"], TG[tg]["sz"]
                    u, wT, aqk, ktl = T["u"][hh], T["wT"][hh], T["aqk"][hh], T["ktl"][hh]
                    ot = env.ring("ot%d" % hh, [128, NV], F32, 2)
                    for cc in range(2):
                        r = slice(64 * cc, 64 * cc + 64)
                        ps = env.pnext()
                        mm(env, ps.t[r, 0:NV], wT.t[:, r], Sb[hh].t[:, :], True, True, [wT, Sb[hh]], [ps])
                        vn = env.ring("vn%d" % hh, [128, NV], BF16, 2)
                        k.op("dve", lambda e: e.tensor_tensor(out=vn.t[r, :], in0=u.t[r, :], in1=ps.t[r, 0:NV], op=ALU.subtract),
                             reads=[u, ps], writes=[vn])
                        yield
                        if True:
                            ps1 = env.pnext()
                            mm(env, ps1.t[r, 0:NV], qb.t[:, o + 64 * cc:o + 64 * cc + 64], Sb[hh].t[:, :], True, True,
                               [qb, Sb[hh]], [ps1])
                            ps2 = env.pnext()
                            mm(env, ps2.t[r, 0:NV], aqk.t[r, r], vn.t[r, :], True, True, [aqk, vn], [ps2])
                        ps3 = env.pnext()
                        mm(env, ps3.t[:, 0:NV], ktl.t[r, :], vn.t[r, :], True, True, [ktl, vn], [ps3])
                        if True:
                            o2 = env.ring("o2%d" % hh, [128, NV], F32, 1)
                            k.op("act", lambda e: e.activation(out=o2.t[r, :], in_=ps2.t[r, 0:NV], func=AF.Copy), reads=[ps2], writes=[o2])
                            k.op("dve", lambda e: e.scalar_tensor_tensor(out=ot.t[r, :], in0=ps1.t[r, 0:NV], scalar=EG.t[r, tt, h:h + 1],
                                                                         in1=o2.t[r, :], op0=ALU.mult, op1=ALU.add),
                                 reads=[ps1, EG, o2], writes=[ot])
                        k.op("dve", lambda e: e.scalar_tensor_tensor(out=Sb[hh].t[:], in0=Sf[hh].t[:],
                                                                     scalar=CD.t[:, tt, cc * NVH + h:cc * NVH + h + 1],
                                                                     in1=ps3.t[:, 0:NV], op0=ALU.mult, op1=ALU.add),
                             reads=[Sf[hh], CD, ps3], writes=[Sb[hh]])
                        k.op("dve", lambda e: e.scalar_tensor_tensor(out=Sf[hh].t[:], in0=Sf[hh].t[:],
                                                                     scalar=CD.t[:, tt, cc * NVH + h:cc * NVH + h + 1],
                                                                     in1=ps3.t[:, 0:NV], op0=ALU.mult, op1=ALU.add),
                             reads=[Sf[hh], CD, ps3], writes=[Sf[hh]])
                        yield
                    if mode == "a":
                        k.dma("act", out=oq_out[h, tt * 128:(tt + 1) * 128, :], in_=ot.t[:], reads=[ot], writes=[], owner=ot,
                              is_output=True)
                    if mode == "b":
                        sq = env.ring("osq", [128, 128], F32, 2)
                        ss = env.ring("oss", [128, 2], F32, 4)
                        k.op("pool", lambda e: e.tensor_tensor(out=sq.t[:], in0=ot.t[:], in1=ot.t[:], op=ALU.mult), reads=[ot], writes=[sq])
                        k.op("dve", lambda e: e.reduce_sum(out=ss.t[:, 0:1], in_=sq.t[:], axis=mybir.AxisListType.X), reads=[sq], writes=[ss])
                        k.op("act", lambda e: e.activation(out=ss.t[:, 1:2], in_=ss.t[:, 0:1], func=AF.Sqrt, scale=1.0 / HD,
                                                           bias=float(RMS_EPS)), reads=[ss], writes=[ss])
                        k.op("dve", lambda e: e.reciprocal(out=ss.t[:, 1:2], in_=ss.t[:, 1:2]), reads=[ss], writes=[ss])
                        onb = env.ring("onb", [128, 128], BF16, 2)
                        k.op("dve", lambda e: e.scalar_tensor_tensor(out=onb.t[:], in0=ot.t[:], scalar=ss.t[:, 1:2], in1=nwbc.t[:],
                                                                     op0=ALU.mult, op1=ALU.mult), reads=[ot, ss, nwbc], writes=[onb])
                        pt = env.ptnext()
                        k.op("pe", lambda e: e.transpose(pt.t[:, 0:128], onb.t[:], env.ident_bf.t[:]), reads=[onb, env.ident_bf], writes=[pt])
                        if tt % 4 == 0:
                            tile["og%d" % hh] = env.ring("og%d" % hh, [128, 512], BF16, 2)
                        og = tile["og%d" % hh]
                        k.op("dve", lambda e: e.tensor_tensor(out=og.t[:, o:o + 128], in0=pt.t[:, 0:128], in1=sz[hh].t[:, cols], op=ALU.mult),
                             reads=[pt, sz[hh]], writes=[og])
                        if tt % 4 == 3:
                            k.dma("act", out=onT_d[h, :, tg * 512:(tg + 1) * 512], in_=og.t[:], reads=[og], writes=[onbufs[h][tg]], owner=og)
                        yield

                def tile_front(tt):
                    for _ in prep(tt):
                        yield
                    gens = [solve(tt, 0), solve(tt, 1)]
                    while gens:
                        nxt = []
                        for g_ in gens:
                            try:
                                next(g_)
                                nxt.append(g_)
                            except StopIteration:
                                pass
                        gens = nxt
                        yield

                def tile_back(tt):
                    gens = [recur(tt, 0), recur(tt, 1)]
                    while gens:
                        nxt = []
                        for g_ in gens:
                            try:
                                next(g_)
                                nxt.append(g_)
                            except StopIteration:
                                pass
                        gens = nxt
                        yield
                    del tile[tt]

                def start_front(t):
                    if t % 4 == 0:
                        proj(t // 4)
                    return tile_front(t)

                def step(g_):
                    try:
                        next(g_)
                        return True
                    except StopIteration:
                        return False

                roundrobin([start_front(0)])
                if dbg <= 4:
                    k.finish()
                    return nc
                carry_f = None
                for tt in range(c.NTT):
                    must = [tile_back(tt)]
                    if carry_f is None and tt + 1 < c.NTT:
                        must.append(start_front(tt + 1))
                    elif carry_f:
                        must.append(carry_f)
                    ahead = start_front(tt + 2) if tt + 2 < c.NTT else None
                    ahead_alive = ahead is not None
                    while must:
                        must = [g_ for g_ in must if step(g_)]
                        if ahead_alive:
                            ahead_alive = step(ahead)
                    carry_f = None if ahead is None else (ahead if ahead_alive else False)
                if mode == "a":
                    for hh in range(2):
                        k.dma("act", out=st_out[2 * kh + hh], in_=Sf[hh].t[:], reads=[Sf[hh]], writes=[], owner=Sf[hh], is_output=True)

        if mode == "b":
            gdn_tail(env, io, vgroups, onT_d, onbufs, h_dram, x1, xo, P)
        k.finish()
    return nc


def gdn_tail(env, io, vgroups, onT_d, onbufs, h_dram, x1, xo, P):
    k, c = env.k, env.cfg
    env.alloc_act()
    hbufs = [[Buf("h%d_%d" % (p, tt)) for tt in range(c.NTT)] for p in range(P)]
    x1bufs = [Buf("x1_%d" % tt) for tt in range(c.NTT)]
    xobufs = [Buf("xo_%d" % tt) for tt in range(c.NTT)]
    parts = []
    h0 = 0
    for gi, G in enumerate(vgroups):
        for tg in range(c.NTG):
            k.dma("sp", out=env.actT[tg].t[:, 0:G, :],
                  in_=onT_d[h0:h0 + G, :, tg * 512:(tg + 1) * 512].rearrange("h p t -> p h t"),
                  reads=[onbufs[h][tg] for h in range(h0, h0 + G)], writes=[env.actT[tg]], owner=env.actT[tg])
        slabs = [[io["w_o_t"][gi][s, pc] for pc in range(G // 4)] for s in range(c.D // 512)]
        type_b(env, G, slabs, hbufs[gi], h_dram[gi])
        parts.append((h_dram[gi], hbufs[gi]))
        h0 += G
    ln_phase(env, io["x"], None, parts, io["ln"][0, :], io["ln"][1, :], x1, x1bufs, True, False)
    ffn_and_ln(env, io, x1, x1bufs, h_dram, hbufs, xo, xobufs, True, False)


def build_gdn_c_program(cfg):
    c = cfg
    NVH, KC = c.NVH, c.KC
    vgroups = [min(16, NVH - g0) for g0 in range(0, NVH, 16)]
    nc = bass.Bass("TRN2", target_bir_lowering=False)
    io = {}
    io["x"] = dram_in(nc, "x", [c.NT, c.D])
    io["ident"] = dram_in(nc, "ident", [128, 128])
    io["oq"] = dram_in(nc, "oq", [NVH, c.NT, 256])
    io["w_z"] = dram_in(nc, "w_z", [NVH, 128, KC * 128])
    io["normw"] = dram_in(nc, "normw", [1, 128])
    io["sin_pt"] = dram_in(nc, "sin_pt", [3, NVH, 128, 128])
    io["sin_s"] = dram_in(nc, "sin_s", [3, NVH, 128, 128])
    io["w_o_t"] = [dram_in(nc, "w_o_t%d" % gi, [c.D // 512, G // 4, 128, 2048]) for gi, G in enumerate(vgroups)]
    io["w_gu_t"] = dram_in(nc, "w_gu_t", [c.F // 128, 2, 128, KC * 128])
    io["w_dn_t"] = [dram_in(nc, "w_dn_t%d" % gi, [c.D // 512, G // 4, 128, 2048]) for gi, G in enumerate(c.fgroups)]
    io["ln"] = dram_in(nc, "ln", [4, c.D])
    xo = nc.dram_tensor("xo", [c.NT, c.D], F32, kind="ExternalOutput").ap()
    P = max(len(vgroups), len(c.fgroups))
    h_dram = [nc.dram_tensor("h%d" % p, [c.NT, c.D], F32, kind="Internal").ap() for p in range(P)]
    x1 = nc.dram_tensor("x1", [c.NT, c.D], F32, kind="Internal").ap()
    onT_d = nc.dram_tensor("onT_d", [NVH, 128, c.NT], BF16, kind="Internal").ap()
    with ExitStack() as es:
        k = K(nc, es)
        env = Env(k, c, io, alloc_act=False, ws_slots=4)
        with env.phase():
            load_x_to_xT(env, io["x"])
        onbufs = [[Buf("on_%d_%d" % (h, tg)) for tg in range(c.NTG)] for h in range(NVH)]
        with env.phase():
            nwbc = k.sbuf("nwbc", [128, 128], F32)
            k.dma("sp", out=nwbc.t[:], in_=io["normw"][0, :].partition_broadcast(128), writes=[nwbc], owner=nwbc)
            ws = env.ws
            for h in range(NVH):
                ws.push((io["w_z"][h], KC * 128))
            pending = []

            def stepg(g_):
                try:
                    next(g_)
                    return True
                except StopIteration:
                    return False

            def zproj(tg, wb):
                ps = env.pnext()
                for kc in range(KC):
                    mm(env, ps.t[:, :], wb.t[:, kc * 128:(kc + 1) * 128], env.xT[tg].t[:, kc, :],
                       kc == 0, kc == KC - 1, [wb, env.xT[tg]], [ps])
                sz = env.ring("sz", [128, 512], BF16, 5)
                k.op("act", lambda e: e.activation(out=sz.t[:], in_=ps.t[:, :], func=AF.Silu), reads=[ps], writes=[sz])
                return sz

            def item(h, tg, sz, Sb):
                og = env.ring("og", [128, 512], BF16, 4)
                oq4 = env.ring("oq4", [128, 4, 256], F32, 4)
                k.dma("sp", out=oq4.t[:], in_=io["oq"][h, tg * 512:(tg + 1) * 512, :].rearrange("(t p) c -> p t c", p=128),
                      writes=[oq4], owner=oq4)
                yield
                pq = env.pnext()
                for t4 in range(4):
                    k.op("pe", lambda e: e.transpose(pq.t[:, t4 * 128:(t4 + 1) * 128], oq4.t[:, t4, 128:256], env.ident_f.t[:]),
                         reads=[oq4, env.ident_f], writes=[pq], inc=(t4 == 3))
                qT = env.ring("qT", [128, 512], BF16, 4)
                k.op("act", lambda e: e.activation(out=qT.t[:], in_=pq.t[:, :], func=AF.Copy), reads=[pq], writes=[qT])
                yield
                pc_ = env.pnext()
                for t4 in range(4):
                    mm(env, pc_.t[:, t4 * 128:(t4 + 1) * 128], qT.t[:, t4 * 128:(t4 + 1) * 128], Sb.t[:], True, True,
                       [qT, Sb], [pc_], inc=(t4 == 3))
                ot = env.ring("ot", [128, 4, 128], F32, 4)
                k.op("dve", lambda e: e.tensor_tensor(out=ot.t[:], in0=pc_.t[:, :].rearrange("p (t c) -> p t c", c=128),
                                                      in1=oq4.t[:, :, 0:128], op=ALU.add), reads=[pc_, oq4], writes=[ot])
                yield
                sq = env.ring("osq", [128, 4, 128], F32, 4)
                ss = env.ring("oss", [128, 8], F32, 4)
                k.op("pool", lambda e: e.tensor_tensor(out=sq.t[:], in0=ot.t[:], in1=ot.t[:], op=ALU.mult), reads=[ot], writes=[sq])
                yield
                k.op("dve", lambda e: e.reduce_sum(out=ss.t[:, 0:4], in_=sq.t[:], axis=mybir.AxisListType.X), reads=[sq], writes=[ss])
                yield
                k.op("act", lambda e: e.activation(out=ss.t[:, 4:8], in_=ss.t[:, 0:4], func=AF.Sqrt, scale=1.0 / HD,
                                                   bias=float(RMS_EPS)), reads=[ss], writes=[ss])
                yield
                k.op("dve", lambda e: e.reciprocal(out=ss.t[:, 4:8], in_=ss.t[:, 4:8]), reads=[ss], writes=[ss])
                onb = env.ring("onb", [128, 4, 128], BF16, 4)
                for t4 in range(4):
                    k.op("dve", lambda e: e.scalar_tensor_tensor(out=onb.t[:, t4, :], in0=ot.t[:, t4, :], scalar=ss.t[:, 4 + t4:5 + t4],
                                                                 in1=nwbc.t[:], op0=ALU.mult, op1=ALU.mult),
                         reads=[ot, ss, nwbc], writes=[onb])
                yield
                pt = env.ptnext()
                for t4 in range(4):
                    k.op("pe", lambda e: e.transpose(pt.t[:, t4 * 128:(t4 + 1) * 128], onb.t[:, t4, :], env.ident_bf.t[:]),
                         reads=[onb, env.ident_bf], writes=[pt], inc=(t4 == 3))
                k.op("dve", lambda e: e.tensor_tensor(out=og.t[:], in0=pt.t[:, 0:512], in1=sz.t[:], op=ALU.mult),
                     reads=[pt, sz], writes=[og])
                k.dma("act", out=onT_d[h, :, tg * 512:(tg + 1) * 512], in_=og.t[:], reads=[og], writes=[onbufs[h][tg]], owner=og)

            for h in range(NVH):
                Sf = env.ring("Sf", [128, 128], F32, 3)
                Sb = env.ring("Sb", [128, 128], BF16, 3)
                k.dma("sp", out=Sf.t[:], in_=io["sin_s"][0, h], writes=[Sf], owner=Sf)
                for j in (1, 2):
                    pt_ = env.ring("foldp", [128, 128], F32, 2)
                    sl_ = env.ring("folds", [128, 128], F32, 2)
                    k.dma("sp", out=pt_.t[:], in_=io["sin_pt"][j, h], writes=[pt_], owner=pt_)
                    k.dma("sp", out=sl_.t[:], in_=io["sin_s"][j, h], writes=[sl_], owner=sl_)
                    ps = env.pnext()
                    mm(env, ps.t[:, 0:128], pt_.t[:], Sf.t[:], True, True, [pt_, Sf], [ps])
                    k.op("dve", lambda e: e.tensor_tensor(out=Sf.t[:], in0=ps.t[:, 0:128], in1=sl_.t[:], op=ALU.add),
                         reads=[ps, sl_], writes=[Sf])
                k.op("act", lambda e: e.activation(out=Sb.t[:], in_=Sf.t[:], func=AF.Copy), reads=[Sf], writes=[Sb])
                wb = ws.take(1)[0]
                for tg in range(c.NTG):
                    pending.append(item(h, tg, zproj(tg, wb), Sb))
                    while len(pending) >= 3:
                        pending[:] = [g_ for g_ in pending if stepg(g_)]
            while pending:
                pending[:] = [g_ for g_ in pending if stepg(g_)]
        gdn_tail(env, io, vgroups, onT_d, onbufs, h_dram, x1, xo, P)
        k.finish()
    return nc


def gdn_c_inputs(cfg, w_in, norm_w, w_out, w_gu, w_dn, ln4):
    c = cfg
    NKH, NVH, KC = c.NKH, c.NVH, c.KC
    KD, VD = NKH * 128, NVH * 128
    o1 = 2 * KD + VD
    d = {"ident": np.eye(128, dtype=np.float32)}
    d["w_z"] = tile_a(w_in, [o1 + h * 128 for h in range(NVH)])
    d["normw"] = np.ascontiguousarray(norm_w, dtype=np.float32).reshape(1, 128)
    g0 = 0
    gi = 0
    while g0 < NVH:
        G = min(16, NVH - g0)
        d["w_o_t%d" % gi] = tile_b(w_out, g0 * 128, G)
        g0 += G
        gi += 1
    d.update(ffn_inputs(cfg, w_gu, w_dn))
    d["ln"] = np.ascontiguousarray(ln4, dtype=np.float32)
    return d


def gdn_inputs(cfg, mode, w_in, conv_w, a_log, dt_bias, norm_w=None, w_out=None, w_gu=None, w_dn=None, ln4=None):
    c = cfg
    NKH, NVH, KC = c.NKH, c.NVH, c.KC
    KD, VD = NKH * 128, NVH * 128
    o1 = 2 * KD + VD
    o2 = o1 + VD
    d = {"ident": np.eye(128, dtype=np.float32), "consts": gdn_consts()}
    cols = []
    for kh in range(NKH):
        cols += [kh * 128, KD + kh * 128, 2 * KD + 2 * kh * 128, 2 * KD + (2 * kh + 1) * 128,
                 o1 + 2 * kh * 128, o1 + (2 * kh + 1) * 128]
    d["w_t"] = tile_a(w_in, cols).reshape(NKH, 6, 128, KC * 128)
    wba = w_in[:, o2:o2 + 2 * NVH].reshape(KC, 128, 2 * NVH).transpose(1, 0, 2)
    d["w_ba"] = np.ascontiguousarray(wba).reshape(128, KC * 2 * NVH)
    cwg = np.empty((128, NKH, 4, 4), np.float32)
    for kh in range(NKH):
        ch = [kh * 128, KD + kh * 128, 2 * KD + 2 * kh * 128, 2 * KD + (2 * kh + 1) * 128]
        for ci in range(4):
            cwg[:, kh, ci, :] = conv_w[:, ch[ci]:ch[ci] + 128].T
    d["cwg"] = cwg.reshape(128, NKH * 16)
    d["alog"] = np.ascontiguousarray(a_log, dtype=np.float32).reshape(1, NVH)
    d["dtb"] = np.ascontiguousarray(dt_bias, dtype=np.float32).reshape(1, NVH)
    if mode == "b":
        d["normw"] = np.ascontiguousarray(norm_w, dtype=np.float32).reshape(1, 128)
        g0 = 0
        gi = 0
        while g0 < NVH:
            G = min(16, NVH - g0)
            d["w_o_t%d" % gi] = tile_b(w_out, g0 * 128, G)
            g0 += G
            gi += 1
        d.update(ffn_inputs(cfg, w_gu, w_dn))
        d["ln"] = np.ascontiguousarray(ln4, dtype=np.float32)
    return d


_PROGS = {}


def _prog(name, cfg):
    if name not in _PROGS:
        if name == "sc":
            _PROGS[name] = build_sc_program(cfg)
        elif name == "ga":
            _PROGS[name] = build_gdn_program(cfg, "a")
        else:
            _PROGS[name] = build_gdn_c_program(cfg)
    return _PROGS[name]


def _halos(xs, cfg):
    out = []
    for cidx in range(N_CORES):
        if cidx % 4 == 0:
            out.append(np.zeros((4, cfg.D), np.float32))
        else:
            out.append(np.ascontiguousarray(xs[cidx - 1][-4:]))
    return out


def kernel(x, sc_w_in, sc_conv_w, sc_w_out, dn_w_in, dn_conv_w, dn_a_log, dn_dt_bias, dn_norm_w, dn_w_out,
           ffn_w_gate_up, ffn_w_down, ln_gain, ln_bias):
    cfg = Cfg()
    f = lambda a: np.ascontiguousarray(np.asarray(a), dtype=np.float32)
    x = f(x)
    NT = cfg.NT
    xs = [np.ascontiguousarray(x[ci // 4, (ci % 4) * NT:(ci % 4 + 1) * NT]) for ci in range(N_CORES)]
    cores = list(range(N_CORES))
    for layer in range(DEPTH):
        j = layer // 2
        ln4 = np.stack([f(ln_gain)[layer, 0], f(ln_bias)[layer, 0], f(ln_gain)[layer, 1], f(ln_bias)[layer, 1]])
        halos = _halos(xs, cfg)
        if layer % 2 == 0:
            base = sc_inputs(cfg, f(sc_w_in)[j], f(sc_conv_w)[j], f(sc_w_out)[j], f(ffn_w_gate_up)[layer],
                             f(ffn_w_down)[layer], ln4)
            maps = [dict(base, x=xs[ci], xh=halos[ci]) for ci in cores]
            res = run_bass_kernel_spmd(_prog("sc", cfg), maps, core_ids=cores)
            xs = [np.asarray(res.results[ci]["xo"]) for ci in cores]
        else:
            base_a = gdn_inputs(cfg, "a", f(dn_w_in)[j], f(dn_conv_w)[j], f(dn_a_log)[j], f(dn_dt_bias)[j])
            maps = [dict(base_a, x=xs[ci], xh=halos[ci]) for ci in cores]
            res = run_bass_kernel_spmd(_prog("ga", cfg), maps, core_ids=cores)
            sts = [np.asarray(res.results[ci]["st"]) for ci in cores]
            oqs = [np.asarray(res.results[ci]["oq"]) for ci in cores]
            base_b = gdn_c_inputs(cfg, f(dn_w_in)[j], f(dn_norm_w)[j], f(dn_w_out)[j], f(ffn_w_gate_up)[layer],
                                  f(ffn_w_down)[layer], ln4)
            eye = np.ascontiguousarray(np.broadcast_to(np.eye(128, dtype=np.float32), (cfg.NVH, 128, 128)))
            zer = np.zeros((cfg.NVH, 128, 128), np.float32)
            maps = []
            for ci in cores:
                q = ci % 4
                pts, ss = [eye] * (3 - q), [zer] * (3 - q)
                for p in range(ci - q, ci):
                    pts.append(np.ascontiguousarray(sts[p][:, :, 128:].transpose(0, 2, 1)))
                    ss.append(np.ascontiguousarray(sts[p][:, :, :128]))
                maps.append(dict(base_b, x=xs[ci], oq=oqs[ci], sin_pt=np.stack(pts), sin_s=np.stack(ss)))
            res = run_bass_kernel_spmd(_prog("gc", cfg), maps, core_ids=cores)
            xs = [np.asarray(res.results[ci]["xo"]) for ci in cores]
    out = np.empty((BATCH, SEQ, D_MODEL), np.float32)
    for ci in cores:
        out[ci // 4, (ci % 4) * NT:(ci % 4 + 1) * NT] = xs[ci]
    return out
```
